# Optimizing a Trainium2 kernel written in Bass

```python
import math
import jax, jax.numpy as jnp
from jax import lax
import numpy as np

D_MODEL = 1024
BATCH = 32
SEQ = 2048
DEPTH = 1

CTX_LEN = 256
GRID_W = 64
S5_WIDTH = 512
S5_GROUP = 16
S5_GROUPS = S5_WIDTH // S5_GROUP
S5_STATE = 64
RW_WIDTH = D_MODEL - S5_WIDTH
RW_HEAD = 64
RW_HEADS = RW_WIDTH // RW_HEAD
RW_DECAY_LORA = 32
RW_AAA_LORA = 32
RW_GATE_LORA = 96
RW_COLS = 3 * RW_WIDTH + RW_DECAY_LORA + RW_AAA_LORA + RW_GATE_LORA
IN_COLS = S5_WIDTH + RW_COLS
PEER_HEADS = 8
PEER_NKEYS = 128
PEER_EXPERTS = PEER_NKEYS * PEER_NKEYS
PEER_QDIM = 256
PEER_HALF = PEER_QDIM // 2
PEER_TOPK = 16
PEER_BLOCK = 128
NORM_EPS = 1e-6
GN_EPS = 64e-5

kernel_name = "hybrid_s5_rwkv7_peer_dit_block"

F32 = jnp.float32


def rmsnorm(x, g):
    xf = x.astype(F32)
    y = xf * lax.rsqrt(jnp.mean(xf * xf, axis=-1, keepdims=True) + NORM_EPS)
    return y.astype(x.dtype) * g


def modulate(h, shift, scale):
    return h * (1 + scale) + shift


def s5_discretize(a_re, a_im, log_dt, b_re, b_im):
    a_re, a_im = a_re.astype(F32), a_im.astype(F32)
    b_re, b_im = b_re.astype(F32), b_im.astype(F32)
    dt = jnp.exp(log_dt.astype(F32))[:, None]
    mag = jnp.exp(dt * a_re)
    ab_re, ab_im = mag * jnp.cos(dt * a_im), mag * jnp.sin(dt * a_im)
    nr, ni = ab_re - 1.0, ab_im
    den = a_re * a_re + a_im * a_im
    cf_re = (nr * a_re + ni * a_im) / den
    cf_im = (ni * a_re - nr * a_im) / den
    bb_re = cf_re[..., None] * b_re - cf_im[..., None] * b_im
    bb_im = cf_re[..., None] * b_im + cf_im[..., None] * b_re
    return ab_re, ab_im, bb_re, bb_im


def _complex_affine_combine(e1, e2):
    a1r, a1i, b1r, b1i = e1
    a2r, a2i, b2r, b2i = e2
    return (a2r * a1r - a2i * a1i, a2r * a1i + a2i * a1r,
            a2r * b1r - a2i * b1i + b2r, a2r * b1i + a2i * b1r + b2i)


def s5_scan(u, disc, h0):
    ab_re, ab_im, bb_re, bb_im = disc
    bn, length, _ = u.shape
    uf = u.astype(F32).reshape(bn, length, S5_GROUPS, S5_GROUP)
    bu_re = jnp.einsum('blgh,gph->blgp', uf, bb_re)
    bu_im = jnp.einsum('blgh,gph->blgp', uf, bb_im)
    if h0 is not None:
        h0_re, h0_im = h0
        bu_re = bu_re.at[:, 0].add(ab_re * h0_re - ab_im * h0_im)
        bu_im = bu_im.at[:, 0].add(ab_re * h0_im + ab_im * h0_re)
    a_re = jnp.broadcast_to(ab_re, (1, length) + ab_re.shape)
    a_im = jnp.broadcast_to(ab_im, (1, length) + ab_im.shape)
    _, _, h_re, h_im = lax.associative_scan(_complex_affine_combine, (a_re, a_im, bu_re, bu_im), axis=1)
    return h_re, h_im


def s5_readout(h, c_re, c_im):
    h_re, h_im = h
    y = (jnp.einsum('blgp,ghp->blgh', h_re, c_re.astype(F32))
         - jnp.einsum('blgp,ghp->blgh', h_im, c_im.astype(F32)))
    return y.reshape(y.shape[0], y.shape[1], S5_WIDTH)


def s5_glu(y, w, b):
    y1 = jax.nn.gelu(y, approximate=False)
    return y1 * jax.nn.sigmoid(y1 @ w + b)


def s5_mixer(u_ctx, u_lat, p, need_ctx):
    y_lat = p['s5_d'] * u_lat
    y_ctx = p['s5_d'] * u_ctx if need_ctx else None
    for d in range(2):
        disc = s5_discretize(p['s5_a_re'][d], p['s5_a_im'][d], p['s5_log_dt'][d],
                             p['s5_b_re'][d], p['s5_b_im'][d])
        flip = (lambda t: jnp.flip(t, axis=1)) if d == 1 else (lambda t: t)
        hc = s5_scan(flip(u_ctx), disc, None)
        hl = s5_scan(flip(u_lat), disc, (hc[0][:, -1], hc[1][:, -1]))
        y_lat = y_lat + flip(s5_readout(hl, p['s5_c_re'][d], p['s5_c_im'][d]))
        if need_ctx:
            y_ctx = y_ctx + flip(s5_readout(hc, p['s5_c_re'][d], p['s5_c_im'][d]))
    out_lat = s5_glu(y_lat, p['s5_w_glu'], p['s5_b_glu']).astype(u_lat.dtype)
    out_ctx = s5_glu(y_ctx, p['s5_w_glu'], p['s5_b_glu']).astype(u_ctx.dtype) if need_ctx else None
    return out_lat, out_ctx


def shift_grid(z, rows):
    bn, length, ch = z.shape
    zg = z.reshape(bn, rows, GRID_W, ch)
    left = jnp.pad(zg[:, :, :-1], ((0, 0), (0, 0), (1, 0), (0, 0)))
    right = jnp.pad(zg[:, :, 1:], ((0, 0), (0, 0), (0, 1), (0, 0)))
    up = jnp.pad(zg[:, :-1], ((0, 0), (1, 0), (0, 0), (0, 0)))
    down = jnp.pad(zg[:, 1:], ((0, 0), (0, 1), (0, 0), (0, 0)))
    sel = jnp.arange(ch) % 4
    out = jnp.where(sel == 0, left, jnp.where(sel == 1, right, jnp.where(sel == 2, up, down)))
    return out.reshape(bn, length, ch)


def shift_seq(z):
    prev = jnp.pad(z[:, :-1], ((0, 0), (1, 0), (0, 0)))
    nxt = jnp.pad(z[:, 1:], ((0, 0), (0, 1), (0, 0)))
    return jnp.where(jnp.arange(z.shape[-1]) % 2 == 0, prev, nxt)


def _heads(t):
    return t.reshape(t.shape[0], t.shape[1], RW_HEADS, RW_HEAD).astype(F32)


def rwkv_streams(z, shifted, p):
    z = z + (shifted - z) * p['rw_mu']
    cuts = [RW_WIDTH, 2 * RW_WIDTH, 3 * RW_WIDTH, 3 * RW_WIDTH + RW_DECAY_LORA,
            3 * RW_WIDTH + RW_DECAY_LORA + RW_AAA_LORA]
    r, k, v, wl, al, gl = jnp.split(z, cuts, axis=-1)
    g = jax.nn.sigmoid(gl) @ p['rw_w_g2']
    kk = _heads(k * p['rw_k_k'])
    kk = kk * lax.rsqrt(jnp.sum(kk * kk, axis=-1, keepdims=True) + 1e-12)
    return {'r': _heads(r), 'k': k, 'v': _heads(v), 'wl': wl, 'al': al, 'g': g, 'kk': kk}


def rwkv_direction(st, p, d):
    w = -jax.nn.softplus(-(p['rw_w0'][d] + jnp.tanh(st['wl']) @ p['rw_w_w2'][d])) - 0.5
    decay = jnp.exp(-jnp.exp(w.astype(F32)))
    a = jax.nn.sigmoid(p['rw_a0'][d] + st['al'] @ p['rw_w_a2'][d])
    kd = _heads(st['k'] * (1 + (a - 1) * p['rw_k_a']))
    bonus = jnp.sum(st['r'] * kd * p['rw_r_k'].astype(F32), axis=-1, keepdims=True) * st['v']
    tm = lambda t: jnp.moveaxis(t, 1, 0)
    xs = (tm(st['r']), tm(_heads(decay)), tm(kd), tm(st['v']), tm(st['kk']), tm(_heads(a)))
    return xs, bonus


def rwkv_scan(xs, s0, reverse):
    def step(s, inp):
        r, w, k, v, kk, a = inp
        sa = jnp.einsum('bhvk,bhk->bhv', s, -kk)
        s = s * w[:, :, None, :] + sa[..., None] * (kk * a)[:, :, None, :] + v[..., None] * k[:, :, None, :]
        return s, jnp.einsum('bhvk,bhk->bhv', s, r)
    return lax.scan(step, s0, xs, reverse=reverse)


def rwkv_output(ys_f, ys_b, bonus, g, p, dtype):
    y = jnp.moveaxis(ys_f + ys_b, 0, 1)
    mu = jnp.mean(y, axis=-1, keepdims=True)
    var = jnp.mean(jnp.square(y - mu), axis=-1, keepdims=True)
    yn = ((y - mu) * lax.rsqrt(var + GN_EPS) * p['rw_ln_w'].reshape(RW_HEADS, RW_HEAD).astype(F32)
          + p['rw_ln_b'].reshape(RW_HEADS, RW_HEAD).astype(F32))
    out = (yn + bonus).reshape(y.shape[0], y.shape[1], RW_WIDTH) * g
    return out.astype(dtype)


def rwkv7_mixer(z_ctx, z_lat, rows, p, need_ctx):
    st_c = rwkv_streams(z_ctx, shift_seq(z_ctx), p)
    st_l = rwkv_streams(z_lat, shift_grid(z_lat, rows), p)
    s0 = jnp.zeros((z_lat.shape[0], RW_HEADS, RW_HEAD, RW_HEAD), F32)
    ys_l, ys_c, bon_l, bon_c = [], [], [], []
    for d in range(2):
        xs_c, b_c = rwkv_direction(st_c, p, d)
        xs_l, b_l = rwkv_direction(st_l, p, d)
        s_ctx, y_c = rwkv_scan(xs_c, s0, d == 1)
        _, y_l = rwkv_scan(xs_l, s_ctx, d == 1)
        ys_l.append(y_l); bon_l.append(b_l); ys_c.append(y_c); bon_c.append(b_c)
    out_lat = rwkv_output(ys_l[0], ys_l[1], bon_l[0] + bon_l[1], st_l['g'], p, z_lat.dtype)
    out_ctx = rwkv_output(ys_c[0], ys_c[1], bon_c[0] + bon_c[1], st_c['g'], p, z_ctx.dtype) if need_ctx else None
    return out_lat, out_ctx


def peer_ffn(h, p):
    bn, length, dm = h.shape
    w_q, keys, u_tab, v_tab = p['peer_w_q'], p['peer_keys'], p['peer_u'], p['peer_v']

    def block(hb):
        q = (hb @ w_q).reshape(PEER_BLOCK, PEER_HEADS, 2, PEER_HALF)
        s = jnp.einsum('thcd,hcnd->thcn', q, keys)
        s1, i1 = lax.top_k(s[:, :, 0], PEER_TOPK)
        s2, i2 = lax.top_k(s[:, :, 1], PEER_TOPK)
        cand_s = (s1[..., :, None] + s2[..., None, :]).reshape(PEER_BLOCK, PEER_HEADS, PEER_TOPK * PEER_TOPK)
        cand_i = (i1[..., :, None] * PEER_NKEYS + i2[..., None, :]).reshape(PEER_BLOCK, PEER_HEADS, PEER_TOPK * PEER_TOPK)
        best_s, pos = lax.top_k(cand_s, PEER_TOPK)
        idx = jnp.take_along_axis(cand_i, pos, axis=-1)
        gate = jax.nn.softmax(best_s.astype(F32), axis=-1)
        u = jnp.take(u_tab, idx, axis=0)
        act = jax.nn.gelu(jnp.einsum('td,thkd->thk', hb, u).astype(F32), approximate=False)
        v = jnp.take(v_tab, idx, axis=0)
        return jnp.einsum('thk,thkd->td', (gate * act).astype(hb.dtype), v)

    out = lax.map(block, h.reshape(-1, PEER_BLOCK, dm))
    return out.reshape(bn, length, dm).astype(h.dtype)


def hybrid_layer(h, hc, c, c_ctx, p, rows, update_ctx):
    mod = jax.nn.silu(c) @ p['w_ada'] + p['b_ada']
    mod_c = jax.nn.silu(c_ctx) @ p['w_ada'] + p['b_ada']
    sh1, sc1, g1, sh2, sc2, g2 = jnp.split(mod[:, None, :], 6, axis=-1)
    csh1, csc1, cg1, csh2, csc2, cg2 = jnp.split(mod_c, 6, axis=-1)

    hn = modulate(rmsnorm(h, p['norm1_g']), sh1, sc1)
    hcn = modulate(rmsnorm(hc, p['norm1_g']), csh1, csc1)
    proj = hn @ p['w_in']
    proj_c = hcn @ p['w_in']
    s5_l, s5_c = s5_mixer(proj_c[..., :S5_WIDTH], proj[..., :S5_WIDTH], p, update_ctx)
    rw_l, rw_c = rwkv7_mixer(proj_c[..., S5_WIDTH:], proj[..., S5_WIDTH:], rows, p, update_ctx)
    h = h + g1 * (jnp.concatenate([s5_l, rw_l], axis=-1) @ p['w_out'])
    h = h + g2 * peer_ffn(modulate(rmsnorm(h, p['norm2_g']), sh2, sc2), p)
    if update_ctx:
        hc = hc + cg1 * (jnp.concatenate([s5_c, rw_c], axis=-1) @ p['w_out'])
        hc = hc + cg2 * peer_ffn(modulate(rmsnorm(hc, p['norm2_g']), csh2, csc2), p)
    return h, hc


def setup_inputs(seed: int = 0) -> dict:
    key = jax.random.key(seed)
    ks = jax.random.split(key, 40)
    n = lambda i, shape, s: jax.random.normal(ks[i], shape, F32) * s
    G, P, Hc = S5_GROUPS, S5_STATE, S5_GROUP
    a_im_base = jnp.pi * jnp.arange(P, dtype=F32)
    return {
        'x': n(0, (BATCH, SEQ, D_MODEL), 1.0),
        'c': n(1, (BATCH, D_MODEL), 1.0),
        'ctx': n(2, (BATCH, CTX_LEN, D_MODEL), 1.0),
        'c_ctx': n(3, (D_MODEL,), 1.0),
        'w_ada': n(4, (DEPTH, D_MODEL, 6 * D_MODEL), 0.5 * D_MODEL ** -0.5),
        'b_ada': n(5, (DEPTH, 6 * D_MODEL), 0.02),
        'norm1_g': 1.0 + n(6, (DEPTH, D_MODEL), 0.02),
        'norm2_g': 1.0 + n(7, (DEPTH, D_MODEL), 0.02),
        'w_in': n(8, (DEPTH, D_MODEL, IN_COLS), D_MODEL ** -0.5),
        's5_a_re': -0.5 + n(9, (DEPTH, 2, G, P), 0.01),
        's5_a_im': a_im_base + n(10, (DEPTH, 2, G, P), 0.01),
        's5_log_dt': jax.random.uniform(ks[11], (DEPTH, 2, G), F32, math.log(1e-3), math.log(1e-1)),
        's5_b_re': n(12, (DEPTH, 2, G, P, Hc), (2 * Hc) ** -0.5),
        's5_b_im': n(13, (DEPTH, 2, G, P, Hc), (2 * Hc) ** -0.5),
        's5_c_re': n(14, (DEPTH, 2, G, Hc, P), (2 * P) ** -0.5),
        's5_c_im': n(15, (DEPTH, 2, G, Hc, P), (2 * P) ** -0.5),
        's5_d': n(16, (DEPTH, S5_WIDTH), 1.0),
        's5_w_glu': n(17, (DEPTH, S5_WIDTH, S5_WIDTH), S5_WIDTH ** -0.5),
        's5_b_glu': n(18, (DEPTH, S5_WIDTH), 0.02),
        'rw_mu': jax.random.uniform(ks[19], (DEPTH, RW_COLS), F32, 0.0, 1.0),
        'rw_w0': jax.random.uniform(ks[20], (DEPTH, 2, RW_WIDTH), F32, -6.5, -1.5),
        'rw_w_w2': n(21, (DEPTH, 2, RW_DECAY_LORA, RW_WIDTH), 0.1 * RW_DECAY_LORA ** -0.5),
        'rw_a0': n(22, (DEPTH, 2, RW_WIDTH), 0.1),
        'rw_w_a2': n(23, (DEPTH, 2, RW_AAA_LORA, RW_WIDTH), 0.1 * RW_AAA_LORA ** -0.5),
        'rw_w_g2': n(24, (DEPTH, RW_GATE_LORA, RW_WIDTH), RW_GATE_LORA ** -0.5),
        'rw_k_k': 0.85 + n(25, (DEPTH, RW_WIDTH), 0.02),
        'rw_k_a': 1.0 + n(26, (DEPTH, RW_WIDTH), 0.02),
        'rw_r_k': n(27, (DEPTH, RW_HEADS, RW_HEAD), 0.1),
        'rw_ln_w': 1.0 + n(28, (DEPTH, RW_WIDTH), 0.02),
        'rw_ln_b': n(29, (DEPTH, RW_WIDTH), 0.02),
        'w_out': n(30, (DEPTH, D_MODEL, D_MODEL), D_MODEL ** -0.5),
        'peer_w_q': n(31, (DEPTH, D_MODEL, PEER_HEADS * PEER_QDIM), D_MODEL ** -0.5),
        'peer_keys': n(32, (DEPTH, PEER_HEADS, 2, PEER_NKEYS, PEER_HALF), PEER_HALF ** -0.5),
        'peer_u': n(33, (DEPTH, PEER_EXPERTS, D_MODEL), D_MODEL ** -0.5),
        'peer_v': n(34, (DEPTH, PEER_EXPERTS, D_MODEL), 0.5 * PEER_HEADS ** -0.5),
        'norm_f_g': 1.0 + n(35, (D_MODEL,), 0.02),
    }


def reference(x, c, ctx, c_ctx, w_ada, b_ada, norm1_g, norm2_g, w_in, s5_a_re, s5_a_im, s5_log_dt,
              s5_b_re, s5_b_im, s5_c_re, s5_c_im, s5_d, s5_w_glu, s5_b_glu, rw_mu, rw_w0, rw_w_w2,
              rw_a0, rw_w_a2, rw_w_g2, rw_k_k, rw_k_a, rw_r_k, rw_ln_w, rw_ln_b, w_out, peer_w_q,
              peer_keys, peer_u, peer_v, norm_f_g):
    rows = x.shape[1] // GRID_W
    h, hc = x, ctx
    for l in range(DEPTH):
        p = {
            'w_ada': w_ada[l], 'b_ada': b_ada[l], 'norm1_g': norm1_g[l], 'norm2_g': norm2_g[l],
            'w_in': w_in[l], 's5_a_re': s5_a_re[l], 's5_a_im': s5_a_im[l], 's5_log_dt': s5_log_dt[l],
            's5_b_re': s5_b_re[l], 's5_b_im': s5_b_im[l], 's5_c_re': s5_c_re[l], 's5_c_im': s5_c_im[l],
            's5_d': s5_d[l], 's5_w_glu': s5_w_glu[l], 's5_b_glu': s5_b_glu[l], 'rw_mu': rw_mu[l],
            'rw_w0': rw_w0[l], 'rw_w_w2': rw_w_w2[l], 'rw_a0': rw_a0[l], 'rw_w_a2': rw_w_a2[l],
            'rw_w_g2': rw_w_g2[l], 'rw_k_k': rw_k_k[l], 'rw_k_a': rw_k_a[l], 'rw_r_k': rw_r_k[l],
            'rw_ln_w': rw_ln_w[l], 'rw_ln_b': rw_ln_b[l], 'w_out': w_out[l], 'peer_w_q': peer_w_q[l],
            'peer_keys': peer_keys[l], 'peer_u': peer_u[l], 'peer_v': peer_v[l],
        }
        h, hc = hybrid_layer(h, hc, c, c_ctx, p, rows, l < DEPTH - 1)
    return rmsnorm(h, norm_f_g)
```

```python
import numpy as np
from contextlib import ExitStack
import concourse.bass as bass
import concourse.mybir as mybir
from concourse.bass_utils import run_bass_kernel_spmd

F32 = mybir.dt.float32
BF16 = mybir.dt.bfloat16
I32 = mybir.dt.int32
U32 = mybir.dt.uint32
AF = mybir.ActivationFunctionType
ALU = mybir.AluOpType
AX = mybir.AxisListType

D = 1024
SEQ = 2048
CTX = 256
LT = SEQ + CTX
INC = 2208
NCORES = 8
NBATCH = 32


class Dep:
    __slots__ = ("w", "r", "excl")

    def __init__(self, excl=False):
        self.w = None
        self.r = {}
        self.excl = excl


def PD():
    return Dep(excl=True)


class Sch:
    ROT = 30000

    def __init__(self, nc, es):
        self.nc = nc
        self.es = es
        self.eng = {"pe": nc.tensor, "dve": nc.vector, "act": nc.scalar, "pool": nc.gpsimd, "sp": nc.sync}
        self.cur = {}
        self.cnt = {}
        self.seen = {e: {} for e in self.eng}
        self.nsem = 0
        for e in self.eng:
            self._newsem(e)
        self.dsems = []
        for i in range(40):
            s = es.enter_context(nc.semaphore(f"dq{i}"))
            self.dsems.append([s, 0])
        self.dnext = 0
        self.swsems = []
        for i in range(16):
            s = es.enter_context(nc.semaphore(f"sq{i}"))
            self.swsems.append([s, 0])
        self.swnext = 0
        self.semobj = {}
        self.ninst = 0
        import os
        self.skip_own = set(os.environ.get("SKIP_OWN", "pe").split(","))

    def _newsem(self, e):
        s = self.es.enter_context(self.nc.semaphore(f"e_{e}_{self.nsem}"))
        self.nsem += 1
        self.cur[e] = s
        self.cnt[e] = 0

    def _wait(self, en, deps):
        best = {}
        for (s, v) in deps:
            k = id(s)
            if k not in best or best[k][1] < v:
                best[k] = (s, v)
        seen = self.seen[en]
        for k, (s, v) in best.items():
            if seen.get(k, 0) >= v:
                continue
            self.eng[en].wait_ge(s, v)
            self.nwait = getattr(self, "nwait", 0) + 1
            seen[k] = v

    def _deps(self, reads, writes):
        deps = []
        for d in reads:
            if d.w is not None:
                deps.append(d.w)
        for d in writes:
            if d.w is not None:
                deps.append(d.w)
            deps.extend(d.r.values())
        return deps

    def _mark(self, ev, reads, writes):
        for d in reads:
            d.r[id(ev[0])] = ev
        for d in writes:
            d.w = ev
            d.r = {}

    def op(self, en, fn, reads=(), writes=()):
        ex = [d for d in reads if d.excl]
        if ex:
            reads = [d for d in reads if not d.excl]
            writes = list(writes) + ex
        deps = self._deps(reads, writes)
        if en in self.skip_own:
            own = id(self.cur[en])
            deps = [d for d in deps if id(d[0]) != own]
        self._wait(en, deps)
        ins = fn(self.eng[en])
        if self.cnt[en] >= self.ROT:
            self._newsem(en)
        self.cnt[en] += 1
        ins.then_inc(self.cur[en], 1)
        ev = (self.cur[en], self.cnt[en])
        self._mark(ev, reads, writes)
        self.ninst += 1
        return ev

    def dma(self, q, fn, reads=(), writes=()):
        if q == "pool":
            slot = self.swsems[self.swnext]
            self.swnext = (self.swnext + 1) % len(self.swsems)
        else:
            slot = self.dsems[self.dnext]
            self.dnext = (self.dnext + 1) % len(self.dsems)
        deps = self._deps(reads, writes)
        if slot[1] > 0:
            deps.append((slot[0], slot[1]))
        self._wait(q, deps)
        ins = fn(self.eng[q])
        slot[1] += 16
        ins.then_inc(slot[0], 16)
        ev = (slot[0], slot[1])
        self._mark(ev, reads, writes)
        self.ninst += 1
        return ev

    def barrier(self):
        evs = [(self.cur[e], self.cnt[e]) for e in self.eng if self.cnt[e] > 0]
        evs += [(s, v) for (s, v) in self.dsems + self.swsems if v > 0]
        for e in self.eng:
            self._wait(e, evs)


def _mm(out, lhsT, rhs, start=True, stop=True, tp=None):
    return lambda e: e.matmul(out, lhsT, rhs, start=start, stop=stop, tile_position=tp)


class K:
    pass


def build(NB=4, upto=99, dbg=()):
    nc = bass.Bass("TRN2", target_bir_lowering=False)
    es = ExitStack()
    k = K()
    k.nc, k.es, k.NB = nc, es, NB
    S = k.S = Sch(nc, es)

    def din(name, shape, dt=F32):
        return nc.dram_tensor(name, list(shape), dt, kind="ExternalInput").ap()

    I = k.I = {}
    I["x"] = din("x", [NB, SEQ, D])
    I["c"] = din("c", [NB, D])
    I["ctx"] = din("ctx", [NB, CTX, D])
    I["c_ctx"] = din("c_ctx", [1, D])
    I["w_ada"] = din("w_ada", [D, 6 * D])
    I["b_ada"] = din("b_ada", [48, 128])
    I["norm1_g"] = din("norm1_g", [8, 128])
    I["norm2_g"] = din("norm2_g", [1, D])
    I["w_in"] = din("w_in", [D, INC])
    I["s5_arow"] = din("s5_arow", [3, 32, 128])
    I["s5_bT"] = din("s5_bT", [2, 128, 1024])
    I["s5_cblk"] = din("s5_cblk", [2, 128, 8, 128])
    I["s5_vec"] = din("s5_vec", [8, 128])
    I["s5_w_glu"] = din("s5_w_glu", [512, 512])
    I["rw_vec"] = din("rw_vec", [42, 128])
    I["rw_wlora"] = din("rw_wlora", [64, 2, 512])
    I["rw_w_g2"] = din("rw_w_g2", [96, 512])
    I["rw_ln"] = din("rw_ln", [2, 512])
    I["w_out"] = din("w_out", [D, D])
    I["peer_w_q"] = din("peer_w_q", [D, 2048])
    I["peer_keys"] = din("peer_keys", [128, 16, 128])
    I["peer_uv"] = din("peer_uv", [16384, 2048])
    I["nvec"] = din("nvec", [2, D])
    k.out = nc.dram_tensor("out", [NB, SEQ, D], F32, kind="ExternalOutput").ap()
    k.dbg = {}
    for (name, shape) in dbg:
        if name in ("PT", "MIXT") or shape is None:
            continue
        k.dbg[name] = nc.dram_tensor(name, list(shape), F32, kind="ExternalOutput").ap()
    dbgn = [d[0] for d in dbg]
    k.PT = nc.dram_tensor("PT", [NB, INC, LT], F32, kind="ExternalOutput" if "PT" in dbgn else "Internal").ap()
    k.PT_dep = [Dep() for _ in range(NB)]
    k.MIXT = nc.dram_tensor("MIXT", [NB, D, SEQ], F32, kind="ExternalOutput" if "MIXT" in dbgn else "Internal").ap()
    k.MIXT_dep = [Dep() for _ in range(NB)]
    k.UVB = nc.dram_tensor("UVB", [16384, 2048], BF16, kind="Internal").ap()
    k.UVB_dep = Dep()

    with es:
        setup_consts(k)
        k.modT = sb(k, "modT", [128, 48, NB + 1])
        k.gs1T = sb(k, "gs1T", [128, 8, NB + 1])
        stage0_mod(k)
        if upto >= 1:
            stage1_proj(k)
        if upto >= 2 and "skip_s5" not in dbgn:
            with ExitStack() as es2:
                k.es_stage = es2
                s5_alloc(k)
                s5_setup(k)
                stage2_s5(k)
        if upto >= 3:
            with ExitStack() as es3:
                k.es_stage = es3
                rw_setup(k)
                stage3_rwkv(k)
        if upto >= 4:
            cast_uv(k)
            stage4_peer(k)
        S.barrier()
        print("ninst", S.ninst, "nwait", getattr(S, "nwait", 0))
    return nc


def sb(k, name, shape, dt=F32):
    return k.es.enter_context(k.nc.sbuf_tensor(name, list(shape), dt))


def ps(k, name, shape, dt=F32):
    return k.es.enter_context(k.nc.psum_tensor(name, list(shape), dt))


def setup_consts(k):
    nc, S = k.nc, k.S
    k.ident = sb(k, "ident", [128, 128])
    k.ident_d = Dep()
    k.iota_p = sb(k, "iota_p", [128, 1], I32)
    k.iota_f = sb(k, "iota_f", [128, 128], I32)
    k.iota_d = Dep()
    S.op("pool", lambda e: e.iota(k.iota_p[:], [[0, 1]], base=0, channel_multiplier=1), writes=[k.iota_d])
    S.op("pool", lambda e: e.iota(k.iota_f[:], [[1, 128]], base=0, channel_multiplier=0), writes=[k.iota_d])
    k.iota_pf = sb(k, "iota_pf", [128, 1])
    k.iota_ff = sb(k, "iota_ff", [128, 128])
    S.op("dve", lambda e: e.tensor_copy(k.iota_pf[:], k.iota_p[:]), reads=[k.iota_d], writes=[k.ident_d])
    S.op("dve", lambda e: e.tensor_copy(k.iota_ff[:], k.iota_f[:]), reads=[k.iota_d], writes=[k.ident_d])
    S.op("dve", lambda e: e.tensor_scalar(k.ident[:], k.iota_ff[:], k.iota_pf[:, 0:1], None, op0=ALU.is_equal),
         reads=[k.ident_d], writes=[k.ident_d])


def stage0_mod(k):
    nc, S, I, NB = k.nc, k.S, k.I, k.NB
    NC5 = NB + 1
    with ExitStack() as es:
        def sbl(name, shape, dt=F32):
            return es.enter_context(nc.sbuf_tensor(name, list(shape), dt))
        crow = sbl("crow", [NC5, D])
        crow_d = Dep()
        S.dma("sp", lambda e: e.dma_start(out=crow[0:NB, :], in_=I["c"][:, :]), writes=[crow_d])
        S.dma("sp", lambda e: e.dma_start(out=crow[NB:NC5, :], in_=I["c_ctx"][:, :]), writes=[crow_d])
        S.op("act", lambda e: e.activation(crow[:], crow[:], AF.Silu), reads=[crow_d], writes=[crow_d])
        cT = sbl("cT", [128, 8, NC5])
        cT_d = Dep()
        vst = sbl("vst", [64, 128])
        vst_d = Dep()
        S.dma("sp", lambda e: e.dma_start(out=vst[0:48, :], in_=I["b_ada"][:, :]), writes=[vst_d])
        S.dma("sp", lambda e: e.dma_start(out=vst[48:56, :], in_=I["norm1_g"][:, :]), writes=[vst_d])
        vT = sbl("vT", [128, 56])
        vT_d = Dep()
        with nc.psum_tensor("p0a", [128, 8, NC5], F32) as pa, nc.psum_tensor("p0b", [128, 56], F32) as pb, \
                nc.psum_tensor("p0c", [128, 48, NC5], F32) as pc:
            pa_d, pb_d, pc_d = PD(), PD(), PD()
            for kd in range(8):
                S.op("pe", lambda e, kd=kd: e.transpose(pa[:, kd, :], crow[0:NC5, kd * 128:(kd + 1) * 128],
                                                        k.ident[0:NC5, 0:NC5]),
                     reads=[crow_d, k.ident_d], writes=[pa_d])
            S.op("dve", lambda e: e.tensor_copy(cT[:], pa[:]), reads=[pa_d], writes=[cT_d])
            S.op("pe", lambda e: e.transpose(pb[:, :], vst[0:56, :], k.ident[0:56, 0:56]),
                 reads=[vst_d, k.ident_d], writes=[pb_d])
            S.op("dve", lambda e: e.tensor_copy(vT[:], pb[:]), reads=[pb_d], writes=[vT_d])
            wt = [sbl(f"wada{i}", [128, 8, 512]) for i in range(2)]
            wt_d = [Dep(), Dep()]
            wv = I["w_ada"].rearrange("(kd p) n -> p kd n", p=128)
            for blk in range(12):
                t, td = wt[blk % 2], wt_d[blk % 2]
                for kd in range(8):
                    S.dma("sp" if kd % 2 == 0 else "act",
                          lambda e, kd=kd, t=t, blk=blk: e.dma_start(out=t[:, kd, :], in_=wv[:, kd, blk * 512:(blk + 1) * 512]),
                          writes=[td])
                for jj in range(4):
                    j = blk * 4 + jj
                    for kd in range(8):
                        S.op("pe", _mm(pc[:, j, :], t[:, kd, jj * 128:(jj + 1) * 128], cT[:, kd, :],
                                       start=(kd == 0), stop=(kd == 7)),
                             reads=[td, cT_d], writes=[pc_d])
            k.modT_d = Dep()
            S.op("dve", lambda e: e.tensor_tensor(k.modT[:], pc[:], vT[:, 0:48].unsqueeze(2).to_broadcast([128, 48, NC5]),
                                                  op=ALU.add),
                 reads=[pc_d, vT_d], writes=[k.modT_d])
        S.op("dve", lambda e: e.tensor_scalar(k.gs1T[:], k.modT[:, 8:16, :], 1.0, None, op0=ALU.add),
             reads=[k.modT_d], writes=[k.modT_d])
        S.op("dve", lambda e: e.tensor_tensor(k.gs1T[:], k.gs1T[:], vT[:, 48:56].unsqueeze(2).to_broadcast([128, 8, NC5]),
                                              op=ALU.mult),
             reads=[vT_d, k.modT_d], writes=[k.modT_d])
        k.S.barrier()


def stage1_proj(k):
    nc, S, I, NB = k.nc, k.S, k.I, k.NB
    with ExitStack() as es:
        def sbl(name, shape, dt=F32):
            return es.enter_context(nc.sbuf_tensor(name, list(shape), dt))
        wbf = sbl("w_in_bf", [128, 8, INC], BF16)
        wbf_d = Dep()
        for kd in range(8):
            S.dma("pool", lambda e, kd=kd: e.dma_start(out=wbf[:, kd, :], in_=I["w_in"][kd * 128:(kd + 1) * 128, :],
                                                       max_dma_last_dim=4096), writes=[wbf_d])
        xt = [sbl(f"xt{i}", [128, D]) for i in range(2)]
        xt_d = [Dep(), Dep()]
        xs = [sbl(f"xs{i}", [128, D]) for i in range(2)]
        xs_d = [Dep(), Dep()]
        junk = sbl("junk1", [128, D])
        junk_d = Dep()
        st = [sbl(f"st{i}", [128, 4]) for i in range(2)]
        hnT = [sbl(f"hnT{i}", [128, 8, 512], BF16) for i in range(2)]
        hnT_d = [Dep(), Dep()]
        ev = [sbl(f"ev{i}", [128, 512]) for i in range(3)]
        ev_d = [Dep() for _ in range(3)]
        ptr = [es.enter_context(nc.psum_tensor(f"ptr{i}", [128, 8, 128], F32)) for i in range(2)]
        ptr_d = [PD(), PD()]
        pmm = [es.enter_context(nc.psum_tensor(f"pmm{i}", [128, 512], F32)) for i in range(3)]
        pmm_d = [PD() for _ in range(3)]
        fch = [(i * 128, 128) for i in range(16)] + [(2048, 64), (2112, 96)]
        k.fch = fch
        ti = 0
        gi = 0
        ei = 0
        for b in range(NB):
            groups = [("ctx", 0, 256)] + [("lat", g * 512, 512) for g in range(4)]
            for (kind, t0, nt) in groups:
                h, hd = hnT[gi % 2], hnT_d[gi % 2]
                gi += 1
                col = b if kind == "lat" else NB
                for tt in range(nt // 128):
                    x_t, x_d = xt[ti % 2], xt_d[ti % 2]
                    xs_t, xsd = xs[ti % 2], xs_d[ti % 2]
                    s_t = st[ti % 2]
                    p_t, p_d = ptr[ti % 2], ptr_d[ti % 2]
                    ti += 1
                    src = I["x"][b, t0 + tt * 128:t0 + (tt + 1) * 128, :] if kind == "lat" else \
                        I["ctx"][b, tt * 128:(tt + 1) * 128, :]
                    S.dma("sp", lambda e, x_t=x_t, src=src: e.dma_start(out=x_t[:], in_=src), writes=[x_d])
                    S.op("act", lambda e, x_t=x_t, s_t=s_t: e.activation(junk[:], x_t[:], AF.Square, accum_out=s_t[:, 0:1]),
                         reads=[x_d], writes=[junk_d, xsd])
                    S.op("act", lambda e, s_t=s_t: e.activation(s_t[:, 1:2], s_t[:, 0:1], AF.Sqrt, bias=1e-6, scale=1.0 / D),
                         reads=[xsd], writes=[xsd])
                    S.op("dve", lambda e, s_t=s_t: e.reciprocal(s_t[:, 2:3], s_t[:, 1:2]), reads=[xsd], writes=[xsd])
                    S.op("act", lambda e, x_t=x_t, xs_t=xs_t, s_t=s_t: e.activation(xs_t[:], x_t[:], AF.Copy, scale=s_t[:, 2:3]),
                         reads=[x_d, xsd], writes=[xsd])
                    for kd in range(8):
                        S.op("pe", lambda e, kd=kd, p_t=p_t, xs_t=xs_t: e.transpose(p_t[:, kd, :], xs_t[:, kd * 128:(kd + 1) * 128],
                                                                                      k.ident[:]),
                             reads=[xsd, k.ident_d], writes=[p_d])
                    for kd in range(8):
                        S.op("dve", lambda e, kd=kd, p_t=p_t, h=h, tt=tt, col=col: e.tensor_scalar(
                            h[:, kd, tt * 128:(tt + 1) * 128], p_t[:, kd, :], k.gs1T[:, kd, col:col + 1],
                            k.modT[:, kd, col:col + 1], op0=ALU.mult, op1=ALU.add),
                            reads=[p_d, k.modT_d], writes=[hd])
                tok0 = t0 if kind == "ctx" else CTX + t0
                for fi, (c0, ncol) in enumerate(fch):
                    pm, pmd = pmm[ei % 3], pmm_d[ei % 3]
                    e_t, e_d = ev[ei % 3], ev_d[ei % 3]
                    ei += 1
                    for kd in range(8):
                        S.op("pe", _mm(pm[0:ncol, 0:nt], wbf[:, kd, c0:c0 + ncol], h[:, kd, 0:nt], start=(kd == 0), stop=(kd == 7)),
                             reads=[wbf_d, hd], writes=[pmd])
                    eng = "act" if fi % 2 == 0 else "dve"
                    if eng == "act":
                        S.op("act", lambda e, pm=pm, e_t=e_t, ncol=ncol, nt=nt: e.copy(e_t[0:ncol, 0:nt], pm[0:ncol, 0:nt]),
                             reads=[pmd], writes=[e_d])
                    else:
                        S.op("dve", lambda e, pm=pm, e_t=e_t, ncol=ncol, nt=nt: e.tensor_copy(e_t[0:ncol, 0:nt], pm[0:ncol, 0:nt]),
                             reads=[pmd], writes=[e_d])
                    S.dma("sp", lambda e, e_t=e_t, ncol=ncol, nt=nt, c0=c0, tok0=tok0, b=b: e.dma_start(
                        out=k.PT[b, c0:c0 + ncol, tok0:tok0 + nt], in_=e_t[0:ncol, 0:nt]),
                        reads=[e_d], writes=[k.PT_dep[b]])
        S.barrier()


_CACHE = {}


def _prep_inputs(inputs, NB, core):
    sl = slice(core * NB, (core + 1) * NB)
    f = lambda a: np.ascontiguousarray(a, dtype=np.float32)
    m = {
        "x": f(inputs["x"][sl]),
        "c": f(inputs["c"][sl]),
        "ctx": f(inputs["ctx"][sl]),
        "c_ctx": f(inputs["c_ctx"].reshape(1, D)),
        "w_ada": f(inputs["w_ada"][0]),
        "b_ada": f(inputs["b_ada"][0].reshape(48, 128)),
        "norm1_g": f(inputs["norm1_g"][0].reshape(8, 128)),
        "norm2_g": f(inputs["norm2_g"][0].reshape(1, D)),
        "w_in": f(inputs["w_in"][0]),
    }
    m.update(_s5_layout(inputs))
    m.update(_rw_layout(inputs))
    m.update(_peer_layout(inputs))
    return m


def kernel(**inputs):
    NB = NBATCH // NCORES
    if "nc" not in _CACHE:
        _CACHE["nc"] = build(NB)
    nc = _CACHE["nc"]
    in_maps = [_prep_inputs(inputs, NB, c) for c in range(NCORES)]
    res = run_bass_kernel_spmd(nc, in_maps, core_ids=list(range(NCORES)))
    return np.concatenate([r["out"] for r in res.results], axis=0)


def _s5_layout(inputs):
    f = lambda a: np.ascontiguousarray(a, dtype=np.float32)
    a_re, a_im, ldt = inputs["s5_a_re"][0], inputs["s5_a_im"][0], inputs["s5_log_dt"][0]
    arow = np.zeros((3, 32, 128), np.float32)
    arow[0] = a_re.reshape(2, 16, 128).reshape(32, 128)
    arow[1] = a_im.reshape(2, 16, 128).reshape(32, 128)
    arow[2] = np.repeat(ldt.reshape(2, 16, 2, 1), 64, axis=3).reshape(32, 128)
    bT = np.zeros((2, 2, 64, 2, 4, 4, 2, 16), np.float32)
    cb = np.zeros((2, 4, 2, 16, 2, 4, 2, 64), np.float32)
    for ri, (bsrc, csrc) in enumerate([(inputs["s5_b_re"][0], inputs["s5_c_re"][0]),
                                       (inputs["s5_b_im"][0], inputs["s5_c_im"][0])]):
        bg = bsrc.reshape(2, 4, 4, 2, 64, 16)
        cg = csrc.reshape(2, 4, 4, 2, 16, 64)
        for gl in range(2):
            bT[ri, gl, :, :, :, :, gl, :] = np.transpose(bg[:, :, :, gl], (3, 0, 1, 2, 4))
            cb[ri, :, gl, :, :, :, gl, :] = np.transpose(cg[:, :, :, gl], (2, 3, 0, 1, 4))
    vec = np.zeros((8, 128), np.float32)
    vec[0:4] = inputs["s5_d"][0].reshape(4, 128)
    vec[4:8] = inputs["s5_b_glu"][0].reshape(4, 128)
    return {"s5_arow": arow, "s5_bT": f(bT.reshape(2, 128, 1024)), "s5_cblk": f(cb.reshape(2, 128, 8, 128)),
            "s5_vec": vec, "s5_w_glu": f(inputs["s5_w_glu"][0])}


def sbs(k, name, shape, dt=F32):
    return k.es_stage.enter_context(k.nc.sbuf_tensor("S_" + name, list(shape), dt))


def s5_alloc(k):
    sb = sbs
    k.winj = [sb(k, f"winj{i}", [128, 8, 128]) for i in range(2)]
    k.rout = [sb(k, f"rout{i}", [128, 8, 128]) for i in range(2)]
    k.pw = [sb(k, f"pw{i}", [128, 32, 17]) for i in range(3)]
    k.lam = [sb(k, f"lam{i}", [128, 32, 8]) for i in range(3)]
    k.s5vT = sb(k, "s5vT", [128, 8])
    k.wglu = sb(k, "wglu", [128, 4, 512])
    k.s5_d = Dep()


def s5_setup(k):
    nc, S, I = k.nc, k.S, k.I
    T = S.op
    with ExitStack() as es:
        def sbl(name, shape, dt=F32):
            return es.enter_context(nc.sbuf_tensor(name, list(shape), dt))
        d0 = Dep()
        rows = sbl("s5rows", [32, 3, 128])
        S.dma("sp", lambda e: e.dma_start(out=rows[:], in_=I["s5_arow"].rearrange("a r c -> r a c")), writes=[d0])
        vrow = sbl("s5vrow", [8, 128])
        S.dma("sp", lambda e: e.dma_start(out=vrow[:], in_=I["s5_vec"][:, :]), writes=[d0])
        S.dma("sp", lambda e: e.dma_start(out=k.wglu[:], in_=I["s5_w_glu"].rearrange("(kc p) n -> p kc n", p=128)),
              writes=[k.s5_d])
        bT = [sbl(f"s5bT{i}", [128, 32, 32]) for i in range(2)]
        cblk = [sbl(f"s5cb{i}", [128, 8, 128]) for i in range(2)]
        for i in range(2):
            S.dma("sp", lambda e, i=i: e.dma_start(out=bT[i][:], in_=I["s5_bT"][i].rearrange("p (a b) -> p a b", b=32)), writes=[d0])
            S.dma("act", lambda e, i=i: e.dma_start(out=cblk[i][:], in_=I["s5_cblk"][i]), writes=[d0])
        aT = sbl("s5aT", [128, 3, 32])
        W = [sbl(f"s5w{i}", [128, 32]) for i in range(14)]
        Wi = sbl("s5wi", [128, 32], I32)
        bb = [sbl(f"s5bb{i}", [128, 32, 32]) for i in range(2)]
        tmp = [sbl(f"s5tmp{i}", [128, 32, 32]) for i in range(2)]
        with nc.psum_tensor("ps5a", [128, 3, 32], F32) as pa, nc.psum_tensor("ps5b", [128, 8], F32) as pb, \
                nc.psum_tensor("ps5c", [128, 4, 128], F32) as pc:
            pd = PD()
            for a in range(3):
                T("pe", lambda e, a=a: e.transpose(pa[:, a, :], rows[:, a, :], k.ident[0:32, 0:32]), reads=[d0, k.ident_d], writes=[pd])
            T("dve", lambda e: e.tensor_copy(aT[:], pa[:]), reads=[pd], writes=[d0])
            T("pe", lambda e: e.transpose(pb[:, :], vrow[:, :], k.ident[0:8, 0:8]), reads=[d0, k.ident_d], writes=[pd])
            T("dve", lambda e: e.tensor_copy(k.s5vT[:], pb[:]), reads=[pd], writes=[k.s5_d])
            are, aim, ldt = aT[:, 0, :], aT[:, 1, :], aT[:, 2, :]
            dt, mag, ang, sn, cs, abr, abi, t1, t2, nr, cfr, cfi, rden, t3 = [w[:] for w in W]

            def tt(o, a, b, op):
                T("dve", lambda e: e.tensor_tensor(o, a, b, op=op), reads=[d0], writes=[d0])

            T("act", lambda e: e.activation(dt, ldt, AF.Exp), reads=[d0], writes=[d0])
            tt(t1, dt, are, ALU.mult)
            T("act", lambda e: e.activation(mag, t1, AF.Exp), reads=[d0], writes=[d0])
            tt(ang, dt, aim, ALU.mult)

            def rsin(o, phase):
                T("dve", lambda e: e.tensor_scalar(t1, ang, 1.0 / (2 * np.pi), phase, op0=ALU.mult, op1=ALU.add), reads=[d0], writes=[d0])
                T("dve", lambda e: e.tensor_copy(Wi[:], t1), reads=[d0], writes=[d0])
                T("dve", lambda e: e.tensor_copy(t2, Wi[:]), reads=[d0], writes=[d0])
                tt(t1, t1, t2, ALU.subtract)
                T("dve", lambda e: e.scalar_tensor_tensor(t2, t1, 0.0, t1, op0=ALU.is_lt, op1=ALU.add), reads=[d0], writes=[d0])
                T("dve", lambda e: e.tensor_scalar(t2, t2, 2 * np.pi, -np.pi, op0=ALU.mult, op1=ALU.add), reads=[d0], writes=[d0])
                T("dve", lambda e: e.tensor_scalar(t2, t2, 3.1415925, -3.1415925, op0=ALU.min, op1=ALU.max), reads=[d0], writes=[d0])
                T("act", lambda e: e.activation(o, t2, AF.Sin), reads=[d0], writes=[d0])

            rsin(sn, 0.5)
            rsin(cs, 0.75)
            tt(abr, mag, cs, ALU.mult)
            tt(abi, mag, sn, ALU.mult)
            T("dve", lambda e: e.tensor_scalar(nr, abr, -1.0, None, op0=ALU.add), reads=[d0], writes=[d0])
            tt(t1, are, are, ALU.mult)
            tt(t2, aim, aim, ALU.mult)
            tt(t1, t1, t2, ALU.add)
            T("dve", lambda e: e.reciprocal(rden, t1), reads=[d0], writes=[d0])
            tt(t1, nr, are, ALU.mult)
            tt(t2, abi, aim, ALU.mult)
            tt(t1, t1, t2, ALU.add)
            tt(cfr, t1, rden, ALU.mult)
            tt(t1, abi, are, ALU.mult)
            tt(t2, nr, aim, ALU.mult)
            tt(t1, t1, t2, ALU.subtract)
            tt(cfi, t1, rden, ALU.mult)
            cfrb = cfr.unsqueeze(2).to_broadcast([128, 32, 32])
            cfib = cfi.unsqueeze(2).to_broadcast([128, 32, 32])
            tt(tmp[0][:], bT[0][:], cfrb, ALU.mult)
            tt(tmp[1][:], bT[1][:], cfib, ALU.mult)
            tt(bb[0][:], tmp[0][:], tmp[1][:], ALU.subtract)
            tt(tmp[0][:], bT[1][:], cfrb, ALU.mult)
            tt(tmp[1][:], bT[0][:], cfib, ALU.mult)
            tt(bb[1][:], tmp[0][:], tmp[1][:], ALU.add)
            for ri in range(2):
                for half in range(2):
                    for cc in range(4):
                        dc = half * 4 + cc
                        T("pe", lambda e, ri=ri, dc=dc, cc=cc: e.transpose(
                            pc[:, cc, :], bb[ri][:, dc * 4:(dc + 1) * 4, :].rearrange("p a b -> p (a b)"), k.ident[:]),
                            reads=[d0, k.ident_d], writes=[pd])
                    T("dve", lambda e, ri=ri, half=half: e.tensor_copy(k.winj[ri][:, half * 4:(half + 1) * 4, :], pc[:]),
                      reads=[pd], writes=[k.s5_d])
            for ri in range(2):
                for half in range(2):
                    for cc in range(4):
                        dc = half * 4 + cc
                        T("pe", lambda e, ri=ri, dc=dc, cc=cc: e.transpose(pc[:, cc, :], cblk[ri][:, dc, :], k.ident[:]),
                          reads=[d0, k.ident_d], writes=[pd])
                    if ri == 0:
                        T("dve", lambda e, half=half: e.tensor_copy(k.rout[0][:, half * 4:(half + 1) * 4, :], pc[:]),
                          reads=[pd], writes=[k.s5_d])
                    else:
                        T("dve", lambda e, half=half: e.tensor_scalar(k.rout[1][:, half * 4:(half + 1) * 4, :], pc[:], -1.0, None,
                                                                      op0=ALU.mult), reads=[pd], writes=[k.s5_d])
            pr, pi_, pn = k.pw
            T("dve", lambda e: e.memset(pr[:, :, 0:1], 1.0), writes=[k.s5_d])
            T("dve", lambda e: e.memset(pi_[:, :, 0:1], 0.0), writes=[k.s5_d])

            def cmul(o_r, o_i, a_r, a_i, b_r, b_i, dep):
                T("dve", lambda e: e.tensor_tensor(t1, a_r, b_r, op=ALU.mult), reads=[dep, d0], writes=[d0])
                T("dve", lambda e: e.tensor_tensor(t2, a_i, b_i, op=ALU.mult), reads=[dep, d0], writes=[d0])
                T("dve", lambda e: e.tensor_tensor(t3, a_r, b_i, op=ALU.mult), reads=[dep, d0], writes=[d0])
                T("dve", lambda e: e.tensor_tensor(rden, a_i, b_r, op=ALU.mult), reads=[dep, d0], writes=[d0])
                T("dve", lambda e: e.tensor_tensor(o_r, t1, t2, op=ALU.subtract), reads=[d0], writes=[dep])
                T("dve", lambda e: e.tensor_tensor(o_i, t3, rden, op=ALU.add), reads=[d0], writes=[dep])

            for n in range(1, 17):
                cmul(pr[:, :, n], pi_[:, :, n], pr[:, :, n - 1], pi_[:, :, n - 1], abr, abi, k.s5_d)
            T("dve", lambda e: e.tensor_scalar(pn[:], pi_[:], -1.0, None, op0=ALU.mult), reads=[k.s5_d], writes=[k.s5_d])
            lr, li, ln = k.lam
            T("dve", lambda e: e.tensor_copy(lr[:, :, 0], pr[:, :, 16]), reads=[k.s5_d], writes=[k.s5_d])
            T("dve", lambda e: e.tensor_copy(li[:, :, 0], pi_[:, :, 16]), reads=[k.s5_d], writes=[k.s5_d])
            for n in range(1, 8):
                cmul(lr[:, :, n], li[:, :, n], lr[:, :, n - 1], li[:, :, n - 1], lr[:, :, n - 1], li[:, :, n - 1], k.s5_d)
            T("dve", lambda e: e.tensor_scalar(ln[:], li[:], -1.0, None, op0=ALU.mult), reads=[k.s5_d], writes=[k.s5_d])
        S.barrier()


def stage2_s5(k):
    nc, S, I, NB = k.nc, k.S, k.I, k.NB
    T = S.op
    NSC = LT // 16
    with ExitStack() as es:
        def sbl(name, shape, dt=F32):
            return es.enter_context(nc.sbuf_tensor(name, list(shape), dt))
        uT = [sbl(f"s5u{i}", [128, LT]) for i in range(2)]
        uT_d = [Dep(), Dep()]
        Z = [[[sbl(f"s5z{jj}{d}{ri}", [128, LT]) for ri in range(2)] for d in range(2)] for jj in range(2)]
        Z_d = [[Dep(), Dep()] for jj in range(2)]
        Bp = [[[[sbl(f"s5B{jj}{d}{pp}{ri}", [128, NSC]) for ri in range(2)] for pp in range(2)] for d in range(2)] for jj in range(2)]
        B_d = [[[Dep(), Dep()] for d in range(2)] for jj in range(2)]
        y1 = sbl("s5y1", [128, 4, SEQ])
        y1_d = Dep()
        og = [sbl(f"s5og{i}", [128, 512]) for i in range(2)]
        og_d = [Dep(), Dep()]
        pin = [es.enter_context(nc.psum_tensor(f"s5pin{i}", [128, 512], F32)) for i in range(2)]
        pin_d = [PD(), PD()]
        py = es.enter_context(nc.psum_tensor("s5py", [128, 4, 512], F32))
        py_d = PD()
        pg = [es.enter_context(nc.psum_tensor(f"s5pg{i}", [128, 512], F32)) for i in range(2)]
        pg_d = [PD(), PD()]
        blocks = [(0, 256)] + [(256 + i * 512, 512) for i in range(4)]
        ipi = [0]

        def s5_chain(j, d, jj):
            q = cur_c[0] * 4 + j
            c = cur_c[0]
            u, ud = cur_u[0], cur_ud[0]
            dq = d * 16 + q
            zr, zi = Z[jj][d]
            zd = Z_d[jj][d]
            Bp_ = Bp[jj][d]
            B_d_ = B_d[jj][d]
            for (c0, n) in blocks:
                if d == 0:
                    z0 = c0
                else:
                    z0 = (c0 - CTX) if c0 >= CTX else SEQ
                for ri in range(2):
                    pp, ppd = pin[ipi[0] % 2], pin_d[ipi[0] % 2]
                    ipi[0] += 1
                    T("pe", _mm(pp[:, 0:n], k.winj[ri][32 * j:32 * j + 32, d * 4 + c, :], u[32 * j:32 * j + 32, c0:c0 + n], tp=(32 * j, 0)),
                      reads=[k.s5_d, ud], writes=[ppd])
                    T("act", lambda e, pp=pp, n=n, z0=z0, ri=ri: e.copy(Z[jj][d][ri][:, z0:z0 + n], pp[:, 0:n]),
                      reads=[ppd], writes=[zd])
                yield
            zrv = zr[:].rearrange("p (j r) -> p j r", r=16)
            ziv = zi[:].rearrange("p (j r) -> p j r", r=16)
            ar = k.pw[0][:, dq, 1:2]
            ai = k.pw[1][:, dq, 1:2]
            nai = k.pw[2][:, dq, 1:2]

            def stt(o, a, sc, bb_, rd, wr):
                T("dve", lambda e: e.scalar_tensor_tensor(o, a, sc, bb_, op0=ALU.mult, op1=ALU.add), reads=rd, writes=wr)

            order = range(1, 16) if d == 0 else range(14, -1, -1)
            for r in order:
                rp = r - 1 if d == 0 else r + 1
                stt(zrv[:, :, r], zrv[:, :, rp], ar, zrv[:, :, r], [zd, k.s5_d], [zd])
                yield
                stt(zrv[:, :, r], ziv[:, :, rp], nai, zrv[:, :, r], [zd, k.s5_d], [zd])
                yield
                stt(ziv[:, :, r], ziv[:, :, rp], ar, ziv[:, :, r], [zd, k.s5_d], [zd])
                yield
                stt(ziv[:, :, r], zrv[:, :, rp], ai, ziv[:, :, r], [zd, k.s5_d], [zd])
                yield
            rb = 15 if d == 0 else 0
            cur, nxt = 0, 1
            T("pool", lambda e: e.tensor_copy(Bp_[0][0][:], zrv[:, :, rb]), reads=[zd], writes=[B_d_[0]])
            T("pool", lambda e: e.tensor_copy(Bp_[0][1][:], ziv[:, :, rb]), reads=[zd], writes=[B_d_[0]])
            yield
            for lv in range(8):
                sh = 1 << lv
                lr_ = k.lam[0][:, dq, lv:lv + 1]
                li_ = k.lam[1][:, dq, lv:lv + 1]
                nli = k.lam[2][:, dq, lv:lv + 1]
                src, dst = Bp_[cur], Bp_[nxt]
                sd, dd = B_d_[cur], B_d_[nxt]
                if d == 0:
                    o_sl, i_sl, k_sl = slice(sh, NSC), slice(0, NSC - sh), slice(0, sh)
                else:
                    o_sl, i_sl, k_sl = slice(0, NSC - sh), slice(sh, NSC), slice(NSC - sh, NSC)
                T("pool", lambda e: e.tensor_copy(dst[0][:, k_sl], src[0][:, k_sl]), reads=[sd], writes=[dd])
                T("pool", lambda e: e.tensor_copy(dst[1][:, k_sl], src[1][:, k_sl]), reads=[sd], writes=[dd])
                stt(dst[0][:, o_sl], src[0][:, i_sl], lr_, src[0][:, o_sl], [sd, k.s5_d], [dd])
                yield
                stt(dst[0][:, o_sl], src[1][:, i_sl], nli, dst[0][:, o_sl], [sd, k.s5_d], [dd])
                yield
                stt(dst[1][:, o_sl], src[1][:, i_sl], lr_, src[1][:, o_sl], [sd, k.s5_d], [dd])
                yield
                stt(dst[1][:, o_sl], src[0][:, i_sl], li_, dst[1][:, o_sl], [sd, k.s5_d], [dd])
                yield
                cur, nxt = nxt, cur
            Sf, Sfd = Bp_[cur], B_d_[cur]
            for r in range(16):
                n = (r + 1) if d == 0 else (16 - r)
                pr_ = k.pw[0][:, dq, n:n + 1]
                pi_ = k.pw[1][:, dq, n:n + 1]
                pni = k.pw[2][:, dq, n:n + 1]
                if d == 0:
                    zs, ss = slice(16, NSC), slice(15, NSC - 1)
                else:
                    zs, ss = slice(0, 128), slice(1, 129)
                stt(zrv[:, zs, r], Sf[0][:, ss], pr_, zrv[:, zs, r], [zd, Sfd, k.s5_d], [zd])
                yield
                stt(zrv[:, zs, r], Sf[1][:, ss], pni, zrv[:, zs, r], [zd, Sfd, k.s5_d], [zd])
                yield
                stt(ziv[:, zs, r], Sf[1][:, ss], pr_, ziv[:, zs, r], [zd, Sfd, k.s5_d], [zd])
                yield
                stt(ziv[:, zs, r], Sf[0][:, ss], pi_, ziv[:, zs, r], [zd, Sfd, k.s5_d], [zd])
                yield
        cur_c, cur_u, cur_ud = [0], [None], [None]
        for b in range(NB):
            for c in range(4):
                u, ud = uT[c % 2], uT_d[c % 2]
                cur_c[0], cur_u[0], cur_ud[0] = c, u, ud
                S.dma("sp", lambda e, u=u, b=b, c=c: e.dma_start(out=u[:], in_=k.PT[b, c * 128:(c + 1) * 128, :]),
                      reads=[k.PT_dep[b]], writes=[ud])
                for jp in range(2):
                    chains = []
                    for jj in range(2):
                        for d in range(2):
                            chains.append(s5_chain(jp * 2 + jj, d, jj))
                    while chains:
                        for g_ in list(chains):
                            try:
                                next(g_)
                            except StopIteration:
                                chains.remove(g_)
                    for jj in range(2):
                        j = jp * 2 + jj
                        for tb in range(4):
                            terms = []
                            for d in range(2):
                                l0 = (CTX if d == 0 else 0) + tb * 512
                                terms.append((k.rout[0][:, d * 4 + c, 32 * j:32 * j + 32], Z[jj][d][0][:, l0:l0 + 512], Z_d[jj][d]))
                                terms.append((k.rout[1][:, d * 4 + c, 32 * j:32 * j + 32], Z[jj][d][1][:, l0:l0 + 512], Z_d[jj][d]))
                            for ti, (lh, rh, dd_) in enumerate(terms):
                                T("pe", _mm(py[32 * j:32 * j + 32, tb, :], lh, rh, start=(ti == 0), stop=(ti == 3), tp=(0, 32 * j)),
                                  reads=[k.s5_d, dd_], writes=[py_d])
                for tb in range(4):
                    T("dve", lambda e, tb=tb, c=c, u=u: e.scalar_tensor_tensor(
                        y1[:, c, tb * 512:(tb + 1) * 512], u[:, CTX + tb * 512:CTX + (tb + 1) * 512], k.s5vT[:, c:c + 1], py[:, tb, :],
                        op0=ALU.mult, op1=ALU.add), reads=[py_d, ud, k.s5_d], writes=[y1_d])
                T("act", lambda e, c=c: e.activation(y1[:, c, :], y1[:, c, :], AF.Gelu), reads=[y1_d], writes=[y1_d])
            gi = 0
            for m in range(4):
                for tb in range(4):
                    p_, pd_ = pg[gi % 2], pg_d[gi % 2]
                    o_, od_ = og[gi % 2], og_d[gi % 2]
                    gi += 1
                    for kc in range(4):
                        T("pe", _mm(p_[:], k.wglu[:, kc, m * 128:(m + 1) * 128], y1[:, kc, tb * 512:(tb + 1) * 512],
                                    start=(kc == 0), stop=(kc == 3)), reads=[k.s5_d, y1_d], writes=[pd_])
                    T("act", lambda e, p_=p_, o_=o_, m=m: e.activation(o_[:], p_[:], AF.Sigmoid, bias=k.s5vT[:, 4 + m:5 + m]),
                      reads=[pd_, k.s5_d], writes=[od_])
                    T("dve", lambda e, o_=o_, m=m, tb=tb: e.tensor_tensor(o_[:], o_[:], y1[:, m, tb * 512:(tb + 1) * 512], op=ALU.mult),
                      reads=[od_, y1_d], writes=[od_])
                    S.dma("sp", lambda e, o_=o_, m=m, tb=tb, b=b: e.dma_start(
                        out=k.MIXT[b, m * 128:(m + 1) * 128, tb * 512:(tb + 1) * 512], in_=o_[:]),
                        reads=[od_], writes=[k.MIXT_dep[b]])
        S.barrier()


def _rw_layout(inputs):
    f = lambda a: np.ascontiguousarray(a, dtype=np.float32)
    vec = np.zeros((42, 128), np.float32)
    mu = inputs["rw_mu"][0]
    vec[0:12] = mu[0:1536].reshape(12, 128)
    vec[12, 0:64] = mu[1536:1600]
    vec[13, 0:96] = mu[1600:1696]
    vec[14:18] = inputs["rw_k_k"][0].reshape(4, 128)
    vec[18:22] = inputs["rw_k_a"][0].reshape(4, 128)
    vec[22:26] = inputs["rw_r_k"][0].reshape(4, 128)
    vec[26:34] = inputs["rw_w0"][0].reshape(8, 128)
    vec[34:42] = inputs["rw_a0"][0].reshape(8, 128)
    wl = np.zeros((64, 2, 512), np.float32)
    wl[0:32] = np.transpose(inputs["rw_w_w2"][0], (1, 0, 2))
    wl[32:64] = np.transpose(inputs["rw_w_a2"][0], (1, 0, 2))
    ln = np.stack([inputs["rw_ln_w"][0], inputs["rw_ln_b"][0]], 0)
    return {"rw_vec": vec, "rw_wlora": f(wl), "rw_w_g2": f(inputs["rw_w_g2"][0]), "rw_ln": f(ln)}


def rw_setup(k):
    nc, S, I = k.nc, k.S, k.I
    T = S.op
    k.rwc = Dep()
    k.rvT = sbs(k, "rvT", [128, 42])
    k.omm = sbs(k, "rw_omm", [128, 14])
    k.muq = sbs(k, "rw_muq", [128, 14, 4])
    k.mue = sbs(k, "rw_mue", [128, 14, 2])
    k.omka = sbs(k, "rw_omka", [128, 4])
    k.wlora = sbs(k, "rw_wlora", [64, 2, 512])
    k.wg2 = sbs(k, "rw_wg2", [96, 512])
    k.lnbc = sbs(k, "rw_lnbc", [128, 2, 512])
    k.bones = sbs(k, "rw_bones", [128, 128])
    k.hsel = sbs(k, "rw_hsel", [128, 2])
    k.mup = sbs(k, "rw_mup", [128, 256])
    k.mlo = sbs(k, "rw_mlo", [128, 256])
    k.ones = sbs(k, "rw_ones", [128, 128])
    with ExitStack() as es:
        def sbl(name, shape, dt=F32):
            return es.enter_context(nc.sbuf_tensor(name, list(shape), dt))
        d0 = Dep()
        vrow = sbl("rwvrow", [42, 128])
        lnrow = sbl("rwlnrow", [1, 2, 512])
        m4 = sbl("rwm4", [128, 4])
        pm4i = sbl("rwpm4i", [128, 1], I32)
        pm4 = sbl("rwpm4", [128, 1])
        one1 = sbl("rwone1", [1, 128])
        S.dma("sp", lambda e: e.dma_start(out=vrow[:], in_=I["rw_vec"][:, :]), writes=[d0])
        S.dma("sp", lambda e: e.dma_start(out=lnrow[:], in_=I["rw_ln"].rearrange("(o a) n -> o a n", o=1)), writes=[d0])
        S.dma("sp", lambda e: e.dma_start(out=k.wlora[:], in_=I["rw_wlora"]), writes=[k.rwc])
        S.dma("sp", lambda e: e.dma_start(out=k.wg2[:], in_=I["rw_w_g2"][:, :]), writes=[k.rwc])
        with nc.psum_tensor("prw0", [128, 42], F32) as p0, nc.psum_tensor("prw1", [128, 2, 512], F32) as p1:
            pd = PD()
            T("pe", lambda e: e.transpose(p0[:, :], vrow[:, :], k.ident[0:42, 0:42]), reads=[d0, k.ident_d], writes=[pd])
            T("dve", lambda e: e.tensor_copy(k.rvT[:], p0[:]), reads=[pd], writes=[k.rwc])
            T("dve", lambda e: e.memset(one1[:], 1.0), writes=[d0])
            for a in range(2):
                T("pe", _mm(p1[:, a, :], one1[0:1, :], lnrow[0:1, a, :]), reads=[d0], writes=[pd])
            T("dve", lambda e: e.tensor_copy(k.lnbc[:], p1[:]), reads=[pd], writes=[k.rwc])
        T("dve", lambda e: e.tensor_scalar(k.omm[:], k.rvT[:, 0:14], -1.0, 1.0, op0=ALU.mult, op1=ALU.add), reads=[k.rwc], writes=[k.rwc])
        T("dve", lambda e: e.tensor_scalar(k.omka[:], k.rvT[:, 18:22], -1.0, 1.0, op0=ALU.mult, op1=ALU.add), reads=[k.rwc], writes=[k.rwc])
        T("dve", lambda e: e.tensor_single_scalar(pm4i[:], k.iota_p[:], 3, op=ALU.bitwise_and), reads=[k.iota_d], writes=[d0])
        T("dve", lambda e: e.tensor_copy(pm4[:], pm4i[:]), reads=[d0], writes=[d0])
        T("dve", lambda e: e.tensor_scalar(m4[:], k.iota_ff[:, 0:4], pm4[:, 0:1], None, op0=ALU.is_equal), reads=[d0, k.ident_d], writes=[d0])
        T("dve", lambda e: e.tensor_tensor(k.muq[:], k.rvT[:, 0:14].unsqueeze(2).to_broadcast([128, 14, 4]),
                                           m4[:].unsqueeze(1).to_broadcast([128, 14, 4]), op=ALU.mult), reads=[d0, k.rwc], writes=[k.rwc])
        T("dve", lambda e: e.tensor_tensor(k.mue[:], k.muq[:, :, 0:2], k.muq[:, :, 2:4], op=ALU.add), reads=[k.rwc], writes=[k.rwc])
        T("dve", lambda e: e.memset(k.bones[:], 0.0), writes=[k.rwc])
        T("dve", lambda e: e.memset(k.bones[0:64, 0:64], 1.0), writes=[k.rwc])
        T("dve", lambda e: e.memset(k.bones[64:128, 64:128], 1.0), writes=[k.rwc])
        T("dve", lambda e: e.memset(k.hsel[:], 0.0), writes=[k.rwc])
        T("dve", lambda e: e.memset(k.hsel[0:64, 0:1], 1.0), writes=[k.rwc])
        T("dve", lambda e: e.memset(k.hsel[64:128, 1:2], 1.0), writes=[k.rwc])
        T("dve", lambda e: e.memset(k.ones[:], 1.0), writes=[k.rwc])
        for (tile_, c0, op) in [(k.mup, 0, ALU.is_gt), (k.mup, 128, ALU.is_ge), (k.mlo, 0, ALU.is_lt), (k.mlo, 128, ALU.is_le)]:
            T("dve", lambda e, tile_=tile_, c0=c0, op=op: e.tensor_scalar(tile_[:, c0:c0 + 128], k.iota_ff[:], k.iota_pf[:, 0:1], None, op0=op),
              reads=[k.ident_d], writes=[k.rwc])
    S.barrier()


def stage3_rwkv(k):
    nc, S, I, NB = k.nc, k.S, k.I, k.NB
    T = S.op
    NCH = LT // 128
    C = 128
    import os
    with ExitStack() as es:
        def sbl(name, shape, dt=F32):
            return es.enter_context(nc.sbuf_tensor(name, list(shape), dt))

        def psl(name, shape):
            return es.enter_context(nc.psum_tensor(name, list(shape), F32))
        WD = BF16 if os.environ.get("RW_BF16", "1") == "1" else F32
        ND = F32 if os.environ.get("RW_NEU32", "1") == "1" else WD
        lora = sbl("rw_lora", [128, LT]); lora_d = Dep()
        sg = sbl("rw_sg", [128, LT]); sg_d = Dep()
        zb = sbl("rw_zb", [128, LT]); zb_d = Dep()
        rT = sbl("rw_r", [128, LT]); kT = sbl("rw_k", [128, LT]); kkT = sbl("rw_kk", [128, LT])
        base_d = Dep()
        Vm = sbl("rw_Vm", [128, NCH, 128]); Vm_d = Dep()
        tA = sbl("rw_tA", [128, LT]); tB = sbl("rw_tB", [128, LT]); tC = sbl("rw_tC", [128, LT]); tD = sbl("rw_tD", [128, LT])
        tmp_d = Dep()
        ARt = sbl("rw_AR", [128, NCH, 2, C], WD); Bt = sbl("rw_Bt", [128, LT], WD); Kt = sbl("rw_Kt", [128, LT], WD)
        Vmb = sbl("rw_Vmb", [128, NCH, 128], WD); identb = sbl("rw_identb", [128, 128], WD); Tstb = sbl("rw_Tb", [128, 64], WD)
        T("dve", lambda e: e.tensor_copy(identb[:], k.ident[:]), reads=[k.ident_d], writes=[k.rwc])
        feat_d = Dep()
        PC = sbl("rw_PC", [128, NCH]); tot = sbl("rw_tot", [128, NCH])
        kdsum = sbl("rw_kdsum", [128, LT]); kds_d = Dep()
        Ysum = sbl("rw_Y", [128, 16, 128]); Y_d = Dep()
        gtm = sbl("rw_gtm", [128, 16, 128]); gtm_d = Dep()
        coef = sbl("rw_coef", [128, 16, 2]); coef_d = Dep()
        gn = [sbl(f"rw_gn{i}", [128, 32]) for i in range(4)]
        Tst = sbl("rw_T", [128, 64]); T_d = Dep()
        T_dh = [Dep(), Dep()]
        Y_dh = [Dep(), Dep()]
        Ttmp = sbl("rw_Ttmp", [128, 64])
        NN = [sbl(f"rw_N{i}", [128, 128], ND) for i in range(4)]; NN_d = [Dep() for _ in range(4)]
        NT_ = [sbl(f"rw_NT{i}", [128, 128], ND) for i in range(4)]; NT_d = [Dep() for _ in range(4)]
        XX = [sbl(f"rw_X{i}", [128, 128], ND) for i in range(4)]; XX_d = [Dep() for _ in range(4)]
        AA = [sbl(f"rw_AA{i}", [128, 512], WD) for i in range(2)]; AA_d = [Dep() for _ in range(2)]
        AN = [sbl(f"rw_AN{i}", [128, 128], ND) for i in range(2)]
        Wsb = [sbl(f"rw_W{i}", [128, 64], ND) for i in range(2)]; Wsb_d = [Dep(), Dep()]
        Usb = [sbl(f"rw_U{i}", [128, 64], WD) for i in range(2)]; Usb_d = [Dep(), Dep()]
        BKtm = [sbl(f"rw_BK{i}", [128, 2, 128], WD) for i in range(2)]; BK_d = [Dep(), Dep()]
        ot = [sbl(f"rw_ot{i}", [128, 512]) for i in range(2)]; ot_d = [Dep(), Dep()]
        pA = [psl(f"rw_pA{i}", [128, 512]) for i in range(2)]; pA_d = [PD(), PD()]
        pN = [psl(f"rw_pN{i}", [128, 512]) for i in range(4)]; pN_d = [PD() for _ in range(4)]
        pS = [psl(f"rw_pS{i}", [128, 512]) for i in range(2)]; pS_d = [PD(), PD()]
        cnt = {"pn": 0, "pa": 0, "nn": 0, "nt": 0, "xx": 0, "hc": 0}

        def next_pn():
            i = cnt["pn"] % 4
            cnt["pn"] += 1
            return pN[i][:, 0:128], pN_d[i]

        def next_pa():
            i = cnt["pa"] % 2
            cnt["pa"] += 1
            return pA[i], pA_d[i]

        def mix_chunk(b, ch, nrows, dst, dst_d):
            r0 = 512 + (ch * 128 if ch < 12 else (1536 if ch == 12 else 1600))
            S.dma("sp", lambda e: e.dma_start(out=zb[0:nrows, :], in_=k.PT[b, r0:r0 + nrows, :]), reads=[k.PT_dep[b]], writes=[zb_d])
            P = slice(0, nrows)
            T("dve", lambda e: e.tensor_scalar(dst[P, :], zb[P, :], k.omm[P, ch:ch + 1], None, op0=ALU.mult),
              reads=[zb_d, k.rwc], writes=[dst_d])

            def acc(o, i_, sc):
                T("dve", lambda e: e.scalar_tensor_tensor(o, i_, sc, o, op0=ALU.mult, op1=ALU.add), reads=[zb_d, k.rwc, dst_d], writes=[dst_d])
            zl = zb[P, CTX:LT].rearrange("p (r c) -> p r c", c=64)
            dl = dst[P, CTX:LT].rearrange("p (r c) -> p r c", c=64)
            acc(dl[:, :, 1:64], zl[:, :, 0:63], k.muq[P, ch, 0:1])
            acc(dl[:, :, 0:63], zl[:, :, 1:64], k.muq[P, ch, 1:2])
            acc(dst[P, CTX + 64:LT], zb[P, CTX:LT - 64], k.muq[P, ch, 2:3])
            acc(dst[P, CTX:LT - 64], zb[P, CTX + 64:LT], k.muq[P, ch, 3:4])
            acc(dst[P, 1:CTX], zb[P, 0:CTX - 1], k.mue[P, ch, 0:1])
            acc(dst[P, 0:CTX - 1], zb[P, 1:CTX], k.mue[P, ch, 1:2])

        blocks = [(0, 512), (512, 512), (1024, 512), (1536, 512), (2048, 256)]
        import os
        STOP = int(os.environ.get("RW_STOP", "99"))
        for b in range(NB):
            mix_chunk(b, 12, 64, lora, lora_d)
            T("act", lambda e: e.activation(lora[0:32, :], lora[0:32, :], AF.Tanh), reads=[lora_d], writes=[lora_d])
            mix_chunk(b, 13, 96, sg, sg_d)
            T("act", lambda e: e.activation(sg[0:96, :], sg[0:96, :], AF.Sigmoid), reads=[sg_d], writes=[sg_d])
            if STOP <= 1:
                break
            for hp in range(4):
                mix_chunk(b, hp, 128, rT, base_d)
                mix_chunk(b, 4 + hp, 128, kT, base_d)
                mix_chunk(b, 8 + hp, 128, tA, tmp_d)
                for ci in range(NCH):
                    pp, ppd = next_pn()
                    T("pe", lambda e, pp=pp, ci=ci: e.transpose(pp, tA[:, ci * 128:(ci + 1) * 128], k.ident[:]),
                      reads=[tmp_d, k.ident_d], writes=[ppd])
                    T("act", lambda e, pp=pp, ci=ci: e.copy(Vm[:, ci, :], pp), reads=[ppd], writes=[Vm_d])
                    T("dve", lambda e, pp=pp, ci=ci: e.tensor_copy(Vmb[:, ci, :], pp), reads=[ppd], writes=[Vm_d])
                T("dve", lambda e: e.tensor_scalar(kkT[:], kT[:], k.rvT[:, 14 + hp:15 + hp], None, op0=ALU.mult), reads=[base_d, k.rwc], writes=[base_d])
                T("dve", lambda e: e.tensor_tensor(tB[:], kkT[:], kkT[:], op=ALU.mult), reads=[base_d], writes=[tmp_d])
                for (c0, n) in blocks:
                    pp, ppd = next_pa()
                    T("pe", _mm(pp[:, 0:n], k.bones[:], tB[:, c0:c0 + n]), reads=[tmp_d, k.rwc], writes=[ppd])
                    T("act", lambda e, pp=pp, c0=c0, n=n: e.activation(tC[:, c0:c0 + n], pp[:, 0:n], AF.Sqrt, bias=1e-12), reads=[ppd], writes=[tmp_d])
                T("dve", lambda e: e.reciprocal(tC[:], tC[:]), reads=[tmp_d], writes=[tmp_d])
                T("dve", lambda e: e.tensor_tensor(kkT[:], kkT[:], tC[:], op=ALU.mult), reads=[tmp_d, base_d], writes=[base_d])
                for lc in range(16):
                    pp, ppd = next_pn()
                    T("pe", _mm(pp, sg[0:96, CTX + lc * 128:CTX + (lc + 1) * 128], k.wg2[0:96, hp * 128:(hp + 1) * 128]),
                      reads=[sg_d, k.rwc], writes=[ppd])
                    T("act", lambda e, pp=pp, lc=lc: e.copy(gtm[:, lc, :], pp), reads=[ppd], writes=[gtm_d])
                if STOP <= 2:
                    break
                for d in range(2):
                    for (c0, n) in blocks:
                        pp, ppd = next_pa()
                        T("pe", _mm(pp[:, 0:n], k.wlora[0:32, d, hp * 128:(hp + 1) * 128], lora[0:32, c0:c0 + n]), reads=[lora_d, k.rwc], writes=[ppd])
                        T("act", lambda e, pp=pp, c0=c0, n=n, d=d, hp=hp: e.activation(
                            tA[:, c0:c0 + n], pp[:, 0:n], AF.Sigmoid, bias=k.rvT[:, 26 + d * 4 + hp:27 + d * 4 + hp]), reads=[ppd, k.rwc], writes=[tmp_d])
                        pp, ppd = next_pa()
                        T("pe", _mm(pp[:, 0:n], k.wlora[32:64, d, hp * 128:(hp + 1) * 128], lora[32:64, c0:c0 + n], tp=(32, 0)),
                          reads=[lora_d, k.rwc], writes=[ppd])
                        T("act", lambda e, pp=pp, c0=c0, n=n, d=d, hp=hp: e.activation(
                            tB[:, c0:c0 + n], pp[:, 0:n], AF.Sigmoid, bias=k.rvT[:, 34 + d * 4 + hp:35 + d * 4 + hp]), reads=[ppd, k.rwc], writes=[tmp_d])
                    T("dve", lambda e: e.tensor_scalar(tA[:], tA[:], -0.6065306597126334, None, op0=ALU.mult), reads=[tmp_d], writes=[tmp_d])
                    T("dve", lambda e: e.tensor_scalar(tC[:], tB[:], k.rvT[:, 18 + hp:19 + hp], k.omka[:, hp:hp + 1], op0=ALU.mult, op1=ALU.add),
                      reads=[tmp_d, k.rwc], writes=[tmp_d])
                    T("dve", lambda e: e.tensor_tensor(tC[:], tC[:], kT[:], op=ALU.mult), reads=[tmp_d, base_d], writes=[tmp_d])
                    if d == 0:
                        T("pool", lambda e: e.tensor_copy(kdsum[:], tC[:]), reads=[tmp_d], writes=[kds_d])
                    else:
                        T("pool", lambda e: e.tensor_tensor(kdsum[:], kdsum[:], tC[:], op=ALU.add), reads=[tmp_d, kds_d], writes=[kds_d])
                    for ci in range(NCH):
                        T("dve", lambda e, ci=ci: e.tensor_tensor_scan(tD[:, ci * C:(ci + 1) * C], k.ones[:], tA[:, ci * C:(ci + 1) * C], 0.0,
                                                                      op0=ALU.mult, op1=ALU.add), reads=[tmp_d, k.rwc], writes=[tmp_d])
                    tDv = tD[:].rearrange("p (c t) -> p c t", t=C)
                    T("dve", lambda e: e.tensor_copy(tot[:], tDv[:, :, C - 1]), reads=[tmp_d], writes=[feat_d])
                    if d == 1:
                        T("dve", lambda e: e.tensor_tensor(tD[:], tA[:], tD[:], op=ALU.subtract), reads=[tmp_d], writes=[tmp_d])
                        T("dve", lambda e: e.tensor_tensor(tDv, tDv, tot[:].unsqueeze(2).to_broadcast([128, NCH, C]), op=ALU.add),
                          reads=[tmp_d, feat_d], writes=[tmp_d])
                    T("act", lambda e: e.activation(PC[:], tot[:], AF.Exp), reads=[feat_d], writes=[feat_d])
                    ARv0 = ARt[:, :, 0, :]
                    ARv1 = ARt[:, :, 1, :]
                    tAv = tA[:].rearrange("p (c t) -> p c t", t=C)
                    T("dve", lambda e: e.tensor_tensor(tA[:], tD[:], tA[:], op=ALU.subtract), reads=[tmp_d], writes=[tmp_d])
                    T("act", lambda e: e.activation(tA[:], tA[:], AF.Exp), reads=[tmp_d], writes=[tmp_d])
                    T("dve", lambda e: e.scalar_tensor_tensor(ARv0, kkT[:].rearrange("p (c t) -> p c t", t=C), -1.0, tAv, op0=ALU.mult, op1=ALU.mult),
                      reads=[tmp_d, base_d], writes=[feat_d])
                    T("act", lambda e: e.activation(tA[:], tD[:], AF.Exp), reads=[tmp_d, feat_d], writes=[tmp_d])
                    T("dve", lambda e: e.tensor_tensor(ARv1, tAv, rT[:].rearrange("p (c t) -> p c t", t=C), op=ALU.mult),
                      reads=[tmp_d, base_d], writes=[feat_d])
                    T("act", lambda e: e.activation(tD[:], tD[:], AF.Exp, scale=-1.0), reads=[tmp_d], writes=[tmp_d])
                    T("dve", lambda e: e.tensor_tensor(tA[:], kkT[:], tB[:], op=ALU.mult), reads=[tmp_d, base_d, feat_d], writes=[tmp_d])
                    T("dve", lambda e: e.tensor_tensor(Bt[:], tA[:], tD[:], op=ALU.mult), reads=[tmp_d], writes=[feat_d])
                    T("dve", lambda e: e.tensor_tensor(Kt[:], tC[:], tD[:], op=ALU.mult), reads=[tmp_d], writes=[feat_d])
                    if STOP <= 3:
                        break
                    T("dve", lambda e: e.memset(Tst[:], 0.0), writes=[T_dh[0], T_dh[1]])
                    T("dve", lambda e: e.memset(Tstb[:], 0.0), writes=[T_dh[0], T_dh[1]])
                    order = list(range(NCH)) if d == 0 else [1, 0] + list(range(NCH - 1, 1, -1))
                    SUB = int(os.environ.get("RW_SUB", "99"))
                    order = order[:int(os.environ.get("RW_NCH", "99"))]
                    m2 = k.mup if d == 0 else k.mlo
                    mT = k.mlo if d == 0 else k.mup
                    for ci in order:
                        cs = slice(ci * C, (ci + 1) * C)
                        is_lat = ci >= 2
                        bk, bkd = BKtm[cnt["hc"] % 2], BK_d[cnt["hc"] % 2]
                        cnt["hc"] += 1
                        for which, src in enumerate((Bt, Kt)):
                            pp, ppd = next_pn()
                            T("pe", _mm(pp, src[:, cs], identb[:]), reads=[feat_d, k.rwc], writes=[ppd])
                            T("act", lambda e, pp=pp, bk=bk, which=which: e.copy(bk[:, which, :], pp), reads=[ppd], writes=[bkd])

                        def head_chain(hh, ci=ci, cs=cs, is_lat=is_lat, bk=bk, bkd=bkd):
                            ph = slice(64 * hh, 64 * hh + 64)
                            tpk = (64 * hh, 0)
                            psb, psd = pS[hh], pS_d[hh]
                            Td = T_dh[hh]
                            pa, pad = next_pa()
                            ni = hh
                            aa, aad = AA[ni], AA_d[ni]
                            arr = ARt[ph, ci, :, :].rearrange("p a t -> p (a t)")
                            T("pe", _mm(pa[:, 0:256], Bt[ph, cs], arr, tp=tpk), reads=[feat_d], writes=[pad])
                            T("pe", _mm(pa[:, 256:512], Kt[ph, cs], arr, tp=tpk), reads=[feat_d], writes=[pad])
                            yield
                            T("dve", lambda e: e.tensor_tensor(
                                aa[:].rearrange("p (a t) -> p a t", a=2), pa[:].rearrange("p (a t) -> p a t", a=2),
                                m2[:].unsqueeze(1).to_broadcast([128, 2, 256]), op=ALU.mult), reads=[pad, k.rwc], writes=[aad])
                            an = AN[ni]
                            T("dve", lambda e: e.tensor_tensor(an[:], pa[:, 0:128], m2[:, 0:128], op=ALU.mult), reads=[pad, k.rwc], writes=[aad])
                            p3, p3d = next_pn()
                            T("pe", _mm(p3, ARt[ph, ci, 0, :], Bt[ph, cs], tp=tpk), reads=[feat_d], writes=[p3d])
                            yield
                            nt0, nt0d = NT_[2 * ni], NT_d[2 * ni]
                            T("dve", lambda e: e.tensor_tensor(nt0[:], p3, mT[:, 0:128], op=ALU.mult), reads=[p3d, k.rwc], writes=[nt0d])
                            x0, x0d = XX[2 * ni], XX_d[2 * ni]
                            T("pool", lambda e: e.tensor_tensor(x0[:], an[:], k.ident[:], op=ALU.add), reads=[aad, k.ident_d], writes=[x0d])
                            curN, curNd = an[:], aad
                            curNT, curNTd = nt0, nt0d
                            curX, curXd = x0, x0d
                            for lv in range(1, 7):
                                nxtNT, nxtNTd = NT_[2 * ni + (lv % 2)], NT_d[2 * ni + (lv % 2)]
                                pq, pqd = next_pn()
                                T("pe", _mm(pq, curN, curNT[:]), reads=[curNd, curNTd], writes=[pqd])
                                T("act", lambda e, pq=pq, nxtNT=nxtNT: e.copy(nxtNT[:], pq), reads=[pqd], writes=[nxtNTd])
                                yield
                                if lv < 6:
                                    nxtN, nxtNd = NN[2 * ni + (lv % 2)], NN_d[2 * ni + (lv % 2)]
                                    pq2, pq2d = next_pn()
                                    T("pe", _mm(pq2, curNT[:], curN), reads=[curNd, curNTd], writes=[pq2d])
                                    T("act", lambda e, pq2=pq2, nxtN=nxtN: e.copy(nxtN[:], pq2), reads=[pq2d], writes=[nxtNd])
                                    yield
                                nxtX, nxtXd = XX[2 * ni + (lv % 2)], XX_d[2 * ni + (lv % 2)]
                                pq3, pq3d = next_pn()
                                T("pe", _mm(pq3, nxtNT[:], curX[:]), reads=[nxtNTd, curXd], writes=[pq3d])
                                T("dve", lambda e, pq3=pq3, nxtX=nxtX, curX=curX: e.tensor_tensor(nxtX[:], pq3, curX[:], op=ALU.add),
                                  reads=[pq3d, curXd], writes=[nxtXd])
                                yield
                                if lv < 6:
                                    curN, curNd = nxtN[:], nxtNd
                                curNT, curNTd = nxtNT, nxtNTd
                                curX, curXd = nxtX, nxtXd
                            vh = Vmb[:, ci, ph]
                            T("pe", _mm(psb[:, 0:64], aa[:, 256:384], vh, start=True, stop=False), reads=[aad, Vm_d], writes=[psd])
                            T("pe", _mm(psb[:, 0:64], ARt[ph, ci, 0, :], Tstb[ph, :], start=False, stop=True, tp=tpk),
                              reads=[feat_d, Td], writes=[psd])
                            wsb, wsd = Wsb[hh], Wsb_d[hh]
                            T("act", lambda e: e.copy(wsb[:], psb[:, 0:64]), reads=[psd], writes=[wsd])
                            yield
                            T("pe", _mm(psb[:, 64:128], curX[:], wsb[:]), reads=[curXd, wsd], writes=[psd])
                            usb, usd = Usb[hh], Usb_d[hh]
                            T("act", lambda e: e.copy(usb[:], psb[:, 64:128]), reads=[psd], writes=[usd])
                            yield
                            if is_lat:
                                yo = psb[:, 128:192]
                                T("pe", _mm(yo, ARt[ph, ci, 1, :], Tstb[ph, :], start=True, stop=False, tp=tpk), reads=[feat_d, Td], writes=[psd])
                                T("pe", _mm(yo, aa[:, 128:256], usb[:], start=False, stop=False), reads=[aad, usd], writes=[psd])
                                T("pe", _mm(yo, aa[:, 384:512], vh, start=False, stop=True), reads=[aad, Vm_d], writes=[psd])
                            to = psb[ph, 192:256]
                            T("pe", _mm(to, bk[:, 0, ph], usb[:], start=True, stop=False, tp=(0, 64 * hh)), reads=[bkd, usd], writes=[psd])
                            T("pe", _mm(to, bk[:, 1, ph], vh, start=False, stop=True, tp=(0, 64 * hh)), reads=[bkd, Vm_d], writes=[psd])
                            yield
                            if is_lat:
                                lc = ci - 2
                                ys = Ysum[:, lc, ph]
                                if d == 0:
                                    T("act", lambda e: e.copy(ys, psb[:, 128:192]), reads=[psd], writes=[Y_dh[hh]])
                                else:
                                    T("dve", lambda e: e.tensor_tensor(ys, psb[:, 128:192], ys, op=ALU.add), reads=[psd, Y_dh[hh]], writes=[Y_dh[hh]])
                            T("dve", lambda e: e.tensor_tensor(Ttmp[ph, :], psb[ph, 192:256], Tst[ph, :], op=ALU.add), reads=[psd, Td], writes=[Td])
                            T("dve", lambda e: e.tensor_scalar(Tst[ph, :], Ttmp[ph, :], PC[ph, ci:ci + 1], None, op0=ALU.mult), reads=[Td, feat_d], writes=[Td])
                            T("act", lambda e: e.copy(Tstb[ph, :], Tst[ph, :]), reads=[Td], writes=[Td])
                            yield

                        chains = [head_chain(0), head_chain(1)]
                        while chains:
                            for g_ in list(chains):
                                try:
                                    next(g_)
                                except StopIteration:
                                    chains.remove(g_)
                if STOP <= 4:
                    break
                T("dve", lambda e: e.tensor_tensor(kdsum[:], kdsum[:], rT[:], op=ALU.mult), reads=[kds_d, base_d], writes=[kds_d])
                T("dve", lambda e: e.tensor_scalar(kdsum[:], kdsum[:], k.rvT[:, 22 + hp:23 + hp], None, op0=ALU.mult), reads=[kds_d, k.rwc], writes=[kds_d])
                pp, ppd = next_pa()
                for lc in range(16):
                    T("pe", _mm(pp[:, 2 * lc:2 * lc + 2], kdsum[:, CTX + lc * 128:CTX + (lc + 1) * 128], k.hsel[:]), reads=[kds_d, k.rwc], writes=[ppd])
                T("dve", lambda e, pp=pp: e.tensor_copy(coef[:].rearrange("p a b -> p (a b)"), pp[:, 0:32]), reads=[ppd], writes=[coef_d])
                Yv = Ysum[:].rearrange("p c (h v) -> p (c h) v", v=64)
                ssum, ssq, mu_, rs_ = [g[:] for g in gn]
                GD = Dep()
                T("dve", lambda e: e.tensor_copy(gn[0][:, 0:1], gn[0][:, 0:1]), reads=[Y_dh[0], Y_dh[1]], writes=[Y_d, Y_dh[0], Y_dh[1]])
                T("dve", lambda e: e.tensor_reduce(ssum, Yv, axis=AX.X, op=ALU.add), reads=[Y_d], writes=[GD])
                tAv3 = tA[:, 0:2048].rearrange("p (c v) -> p c v", v=64)
                T("dve", lambda e: e.tensor_tensor(tAv3, Yv, Yv, op=ALU.mult), reads=[Y_d, tmp_d], writes=[tmp_d])
                T("dve", lambda e: e.tensor_reduce(ssq, tAv3, axis=AX.X, op=ALU.add), reads=[tmp_d], writes=[GD])
                T("dve", lambda e: e.tensor_scalar(mu_, ssum, 1.0 / 64, None, op0=ALU.mult), reads=[GD], writes=[GD])
                T("dve", lambda e: e.tensor_tensor(ssum, mu_, mu_, op=ALU.mult), reads=[GD], writes=[GD])
                T("dve", lambda e: e.scalar_tensor_tensor(ssq, ssq, 1.0 / 64, ssum, op0=ALU.mult, op1=ALU.subtract), reads=[GD], writes=[GD])
                T("act", lambda e: e.activation(ssq, ssq, AF.Sqrt, bias=64e-5), reads=[GD], writes=[GD])
                T("dve", lambda e: e.reciprocal(rs_, ssq), reads=[GD], writes=[GD])
                T("dve", lambda e: e.tensor_tensor(Yv, Yv, mu_.unsqueeze(2).to_broadcast([128, 32, 64]), op=ALU.subtract), reads=[GD, Y_d], writes=[Y_d])
                T("dve", lambda e: e.tensor_tensor(Yv, Yv, rs_.unsqueeze(2).to_broadcast([128, 32, 64]), op=ALU.mult), reads=[GD, Y_d], writes=[Y_d])
                lnw = k.lnbc[:, 0, hp * 128:(hp + 1) * 128].unsqueeze(1).to_broadcast([128, 16, 128])
                lnb = k.lnbc[:, 1, hp * 128:(hp + 1) * 128].unsqueeze(1).to_broadcast([128, 16, 128])
                T("dve", lambda e: e.tensor_tensor(Ysum[:], Ysum[:], lnw, op=ALU.mult), reads=[Y_d, k.rwc], writes=[Y_d])
                T("dve", lambda e: e.tensor_tensor(Ysum[:], Ysum[:], lnb, op=ALU.add), reads=[Y_d, k.rwc], writes=[Y_d])
                Vl = Vm[:, 2:18, :].rearrange("p c (h v) -> p (c h) v", v=64)
                T("dve", lambda e: e.tensor_tensor(tAv3, Vl, coef[:].rearrange("p a b -> p (a b)").unsqueeze(2).to_broadcast([128, 32, 64]), op=ALU.mult),
                  reads=[Vm_d, coef_d, tmp_d], writes=[tmp_d])
                T("dve", lambda e: e.tensor_tensor(Yv, Yv, tAv3, op=ALU.add), reads=[tmp_d, Y_d], writes=[Y_d])
                T("dve", lambda e: e.tensor_tensor(Ysum[:], Ysum[:], gtm[:], op=ALU.mult), reads=[Y_d, gtm_d], writes=[Y_d])
                for tb in range(4):
                    o_, od_ = ot[tb % 2], ot_d[tb % 2]
                    pp, ppd = next_pa()
                    for q in range(4):
                        T("pe", lambda e, pp=pp, q=q, tb=tb: e.transpose(pp[:, q * 128:(q + 1) * 128], Ysum[:, tb * 4 + q, :], k.ident[:]),
                          reads=[Y_d, k.ident_d], writes=[ppd])
                    T("act", lambda e, pp=pp, o_=o_: e.copy(o_[:], pp[:]), reads=[ppd], writes=[od_])
                    S.dma("sp", lambda e, o_=o_, tb=tb, b=b, hp=hp: e.dma_start(
                        out=k.MIXT[b, 512 + hp * 128:512 + (hp + 1) * 128, tb * 512:(tb + 1) * 512], in_=o_[:]),
                        reads=[od_], writes=[k.MIXT_dep[b]])
                T("dve", lambda e: e.tensor_copy(gn[0][:, 0:1], gn[0][:, 0:1]), writes=[Y_d, Y_dh[0], Y_dh[1]])
        S.barrier()


_PEER_CACHE = {}


def _peer_layout(inputs):
    f = lambda a: np.ascontiguousarray(a, dtype=np.float32)
    key = id(inputs["peer_u"])
    if key not in _PEER_CACHE:
        _PEER_CACHE.clear()
        _PEER_CACHE[key] = {
            "w_out": f(inputs["w_out"][0]),
            "peer_w_q": f(inputs["peer_w_q"][0]),
            "peer_keys": f(np.transpose(inputs["peer_keys"][0], (2, 0, 1, 3)).reshape(128, 16, 128)),
            "peer_uv": f(np.concatenate([inputs["peer_u"][0], inputs["peer_v"][0]], axis=1)),
            "nvec": f(np.stack([inputs["norm2_g"][0], inputs["norm_f_g"]], 0)),
        }
    return _PEER_CACHE[key]


def cast_uv(k):
    nc, S, I = k.nc, k.S, k.I
    T = S.op
    with ExitStack() as es:
        fb = [es.enter_context(nc.sbuf_tensor(f"cv_f{i}", [128, 4, 2048], F32)) for i in range(2)]
        bb = [es.enter_context(nc.sbuf_tensor(f"cv_b{i}", [128, 4, 2048], BF16)) for i in range(2)]
        fd = [Dep(), Dep()]
        bd = [Dep(), Dep()]
        for i in range(32):
            f_, fdd, b_, bdd = fb[i % 2], fd[i % 2], bb[i % 2], bd[i % 2]
            src = I["peer_uv"][i * 512:(i + 1) * 512, :].rearrange("(p r) n -> p r n", r=4)
            dst = k.UVB[i * 512:(i + 1) * 512, :].rearrange("(p r) n -> p r n", r=4)
            S.dma("sp", lambda e, f_=f_, src=src: e.dma_start(out=f_[:], in_=src), writes=[fdd])
            if i % 2 == 0:
                T("act", lambda e, f_=f_, b_=b_: e.copy(b_[:], f_[:]), reads=[fdd], writes=[bdd])
            else:
                T("dve", lambda e, f_=f_, b_=b_: e.tensor_copy(b_[:], f_[:]), reads=[fdd], writes=[bdd])
            S.dma("act", lambda e, b_=b_, dst=dst: e.dma_start(out=dst, in_=b_[:]), reads=[bdd], writes=[k.UVB_dep])
        S.barrier()


def stage4_peer(k):
    nc, S, I, NB = k.nc, k.S, k.I, k.NB
    T = S.op
    import os
    NT4 = int(os.environ.get("P4_TILES", "16"))
    NSLOT = int(os.environ.get("P4_SLOTS", "128"))
    with ExitStack() as es:
        def sbl(name, shape, dt=F32):
            return es.enter_context(nc.sbuf_tensor("P_" + name, list(shape), dt))

        def psl(name):
            return es.enter_context(nc.psum_tensor("PP_" + name, [128, 512], F32))
        cst = Dep()
        wq = sbl("wq", [128, 8, 2048], BF16)
        wo = sbl("wo", [128, 8, 1024], BF16)
        for kd in range(8):
            S.dma("pool", lambda e, kd=kd: e.dma_start(out=wq[:, kd, :], in_=I["peer_w_q"][kd * 128:(kd + 1) * 128, :], max_dma_last_dim=4096), writes=[cst])
            S.dma("pool", lambda e, kd=kd: e.dma_start(out=wo[:, kd, :], in_=I["w_out"][kd * 128:(kd + 1) * 128, :], max_dma_last_dim=4096), writes=[cst])
        keysT = sbl("keysT", [128, 16, 128])
        nbc = sbl("nbc", [128, 2, D])
        ones = sbl("ones", [128, 128])
        io16 = sbl("io16", [128, 16])
        bc = sbl("bc", [128, 4, D]); bc_d = Dep()
        xt = sbl("xt", [128, D]); xt_d = Dep()
        mx = sbl("mx", [128, 8, 128]); mx_d = Dep()
        mxb = sbl("mxb", [128, 8, 128], BF16); mxb_d = Dep()
        h1 = sbl("h1", [128, D]); h1_d = Dep()
        hb = sbl("hb", [128, D]); hb_d = Dep()
        junk = sbl("junk", [128, D]); junk_d = Dep()
        junkb = sbl("junkb", [128, D], BF16)
        hbb = sbl("hbb", [128, D], BF16); hbb_d = Dep()
        hbT = sbl("hbT", [128, 8, 128], BF16); hbT_d = Dep()
        qT = sbl("qT", [128, 16, 128]); qT_d = Dep()
        sc = sbl("sc", [128, 16, 128]); sc_d = Dep()
        sc2 = sbl("sc2", [128, 1, 128]); sc2_d = Dep()
        m16 = sbl("m16", [128, 16, 16]); i16 = sbl("i16", [128, 16, 16], U32); i16f = sbl("i16f", [128, 16, 16])
        cand = sbl("cand", [128, 8, 256]); cand2 = sbl("cand2", [128, 8, 256])
        best = sbl("best", [128, 8, 16]); pos = sbl("pos", [128, 8, 16], U32)
        pa_i = sbl("pa_i", [128, 8, 16], U32); pb_i = sbl("pb_i", [128, 8, 16], U32)
        pa_f = sbl("pa_f", [128, 8, 16]); pb_f = sbl("pb_f", [128, 8, 16])
        eq = cand2[:].rearrange("p h (a b) -> p h a b", b=16)
        i1s = sbl("i1s", [128, 8, 16]); i2s = sbl("i2s", [128, 8, 16])
        idxf = sbl("idxf", [128, 128]); idxi = sbl("idxi", [128, 128], I32)
        gate = sbl("gate", [128, 8, 16]); gsm = sbl("gsm", [128, 8, 2])
        tk_d = Dep()
        actr = sbl("actr", [128, 128]); act_d = Dep()
        agd = [Dep() for _ in range(64)]
        asd = [Dep() for _ in range(128)]
        wgt = sbl("wgt", [128, 128])
        st = sbl("st", [128, 8]); st_d = Dep()
        NG = int(os.environ.get("P4_NG", "4"))
        prod = [sbl(f"prod{i}", [128, 1024]) for i in range(2)]; prod_d = [Dep(), Dep()]
        SPLIT = os.environ.get("P4_SPLIT", "1") == "1"
        dg = [sbl(f"dg{i}", [128, 128]) for i in range(4)]; dg_d = [Dep() for _ in range(4)]
        dgb = [sbl(f"dgb{i}", [128, 128], BF16) for i in range(4)]; dgb_d = [Dep() for _ in range(4)]
        oo = sbl("oo", [128, D]); oo_d = Dep()
        pb_ = [psl(f"b{i}") for i in range(8)]; pb_d = [PD() for _ in range(8)]
        d0 = Dep()
        with ExitStack() as es2:
            krow = es2.enter_context(nc.sbuf_tensor("P_krow", [128, 16, 128], F32))
            nrow = es2.enter_context(nc.sbuf_tensor("P_nrow", [1, 2, D], F32))
            one1 = es2.enter_context(nc.sbuf_tensor("P_one1", [1, 128], F32))
            S.dma("sp", lambda e: e.dma_start(out=krow[:], in_=I["peer_keys"]), writes=[d0])
            S.dma("sp", lambda e: e.dma_start(out=nrow[:], in_=I["nvec"].rearrange("(o a) n -> o a n", o=1)), writes=[d0])
            T("dve", lambda e: e.memset(one1[:], 1.0), writes=[d0])
            T("dve", lambda e: e.memset(ones[:], 1.0), writes=[cst])
            T("dve", lambda e: e.tensor_copy(io16[:], k.iota_ff[:, 0:16]), reads=[k.ident_d], writes=[cst])
            for j in range(16):
                T("pe", lambda e, j=j: e.transpose(pb_[j % 4][:, 0:128], krow[:, j, :], k.ident[:]), reads=[d0, k.ident_d], writes=[pb_d[j % 4]])
                T("act", lambda e, j=j: e.copy(keysT[:, j, :], pb_[j % 4][:, 0:128]), reads=[pb_d[j % 4]], writes=[cst])
            for a in range(2):
                for hf in range(2):
                    T("pe", _mm(pb_[4 + hf][:, :], one1[0:1, :], nrow[0:1, a, hf * 512:(hf + 1) * 512]), reads=[d0], writes=[pb_d[4 + hf]])
                    T("act", lambda e, a=a, hf=hf: e.copy(nbc[:, a, hf * 512:(hf + 1) * 512], pb_[4 + hf][:, :]), reads=[pb_d[4 + hf]], writes=[cst])
            S.barrier()
        gb = [sbl(f"gb{i}", [128, 2, 2048], BF16) for i in range(NG)]; gb_d = [[Dep(), Dep()] for _ in range(NG)]
        NC5 = NB + 1
        gcnt = 0
        dcnt = 0
        for b in range(NB):
            for vi, j0 in enumerate([16, 32, 24, 40]):
                for jj in range(8):
                    dgt, dgd = dg[dcnt % 4], dg_d[dcnt % 4]
                    pp, ppd = pb_[dcnt % 4], pb_d[dcnt % 4]
                    dcnt += 1
                    T("dve", lambda e, dgt=dgt, j0=j0, jj=jj, b=b: e.tensor_scalar(dgt[:], k.ident[:], k.modT[:, j0 + jj, b:b + 1], None, op0=ALU.mult),
                      reads=[k.ident_d, k.modT_d], writes=[dgd])
                    T("pe", _mm(pp[:, 0:128], ones[:], dgt[:]), reads=[cst, dgd], writes=[ppd])
                    T("act", lambda e, pp=pp, vi=vi, jj=jj: e.copy(bc[:, vi, jj * 128:(jj + 1) * 128], pp[:, 0:128]), reads=[ppd], writes=[bc_d])
            T("dve", lambda e: e.tensor_scalar(bc[:, 1, :], bc[:, 1, :], 1.0, None, op0=ALU.add), reads=[bc_d], writes=[bc_d])
            T("dve", lambda e: e.tensor_tensor(bc[:, 1, :], bc[:, 1, :], nbc[:, 0, :], op=ALU.mult), reads=[bc_d, cst], writes=[bc_d])
            for tt in range(NT4):
                t0 = tt * 128
                S.dma("sp", lambda e, b=b, t0=t0: e.dma_start(out=xt[:], in_=I["x"][b, t0:t0 + 128, :]), writes=[xt_d])
                S.dma("act", lambda e, b=b, t0=t0: e.dma_start(out=mx[:], in_=k.MIXT[b].rearrange("(kc p) t -> p kc t", p=128)[:, :, t0:t0 + 128]),
                      reads=[k.MIXT_dep[b]], writes=[mx_d])
                T("act", lambda e: e.copy(mxb[:], mx[:]), reads=[mx_d], writes=[mxb_d])
                for hf in range(2):
                    for kc in range(8):
                        T("pe", _mm(pb_[hf][:, :], mxb[:, kc, :], wo[:, kc, hf * 512:(hf + 1) * 512], start=(kc == 0), stop=(kc == 7)),
                          reads=[mxb_d, cst], writes=[pb_d[hf]])
                    hs = slice(hf * 512, (hf + 1) * 512)
                    T("dve", lambda e, hf=hf, hs=hs: e.tensor_tensor(h1[:, hs], pb_[hf][:, :], bc[:, 0, hs], op=ALU.mult), reads=[pb_d[hf], bc_d], writes=[h1_d])
                    T("dve", lambda e, hs=hs: e.tensor_tensor(h1[:, hs], h1[:, hs], xt[:, hs], op=ALU.add), reads=[xt_d, h1_d], writes=[h1_d])
                T("act", lambda e: e.activation(junk[:], h1[:], AF.Square, accum_out=st[:, 0:1]), reads=[h1_d], writes=[junk_d, st_d])
                T("act", lambda e: e.activation(st[:, 1:2], st[:, 0:1], AF.Sqrt, bias=1e-6, scale=1.0 / D), reads=[st_d], writes=[st_d])
                T("dve", lambda e: e.reciprocal(st[:, 2:3], st[:, 1:2]), reads=[st_d], writes=[st_d])
                T("act", lambda e: e.activation(hb[:], h1[:], AF.Copy, scale=st[:, 2:3]), reads=[h1_d, st_d], writes=[hb_d])
                T("dve", lambda e: e.tensor_tensor(hb[:], hb[:], bc[:, 1, :], op=ALU.mult), reads=[hb_d, bc_d], writes=[hb_d])
                T("dve", lambda e: e.tensor_tensor(hb[:], hb[:], bc[:, 2, :], op=ALU.add), reads=[hb_d, bc_d], writes=[hb_d])
                T("act", lambda e: e.copy(hbb[:], hb[:]), reads=[hb_d], writes=[hbb_d])
                for kd in range(8):
                    bk_ = 2 + kd // 4
                    T("pe", lambda e, kd=kd, bk_=bk_: e.transpose(pb_[bk_][:, (kd % 4) * 128:(kd % 4 + 1) * 128], hb[:, kd * 128:(kd + 1) * 128], k.ident[:]),
                      reads=[hb_d, k.ident_d], writes=[pb_d[bk_]])
                for q in range(2):
                    T("act", lambda e, q=q: e.copy(hbT[:, q * 4:(q + 1) * 4, :].rearrange("p a t -> p (a t)"), pb_[2 + q][:, :]), reads=[pb_d[2 + q]], writes=[hbT_d])
                for j in range(16):
                    pp, ppd = pb_[4 + j % 2], pb_d[4 + j % 2]
                    for kd in range(8):
                        T("pe", _mm(pp[:, 0:128], wq[:, kd, j * 128:(j + 1) * 128], hbT[:, kd, :], start=(kd == 0), stop=(kd == 7)),
                          reads=[cst, hbT_d], writes=[ppd])
                    T("act", lambda e, pp=pp, j=j: e.copy(qT[:, j, :], pp[:, 0:128]), reads=[ppd], writes=[qT_d])
                for j in range(16):
                    bk_ = j // 4
                    T("pe", _mm(pb_[bk_][:, (j % 4) * 128:(j % 4 + 1) * 128], qT[:, j, :], keysT[:, j, :]), reads=[qT_d, cst], writes=[pb_d[bk_]])
                for q in range(4):
                    T("act", lambda e, q=q: e.copy(sc[:, q * 4:(q + 1) * 4, :].rearrange("p a n -> p (a n)"), pb_[q][:, :]), reads=[pb_d[q]], writes=[sc_d])
                for j in range(16):
                    T("dve", lambda e, j=j: e.max(m16[:, j, 0:8], sc[:, j, :]), reads=[sc_d], writes=[tk_d])
                    T("dve", lambda e, j=j: e.match_replace(sc2[:, 0, :], m16[:, j, 0:8], sc[:, j, :], -3.0e38), reads=[sc_d, tk_d], writes=[sc2_d])
                    T("dve", lambda e, j=j: e.max(m16[:, j, 8:16], sc2[:, 0, :]), reads=[sc2_d], writes=[tk_d])
                    T("dve", lambda e, j=j: e.max_index(i16[:, j, 0:8], m16[:, j, 0:8], sc[:, j, :]), reads=[sc_d, tk_d], writes=[tk_d])
                    T("dve", lambda e, j=j: e.max_index(i16[:, j, 8:16], m16[:, j, 8:16], sc2[:, 0, :]), reads=[sc2_d, tk_d], writes=[tk_d])
                T("dve", lambda e: e.tensor_copy(i16f[:], i16[:]), reads=[tk_d], writes=[tk_d])
                m16v = m16[:].rearrange("p (h c) a -> p h c a", c=2)
                i16v = i16f[:].rearrange("p (h c) a -> p h c a", c=2)
                candv = cand[:].rearrange("p h (a b) -> p h a b", b=16)
                T("dve", lambda e: e.tensor_tensor(candv, m16v[:, :, 0, :].unsqueeze(3).to_broadcast([128, 8, 16, 16]),
                                                   m16v[:, :, 1, :].unsqueeze(2).to_broadcast([128, 8, 16, 16]), op=ALU.add), reads=[tk_d], writes=[tk_d])
                for h in range(8):
                    T("dve", lambda e, h=h: e.max(best[:, h, 0:8], cand[:, h, :]), reads=[tk_d], writes=[tk_d])
                    T("dve", lambda e, h=h: e.match_replace(cand2[:, h, :], best[:, h, 0:8], cand[:, h, :], -3.0e38), reads=[tk_d], writes=[tk_d])
                    T("dve", lambda e, h=h: e.max(best[:, h, 8:16], cand2[:, h, :]), reads=[tk_d], writes=[tk_d])
                    T("dve", lambda e, h=h: e.max_index(pos[:, h, 0:8], best[:, h, 0:8], cand[:, h, :]), reads=[tk_d], writes=[tk_d])
                    T("dve", lambda e, h=h: e.max_index(pos[:, h, 8:16], best[:, h, 8:16], cand2[:, h, :]), reads=[tk_d], writes=[tk_d])
                T("dve", lambda e: e.tensor_tensor(gate[:], best[:], best[:, :, 0:1].to_broadcast([128, 8, 16]), op=ALU.subtract), reads=[tk_d], writes=[tk_d])
                T("act", lambda e: e.activation(gate[:], gate[:], AF.Exp), reads=[tk_d], writes=[tk_d])
                T("dve", lambda e: e.tensor_reduce(gsm[:, :, 0], gate[:], axis=AX.X, op=ALU.add), reads=[tk_d], writes=[tk_d])
                T("dve", lambda e: e.reciprocal(gsm[:, :, 1], gsm[:, :, 0]), reads=[tk_d], writes=[tk_d])
                T("dve", lambda e: e.tensor_tensor(gate[:], gate[:], gsm[:, :, 1:2].to_broadcast([128, 8, 16]), op=ALU.mult), reads=[tk_d], writes=[tk_d])
                T("dve", lambda e: e.tensor_single_scalar(pa_i[:], pos[:], 4, op=ALU.logical_shift_right), reads=[tk_d], writes=[tk_d])
                T("dve", lambda e: e.tensor_single_scalar(pb_i[:], pos[:], 15, op=ALU.bitwise_and), reads=[tk_d], writes=[tk_d])
                T("dve", lambda e: e.tensor_copy(pa_f[:], pa_i[:]), reads=[tk_d], writes=[tk_d])
                T("dve", lambda e: e.tensor_copy(pb_f[:], pb_i[:]), reads=[tk_d], writes=[tk_d])
                io_b = io16[:].unsqueeze(1).unsqueeze(1).to_broadcast([128, 8, 16, 16])
                for (pf, cc, dst) in [(pa_f, 0, i1s), (pb_f, 1, i2s)]:
                    T("dve", lambda e, pf=pf: e.tensor_tensor(eq, pf[:].unsqueeze(3).to_broadcast([128, 8, 16, 16]), io_b, op=ALU.is_equal),
                      reads=[tk_d, cst], writes=[tk_d])
                    T("dve", lambda e, cc=cc: e.tensor_tensor(eq, eq, i16v[:, :, cc, :].unsqueeze(2).to_broadcast([128, 8, 16, 16]), op=ALU.mult),
                      reads=[tk_d], writes=[tk_d])
                    T("dve", lambda e, dst=dst: e.tensor_reduce(dst[:], eq, axis=AX.X, op=ALU.add), reads=[tk_d], writes=[tk_d])
                T("dve", lambda e: e.scalar_tensor_tensor(idxf[:], i1s[:].rearrange("p h k -> p (h k)"), 128.0, i2s[:].rearrange("p h k -> p (h k)"),
                                                          op0=ALU.mult, op1=ALU.add), reads=[tk_d], writes=[tk_d])
                T("dve", lambda e: e.tensor_copy(idxi[:], idxf[:]), reads=[tk_d], writes=[tk_d])
                gflat = gate[:].rearrange("p h k -> p (h k)")
                NGRP = NSLOT // 2
                ginfo = {}

                def stage_a(g):
                    nonlocal gcnt
                    gbt, gbd = gb[gcnt % NG], gb_d[gcnt % NG]
                    gcnt += 1
                    ginfo[g] = (gbt, gbd)
                    for s2 in range(2):
                        slot = g * 2 + s2
                        S.dma("pool", lambda e, gbt=gbt, s2=s2, slot=slot: e.indirect_dma_start(
                            out=gbt[:, s2, :], out_offset=None, in_=k.UVB[:, :],
                            in_offset=bass.IndirectOffsetOnAxis(ap=idxi[:, slot:slot + 1], axis=0)),
                            reads=[tk_d, k.UVB_dep], writes=[gbd[s2]])
                    for s2 in range(2):
                        slot = g * 2 + s2
                        if s2 == 1 and SPLIT:
                            pr_, prd_ = prod[g % 2], prod_d[g % 2]
                            T("pool", lambda e, gbt=gbt, pr_=pr_: e.tensor_tensor(pr_[:], gbt[:, 1, 0:1024], hbb[:], op=ALU.mult),
                              reads=[gbd[1], hbb_d], writes=[prd_])
                            T("act", lambda e, pr_=pr_, slot=slot: e.activation(junk[:], pr_[:], AF.Copy, accum_out=actr[:, slot:slot + 1]),
                              reads=[prd_], writes=[junk_d, asd[slot]])
                            continue
                        T("dve", lambda e, gbt=gbt, s2=s2, slot=slot: e.scalar_tensor_tensor(
                            junkb[:], gbt[:, s2, 0:1024], 1.0, hbb[:], op0=ALU.mult, op1=ALU.mult,
                            accum_out=actr[:, slot:slot + 1]), reads=[gbd[s2], hbb_d], writes=[asd[slot]])
                    sl = slice(g * 2, g * 2 + 2)
                    T("act", lambda e, sl=sl: e.activation(wgt[:, sl], actr[:, sl], AF.Gelu), reads=[asd[g * 2], asd[g * 2 + 1]], writes=[agd[g]])

                def stage_b(g):
                    nonlocal dcnt
                    gbt, gbd = ginfo.pop(g)
                    ad_ = agd[g]
                    sl = slice(g * 2, g * 2 + 2)
                    T("dve", lambda e, sl=sl: e.tensor_tensor(wgt[:, sl], wgt[:, sl], gflat[:, sl], op=ALU.mult), reads=[ad_, tk_d], writes=[ad_])
                    for s2 in range(2):
                        slot = g * 2 + s2
                        dgt, dgd = dgb[dcnt % 4], dgb_d[dcnt % 4]
                        dcnt += 1
                        T("act", lambda e, dgt=dgt, slot=slot: e.activation(dgt[:], k.ident[:], AF.Copy, scale=wgt[:, slot:slot + 1]),
                          reads=[k.ident_d, ad_], writes=[dgd])
                        for hf in range(2):
                            T("pe", _mm(pb_[6 + hf][:, :], dgt[:], gbt[:, s2, 1024 + hf * 512:1024 + (hf + 1) * 512],
                                        start=(slot == 0), stop=(slot == NSLOT - 1)), reads=[dgd, gbd[s2]], writes=[pb_d[6 + hf]])

                SKEW = 2
                for g in range(NGRP + SKEW):
                    if g < NGRP:
                        stage_a(g)
                    if g >= SKEW:
                        stage_b(g - SKEW)
                for hf in range(2):
                    hs = slice(hf * 512, (hf + 1) * 512)
                    T("dve", lambda e, hf=hf, hs=hs: e.tensor_tensor(oo[:, hs], pb_[6 + hf][:, :], bc[:, 3, hs], op=ALU.mult), reads=[pb_d[6 + hf], bc_d], writes=[oo_d])
                    T("dve", lambda e, hs=hs: e.tensor_tensor(oo[:, hs], oo[:, hs], h1[:, hs], op=ALU.add), reads=[oo_d, h1_d], writes=[oo_d])
                T("act", lambda e: e.activation(junk[:], oo[:], AF.Square, accum_out=st[:, 4:5]), reads=[oo_d], writes=[junk_d, st_d])
                T("act", lambda e: e.activation(st[:, 5:6], st[:, 4:5], AF.Sqrt, bias=1e-6, scale=1.0 / D), reads=[st_d], writes=[st_d])
                T("dve", lambda e: e.reciprocal(st[:, 6:7], st[:, 5:6]), reads=[st_d], writes=[st_d])
                T("act", lambda e: e.activation(oo[:], oo[:], AF.Copy, scale=st[:, 6:7]), reads=[oo_d, st_d], writes=[oo_d])
                T("dve", lambda e: e.tensor_tensor(oo[:], oo[:], nbc[:, 1, :], op=ALU.mult), reads=[oo_d, cst], writes=[oo_d])
                S.dma("sp", lambda e, b=b, t0=t0: e.dma_start(out=k.out[b, t0:t0 + 128, :], in_=oo[:]), reads=[oo_d], writes=[Dep()])
        S.barrier()
```

```python
import numpy as np
from contextlib import ExitStack
import concourse.bass as bass
import concourse.mybir as mybir
from concourse.bass_utils import run_bass_kernel_spmd

F32 = mybir.dt.float32
BF16 = mybir.dt.bfloat16
I32 = mybir.dt.int32
U32 = mybir.dt.uint32
AF = mybir.ActivationFunctionType
ALU = mybir.AluOpType
AX = mybir.AxisListType

D = 1024
SEQ = 2048
CTX = 256
LT = SEQ + CTX
INC = 2208
NCORES = 8
NBATCH = 32


class Dep:
    __slots__ = ("w", "r", "excl")

    def __init__(self, excl=False):
        self.w = None
        self.r = {}
        self.excl = excl


def PD():
    return Dep(excl=True)


class Sch:
    ROT = 30000

    def __init__(self, nc, es):
        self.nc = nc
        self.es = es
        self.eng = {"pe": nc.tensor, "dve": nc.vector, "act": nc.scalar, "pool": nc.gpsimd, "sp": nc.sync}
        self.cur = {}
        self.cnt = {}
        self.seen = {e: {} for e in self.eng}
        self.nsem = 0
        for e in self.eng:
            self._newsem(e)
        self.dsems = []
        for i in range(40):
            s = es.enter_context(nc.semaphore(f"dq{i}"))
            self.dsems.append([s, 0])
        self.dnext = 0
        self.swsems = []
        for i in range(16):
            s = es.enter_context(nc.semaphore(f"sq{i}"))
            self.swsems.append([s, 0])
        self.swnext = 0
        self.semobj = {}
        self.ninst = 0
        import os
        self.skip_own = set(os.environ.get("SKIP_OWN", "pe").split(","))

    def _newsem(self, e):
        s = self.es.enter_context(self.nc.semaphore(f"e_{e}_{self.nsem}"))
        self.nsem += 1
        self.cur[e] = s
        self.cnt[e] = 0

    def _wait(self, en, deps):
        best = {}
        for (s, v) in deps:
            k = id(s)
            if k not in best or best[k][1] < v:
                best[k] = (s, v)
        seen = self.seen[en]
        for k, (s, v) in best.items():
            if seen.get(k, 0) >= v:
                continue
            self.eng[en].wait_ge(s, v)
            self.nwait = getattr(self, "nwait", 0) + 1
            seen[k] = v

    def _deps(self, reads, writes):
        deps = []
        for d in reads:
            if d.w is not None:
                deps.append(d.w)
        for d in writes:
            if d.w is not None:
                deps.append(d.w)
            deps.extend(d.r.values())
        return deps

    def _mark(self, ev, reads, writes):
        for d in reads:
            d.r[id(ev[0])] = ev
        for d in writes:
            d.w = ev
            d.r = {}

    def op(self, en, fn, reads=(), writes=()):
        ex = [d for d in reads if d.excl]
        if ex:
            reads = [d for d in reads if not d.excl]
            writes = list(writes) + ex
        deps = self._deps(reads, writes)
        if en in self.skip_own:
            own = id(self.cur[en])
            deps = [d for d in deps if id(d[0]) != own]
        self._wait(en, deps)
        ins = fn(self.eng[en])
        if self.cnt[en] >= self.ROT:
            self._newsem(en)
        self.cnt[en] += 1
        ins.then_inc(self.cur[en], 1)
        ev = (self.cur[en], self.cnt[en])
        self._mark(ev, reads, writes)
        self.ninst += 1
        return ev

    def dma(self, q, fn, reads=(), writes=()):
        if q == "pool":
            slot = self.swsems[self.swnext]
            self.swnext = (self.swnext + 1) % len(self.swsems)
        else:
            slot = self.dsems[self.dnext]
            self.dnext = (self.dnext + 1) % len(self.dsems)
        deps = self._deps(reads, writes)
        if slot[1] > 0:
            deps.append((slot[0], slot[1]))
        self._wait(q, deps)
        ins = fn(self.eng[q])
        slot[1] += 16
        ins.then_inc(slot[0], 16)
        ev = (slot[0], slot[1])
        self._mark(ev, reads, writes)
        self.ninst += 1
        return ev

    def barrier(self):
        evs = [(self.cur[e], self.cnt[e]) for e in self.eng if self.cnt[e] > 0]
        evs += [(s, v) for (s, v) in self.dsems + self.swsems if v > 0]
        for e in self.eng:
            self._wait(e, evs)


def _mm(out, lhsT, rhs, start=True, stop=True, tp=None):
    return lambda e: e.matmul(out, lhsT, rhs, start=start, stop=stop, tile_position=tp)


class K:
    pass


def build(NB=4, upto=99, dbg=()):
    nc = bass.Bass("TRN2", target_bir_lowering=False)
    es = ExitStack()
    k = K()
    k.nc, k.es, k.NB = nc, es, NB
    S = k.S = Sch(nc, es)

    def din(name, shape, dt=F32):
        return nc.dram_tensor(name, list(shape), dt, kind="ExternalInput").ap()

    I = k.I = {}
    I["x"] = din("x", [NB, SEQ, D])
    I["c"] = din("c", [NB, D])
    I["ctx"] = din("ctx", [NB, CTX, D])
    I["c_ctx"] = din("c_ctx", [1, D])
    I["w_ada"] = din("w_ada", [D, 6 * D])
    I["b_ada"] = din("b_ada", [48, 128])
    I["norm1_g"] = din("norm1_g", [8, 128])
    I["norm2_g"] = din("norm2_g", [1, D])
    I["w_in"] = din("w_in", [D, INC])
    I["s5_arow"] = din("s5_arow", [3, 32, 128])
    I["s5_bT"] = din("s5_bT", [2, 128, 1024])
    I["s5_cblk"] = din("s5_cblk", [2, 128, 8, 128])
    I["s5_vec"] = din("s5_vec", [8, 128])
    I["s5_w_glu"] = din("s5_w_glu", [512, 512])
    I["rw_vec"] = din("rw_vec", [42, 128])
    I["rw_wlora"] = din("rw_wlora", [64, 2, 512])
    I["rw_w_g2"] = din("rw_w_g2", [96, 512])
    I["rw_ln"] = din("rw_ln", [2, 512])
    I["w_out"] = din("w_out", [D, D])
    I["peer_w_q"] = din("peer_w_q", [D, 2048])
    I["peer_keys"] = din("peer_keys", [128, 16, 128])
    I["peer_uv"] = din("peer_uv", [16384, 2048])
    I["nvec"] = din("nvec", [2, D])
    k.out = nc.dram_tensor("out", [NB, SEQ, D], F32, kind="ExternalOutput").ap()
    k.dbg = {}
    for (name, shape) in dbg:
        if name in ("PT", "MIXT") or shape is None:
            continue
        k.dbg[name] = nc.dram_tensor(name, list(shape), F32, kind="ExternalOutput").ap()
    dbgn = [d[0] for d in dbg]
    k.PT = nc.dram_tensor("PT", [NB, INC, LT], F32, kind="ExternalOutput" if "PT" in dbgn else "Internal").ap()
    k.PT_dep = [Dep() for _ in range(NB)]
    k.MIXT = nc.dram_tensor("MIXT", [NB, D, SEQ], F32, kind="ExternalOutput" if "MIXT" in dbgn else "Internal").ap()
    k.MIXT_dep = [Dep() for _ in range(NB)]
    k.UVB = nc.dram_tensor("UVB", [16384, 2048], BF16, kind="Internal").ap()
    k.UVB_dep = Dep()

    with es:
        setup_consts(k)
        k.modT = sb(k, "modT", [128, 48, NB + 1])
        k.gs1T = sb(k, "gs1T", [128, 8, NB + 1])
        stage0_mod(k)
        if upto >= 1:
            stage1_proj(k)
        if upto >= 2 and "skip_s5" not in dbgn:
            with ExitStack() as es2:
                k.es_stage = es2
                s5_alloc(k)
                s5_setup(k)
                stage2_s5(k)
        if upto >= 3:
            with ExitStack() as es3:
                k.es_stage = es3
                rw_setup(k)
                stage3_rwkv(k)
        if upto >= 4:
            cast_uv(k)
            stage4_peer(k)
        S.barrier()
        print("ninst", S.ninst, "nwait", getattr(S, "nwait", 0))
    return nc


def sb(k, name, shape, dt=F32):
    return k.es.enter_context(k.nc.sbuf_tensor(name, list(shape), dt))


def ps(k, name, shape, dt=F32):
    return k.es.enter_context(k.nc.psum_tensor(name, list(shape), dt))


def setup_consts(k):
    nc, S = k.nc, k.S
    k.ident = sb(k, "ident", [128, 128])
    k.ident_d = Dep()
    k.iota_p = sb(k, "iota_p", [128, 1], I32)
    k.iota_f = sb(k, "iota_f", [128, 128], I32)
    k.iota_d = Dep()
    S.op("pool", lambda e: e.iota(k.iota_p[:], [[0, 1]], base=0, channel_multiplier=1), writes=[k.iota_d])
    S.op("pool", lambda e: e.iota(k.iota_f[:], [[1, 128]], base=0, channel_multiplier=0), writes=[k.iota_d])
    k.iota_pf = sb(k, "iota_pf", [128, 1])
    k.iota_ff = sb(k, "iota_ff", [128, 128])
    S.op("dve", lambda e: e.tensor_copy(k.iota_pf[:], k.iota_p[:]), reads=[k.iota_d], writes=[k.ident_d])
    S.op("dve", lambda e: e.tensor_copy(k.iota_ff[:], k.iota_f[:]), reads=[k.iota_d], writes=[k.ident_d])
    S.op("dve", lambda e: e.tensor_scalar(k.ident[:], k.iota_ff[:], k.iota_pf[:, 0:1], None, op0=ALU.is_equal),
         reads=[k.ident_d], writes=[k.ident_d])


def stage0_mod(k):
    nc, S, I, NB = k.nc, k.S, k.I, k.NB
    NC5 = NB + 1
    with ExitStack() as es:
        def sbl(name, shape, dt=F32):
            return es.enter_context(nc.sbuf_tensor(name, list(shape), dt))
        crow = sbl("crow", [NC5, D])
        crow_d = Dep()
        S.dma("sp", lambda e: e.dma_start(out=crow[0:NB, :], in_=I["c"][:, :]), writes=[crow_d])
        S.dma("sp", lambda e: e.dma_start(out=crow[NB:NC5, :], in_=I["c_ctx"][:, :]), writes=[crow_d])
        S.op("act", lambda e: e.activation(crow[:], crow[:], AF.Silu), reads=[crow_d], writes=[crow_d])
        cT = sbl("cT", [128, 8, NC5])
        cT_d = Dep()
        vst = sbl("vst", [64, 128])
        vst_d = Dep()
        S.dma("sp", lambda e: e.dma_start(out=vst[0:48, :], in_=I["b_ada"][:, :]), writes=[vst_d])
        S.dma("sp", lambda e: e.dma_start(out=vst[48:56, :], in_=I["norm1_g"][:, :]), writes=[vst_d])
        vT = sbl("vT", [128, 56])
        vT_d = Dep()
        with nc.psum_tensor("p0a", [128, 8, NC5], F32) as pa, nc.psum_tensor("p0b", [128, 56], F32) as pb, \
                nc.psum_tensor("p0c", [128, 48, NC5], F32) as pc:
            pa_d, pb_d, pc_d = PD(), PD(), PD()
            for kd in range(8):
                S.op("pe", lambda e, kd=kd: e.transpose(pa[:, kd, :], crow[0:NC5, kd * 128:(kd + 1) * 128],
                                                        k.ident[0:NC5, 0:NC5]),
                     reads=[crow_d, k.ident_d], writes=[pa_d])
            S.op("dve", lambda e: e.tensor_copy(cT[:], pa[:]), reads=[pa_d], writes=[cT_d])
            S.op("pe", lambda e: e.transpose(pb[:, :], vst[0:56, :], k.ident[0:56, 0:56]),
                 reads=[vst_d, k.ident_d], writes=[pb_d])
            S.op("dve", lambda e: e.tensor_copy(vT[:], pb[:]), reads=[pb_d], writes=[vT_d])
            wt = [sbl(f"wada{i}", [128, 8, 512]) for i in range(2)]
            wt_d = [Dep(), Dep()]
            wv = I["w_ada"].rearrange("(kd p) n -> p kd n", p=128)
            for blk in range(12):
                t, td = wt[blk % 2], wt_d[blk % 2]
                for kd in range(8):
                    S.dma("sp" if kd % 2 == 0 else "act",
                          lambda e, kd=kd, t=t, blk=blk: e.dma_start(out=t[:, kd, :], in_=wv[:, kd, blk * 512:(blk + 1) * 512]),
                          writes=[td])
                for jj in range(4):
                    j = blk * 4 + jj
                    for kd in range(8):
                        S.op("pe", _mm(pc[:, j, :], t[:, kd, jj * 128:(jj + 1) * 128], cT[:, kd, :],
                                       start=(kd == 0), stop=(kd == 7)),
                             reads=[td, cT_d], writes=[pc_d])
            k.modT_d = Dep()
            S.op("dve", lambda e: e.tensor_tensor(k.modT[:], pc[:], vT[:, 0:48].unsqueeze(2).to_broadcast([128, 48, NC5]),
                                                  op=ALU.add),
                 reads=[pc_d, vT_d], writes=[k.modT_d])
        S.op("dve", lambda e: e.tensor_scalar(k.gs1T[:], k.modT[:, 8:16, :], 1.0, None, op0=ALU.add),
             reads=[k.modT_d], writes=[k.modT_d])
        S.op("dve", lambda e: e.tensor_tensor(k.gs1T[:], k.gs1T[:], vT[:, 48:56].unsqueeze(2).to_broadcast([128, 8, NC5]),
                                              op=ALU.mult),
             reads=[vT_d, k.modT_d], writes=[k.modT_d])
        k.S.barrier()


def stage1_proj(k):
    nc, S, I, NB = k.nc, k.S, k.I, k.NB
    with ExitStack() as es:
        def sbl(name, shape, dt=F32):
            return es.enter_context(nc.sbuf_tensor(name, list(shape), dt))
        wbf = sbl("w_in_bf", [128, 8, INC], BF16)
        wbf_d = Dep()
        for kd in range(8):
            S.dma("pool", lambda e, kd=kd: e.dma_start(out=wbf[:, kd, :], in_=I["w_in"][kd * 128:(kd + 1) * 128, :],
                                                       max_dma_last_dim=4096), writes=[wbf_d])
        xt = [sbl(f"xt{i}", [128, D]) for i in range(2)]
        xt_d = [Dep(), Dep()]
        xs = [sbl(f"xs{i}", [128, D]) for i in range(2)]
        xs_d = [Dep(), Dep()]
        junk = sbl("junk1", [128, D])
        junk_d = Dep()
        st = [sbl(f"st{i}", [128, 4]) for i in range(2)]
        hnT = [sbl(f"hnT{i}", [128, 8, 512], BF16) for i in range(2)]
        hnT_d = [Dep(), Dep()]
        ev = [sbl(f"ev{i}", [128, 512]) for i in range(3)]
        ev_d = [Dep() for _ in range(3)]
        ptr = [es.enter_context(nc.psum_tensor(f"ptr{i}", [128, 8, 128], F32)) for i in range(2)]
        ptr_d = [PD(), PD()]
        pmm = [es.enter_context(nc.psum_tensor(f"pmm{i}", [128, 512], F32)) for i in range(3)]
        pmm_d = [PD() for _ in range(3)]
        fch = [(i * 128, 128) for i in range(16)] + [(2048, 64), (2112, 96)]
        k.fch = fch
        ti = 0
        gi = 0
        ei = 0
        for b in range(NB):
            groups = [("ctx", 0, 256)] + [("lat", g * 512, 512) for g in range(4)]
            for (kind, t0, nt) in groups:
                h, hd = hnT[gi % 2], hnT_d[gi % 2]
                gi += 1
                col = b if kind == "lat" else NB
                for tt in range(nt // 128):
                    x_t, x_d = xt[ti % 2], xt_d[ti % 2]
                    xs_t, xsd = xs[ti % 2], xs_d[ti % 2]
                    s_t = st[ti % 2]
                    p_t, p_d = ptr[ti % 2], ptr_d[ti % 2]
                    ti += 1
                    src = I["x"][b, t0 + tt * 128:t0 + (tt + 1) * 128, :] if kind == "lat" else \
                        I["ctx"][b, tt * 128:(tt + 1) * 128, :]
                    S.dma("sp", lambda e, x_t=x_t, src=src: e.dma_start(out=x_t[:], in_=src), writes=[x_d])
                    S.op("act", lambda e, x_t=x_t, s_t=s_t: e.activation(junk[:], x_t[:], AF.Square, accum_out=s_t[:, 0:1]),
                         reads=[x_d], writes=[junk_d, xsd])
                    S.op("act", lambda e, s_t=s_t: e.activation(s_t[:, 1:2], s_t[:, 0:1], AF.Sqrt, bias=1e-6, scale=1.0 / D),
                         reads=[xsd], writes=[xsd])
                    S.op("dve", lambda e, s_t=s_t: e.reciprocal(s_t[:, 2:3], s_t[:, 1:2]), reads=[xsd], writes=[xsd])
                    S.op("act", lambda e, x_t=x_t, xs_t=xs_t, s_t=s_t: e.activation(xs_t[:], x_t[:], AF.Copy, scale=s_t[:, 2:3]),
                         reads=[x_d, xsd], writes=[xsd])
                    for kd in range(8):
                        S.op("pe", lambda e, kd=kd, p_t=p_t, xs_t=xs_t: e.transpose(p_t[:, kd, :], xs_t[:, kd * 128:(kd + 1) * 128],
                                                                                      k.ident[:]),
                             reads=[xsd, k.ident_d], writes=[p_d])
                    for kd in range(8):
                        S.op("dve", lambda e, kd=kd, p_t=p_t, h=h, tt=tt, col=col: e.tensor_scalar(
                            h[:, kd, tt * 128:(tt + 1) * 128], p_t[:, kd, :], k.gs1T[:, kd, col:col + 1],
                            k.modT[:, kd, col:col + 1], op0=ALU.mult, op1=ALU.add),
                            reads=[p_d, k.modT_d], writes=[hd])
                tok0 = t0 if kind == "ctx" else CTX + t0
                for fi, (c0, ncol) in enumerate(fch):
                    pm, pmd = pmm[ei % 3], pmm_d[ei % 3]
                    e_t, e_d = ev[ei % 3], ev_d[ei % 3]
                    ei += 1
                    for kd in range(8):
                        S.op("pe", _mm(pm[0:ncol, 0:nt], wbf[:, kd, c0:c0 + ncol], h[:, kd, 0:nt], start=(kd == 0), stop=(kd == 7)),
                             reads=[wbf_d, hd], writes=[pmd])
                    eng = "act" if fi % 2 == 0 else "dve"
                    if eng == "act":
                        S.op("act", lambda e, pm=pm, e_t=e_t, ncol=ncol, nt=nt: e.copy(e_t[0:ncol, 0:nt], pm[0:ncol, 0:nt]),
                             reads=[pmd], writes=[e_d])
                    else:
                        S.op("dve", lambda e, pm=pm, e_t=e_t, ncol=ncol, nt=nt: e.tensor_copy(e_t[0:ncol, 0:nt], pm[0:ncol, 0:nt]),
                             reads=[pmd], writes=[e_d])
                    S.dma("sp", lambda e, e_t=e_t, ncol=ncol, nt=nt, c0=c0, tok0=tok0, b=b: e.dma_start(
                        out=k.PT[b, c0:c0 + ncol, tok0:tok0 + nt], in_=e_t[0:ncol, 0:nt]),
                        reads=[e_d], writes=[k.PT_dep[b]])
        S.barrier()


_CACHE = {}


def _prep_inputs(inputs, NB, core):
    sl = slice(core * NB, (core + 1) * NB)
    f = lambda a: np.ascontiguousarray(a, dtype=np.float32)
    m = {
        "x": f(inputs["x"][sl]),
        "c": f(inputs["c"][sl]),
        "ctx": f(inputs["ctx"][sl]),
        "c_ctx": f(inputs["c_ctx"].reshape(1, D)),
        "w_ada": f(inputs["w_ada"][0]),
        "b_ada": f(inputs["b_ada"][0].reshape(48, 128)),
        "norm1_g": f(inputs["norm1_g"][0].reshape(8, 128)),
        "norm2_g": f(inputs["norm2_g"][0].reshape(1, D)),
        "w_in": f(inputs["w_in"][0]),
    }
    m.update(_s5_layout(inputs))
    m.update(_rw_layout(inputs))
    m.update(_peer_layout(inputs))
    return m


def kernel(**inputs):
    NB = NBATCH // NCORES
    if "nc" not in _CACHE:
        _CACHE["nc"] = build(NB)
    nc = _CACHE["nc"]
    in_maps = [_prep_inputs(inputs, NB, c) for c in range(NCORES)]
    res = run_bass_kernel_spmd(nc, in_maps, core_ids=list(range(NCORES)))
    return np.concatenate([r["out"] for r in res.results], axis=0)


def _s5_layout(inputs):
    f = lambda a: np.ascontiguousarray(a, dtype=np.float32)
    a_re, a_im, ldt = inputs["s5_a_re"][0], inputs["s5_a_im"][0], inputs["s5_log_dt"][0]
    arow = np.zeros((3, 32, 128), np.float32)
    arow[0] = a_re.reshape(2, 16, 128).reshape(32, 128)
    arow[1] = a_im.reshape(2, 16, 128).reshape(32, 128)
    arow[2] = np.repeat(ldt.reshape(2, 16, 2, 1), 64, axis=3).reshape(32, 128)
    bT = np.zeros((2, 2, 64, 2, 4, 4, 2, 16), np.float32)
    cb = np.zeros((2, 4, 2, 16, 2, 4, 2, 64), np.float32)
    for ri, (bsrc, csrc) in enumerate([(inputs["s5_b_re"][0], inputs["s5_c_re"][0]),
                                       (inputs["s5_b_im"][0], inputs["s5_c_im"][0])]):
        bg = bsrc.reshape(2, 4, 4, 2, 64, 16)
        cg = csrc.reshape(2, 4, 4, 2, 16, 64)
        for gl in range(2):
            bT[ri, gl, :, :, :, :, gl, :] = np.transpose(bg[:, :, :, gl], (3, 0, 1, 2, 4))
            cb[ri, :, gl, :, :, :, gl, :] = np.transpose(cg[:, :, :, gl], (2, 3, 0, 1, 4))
    vec = np.zeros((8, 128), np.float32)
    vec[0:4] = inputs["s5_d"][0].reshape(4, 128)
    vec[4:8] = inputs["s5_b_glu"][0].reshape(4, 128)
    return {"s5_arow": arow, "s5_bT": f(bT.reshape(2, 128, 1024)), "s5_cblk": f(cb.reshape(2, 128, 8, 128)),
            "s5_vec": vec, "s5_w_glu": f(inputs["s5_w_glu"][0])}


def sbs(k, name, shape, dt=F32):
    return k.es_stage.enter_context(k.nc.sbuf_tensor("S_" + name, list(shape), dt))


def s5_alloc(k):
    sb = sbs
    k.winj = [sb(k, f"winj{i}", [128, 8, 128]) for i in range(2)]
    k.rout = [sb(k, f"rout{i}", [128, 8, 128]) for i in range(2)]
    k.pw = [sb(k, f"pw{i}", [128, 32, 17]) for i in range(3)]
    k.lam = [sb(k, f"lam{i}", [128, 32, 8]) for i in range(3)]
    k.s5vT = sb(k, "s5vT", [128, 8])
    k.wglu = sb(k, "wglu", [128, 4, 512])
    k.s5_d = Dep()


def s5_setup(k):
    nc, S, I = k.nc, k.S, k.I
    T = S.op
    with ExitStack() as es:
        def sbl(name, shape, dt=F32):
            return es.enter_context(nc.sbuf_tensor(name, list(shape), dt))
        d0 = Dep()
        rows = sbl("s5rows", [32, 3, 128])
        S.dma("sp", lambda e: e.dma_start(out=rows[:], in_=I["s5_arow"].rearrange("a r c -> r a c")), writes=[d0])
        vrow = sbl("s5vrow", [8, 128])
        S.dma("sp", lambda e: e.dma_start(out=vrow[:], in_=I["s5_vec"][:, :]), writes=[d0])
        S.dma("sp", lambda e: e.dma_start(out=k.wglu[:], in_=I["s5_w_glu"].rearrange("(kc p) n -> p kc n", p=128)),
              writes=[k.s5_d])
        bT = [sbl(f"s5bT{i}", [128, 32, 32]) for i in range(2)]
        cblk = [sbl(f"s5cb{i}", [128, 8, 128]) for i in range(2)]
        for i in range(2):
            S.dma("sp", lambda e, i=i: e.dma_start(out=bT[i][:], in_=I["s5_bT"][i].rearrange("p (a b) -> p a b", b=32)), writes=[d0])
            S.dma("act", lambda e, i=i: e.dma_start(out=cblk[i][:], in_=I["s5_cblk"][i]), writes=[d0])
        aT = sbl("s5aT", [128, 3, 32])
        W = [sbl(f"s5w{i}", [128, 32]) for i in range(14)]
        Wi = sbl("s5wi", [128, 32], I32)
        bb = [sbl(f"s5bb{i}", [128, 32, 32]) for i in range(2)]
        tmp = [sbl(f"s5tmp{i}", [128, 32, 32]) for i in range(2)]
        with nc.psum_tensor("ps5a", [128, 3, 32], F32) as pa, nc.psum_tensor("ps5b", [128, 8], F32) as pb, \
                nc.psum_tensor("ps5c", [128, 4, 128], F32) as pc:
            pd = PD()
            for a in range(3):
                T("pe", lambda e, a=a: e.transpose(pa[:, a, :], rows[:, a, :], k.ident[0:32, 0:32]), reads=[d0, k.ident_d], writes=[pd])
            T("dve", lambda e: e.tensor_copy(aT[:], pa[:]), reads=[pd], writes=[d0])
            T("pe", lambda e: e.transpose(pb[:, :], vrow[:, :], k.ident[0:8, 0:8]), reads=[d0, k.ident_d], writes=[pd])
            T("dve", lambda e: e.tensor_copy(k.s5vT[:], pb[:]), reads=[pd], writes=[k.s5_d])
            are, aim, ldt = aT[:, 0, :], aT[:, 1, :], aT[:, 2, :]
            dt, mag, ang, sn, cs, abr, abi, t1, t2, nr, cfr, cfi, rden, t3 = [w[:] for w in W]

            def tt(o, a, b, op):
                T("dve", lambda e: e.tensor_tensor(o, a, b, op=op), reads=[d0], writes=[d0])

            T("act", lambda e: e.activation(dt, ldt, AF.Exp), reads=[d0], writes=[d0])
            tt(t1, dt, are, ALU.mult)
            T("act", lambda e: e.activation(mag, t1, AF.Exp), reads=[d0], writes=[d0])
            tt(ang, dt, aim, ALU.mult)

            def rsin(o, phase):
                T("dve", lambda e: e.tensor_scalar(t1, ang, 1.0 / (2 * np.pi), phase, op0=ALU.mult, op1=ALU.add), reads=[d0], writes=[d0])
                T("dve", lambda e: e.tensor_copy(Wi[:], t1), reads=[d0], writes=[d0])
                T("dve", lambda e: e.tensor_copy(t2, Wi[:]), reads=[d0], writes=[d0])
                tt(t1, t1, t2, ALU.subtract)
                T("dve", lambda e: e.scalar_tensor_tensor(t2, t1, 0.0, t1, op0=ALU.is_lt, op1=ALU.add), reads=[d0], writes=[d0])
                T("dve", lambda e: e.tensor_scalar(t2, t2, 2 * np.pi, -np.pi, op0=ALU.mult, op1=ALU.add), reads=[d0], writes=[d0])
                T("dve", lambda e: e.tensor_scalar(t2, t2, 3.1415925, -3.1415925, op0=ALU.min, op1=ALU.max), reads=[d0], writes=[d0])
                T("act", lambda e: e.activation(o, t2, AF.Sin), reads=[d0], writes=[d0])

            rsin(sn, 0.5)
            rsin(cs, 0.75)
            tt(abr, mag, cs, ALU.mult)
            tt(abi, mag, sn, ALU.mult)
            T("dve", lambda e: e.tensor_scalar(nr, abr, -1.0, None, op0=ALU.add), reads=[d0], writes=[d0])
            tt(t1, are, are, ALU.mult)
            tt(t2, aim, aim, ALU.mult)
            tt(t1, t1, t2, ALU.add)
            T("dve", lambda e: e.reciprocal(rden, t1), reads=[d0], writes=[d0])
            tt(t1, nr, are, ALU.mult)
            tt(t2, abi, aim, ALU.mult)
            tt(t1, t1, t2, ALU.add)
            tt(cfr, t1, rden, ALU.mult)
            tt(t1, abi, are, ALU.mult)
            tt(t2, nr, aim, ALU.mult)
            tt(t1, t1, t2, ALU.subtract)
            tt(cfi, t1, rden, ALU.mult)
            cfrb = cfr.unsqueeze(2).to_broadcast([128, 32, 32])
            cfib = cfi.unsqueeze(2).to_broadcast([128, 32, 32])
            tt(tmp[0][:], bT[0][:], cfrb, ALU.mult)
            tt(tmp[1][:], bT[1][:], cfib, ALU.mult)
            tt(bb[0][:], tmp[0][:], tmp[1][:], ALU.subtract)
            tt(tmp[0][:], bT[1][:], cfrb, ALU.mult)
            tt(tmp[1][:], bT[0][:], cfib, ALU.mult)
            tt(bb[1][:], tmp[0][:], tmp[1][:], ALU.add)
            for ri in range(2):
                for half in range(2):
                    for cc in range(4):
                        dc = half * 4 + cc
                        T("pe", lambda e, ri=ri, dc=dc, cc=cc: e.transpose(
                            pc[:, cc, :], bb[ri][:, dc * 4:(dc + 1) * 4, :].rearrange("p a b -> p (a b)"), k.ident[:]),
                            reads=[d0, k.ident_d], writes=[pd])
                    T("dve", lambda e, ri=ri, half=half: e.tensor_copy(k.winj[ri][:, half * 4:(half + 1) * 4, :], pc[:]),
                      reads=[pd], writes=[k.s5_d])
            for ri in range(2):
                for half in range(2):
                    for cc in range(4):
                        dc = half * 4 + cc
                        T("pe", lambda e, ri=ri, dc=dc, cc=cc: e.transpose(pc[:, cc, :], cblk[ri][:, dc, :], k.ident[:]),
                          reads=[d0, k.ident_d], writes=[pd])
                    if ri == 0:
                        T("dve", lambda e, half=half: e.tensor_copy(k.rout[0][:, half * 4:(half + 1) * 4, :], pc[:]),
                          reads=[pd], writes=[k.s5_d])
                    else:
                        T("dve", lambda e, half=half: e.tensor_scalar(k.rout[1][:, half * 4:(half + 1) * 4, :], pc[:], -1.0, None,
                                                                      op0=ALU.mult), reads=[pd], writes=[k.s5_d])
            pr, pi_, pn = k.pw
            T("dve", lambda e: e.memset(pr[:, :, 0:1], 1.0), writes=[k.s5_d])
            T("dve", lambda e: e.memset(pi_[:, :, 0:1], 0.0), writes=[k.s5_d])

            def cmul(o_r, o_i, a_r, a_i, b_r, b_i, dep):
                T("dve", lambda e: e.tensor_tensor(t1, a_r, b_r, op=ALU.mult), reads=[dep, d0], writes=[d0])
                T("dve", lambda e: e.tensor_tensor(t2, a_i, b_i, op=ALU.mult), reads=[dep, d0], writes=[d0])
                T("dve", lambda e: e.tensor_tensor(t3, a_r, b_i, op=ALU.mult), reads=[dep, d0], writes=[d0])
                T("dve", lambda e: e.tensor_tensor(rden, a_i, b_r, op=ALU.mult), reads=[dep, d0], writes=[d0])
                T("dve", lambda e: e.tensor_tensor(o_r, t1, t2, op=ALU.subtract), reads=[d0], writes=[dep])
                T("dve", lambda e: e.tensor_tensor(o_i, t3, rden, op=ALU.add), reads=[d0], writes=[dep])

            for n in range(1, 17):
                cmul(pr[:, :, n], pi_[:, :, n], pr[:, :, n - 1], pi_[:, :, n - 1], abr, abi, k.s5_d)
            T("dve", lambda e: e.tensor_scalar(pn[:], pi_[:], -1.0, None, op0=ALU.mult), reads=[k.s5_d], writes=[k.s5_d])
            lr, li, ln = k.lam
            T("dve", lambda e: e.tensor_copy(lr[:, :, 0], pr[:, :, 16]), reads=[k.s5_d], writes=[k.s5_d])
            T("dve", lambda e: e.tensor_copy(li[:, :, 0], pi_[:, :, 16]), reads=[k.s5_d], writes=[k.s5_d])
            for n in range(1, 8):
                cmul(lr[:, :, n], li[:, :, n], lr[:, :, n - 1], li[:, :, n - 1], lr[:, :, n - 1], li[:, :, n - 1], k.s5_d)
            T("dve", lambda e: e.tensor_scalar(ln[:], li[:], -1.0, None, op0=ALU.mult), reads=[k.s5_d], writes=[k.s5_d])
        S.barrier()


def stage2_s5(k):
    nc, S, I, NB = k.nc, k.S, k.I, k.NB
    T = S.op
    NSC = LT // 16
    with ExitStack() as es:
        def sbl(name, shape, dt=F32):
            return es.enter_context(nc.sbuf_tensor(name, list(shape), dt))
        uT = [sbl(f"s5u{i}", [128, LT]) for i in range(2)]
        uT_d = [Dep(), Dep()]
        Z = [[[sbl(f"s5z{jj}{d}{ri}", [128, LT]) for ri in range(2)] for d in range(2)] for jj in range(2)]
        Z_d = [[Dep(), Dep()] for jj in range(2)]
        Bp = [[[[sbl(f"s5B{jj}{d}{pp}{ri}", [128, NSC]) for ri in range(2)] for pp in range(2)] for d in range(2)] for jj in range(2)]
        B_d = [[[Dep(), Dep()] for d in range(2)] for jj in range(2)]
        y1 = sbl("s5y1", [128, 4, SEQ])
        y1_d = Dep()
        og = [sbl(f"s5og{i}", [128, 512]) for i in range(2)]
        og_d = [Dep(), Dep()]
        pin = [es.enter_context(nc.psum_tensor(f"s5pin{i}", [128, 512], F32)) for i in range(2)]
        pin_d = [PD(), PD()]
        py = es.enter_context(nc.psum_tensor("s5py", [128, 4, 512], F32))
        py_d = PD()
        pg = [es.enter_context(nc.psum_tensor(f"s5pg{i}", [128, 512], F32)) for i in range(2)]
        pg_d = [PD(), PD()]
        blocks = [(0, 256)] + [(256 + i * 512, 512) for i in range(4)]
        ipi = [0]

        def s5_chain(j, d, jj):
            q = cur_c[0] * 4 + j
            c = cur_c[0]
            u, ud = cur_u[0], cur_ud[0]
            dq = d * 16 + q
            zr, zi = Z[jj][d]
            zd = Z_d[jj][d]
            Bp_ = Bp[jj][d]
            B_d_ = B_d[jj][d]
            for (c0, n) in blocks:
                if d == 0:
                    z0 = c0
                else:
                    z0 = (c0 - CTX) if c0 >= CTX else SEQ
                for ri in range(2):
                    pp, ppd = pin[ipi[0] % 2], pin_d[ipi[0] % 2]
                    ipi[0] += 1
                    T("pe", _mm(pp[:, 0:n], k.winj[ri][32 * j:32 * j + 32, d * 4 + c, :], u[32 * j:32 * j + 32, c0:c0 + n], tp=(32 * j, 0)),
                      reads=[k.s5_d, ud], writes=[ppd])
                    T("act", lambda e, pp=pp, n=n, z0=z0, ri=ri: e.copy(Z[jj][d][ri][:, z0:z0 + n], pp[:, 0:n]),
                      reads=[ppd], writes=[zd])
                yield
            zrv = zr[:].rearrange("p (j r) -> p j r", r=16)
            ziv = zi[:].rearrange("p (j r) -> p j r", r=16)
            ar = k.pw[0][:, dq, 1:2]
            ai = k.pw[1][:, dq, 1:2]
            nai = k.pw[2][:, dq, 1:2]

            def stt(o, a, sc, bb_, rd, wr):
                T("dve", lambda e: e.scalar_tensor_tensor(o, a, sc, bb_, op0=ALU.mult, op1=ALU.add), reads=rd, writes=wr)

            order = range(1, 16) if d == 0 else range(14, -1, -1)
            for r in order:
                rp = r - 1 if d == 0 else r + 1
                stt(zrv[:, :, r], zrv[:, :, rp], ar, zrv[:, :, r], [zd, k.s5_d], [zd])
                yield
                stt(zrv[:, :, r], ziv[:, :, rp], nai, zrv[:, :, r], [zd, k.s5_d], [zd])
                yield
                stt(ziv[:, :, r], ziv[:, :, rp], ar, ziv[:, :, r], [zd, k.s5_d], [zd])
                yield
                stt(ziv[:, :, r], zrv[:, :, rp], ai, ziv[:, :, r], [zd, k.s5_d], [zd])
                yield
            rb = 15 if d == 0 else 0
            cur, nxt = 0, 1
            T("pool", lambda e: e.tensor_copy(Bp_[0][0][:], zrv[:, :, rb]), reads=[zd], writes=[B_d_[0]])
            T("pool", lambda e: e.tensor_copy(Bp_[0][1][:], ziv[:, :, rb]), reads=[zd], writes=[B_d_[0]])
            yield
            for lv in range(8):
                sh = 1 << lv
                lr_ = k.lam[0][:, dq, lv:lv + 1]
                li_ = k.lam[1][:, dq, lv:lv + 1]
                nli = k.lam[2][:, dq, lv:lv + 1]
                src, dst = Bp_[cur], Bp_[nxt]
                sd, dd = B_d_[cur], B_d_[nxt]
                if d == 0:
                    o_sl, i_sl, k_sl = slice(sh, NSC), slice(0, NSC - sh), slice(0, sh)
                else:
                    o_sl, i_sl, k_sl = slice(0, NSC - sh), slice(sh, NSC), slice(NSC - sh, NSC)
                T("pool", lambda e: e.tensor_copy(dst[0][:, k_sl], src[0][:, k_sl]), reads=[sd], writes=[dd])
                T("pool", lambda e: e.tensor_copy(dst[1][:, k_sl], src[1][:, k_sl]), reads=[sd], writes=[dd])
                stt(dst[0][:, o_sl], src[0][:, i_sl], lr_, src[0][:, o_sl], [sd, k.s5_d], [dd])
                yield
                stt(dst[0][:, o_sl], src[1][:, i_sl], nli, dst[0][:, o_sl], [sd, k.s5_d], [dd])
                yield
                stt(dst[1][:, o_sl], src[1][:, i_sl], lr_, src[1][:, o_sl], [sd, k.s5_d], [dd])
                yield
                stt(dst[1][:, o_sl], src[0][:, i_sl], li_, dst[1][:, o_sl], [sd, k.s5_d], [dd])
                yield
                cur, nxt = nxt, cur
            Sf, Sfd = Bp_[cur], B_d_[cur]
            for r in range(16):
                n = (r + 1) if d == 0 else (16 - r)
                pr_ = k.pw[0][:, dq, n:n + 1]
                pi_ = k.pw[1][:, dq, n:n + 1]
                pni = k.pw[2][:, dq, n:n + 1]
                if d == 0:
                    zs, ss = slice(16, NSC), slice(15, NSC - 1)
                else:
                    zs, ss = slice(0, 128), slice(1, 129)
                stt(zrv[:, zs, r], Sf[0][:, ss], pr_, zrv[:, zs, r], [zd, Sfd, k.s5_d], [zd])
                yield
                stt(zrv[:, zs, r], Sf[1][:, ss], pni, zrv[:, zs, r], [zd, Sfd, k.s5_d], [zd])
                yield
                stt(ziv[:, zs, r], Sf[1][:, ss], pr_, ziv[:, zs, r], [zd, Sfd, k.s5_d], [zd])
                yield
                stt(ziv[:, zs, r], Sf[0][:, ss], pi_, ziv[:, zs, r], [zd, Sfd, k.s5_d], [zd])
                yield
        cur_c, cur_u, cur_ud = [0], [None], [None]
        for b in range(NB):
            for c in range(4):
                u, ud = uT[c % 2], uT_d[c % 2]
                cur_c[0], cur_u[0], cur_ud[0] = c, u, ud
                S.dma("sp", lambda e, u=u, b=b, c=c: e.dma_start(out=u[:], in_=k.PT[b, c * 128:(c + 1) * 128, :]),
                      reads=[k.PT_dep[b]], writes=[ud])
                for jp in range(2):
                    chains = []
                    for jj in range(2):
                        for d in range(2):
                            chains.append(s5_chain(jp * 2 + jj, d, jj))
                    while chains:
                        for g_ in list(chains):
                            try:
                                next(g_)
                            except StopIteration:
                                chains.remove(g_)
                    for jj in range(2):
                        j = jp * 2 + jj
                        for tb in range(4):
                            terms = []
                            for d in range(2):
                                l0 = (CTX if d == 0 else 0) + tb * 512
                                terms.append((k.rout[0][:, d * 4 + c, 32 * j:32 * j + 32], Z[jj][d][0][:, l0:l0 + 512], Z_d[jj][d]))
                                terms.append((k.rout[1][:, d * 4 + c, 32 * j:32 * j + 32], Z[jj][d][1][:, l0:l0 + 512], Z_d[jj][d]))
                            for ti, (lh, rh, dd_) in enumerate(terms):
                                T("pe", _mm(py[32 * j:32 * j + 32, tb, :], lh, rh, start=(ti == 0), stop=(ti == 3), tp=(0, 32 * j)),
                                  reads=[k.s5_d, dd_], writes=[py_d])
                for tb in range(4):
                    T("dve", lambda e, tb=tb, c=c, u=u: e.scalar_tensor_tensor(
                        y1[:, c, tb * 512:(tb + 1) * 512], u[:, CTX + tb * 512:CTX + (tb + 1) * 512], k.s5vT[:, c:c + 1], py[:, tb, :],
                        op0=ALU.mult, op1=ALU.add), reads=[py_d, ud, k.s5_d], writes=[y1_d])
                T("act", lambda e, c=c: e.activation(y1[:, c, :], y1[:, c, :], AF.Gelu), reads=[y1_d], writes=[y1_d])
            gi = 0
            for m in range(4):
                for tb in range(4):
                    p_, pd_ = pg[gi % 2], pg_d[gi % 2]
                    o_, od_ = og[gi % 2], og_d[gi % 2]
                    gi += 1
                    for kc in range(4):
                        T("pe", _mm(p_[:], k.wglu[:, kc, m * 128:(m + 1) * 128], y1[:, kc, tb * 512:(tb + 1) * 512],
                                    start=(kc == 0), stop=(kc == 3)), reads=[k.s5_d, y1_d], writes=[pd_])
                    T("act", lambda e, p_=p_, o_=o_, m=m: e.activation(o_[:], p_[:], AF.Sigmoid, bias=k.s5vT[:, 4 + m:5 + m]),
                      reads=[pd_, k.s5_d], writes=[od_])
                    T("dve", lambda e, o_=o_, m=m, tb=tb: e.tensor_tensor(o_[:], o_[:], y1[:, m, tb * 512:(tb + 1) * 512], op=ALU.mult),
                      reads=[od_, y1_d], writes=[od_])
                    S.dma("sp", lambda e, o_=o_, m=m, tb=tb, b=b: e.dma_start(
                        out=k.MIXT[b, m * 128:(m + 1) * 128, tb * 512:(tb + 1) * 512], in_=o_[:]),
                        reads=[od_], writes=[k.MIXT_dep[b]])
        S.barrier()


def _rw_layout(inputs):
    f = lambda a: np.ascontiguousarray(a, dtype=np.float32)
    vec = np.zeros((42, 128), np.float32)
    mu = inputs["rw_mu"][0]
    vec[0:12] = mu[0:1536].reshape(12, 128)
    vec[12, 0:64] = mu[1536:1600]
    vec[13, 0:96] = mu[1600:1696]
    vec[14:18] = inputs["rw_k_k"][0].reshape(4, 128)
    vec[18:22] = inputs["rw_k_a"][0].reshape(4, 128)
    vec[22:26] = inputs["rw_r_k"][0].reshape(4, 128)
    vec[26:34] = inputs["rw_w0"][0].reshape(8, 128)
    vec[34:42] = inputs["rw_a0"][0].reshape(8, 128)
    wl = np.zeros((64, 2, 512), np.float32)
    wl[0:32] = np.transpose(inputs["rw_w_w2"][0], (1, 0, 2))
    wl[32:64] = np.transpose(inputs["rw_w_a2"][0], (1, 0, 2))
    ln = np.stack([inputs["rw_ln_w"][0], inputs["rw_ln_b"][0]], 0)
    return {"rw_vec": vec, "rw_wlora": f(wl), "rw_w_g2": f(inputs["rw_w_g2"][0]), "rw_ln": f(ln)}


def rw_setup(k):
    nc, S, I = k.nc, k.S, k.I
    T = S.op
    k.rwc = Dep()
    k.rvT = sbs(k, "rvT", [128, 42])
    k.omm = sbs(k, "rw_omm", [128, 14])
    k.muq = sbs(k, "rw_muq", [128, 14, 4])
    k.mue = sbs(k, "rw_mue", [128, 14, 2])
    k.omka = sbs(k, "rw_omka", [128, 4])
    k.wlora = sbs(k, "rw_wlora", [64, 2, 512])
    k.wg2 = sbs(k, "rw_wg2", [96, 512])
    k.lnbc = sbs(k, "rw_lnbc", [128, 2, 512])
    k.bones = sbs(k, "rw_bones", [128, 128])
    k.hsel = sbs(k, "rw_hsel", [128, 2])
    k.mup = sbs(k, "rw_mup", [128, 256])
    k.mlo = sbs(k, "rw_mlo", [128, 256])
    k.ones = sbs(k, "rw_ones", [128, 128])
    with ExitStack() as es:
        def sbl(name, shape, dt=F32):
            return es.enter_context(nc.sbuf_tensor(name, list(shape), dt))
        d0 = Dep()
        vrow = sbl("rwvrow", [42, 128])
        lnrow = sbl("rwlnrow", [1, 2, 512])
        m4 = sbl("rwm4", [128, 4])
        pm4i = sbl("rwpm4i", [128, 1], I32)
        pm4 = sbl("rwpm4", [128, 1])
        one1 = sbl("rwone1", [1, 128])
        S.dma("sp", lambda e: e.dma_start(out=vrow[:], in_=I["rw_vec"][:, :]), writes=[d0])
        S.dma("sp", lambda e: e.dma_start(out=lnrow[:], in_=I["rw_ln"].rearrange("(o a) n -> o a n", o=1)), writes=[d0])
        S.dma("sp", lambda e: e.dma_start(out=k.wlora[:], in_=I["rw_wlora"]), writes=[k.rwc])
        S.dma("sp", lambda e: e.dma_start(out=k.wg2[:], in_=I["rw_w_g2"][:, :]), writes=[k.rwc])
        with nc.psum_tensor("prw0", [128, 42], F32) as p0, nc.psum_tensor("prw1", [128, 2, 512], F32) as p1:
            pd = PD()
            T("pe", lambda e: e.transpose(p0[:, :], vrow[:, :], k.ident[0:42, 0:42]), reads=[d0, k.ident_d], writes=[pd])
            T("dve", lambda e: e.tensor_copy(k.rvT[:], p0[:]), reads=[pd], writes=[k.rwc])
            T("dve", lambda e: e.memset(one1[:], 1.0), writes=[d0])
            for a in range(2):
                T("pe", _mm(p1[:, a, :], one1[0:1, :], lnrow[0:1, a, :]), reads=[d0], writes=[pd])
            T("dve", lambda e: e.tensor_copy(k.lnbc[:], p1[:]), reads=[pd], writes=[k.rwc])
        T("dve", lambda e: e.tensor_scalar(k.omm[:], k.rvT[:, 0:14], -1.0, 1.0, op0=ALU.mult, op1=ALU.add), reads=[k.rwc], writes=[k.rwc])
        T("dve", lambda e: e.tensor_scalar(k.omka[:], k.rvT[:, 18:22], -1.0, 1.0, op0=ALU.mult, op1=ALU.add), reads=[k.rwc], writes=[k.rwc])
        T("dve", lambda e: e.tensor_single_scalar(pm4i[:], k.iota_p[:], 3, op=ALU.bitwise_and), reads=[k.iota_d], writes=[d0])
        T("dve", lambda e: e.tensor_copy(pm4[:], pm4i[:]), reads=[d0], writes=[d0])
        T("dve", lambda e: e.tensor_scalar(m4[:], k.iota_ff[:, 0:4], pm4[:, 0:1], None, op0=ALU.is_equal), reads=[d0, k.ident_d], writes=[d0])
        T("dve", lambda e: e.tensor_tensor(k.muq[:], k.rvT[:, 0:14].unsqueeze(2).to_broadcast([128, 14, 4]),
                                           m4[:].unsqueeze(1).to_broadcast([128, 14, 4]), op=ALU.mult), reads=[d0, k.rwc], writes=[k.rwc])
        T("dve", lambda e: e.tensor_tensor(k.mue[:], k.muq[:, :, 0:2], k.muq[:, :, 2:4], op=ALU.add), reads=[k.rwc], writes=[k.rwc])
        T("dve", lambda e: e.memset(k.bones[:], 0.0), writes=[k.rwc])
        T("dve", lambda e: e.memset(k.bones[0:64, 0:64], 1.0), writes=[k.rwc])
        T("dve", lambda e: e.memset(k.bones[64:128, 64:128], 1.0), writes=[k.rwc])
        T("dve", lambda e: e.memset(k.hsel[:], 0.0), writes=[k.rwc])
        T("dve", lambda e: e.memset(k.hsel[0:64, 0:1], 1.0), writes=[k.rwc])
        T("dve", lambda e: e.memset(k.hsel[64:128, 1:2], 1.0), writes=[k.rwc])
        T("dve", lambda e: e.memset(k.ones[:], 1.0), writes=[k.rwc])
        for (tile_, c0, op) in [(k.mup, 0, ALU.is_gt), (k.mup, 128, ALU.is_ge), (k.mlo, 0, ALU.is_lt), (k.mlo, 128, ALU.is_le)]:
            T("dve", lambda e, tile_=tile_, c0=c0, op=op: e.tensor_scalar(tile_[:, c0:c0 + 128], k.iota_ff[:], k.iota_pf[:, 0:1], None, op0=op),
              reads=[k.ident_d], writes=[k.rwc])
    S.barrier()


def stage3_rwkv(k):
    nc, S, I, NB = k.nc, k.S, k.I, k.NB
    T = S.op
    NCH = LT // 128
    C = 128
    import os
    with ExitStack() as es:
        def sbl(name, shape, dt=F32):
            return es.enter_context(nc.sbuf_tensor(name, list(shape), dt))

        def psl(name, shape):
            return es.enter_context(nc.psum_tensor(name, list(shape), F32))
        WD = BF16 if os.environ.get("RW_BF16", "1") == "1" else F32
        ND = F32 if os.environ.get("RW_NEU32", "1") == "1" else WD
        lora = sbl("rw_lora", [128, LT]); lora_d = Dep()
        sg = sbl("rw_sg", [128, LT]); sg_d = Dep()
        zb = sbl("rw_zb", [128, LT]); zb_d = Dep()
        rT = sbl("rw_r", [128, LT]); kT = sbl("rw_k", [128, LT]); kkT = sbl("rw_kk", [128, LT])
        base_d = Dep()
        Vm = sbl("rw_Vm", [128, NCH, 128]); Vm_d = Dep()
        tA = sbl("rw_tA", [128, LT]); tB = sbl("rw_tB", [128, LT]); tC = sbl("rw_tC", [128, LT]); tD = sbl("rw_tD", [128, LT])
        tmp_d = Dep()
        ARt = sbl("rw_AR", [128, NCH, 2, C], WD); Bt = sbl("rw_Bt", [128, LT], WD); Kt = sbl("rw_Kt", [128, LT], WD)
        Vmb = sbl("rw_Vmb", [128, NCH, 128], WD); identb = sbl("rw_identb", [128, 128], WD); Tstb = sbl("rw_Tb", [128, 64], WD)
        T("dve", lambda e: e.tensor_copy(identb[:], k.ident[:]), reads=[k.ident_d], writes=[k.rwc])
        feat_d = Dep()
        PC = sbl("rw_PC", [128, NCH]); tot = sbl("rw_tot", [128, NCH])
        kdsum = sbl("rw_kdsum", [128, LT]); kds_d = Dep()
        Ysum = sbl("rw_Y", [128, 16, 128]); Y_d = Dep()
        gtm = sbl("rw_gtm", [128, 16, 128]); gtm_d = Dep()
        coef = sbl("rw_coef", [128, 16, 2]); coef_d = Dep()
        gn = [sbl(f"rw_gn{i}", [128, 32]) for i in range(4)]
        Tst = sbl("rw_T", [128, 64]); T_d = Dep()
        T_dh = [Dep(), Dep()]
        Y_dh = [Dep(), Dep()]
        Ttmp = sbl("rw_Ttmp", [128, 64])
        NN = [sbl(f"rw_N{i}", [128, 128], ND) for i in range(4)]; NN_d = [Dep() for _ in range(4)]
        NT_ = [sbl(f"rw_NT{i}", [128, 128], ND) for i in range(4)]; NT_d = [Dep() for _ in range(4)]
        XX = [sbl(f"rw_X{i}", [128, 128], ND) for i in range(4)]; XX_d = [Dep() for _ in range(4)]
        AA = [sbl(f"rw_AA{i}", [128, 512], WD) for i in range(2)]; AA_d = [Dep() for _ in range(2)]
        AN = [sbl(f"rw_AN{i}", [128, 128], ND) for i in range(2)]
        Wsb = [sbl(f"rw_W{i}", [128, 64], ND) for i in range(2)]; Wsb_d = [Dep(), Dep()]
        Usb = [sbl(f"rw_U{i}", [128, 64], WD) for i in range(2)]; Usb_d = [Dep(), Dep()]
        BKtm = [sbl(f"rw_BK{i}", [128, 2, 128], WD) for i in range(2)]; BK_d = [Dep(), Dep()]
        ot = [sbl(f"rw_ot{i}", [128, 512]) for i in range(2)]; ot_d = [Dep(), Dep()]
        pA = [psl(f"rw_pA{i}", [128, 512]) for i in range(2)]; pA_d = [PD(), PD()]
        pN = [psl(f"rw_pN{i}", [128, 512]) for i in range(4)]; pN_d = [PD() for _ in range(4)]
        pS = [psl(f"rw_pS{i}", [128, 512]) for i in range(2)]; pS_d = [PD(), PD()]
        cnt = {"pn": 0, "pa": 0, "nn": 0, "nt": 0, "xx": 0, "hc": 0}

        def next_pn():
            i = cnt["pn"] % 4
            cnt["pn"] += 1
            return pN[i][:, 0:128], pN_d[i]

        def next_pa():
            i = cnt["pa"] % 2
            cnt["pa"] += 1
            return pA[i], pA_d[i]

        def mix_chunk(b, ch, nrows, dst, dst_d):
            r0 = 512 + (ch * 128 if ch < 12 else (1536 if ch == 12 else 1600))
            S.dma("sp", lambda e: e.dma_start(out=zb[0:nrows, :], in_=k.PT[b, r0:r0 + nrows, :]), reads=[k.PT_dep[b]], writes=[zb_d])
            P = slice(0, nrows)
            T("dve", lambda e: e.tensor_scalar(dst[P, :], zb[P, :], k.omm[P, ch:ch + 1], None, op0=ALU.mult),
              reads=[zb_d, k.rwc], writes=[dst_d])

            def acc(o, i_, sc):
                T("dve", lambda e: e.scalar_tensor_tensor(o, i_, sc, o, op0=ALU.mult, op1=ALU.add), reads=[zb_d, k.rwc, dst_d], writes=[dst_d])
            zl = zb[P, CTX:LT].rearrange("p (r c) -> p r c", c=64)
            dl = dst[P, CTX:LT].rearrange("p (r c) -> p r c", c=64)
            acc(dl[:, :, 1:64], zl[:, :, 0:63], k.muq[P, ch, 0:1])
            acc(dl[:, :, 0:63], zl[:, :, 1:64], k.muq[P, ch, 1:2])
            acc(dst[P, CTX + 64:LT], zb[P, CTX:LT - 64], k.muq[P, ch, 2:3])
            acc(dst[P, CTX:LT - 64], zb[P, CTX + 64:LT], k.muq[P, ch, 3:4])
            acc(dst[P, 1:CTX], zb[P, 0:CTX - 1], k.mue[P, ch, 0:1])
            acc(dst[P, 0:CTX - 1], zb[P, 1:CTX], k.mue[P, ch, 1:2])

        blocks = [(0, 512), (512, 512), (1024, 512), (1536, 512), (2048, 256)]
        import os
        STOP = int(os.environ.get("RW_STOP", "99"))
        for b in range(NB):
            mix_chunk(b, 12, 64, lora, lora_d)
            T("act", lambda e: e.activation(lora[0:32, :], lora[0:32, :], AF.Tanh), reads=[lora_d], writes=[lora_d])
            mix_chunk(b, 13, 96, sg, sg_d)
            T("act", lambda e: e.activation(sg[0:96, :], sg[0:96, :], AF.Sigmoid), reads=[sg_d], writes=[sg_d])
            if STOP <= 1:
                break
            for hp in range(4):
                mix_chunk(b, hp, 128, rT, base_d)
                mix_chunk(b, 4 + hp, 128, kT, base_d)
                mix_chunk(b, 8 + hp, 128, tA, tmp_d)
                for ci in range(NCH):
                    pp, ppd = next_pn()
                    T("pe", lambda e, pp=pp, ci=ci: e.transpose(pp, tA[:, ci * 128:(ci + 1) * 128], k.ident[:]),
                      reads=[tmp_d, k.ident_d], writes=[ppd])
                    T("act", lambda e, pp=pp, ci=ci: e.copy(Vm[:, ci, :], pp), reads=[ppd], writes=[Vm_d])
                    T("dve", lambda e, pp=pp, ci=ci: e.tensor_copy(Vmb[:, ci, :], pp), reads=[ppd], writes=[Vm_d])
                T("dve", lambda e: e.tensor_scalar(kkT[:], kT[:], k.rvT[:, 14 + hp:15 + hp], None, op0=ALU.mult), reads=[base_d, k.rwc], writes=[base_d])
                T("dve", lambda e: e.tensor_tensor(tB[:], kkT[:], kkT[:], op=ALU.mult), reads=[base_d], writes=[tmp_d])
                for (c0, n) in blocks:
                    pp, ppd = next_pa()
                    T("pe", _mm(pp[:, 0:n], k.bones[:], tB[:, c0:c0 + n]), reads=[tmp_d, k.rwc], writes=[ppd])
                    T("act", lambda e, pp=pp, c0=c0, n=n: e.activation(tC[:, c0:c0 + n], pp[:, 0:n], AF.Sqrt, bias=1e-12), reads=[ppd], writes=[tmp_d])
                T("dve", lambda e: e.reciprocal(tC[:], tC[:]), reads=[tmp_d], writes=[tmp_d])
                T("dve", lambda e: e.tensor_tensor(kkT[:], kkT[:], tC[:], op=ALU.mult), reads=[tmp_d, base_d], writes=[base_d])
                for lc in range(16):
                    pp, ppd = next_pn()
                    T("pe", _mm(pp, sg[0:96, CTX + lc * 128:CTX + (lc + 1) * 128], k.wg2[0:96, hp * 128:(hp + 1) * 128]),
                      reads=[sg_d, k.rwc], writes=[ppd])
                    T("act", lambda e, pp=pp, lc=lc: e.copy(gtm[:, lc, :], pp), reads=[ppd], writes=[gtm_d])
                if STOP <= 2:
                    break
                for d in range(2):
                    for (c0, n) in blocks:
                        pp, ppd = next_pa()
                        T("pe", _mm(pp[:, 0:n], k.wlora[0:32, d, hp * 128:(hp + 1) * 128], lora[0:32, c0:c0 + n]), reads=[lora_d, k.rwc], writes=[ppd])
                        T("act", lambda e, pp=pp, c0=c0, n=n, d=d, hp=hp: e.activation(
                            tA[:, c0:c0 + n], pp[:, 0:n], AF.Sigmoid, bias=k.rvT[:, 26 + d * 4 + hp:27 + d * 4 + hp]), reads=[ppd, k.rwc], writes=[tmp_d])
                        pp, ppd = next_pa()
                        T("pe", _mm(pp[:, 0:n], k.wlora[32:64, d, hp * 128:(hp + 1) * 128], lora[32:64, c0:c0 + n], tp=(32, 0)),
                          reads=[lora_d, k.rwc], writes=[ppd])
                        T("act", lambda e, pp=pp, c0=c0, n=n, d=d, hp=hp: e.activation(
                            tB[:, c0:c0 + n], pp[:, 0:n], AF.Sigmoid, bias=k.rvT[:, 34 + d * 4 + hp:35 + d * 4 + hp]), reads=[ppd, k.rwc], writes=[tmp_d])
                    T("dve", lambda e: e.tensor_scalar(tA[:], tA[:], -0.6065306597126334, None, op0=ALU.mult), reads=[tmp_d], writes=[tmp_d])
                    T("dve", lambda e: e.tensor_scalar(tC[:], tB[:], k.rvT[:, 18 + hp:19 + hp], k.omka[:, hp:hp + 1], op0=ALU.mult, op1=ALU.add),
                      reads=[tmp_d, k.rwc], writes=[tmp_d])
                    T("dve", lambda e: e.tensor_tensor(tC[:], tC[:], kT[:], op=ALU.mult), reads=[tmp_d, base_d], writes=[tmp_d])
                    if d == 0:
                        T("pool", lambda e: e.tensor_copy(kdsum[:], tC[:]), reads=[tmp_d], writes=[kds_d])
                    else:
                        T("pool", lambda e: e.tensor_tensor(kdsum[:], kdsum[:], tC[:], op=ALU.add), reads=[tmp_d, kds_d], writes=[kds_d])
                    for ci in range(NCH):
                        T("dve", lambda e, ci=ci: e.tensor_tensor_scan(tD[:, ci * C:(ci + 1) * C], k.ones[:], tA[:, ci * C:(ci + 1) * C], 0.0,
                                                                      op0=ALU.mult, op1=ALU.add), reads=[tmp_d, k.rwc], writes=[tmp_d])
                    tDv = tD[:].rearrange("p (c t) -> p c t", t=C)
                    T("dve", lambda e: e.tensor_copy(tot[:], tDv[:, :, C - 1]), reads=[tmp_d], writes=[feat_d])
                    if d == 1:
                        T("dve", lambda e: e.tensor_tensor(tD[:], tA[:], tD[:], op=ALU.subtract), reads=[tmp_d], writes=[tmp_d])
                        T("dve", lambda e: e.tensor_tensor(tDv, tDv, tot[:].unsqueeze(2).to_broadcast([128, NCH, C]), op=ALU.add),
                          reads=[tmp_d, feat_d], writes=[tmp_d])
                    T("act", lambda e: e.activation(PC[:], tot[:], AF.Exp), reads=[feat_d], writes=[feat_d])
                    ARv0 = ARt[:, :, 0, :]
                    ARv1 = ARt[:, :, 1, :]
                    tAv = tA[:].rearrange("p (c t) -> p c t", t=C)
                    T("dve", lambda e: e.tensor_tensor(tA[:], tD[:], tA[:], op=ALU.subtract), reads=[tmp_d], writes=[tmp_d])
                    T("act", lambda e: e.activation(tA[:], tA[:], AF.Exp), reads=[tmp_d], writes=[tmp_d])
                    T("dve", lambda e: e.scalar_tensor_tensor(ARv0, kkT[:].rearrange("p (c t) -> p c t", t=C), -1.0, tAv, op0=ALU.mult, op1=ALU.mult),
                      reads=[tmp_d, base_d], writes=[feat_d])
                    T("act", lambda e: e.activation(tA[:], tD[:], AF.Exp), reads=[tmp_d, feat_d], writes=[tmp_d])
                    T("dve", lambda e: e.tensor_tensor(ARv1, tAv, rT[:].rearrange("p (c t) -> p c t", t=C), op=ALU.mult),
                      reads=[tmp_d, base_d], writes=[feat_d])
                    T("act", lambda e: e.activation(tD[:], tD[:], AF.Exp, scale=-1.0), reads=[tmp_d], writes=[tmp_d])
                    T("dve", lambda e: e.tensor_tensor(tA[:], kkT[:], tB[:], op=ALU.mult), reads=[tmp_d, base_d, feat_d], writes=[tmp_d])
                    T("dve", lambda e: e.tensor_tensor(Bt[:], tA[:], tD[:], op=ALU.mult), reads=[tmp_d], writes=[feat_d])
                    T("dve", lambda e: e.tensor_tensor(Kt[:], tC[:], tD[:], op=ALU.mult), reads=[tmp_d], writes=[feat_d])
                    if STOP <= 3:
                        break
                    T("dve", lambda e: e.memset(Tst[:], 0.0), writes=[T_dh[0], T_dh[1]])
                    T("dve", lambda e: e.memset(Tstb[:], 0.0), writes=[T_dh[0], T_dh[1]])
                    order = list(range(NCH)) if d == 0 else [1, 0] + list(range(NCH - 1, 1, -1))
                    SUB = int(os.environ.get("RW_SUB", "99"))
                    order = order[:int(os.environ.get("RW_NCH", "99"))]
                    m2 = k.mup if d == 0 else k.mlo
                    mT = k.mlo if d == 0 else k.mup
                    for ci in order:
                        cs = slice(ci * C, (ci + 1) * C)
                        is_lat = ci >= 2
                        bk, bkd = BKtm[cnt["hc"] % 2], BK_d[cnt["hc"] % 2]
                        cnt["hc"] += 1
                        for which, src in enumerate((Bt, Kt)):
                            pp, ppd = next_pn()
                            T("pe", _mm(pp, src[:, cs], identb[:]), reads=[feat_d, k.rwc], writes=[ppd])
                            T("act", lambda e, pp=pp, bk=bk, which=which: e.copy(bk[:, which, :], pp), reads=[ppd], writes=[bkd])

                        def head_chain(hh, ci=ci, cs=cs, is_lat=is_lat, bk=bk, bkd=bkd):
                            ph = slice(64 * hh, 64 * hh + 64)
                            tpk = (64 * hh, 0)
                            psb, psd = pS[hh], pS_d[hh]
                            Td = T_dh[hh]
                            pa, pad = next_pa()
                            ni = hh
                            aa, aad = AA[ni], AA_d[ni]
                            arr = ARt[ph, ci, :, :].rearrange("p a t -> p (a t)")
                            T("pe", _mm(pa[:, 0:256], Bt[ph, cs], arr, tp=tpk), reads=[feat_d], writes=[pad])
                            T("pe", _mm(pa[:, 256:512], Kt[ph, cs], arr, tp=tpk), reads=[feat_d], writes=[pad])
                            yield
                            T("dve", lambda e: e.tensor_tensor(
                                aa[:].rearrange("p (a t) -> p a t", a=2), pa[:].rearrange("p (a t) -> p a t", a=2),
                                m2[:].unsqueeze(1).to_broadcast([128, 2, 256]), op=ALU.mult), reads=[pad, k.rwc], writes=[aad])
                            an = AN[ni]
                            T("dve", lambda e: e.tensor_tensor(an[:], pa[:, 0:128], m2[:, 0:128], op=ALU.mult), reads=[pad, k.rwc], writes=[aad])
                            p3, p3d = next_pn()
                            T("pe", _mm(p3, ARt[ph, ci, 0, :], Bt[ph, cs], tp=tpk), reads=[feat_d], writes=[p3d])
                            yield
                            nt0, nt0d = NT_[2 * ni], NT_d[2 * ni]
                            T("dve", lambda e: e.tensor_tensor(nt0[:], p3, mT[:, 0:128], op=ALU.mult), reads=[p3d, k.rwc], writes=[nt0d])
                            x0, x0d = XX[2 * ni], XX_d[2 * ni]
                            T("pool", lambda e: e.tensor_tensor(x0[:], an[:], k.ident[:], op=ALU.add), reads=[aad, k.ident_d], writes=[x0d])
                            curN, curNd = an[:], aad
                            curNT, curNTd = nt0, nt0d
                            curX, curXd = x0, x0d
                            for lv in range(1, 7):
                                nxtNT, nxtNTd = NT_[2 * ni + (lv % 2)], NT_d[2 * ni + (lv % 2)]
                                pq, pqd = next_pn()
                                T("pe", _mm(pq, curN, curNT[:]), reads=[curNd, curNTd], writes=[pqd])
                                T("act", lambda e, pq=pq, nxtNT=nxtNT: e.copy(nxtNT[:], pq), reads=[pqd], writes=[nxtNTd])
                                yield
                                if lv < 6:
                                    nxtN, nxtNd = NN[2 * ni + (lv % 2)], NN_d[2 * ni + (lv % 2)]
                                    pq2, pq2d = next_pn()
                                    T("pe", _mm(pq2, curNT[:], curN), reads=[curNd, curNTd], writes=[pq2d])
                                    T("act", lambda e, pq2=pq2, nxtN=nxtN: e.copy(nxtN[:], pq2), reads=[pq2d], writes=[nxtNd])
                                    yield
                                nxtX, nxtXd = XX[2 * ni + (lv % 2)], XX_d[2 * ni + (lv % 2)]
                                pq3, pq3d = next_pn()
                                T("pe", _mm(pq3, nxtNT[:], curX[:]), reads=[nxtNTd, curXd], writes=[pq3d])
                                T("dve", lambda e, pq3=pq3, nxtX=nxtX, curX=curX: e.tensor_tensor(nxtX[:], pq3, curX[:], op=ALU.add),
                                  reads=[pq3d, curXd], writes=[nxtXd])
                                yield
                                if lv < 6:
                                    curN, curNd = nxtN[:], nxtNd
                                curNT, curNTd = nxtNT, nxtNTd
                                curX, curXd = nxtX, nxtXd
                            vh = Vmb[:, ci, ph]
                            T("pe", _mm(psb[:, 0:64], aa[:, 256:384], vh, start=True, stop=False), reads=[aad, Vm_d], writes=[psd])
                            T("pe", _mm(psb[:, 0:64], ARt[ph, ci, 0, :], Tstb[ph, :], start=False, stop=True, tp=tpk),
                              reads=[feat_d, Td], writes=[psd])
                            wsb, wsd = Wsb[hh], Wsb_d[hh]
                            T("act", lambda e: e.copy(wsb[:], psb[:, 0:64]), reads=[psd], writes=[wsd])
                            yield
                            T("pe", _mm(psb[:, 64:128], curX[:], wsb[:]), reads=[curXd, wsd], writes=[psd])
                            usb, usd = Usb[hh], Usb_d[hh]
                            T("act", lambda e: e.copy(usb[:], psb[:, 64:128]), reads=[psd], writes=[usd])
                            yield
                            if is_lat:
                                yo = psb[:, 128:192]
                                T("pe", _mm(yo, ARt[ph, ci, 1, :], Tstb[ph, :], start=True, stop=False, tp=tpk), reads=[feat_d, Td], writes=[psd])
                                T("pe", _mm(yo, aa[:, 128:256], usb[:], start=False, stop=False), reads=[aad, usd], writes=[psd])
                                T("pe", _mm(yo, aa[:, 384:512], vh, start=False, stop=True), reads=[aad, Vm_d], writes=[psd])
                            to = psb[ph, 192:256]
                            T("pe", _mm(to, bk[:, 0, ph], usb[:], start=True, stop=False, tp=(0, 64 * hh)), reads=[bkd, usd], writes=[psd])
                            T("pe", _mm(to, bk[:, 1, ph], vh, start=False, stop=True, tp=(0, 64 * hh)), reads=[bkd, Vm_d], writes=[psd])
                            yield
                            if is_lat:
                                lc = ci - 2
                                ys = Ysum[:, lc, ph]
                                if d == 0:
                                    T("act", lambda e: e.copy(ys, psb[:, 128:192]), reads=[psd], writes=[Y_dh[hh]])
                                else:
                                    T("dve", lambda e: e.tensor_tensor(ys, psb[:, 128:192], ys, op=ALU.add), reads=[psd, Y_dh[hh]], writes=[Y_dh[hh]])
                            T("dve", lambda e: e.tensor_tensor(Ttmp[ph, :], psb[ph, 192:256], Tst[ph, :], op=ALU.add), reads=[psd, Td], writes=[Td])
                            T("dve", lambda e: e.tensor_scalar(Tst[ph, :], Ttmp[ph, :], PC[ph, ci:ci + 1], None, op0=ALU.mult), reads=[Td, feat_d], writes=[Td])
                            T("act", lambda e: e.copy(Tstb[ph, :], Tst[ph, :]), reads=[Td], writes=[Td])
                            yield

                        chains = [head_chain(0), head_chain(1)]
                        while chains:
                            for g_ in list(chains):
                                try:
                                    next(g_)
                                except StopIteration:
                                    chains.remove(g_)
                if STOP <= 4:
                    break
                T("dve", lambda e: e.tensor_tensor(kdsum[:], kdsum[:], rT[:], op=ALU.mult), reads=[kds_d, base_d], writes=[kds_d])
                T("dve", lambda e: e.tensor_scalar(kdsum[:], kdsum[:], k.rvT[:, 22 + hp:23 + hp], None, op0=ALU.mult), reads=[kds_d, k.rwc], writes=[kds_d])
                pp, ppd = next_pa()
                for lc in range(16):
                    T("pe", _mm(pp[:, 2 * lc:2 * lc + 2], kdsum[:, CTX + lc * 128:CTX + (lc + 1) * 128], k.hsel[:]), reads=[kds_d, k.rwc], writes=[ppd])
                T("dve", lambda e, pp=pp: e.tensor_copy(coef[:].rearrange("p a b -> p (a b)"), pp[:, 0:32]), reads=[ppd], writes=[coef_d])
                Yv = Ysum[:].rearrange("p c (h v) -> p (c h) v", v=64)
                ssum, ssq, mu_, rs_ = [g[:] for g in gn]
                GD = Dep()
                T("dve", lambda e: e.tensor_copy(gn[0][:, 0:1], gn[0][:, 0:1]), reads=[Y_dh[0], Y_dh[1]], writes=[Y_d, Y_dh[0], Y_dh[1]])
                T("dve", lambda e: e.tensor_reduce(ssum, Yv, axis=AX.X, op=ALU.add), reads=[Y_d], writes=[GD])
                tAv3 = tA[:, 0:2048].rearrange("p (c v) -> p c v", v=64)
                T("dve", lambda e: e.tensor_tensor(tAv3, Yv, Yv, op=ALU.mult), reads=[Y_d, tmp_d], writes=[tmp_d])
                T("dve", lambda e: e.tensor_reduce(ssq, tAv3, axis=AX.X, op=ALU.add), reads=[tmp_d], writes=[GD])
                T("dve", lambda e: e.tensor_scalar(mu_, ssum, 1.0 / 64, None, op0=ALU.mult), reads=[GD], writes=[GD])
                T("dve", lambda e: e.tensor_tensor(ssum, mu_, mu_, op=ALU.mult), reads=[GD], writes=[GD])
                T("dve", lambda e: e.scalar_tensor_tensor(ssq, ssq, 1.0 / 64, ssum, op0=ALU.mult, op1=ALU.subtract), reads=[GD], writes=[GD])
                T("act", lambda e: e.activation(ssq, ssq, AF.Sqrt, bias=64e-5), reads=[GD], writes=[GD])
                T("dve", lambda e: e.reciprocal(rs_, ssq), reads=[GD], writes=[GD])
                T("dve", lambda e: e.tensor_tensor(Yv, Yv, mu_.unsqueeze(2).to_broadcast([128, 32, 64]), op=ALU.subtract), reads=[GD, Y_d], writes=[Y_d])
                T("dve", lambda e: e.tensor_tensor(Yv, Yv, rs_.unsqueeze(2).to_broadcast([128, 32, 64]), op=ALU.mult), reads=[GD, Y_d], writes=[Y_d])
                lnw = k.lnbc[:, 0, hp * 128:(hp + 1) * 128].unsqueeze(1).to_broadcast([128, 16, 128])
                lnb = k.lnbc[:, 1, hp * 128:(hp + 1) * 128].unsqueeze(1).to_broadcast([128, 16, 128])
                T("dve", lambda e: e.tensor_tensor(Ysum[:], Ysum[:], lnw, op=ALU.mult), reads=[Y_d, k.rwc], writes=[Y_d])
                T("dve", lambda e: e.tensor_tensor(Ysum[:], Ysum[:], lnb, op=ALU.add), reads=[Y_d, k.rwc], writes=[Y_d])
                Vl = Vm[:, 2:18, :].rearrange("p c (h v) -> p (c h) v", v=64)
                T("dve", lambda e: e.tensor_tensor(tAv3, Vl, coef[:].rearrange("p a b -> p (a b)").unsqueeze(2).to_broadcast([128, 32, 64]), op=ALU.mult),
                  reads=[Vm_d, coef_d, tmp_d], writes=[tmp_d])
                T("dve", lambda e: e.tensor_tensor(Yv, Yv, tAv3, op=ALU.add), reads=[tmp_d, Y_d], writes=[Y_d])
                T("dve", lambda e: e.tensor_tensor(Ysum[:], Ysum[:], gtm[:], op=ALU.mult), reads=[Y_d, gtm_d], writes=[Y_d])
                for tb in range(4):
                    o_, od_ = ot[tb % 2], ot_d[tb % 2]
                    pp, ppd = next_pa()
                    for q in range(4):
                        T("pe", lambda e, pp=pp, q=q, tb=tb: e.transpose(pp[:, q * 128:(q + 1) * 128], Ysum[:, tb * 4 + q, :], k.ident[:]),
                          reads=[Y_d, k.ident_d], writes=[ppd])
                    T("act", lambda e, pp=pp, o_=o_: e.copy(o_[:], pp[:]), reads=[ppd], writes=[od_])
                    S.dma("sp", lambda e, o_=o_, tb=tb, b=b, hp=hp: e.dma_start(
                        out=k.MIXT[b, 512 + hp * 128:512 + (hp + 1) * 128, tb * 512:(tb + 1) * 512], in_=o_[:]),
                        reads=[od_], writes=[k.MIXT_dep[b]])
                T("dve", lambda e: e.tensor_copy(gn[0][:, 0:1], gn[0][:, 0:1]), writes=[Y_d, Y_dh[0], Y_dh[1]])
        S.barrier()


_PEER_CACHE = {}


def _peer_layout(inputs):
    f = lambda a: np.ascontiguousarray(a, dtype=np.float32)
    key = id(inputs["peer_u"])
    if key not in _PEER_CACHE:
        _PEER_CACHE.clear()
        _PEER_CACHE[key] = {
            "w_out": f(inputs["w_out"][0]),
            "peer_w_q": f(inputs["peer_w_q"][0]),
            "peer_keys": f(np.transpose(inputs["peer_keys"][0], (2, 0, 1, 3)).reshape(128, 16, 128)),
            "peer_uv": f(np.concatenate([inputs["peer_u"][0], inputs["peer_v"][0]], axis=1)),
            "nvec": f(np.stack([inputs["norm2_g"][0], inputs["norm_f_g"]], 0)),
        }
    return _PEER_CACHE[key]


def cast_uv(k):
    nc, S, I = k.nc, k.S, k.I
    T = S.op
    with ExitStack() as es:
        fb = [es.enter_context(nc.sbuf_tensor(f"cv_f{i}", [128, 4, 2048], F32)) for i in range(2)]
        bb = [es.enter_context(nc.sbuf_tensor(f"cv_b{i}", [128, 4, 2048], BF16)) for i in range(2)]
        fd = [Dep(), Dep()]
        bd = [Dep(), Dep()]
        for i in range(32):
            f_, fdd, b_, bdd = fb[i % 2], fd[i % 2], bb[i % 2], bd[i % 2]
            src = I["peer_uv"][i * 512:(i + 1) * 512, :].rearrange("(p r) n -> p r n", r=4)
            dst = k.UVB[i * 512:(i + 1) * 512, :].rearrange("(p r) n -> p r n", r=4)
            S.dma("sp", lambda e, f_=f_, src=src: e.dma_start(out=f_[:], in_=src), writes=[fdd])
            if i % 2 == 0:
                T("act", lambda e, f_=f_, b_=b_: e.copy(b_[:], f_[:]), reads=[fdd], writes=[bdd])
            else:
                T("dve", lambda e, f_=f_, b_=b_: e.tensor_copy(b_[:], f_[:]), reads=[fdd], writes=[bdd])
            S.dma("act", lambda e, b_=b_, dst=dst: e.dma_start(out=dst, in_=b_[:]), reads=[bdd], writes=[k.UVB_dep])
        S.barrier()


def stage4_peer(k):
    nc, S, I, NB = k.nc, k.S, k.I, k.NB
    T = S.op
    import os
    NT4 = int(os.environ.get("P4_TILES", "16"))
    NSLOT = int(os.environ.get("P4_SLOTS", "128"))
    with ExitStack() as es:
        def sbl(name, shape, dt=F32):
            return es.enter_context(nc.sbuf_tensor("P_" + name, list(shape), dt))

        def psl(name):
            return es.enter_context(nc.psum_tensor("PP_" + name, [128, 512], F32))
        cst = Dep()
        wq = sbl("wq", [128, 8, 2048], BF16)
        wo = sbl("wo", [128, 8, 1024], BF16)
        for kd in range(8):
            S.dma("pool", lambda e, kd=kd: e.dma_start(out=wq[:, kd, :], in_=I["peer_w_q"][kd * 128:(kd + 1) * 128, :], max_dma_last_dim=4096), writes=[cst])
            S.dma("pool", lambda e, kd=kd: e.dma_start(out=wo[:, kd, :], in_=I["w_out"][kd * 128:(kd + 1) * 128, :], max_dma_last_dim=4096), writes=[cst])
        keysT = sbl("keysT", [128, 16, 128])
        nbc = sbl("nbc", [128, 2, D])
        ones = sbl("ones", [128, 128])
        io16 = sbl("io16", [128, 16])
        bc = sbl("bc", [128, 4, D]); bc_d = Dep()
        xt = sbl("xt", [128, D]); xt_d = Dep()
        mx = sbl("mx", [128, 8, 128]); mx_d = Dep()
        mxb = sbl("mxb", [128, 8, 128], BF16); mxb_d = Dep()
        h1 = sbl("h1", [128, D]); h1_d = Dep()
        hb = sbl("hb", [128, D]); hb_d = Dep()
        junk = sbl("junk", [128, D]); junk_d = Dep()
        junkb = sbl("junkb", [128, D], BF16)
        hbb = sbl("hbb", [128, D], BF16); hbb_d = Dep()
        hbT = sbl("hbT", [128, 8, 128], BF16); hbT_d = Dep()
        qT = sbl("qT", [128, 16, 128]); qT_d = Dep()
        sc = sbl("sc", [128, 16, 128]); sc_d = Dep()
        sc2 = sbl("sc2", [128, 1, 128]); sc2_d = Dep()
        m16 = sbl("m16", [128, 16, 16]); i16 = sbl("i16", [128, 16, 16], U32); i16f = sbl("i16f", [128, 16, 16])
        cand = sbl("cand", [128, 8, 256]); cand2 = sbl("cand2", [128, 8, 256])
        best = sbl("best", [128, 8, 16]); pos = sbl("pos", [128, 8, 16], U32)
        pa_i = sbl("pa_i", [128, 8, 16], U32); pb_i = sbl("pb_i", [128, 8, 16], U32)
        pa_f = sbl("pa_f", [128, 8, 16]); pb_f = sbl("pb_f", [128, 8, 16])
        eq = cand2[:].rearrange("p h (a b) -> p h a b", b=16)
        i1s = sbl("i1s", [128, 8, 16]); i2s = sbl("i2s", [128, 8, 16])
        idxf = sbl("idxf", [128, 128]); idxi = sbl("idxi", [128, 128], I32)
        gate = sbl("gate", [128, 8, 16]); gsm = sbl("gsm", [128, 8, 2])
        tk_d = Dep()
        actr = sbl("actr", [128, 128]); act_d = Dep()
        agd = [Dep() for _ in range(64)]
        asd = [Dep() for _ in range(128)]
        wgt = sbl("wgt", [128, 128])
        st = sbl("st", [128, 8]); st_d = Dep()
        SPLIT = os.environ.get("P4_SPLIT", "0") == "1"
        NG = int(os.environ.get("P4_NG", "4" if SPLIT else "5"))
        if SPLIT:
            prod = [sbl(f"prod{i}", [128, 1024]) for i in range(2)]; prod_d = [Dep(), Dep()]
        dg = [sbl(f"dg{i}", [128, 128]) for i in range(4)]; dg_d = [Dep() for _ in range(4)]
        dgb = [sbl(f"dgb{i}", [128, 128], BF16) for i in range(4)]; dgb_d = [Dep() for _ in range(4)]
        oo = sbl("oo", [128, D]); oo_d = Dep()
        pb_ = [psl(f"b{i}") for i in range(8)]; pb_d = [PD() for _ in range(8)]
        d0 = Dep()
        with ExitStack() as es2:
            krow = es2.enter_context(nc.sbuf_tensor("P_krow", [128, 16, 128], F32))
            nrow = es2.enter_context(nc.sbuf_tensor("P_nrow", [1, 2, D], F32))
            one1 = es2.enter_context(nc.sbuf_tensor("P_one1", [1, 128], F32))
            S.dma("sp", lambda e: e.dma_start(out=krow[:], in_=I["peer_keys"]), writes=[d0])
            S.dma("sp", lambda e: e.dma_start(out=nrow[:], in_=I["nvec"].rearrange("(o a) n -> o a n", o=1)), writes=[d0])
            T("dve", lambda e: e.memset(one1[:], 1.0), writes=[d0])
            T("dve", lambda e: e.memset(ones[:], 1.0), writes=[cst])
            T("dve", lambda e: e.tensor_copy(io16[:], k.iota_ff[:, 0:16]), reads=[k.ident_d], writes=[cst])
            for j in range(16):
                T("pe", lambda e, j=j: e.transpose(pb_[j % 4][:, 0:128], krow[:, j, :], k.ident[:]), reads=[d0, k.ident_d], writes=[pb_d[j % 4]])
                T("act", lambda e, j=j: e.copy(keysT[:, j, :], pb_[j % 4][:, 0:128]), reads=[pb_d[j % 4]], writes=[cst])
            for a in range(2):
                for hf in range(2):
                    T("pe", _mm(pb_[4 + hf][:, :], one1[0:1, :], nrow[0:1, a, hf * 512:(hf + 1) * 512]), reads=[d0], writes=[pb_d[4 + hf]])
                    T("act", lambda e, a=a, hf=hf: e.copy(nbc[:, a, hf * 512:(hf + 1) * 512], pb_[4 + hf][:, :]), reads=[pb_d[4 + hf]], writes=[cst])
            S.barrier()
        gb = [sbl(f"gb{i}", [128, 2, 2048], BF16) for i in range(NG)]; gb_d = [[Dep(), Dep()] for _ in range(NG)]
        NC5 = NB + 1
        gcnt = 0
        dcnt = 0
        for b in range(NB):
            for vi, j0 in enumerate([16, 32, 24, 40]):
                for jj in range(8):
                    dgt, dgd = dg[dcnt % 4], dg_d[dcnt % 4]
                    pp, ppd = pb_[dcnt % 4], pb_d[dcnt % 4]
                    dcnt += 1
                    T("dve", lambda e, dgt=dgt, j0=j0, jj=jj, b=b: e.tensor_scalar(dgt[:], k.ident[:], k.modT[:, j0 + jj, b:b + 1], None, op0=ALU.mult),
                      reads=[k.ident_d, k.modT_d], writes=[dgd])
                    T("pe", _mm(pp[:, 0:128], ones[:], dgt[:]), reads=[cst, dgd], writes=[ppd])
                    T("act", lambda e, pp=pp, vi=vi, jj=jj: e.copy(bc[:, vi, jj * 128:(jj + 1) * 128], pp[:, 0:128]), reads=[ppd], writes=[bc_d])
            T("dve", lambda e: e.tensor_scalar(bc[:, 1, :], bc[:, 1, :], 1.0, None, op0=ALU.add), reads=[bc_d], writes=[bc_d])
            T("dve", lambda e: e.tensor_tensor(bc[:, 1, :], bc[:, 1, :], nbc[:, 0, :], op=ALU.mult), reads=[bc_d, cst], writes=[bc_d])
            for tt in range(NT4):
                t0 = tt * 128
                S.dma("sp", lambda e, b=b, t0=t0: e.dma_start(out=xt[:], in_=I["x"][b, t0:t0 + 128, :]), writes=[xt_d])
                S.dma("act", lambda e, b=b, t0=t0: e.dma_start(out=mx[:], in_=k.MIXT[b].rearrange("(kc p) t -> p kc t", p=128)[:, :, t0:t0 + 128]),
                      reads=[k.MIXT_dep[b]], writes=[mx_d])
                T("act", lambda e: e.copy(mxb[:], mx[:]), reads=[mx_d], writes=[mxb_d])
                for hf in range(2):
                    for kc in range(8):
                        T("pe", _mm(pb_[hf][:, :], mxb[:, kc, :], wo[:, kc, hf * 512:(hf + 1) * 512], start=(kc == 0), stop=(kc == 7)),
                          reads=[mxb_d, cst], writes=[pb_d[hf]])
                    hs = slice(hf * 512, (hf + 1) * 512)
                    T("dve", lambda e, hf=hf, hs=hs: e.tensor_tensor(h1[:, hs], pb_[hf][:, :], bc[:, 0, hs], op=ALU.mult), reads=[pb_d[hf], bc_d], writes=[h1_d])
                    T("dve", lambda e, hs=hs: e.tensor_tensor(h1[:, hs], h1[:, hs], xt[:, hs], op=ALU.add), reads=[xt_d, h1_d], writes=[h1_d])
                T("act", lambda e: e.activation(junk[:], h1[:], AF.Square, accum_out=st[:, 0:1]), reads=[h1_d], writes=[junk_d, st_d])
                T("act", lambda e: e.activation(st[:, 1:2], st[:, 0:1], AF.Sqrt, bias=1e-6, scale=1.0 / D), reads=[st_d], writes=[st_d])
                T("dve", lambda e: e.reciprocal(st[:, 2:3], st[:, 1:2]), reads=[st_d], writes=[st_d])
                T("act", lambda e: e.activation(hb[:], h1[:], AF.Copy, scale=st[:, 2:3]), reads=[h1_d, st_d], writes=[hb_d])
                T("dve", lambda e: e.tensor_tensor(hb[:], hb[:], bc[:, 1, :], op=ALU.mult), reads=[hb_d, bc_d], writes=[hb_d])
                T("dve", lambda e: e.tensor_tensor(hb[:], hb[:], bc[:, 2, :], op=ALU.add), reads=[hb_d, bc_d], writes=[hb_d])
                T("act", lambda e: e.copy(hbb[:], hb[:]), reads=[hb_d], writes=[hbb_d])
                for kd in range(8):
                    bk_ = 2 + kd // 4
                    T("pe", lambda e, kd=kd, bk_=bk_: e.transpose(pb_[bk_][:, (kd % 4) * 128:(kd % 4 + 1) * 128], hb[:, kd * 128:(kd + 1) * 128], k.ident[:]),
                      reads=[hb_d, k.ident_d], writes=[pb_d[bk_]])
                for q in range(2):
                    T("act", lambda e, q=q: e.copy(hbT[:, q * 4:(q + 1) * 4, :].rearrange("p a t -> p (a t)"), pb_[2 + q][:, :]), reads=[pb_d[2 + q]], writes=[hbT_d])
                for j in range(16):
                    pp, ppd = pb_[4 + j % 2], pb_d[4 + j % 2]
                    for kd in range(8):
                        T("pe", _mm(pp[:, 0:128], wq[:, kd, j * 128:(j + 1) * 128], hbT[:, kd, :], start=(kd == 0), stop=(kd == 7)),
                          reads=[cst, hbT_d], writes=[ppd])
                    T("act", lambda e, pp=pp, j=j: e.copy(qT[:, j, :], pp[:, 0:128]), reads=[ppd], writes=[qT_d])
                for j in range(16):
                    bk_ = j // 4
                    T("pe", _mm(pb_[bk_][:, (j % 4) * 128:(j % 4 + 1) * 128], qT[:, j, :], keysT[:, j, :]), reads=[qT_d, cst], writes=[pb_d[bk_]])
                for q in range(4):
                    T("act", lambda e, q=q: e.copy(sc[:, q * 4:(q + 1) * 4, :].rearrange("p a n -> p (a n)"), pb_[q][:, :]), reads=[pb_d[q]], writes=[sc_d])
                for j in range(16):
                    T("dve", lambda e, j=j: e.max(m16[:, j, 0:8], sc[:, j, :]), reads=[sc_d], writes=[tk_d])
                    T("dve", lambda e, j=j: e.match_replace(sc2[:, 0, :], m16[:, j, 0:8], sc[:, j, :], -3.0e38), reads=[sc_d, tk_d], writes=[sc2_d])
                    T("dve", lambda e, j=j: e.max(m16[:, j, 8:16], sc2[:, 0, :]), reads=[sc2_d], writes=[tk_d])
                    T("dve", lambda e, j=j: e.max_index(i16[:, j, 0:8], m16[:, j, 0:8], sc[:, j, :]), reads=[sc_d, tk_d], writes=[tk_d])
                    T("dve", lambda e, j=j: e.max_index(i16[:, j, 8:16], m16[:, j, 8:16], sc2[:, 0, :]), reads=[sc2_d, tk_d], writes=[tk_d])
                T("dve", lambda e: e.tensor_copy(i16f[:], i16[:]), reads=[tk_d], writes=[tk_d])
                m16v = m16[:].rearrange("p (h c) a -> p h c a", c=2)
                i16v = i16f[:].rearrange("p (h c) a -> p h c a", c=2)
                candv = cand[:].rearrange("p h (a b) -> p h a b", b=16)
                T("dve", lambda e: e.tensor_tensor(candv, m16v[:, :, 0, :].unsqueeze(3).to_broadcast([128, 8, 16, 16]),
                                                   m16v[:, :, 1, :].unsqueeze(2).to_broadcast([128, 8, 16, 16]), op=ALU.add), reads=[tk_d], writes=[tk_d])
                for h in range(8):
                    T("dve", lambda e, h=h: e.max(best[:, h, 0:8], cand[:, h, :]), reads=[tk_d], writes=[tk_d])
                    T("dve", lambda e, h=h: e.match_replace(cand2[:, h, :], best[:, h, 0:8], cand[:, h, :], -3.0e38), reads=[tk_d], writes=[tk_d])
                    T("dve", lambda e, h=h: e.max(best[:, h, 8:16], cand2[:, h, :]), reads=[tk_d], writes=[tk_d])
                    T("dve", lambda e, h=h: e.max_index(pos[:, h, 0:8], best[:, h, 0:8], cand[:, h, :]), reads=[tk_d], writes=[tk_d])
                    T("dve", lambda e, h=h: e.max_index(pos[:, h, 8:16], best[:, h, 8:16], cand2[:, h, :]), reads=[tk_d], writes=[tk_d])
                T("dve", lambda e: e.tensor_tensor(gate[:], best[:], best[:, :, 0:1].to_broadcast([128, 8, 16]), op=ALU.subtract), reads=[tk_d], writes=[tk_d])
                T("act", lambda e: e.activation(gate[:], gate[:], AF.Exp), reads=[tk_d], writes=[tk_d])
                T("dve", lambda e: e.tensor_reduce(gsm[:, :, 0], gate[:], axis=AX.X, op=ALU.add), reads=[tk_d], writes=[tk_d])
                T("dve", lambda e: e.reciprocal(gsm[:, :, 1], gsm[:, :, 0]), reads=[tk_d], writes=[tk_d])
                T("dve", lambda e: e.tensor_tensor(gate[:], gate[:], gsm[:, :, 1:2].to_broadcast([128, 8, 16]), op=ALU.mult), reads=[tk_d], writes=[tk_d])
                T("dve", lambda e: e.tensor_single_scalar(pa_i[:], pos[:], 4, op=ALU.logical_shift_right), reads=[tk_d], writes=[tk_d])
                T("dve", lambda e: e.tensor_single_scalar(pb_i[:], pos[:], 15, op=ALU.bitwise_and), reads=[tk_d], writes=[tk_d])
                T("dve", lambda e: e.tensor_copy(pa_f[:], pa_i[:]), reads=[tk_d], writes=[tk_d])
                T("dve", lambda e: e.tensor_copy(pb_f[:], pb_i[:]), reads=[tk_d], writes=[tk_d])
                io_b = io16[:].unsqueeze(1).unsqueeze(1).to_broadcast([128, 8, 16, 16])
                for (pf, cc, dst) in [(pa_f, 0, i1s), (pb_f, 1, i2s)]:
                    T("dve", lambda e, pf=pf: e.tensor_tensor(eq, pf[:].unsqueeze(3).to_broadcast([128, 8, 16, 16]), io_b, op=ALU.is_equal),
                      reads=[tk_d, cst], writes=[tk_d])
                    T("dve", lambda e, cc=cc: e.tensor_tensor(eq, eq, i16v[:, :, cc, :].unsqueeze(2).to_broadcast([128, 8, 16, 16]), op=ALU.mult),
                      reads=[tk_d], writes=[tk_d])
                    T("dve", lambda e, dst=dst: e.tensor_reduce(dst[:], eq, axis=AX.X, op=ALU.add), reads=[tk_d], writes=[tk_d])
                T("dve", lambda e: e.scalar_tensor_tensor(idxf[:], i1s[:].rearrange("p h k -> p (h k)"), 128.0, i2s[:].rearrange("p h k -> p (h k)"),
                                                          op0=ALU.mult, op1=ALU.add), reads=[tk_d], writes=[tk_d])
                T("dve", lambda e: e.tensor_copy(idxi[:], idxf[:]), reads=[tk_d], writes=[tk_d])
                gflat = gate[:].rearrange("p h k -> p (h k)")
                NGRP = NSLOT // 2
                ginfo = {}

                def stage_a(g):
                    nonlocal gcnt
                    gbt, gbd = gb[gcnt % NG], gb_d[gcnt % NG]
                    gcnt += 1
                    ginfo[g] = (gbt, gbd)
                    for s2 in range(2):
                        slot = g * 2 + s2
                        S.dma("pool", lambda e, gbt=gbt, s2=s2, slot=slot: e.indirect_dma_start(
                            out=gbt[:, s2, :], out_offset=None, in_=k.UVB[:, :],
                            in_offset=bass.IndirectOffsetOnAxis(ap=idxi[:, slot:slot + 1], axis=0)),
                            reads=[tk_d, k.UVB_dep], writes=[gbd[s2]])
                    for s2 in range(2):
                        slot = g * 2 + s2
                        if s2 == 1 and SPLIT:
                            pr_, prd_ = prod[g % 2], prod_d[g % 2]
                            T("pool", lambda e, gbt=gbt, pr_=pr_: e.tensor_tensor(pr_[:], gbt[:, 1, 0:1024], hbb[:], op=ALU.mult),
                              reads=[gbd[1], hbb_d], writes=[prd_])
                            T("act", lambda e, pr_=pr_, slot=slot: e.activation(junk[:], pr_[:], AF.Copy, accum_out=actr[:, slot:slot + 1]),
                              reads=[prd_], writes=[junk_d, asd[slot]])
                            continue
                        T("dve", lambda e, gbt=gbt, s2=s2, slot=slot: e.scalar_tensor_tensor(
                            junkb[:], gbt[:, s2, 0:1024], 1.0, hbb[:], op0=ALU.mult, op1=ALU.mult,
                            accum_out=actr[:, slot:slot + 1]), reads=[gbd[s2], hbb_d], writes=[asd[slot]])
                    sl = slice(g * 2, g * 2 + 2)
                    T("act", lambda e, sl=sl: e.activation(wgt[:, sl], actr[:, sl], AF.Gelu), reads=[asd[g * 2], asd[g * 2 + 1]], writes=[agd[g]])

                def stage_b(g):
                    nonlocal dcnt
                    gbt, gbd = ginfo.pop(g)
                    ad_ = agd[g]
                    sl = slice(g * 2, g * 2 + 2)
                    T("dve", lambda e, sl=sl: e.tensor_tensor(wgt[:, sl], wgt[:, sl], gflat[:, sl], op=ALU.mult), reads=[ad_, tk_d], writes=[ad_])
                    for s2 in range(2):
                        slot = g * 2 + s2
                        dgt, dgd = dgb[dcnt % 4], dgb_d[dcnt % 4]
                        dcnt += 1
                        T("act", lambda e, dgt=dgt, slot=slot: e.activation(dgt[:], k.ident[:], AF.Copy, scale=wgt[:, slot:slot + 1]),
                          reads=[k.ident_d, ad_], writes=[dgd])
                        for hf in range(2):
                            T("pe", _mm(pb_[6 + hf][:, :], dgt[:], gbt[:, s2, 1024 + hf * 512:1024 + (hf + 1) * 512],
                                        start=(slot == 0), stop=(slot == NSLOT - 1)), reads=[dgd, gbd[s2]], writes=[pb_d[6 + hf]])

                SKEW = 2
                for g in range(NGRP + SKEW):
                    if g < NGRP:
                        stage_a(g)
                    if g >= SKEW:
                        stage_b(g - SKEW)
                for hf in range(2):
                    hs = slice(hf * 512, (hf + 1) * 512)
                    T("dve", lambda e, hf=hf, hs=hs: e.tensor_tensor(oo[:, hs], pb_[6 + hf][:, :], bc[:, 3, hs], op=ALU.mult), reads=[pb_d[6 + hf], bc_d], writes=[oo_d])
                    T("dve", lambda e, hs=hs: e.tensor_tensor(oo[:, hs], oo[:, hs], h1[:, hs], op=ALU.add), reads=[oo_d, h1_d], writes=[oo_d])
                T("act", lambda e: e.activation(junk[:], oo[:], AF.Square, accum_out=st[:, 4:5]), reads=[oo_d], writes=[junk_d, st_d])
                T("act", lambda e: e.activation(st[:, 5:6], st[:, 4:5], AF.Sqrt, bias=1e-6, scale=1.0 / D), reads=[st_d], writes=[st_d])
                T("dve", lambda e: e.reciprocal(st[:, 6:7], st[:, 5:6]), reads=[st_d], writes=[st_d])
                T("act", lambda e: e.activation(oo[:], oo[:], AF.Copy, scale=st[:, 6:7]), reads=[oo_d, st_d], writes=[oo_d])
                T("dve", lambda e: e.tensor_tensor(oo[:], oo[:], nbc[:, 1, :], op=ALU.mult), reads=[oo_d, cst], writes=[oo_d])
                S.dma("sp", lambda e, b=b, t0=t0: e.dma_start(out=k.out[b, t0:t0 + 128, :], in_=oo[:]), reads=[oo_d], writes=[Dep()])
        S.barrier()
```

```python
import numpy as np
from contextlib import ExitStack
import concourse.bass as bass
import concourse.mybir as mybir
from concourse.bass_utils import run_bass_kernel_spmd

F32 = mybir.dt.float32
BF16 = mybir.dt.bfloat16
I32 = mybir.dt.int32
U32 = mybir.dt.uint32
AF = mybir.ActivationFunctionType
ALU = mybir.AluOpType
AX = mybir.AxisListType

D = 1024
SEQ = 2048
CTX = 256
LT = SEQ + CTX
INC = 2208
NCORES = 8
NBATCH = 32


class Dep:
    __slots__ = ("w", "r", "excl")

    def __init__(self, excl=False):
        self.w = None
        self.r = {}
        self.excl = excl


def PD():
    return Dep(excl=True)


class Sch:
    ROT = 30000

    def __init__(self, nc, es):
        self.nc = nc
        self.es = es
        self.eng = {"pe": nc.tensor, "dve": nc.vector, "act": nc.scalar, "pool": nc.gpsimd, "sp": nc.sync}
        self.cur = {}
        self.cnt = {}
        self.seen = {e: {} for e in self.eng}
        self.nsem = 0
        for e in self.eng:
            self._newsem(e)
        self.dsems = []
        for i in range(40):
            s = es.enter_context(nc.semaphore(f"dq{i}"))
            self.dsems.append([s, 0])
        self.dnext = 0
        self.swsems = []
        for i in range(16):
            s = es.enter_context(nc.semaphore(f"sq{i}"))
            self.swsems.append([s, 0])
        self.swnext = 0
        self.semobj = {}
        self.ninst = 0
        import os
        self.skip_own = set(os.environ.get("SKIP_OWN", "pe").split(","))

    def _newsem(self, e):
        s = self.es.enter_context(self.nc.semaphore(f"e_{e}_{self.nsem}"))
        self.nsem += 1
        self.cur[e] = s
        self.cnt[e] = 0

    def _wait(self, en, deps):
        best = {}
        for (s, v) in deps:
            k = id(s)
            if k not in best or best[k][1] < v:
                best[k] = (s, v)
        seen = self.seen[en]
        for k, (s, v) in best.items():
            if seen.get(k, 0) >= v:
                continue
            self.eng[en].wait_ge(s, v)
            self.nwait = getattr(self, "nwait", 0) + 1
            seen[k] = v

    def _deps(self, reads, writes):
        deps = []
        for d in reads:
            if d.w is not None:
                deps.append(d.w)
        for d in writes:
            if d.w is not None:
                deps.append(d.w)
            deps.extend(d.r.values())
        return deps

    def _mark(self, ev, reads, writes):
        for d in reads:
            d.r[id(ev[0])] = ev
        for d in writes:
            d.w = ev
            d.r = {}

    def op(self, en, fn, reads=(), writes=()):
        ex = [d for d in reads if d.excl]
        if ex:
            reads = [d for d in reads if not d.excl]
            writes = list(writes) + ex
        deps = self._deps(reads, writes)
        if en in self.skip_own:
            own = id(self.cur[en])
            deps = [d for d in deps if id(d[0]) != own]
        self._wait(en, deps)
        ins = fn(self.eng[en])
        if self.cnt[en] >= self.ROT:
            self._newsem(en)
        self.cnt[en] += 1
        ins.then_inc(self.cur[en], 1)
        ev = (self.cur[en], self.cnt[en])
        self._mark(ev, reads, writes)
        self.ninst += 1
        return ev

    def dma(self, q, fn, reads=(), writes=()):
        if q == "pool":
            slot = self.swsems[self.swnext]
            self.swnext = (self.swnext + 1) % len(self.swsems)
        else:
            slot = self.dsems[self.dnext]
            self.dnext = (self.dnext + 1) % len(self.dsems)
        deps = self._deps(reads, writes)
        if slot[1] > 0:
            deps.append((slot[0], slot[1]))
        self._wait(q, deps)
        ins = fn(self.eng[q])
        slot[1] += 16
        ins.then_inc(slot[0], 16)
        ev = (slot[0], slot[1])
        self._mark(ev, reads, writes)
        self.ninst += 1
        return ev

    def barrier(self):
        evs = [(self.cur[e], self.cnt[e]) for e in self.eng if self.cnt[e] > 0]
        evs += [(s, v) for (s, v) in self.dsems + self.swsems if v > 0]
        for e in self.eng:
            self._wait(e, evs)


def _mm(out, lhsT, rhs, start=True, stop=True, tp=None):
    return lambda e: e.matmul(out, lhsT, rhs, start=start, stop=stop, tile_position=tp)


class K:
    pass


def build(NB=4, upto=99, dbg=()):
    nc = bass.Bass("TRN2", target_bir_lowering=False)
    es = ExitStack()
    k = K()
    k.nc, k.es, k.NB = nc, es, NB
    S = k.S = Sch(nc, es)

    def din(name, shape, dt=F32):
        return nc.dram_tensor(name, list(shape), dt, kind="ExternalInput").ap()

    I = k.I = {}
    I["x"] = din("x", [NB, SEQ, D])
    I["c"] = din("c", [NB, D])
    I["ctx"] = din("ctx", [NB, CTX, D])
    I["c_ctx"] = din("c_ctx", [1, D])
    I["w_ada"] = din("w_ada", [D, 6 * D])
    I["b_ada"] = din("b_ada", [48, 128])
    I["norm1_g"] = din("norm1_g", [8, 128])
    I["norm2_g"] = din("norm2_g", [1, D])
    I["w_in"] = din("w_in", [D, INC])
    I["s5_arow"] = din("s5_arow", [3, 32, 128])
    I["s5_bT"] = din("s5_bT", [2, 128, 1024])
    I["s5_cblk"] = din("s5_cblk", [2, 128, 8, 128])
    I["s5_vec"] = din("s5_vec", [8, 128])
    I["s5_w_glu"] = din("s5_w_glu", [512, 512])
    I["rw_vec"] = din("rw_vec", [42, 128])
    I["rw_wlora"] = din("rw_wlora", [64, 2, 512])
    I["rw_w_g2"] = din("rw_w_g2", [96, 512])
    I["rw_ln"] = din("rw_ln", [2, 512])
    I["w_out"] = din("w_out", [D, D])
    I["peer_w_q"] = din("peer_w_q", [D, 2048])
    I["peer_keys"] = din("peer_keys", [128, 16, 128])
    I["peer_uv"] = din("peer_uv", [16384, 2048])
    I["nvec"] = din("nvec", [2, D])
    k.out = nc.dram_tensor("out", [NB, SEQ, D], F32, kind="ExternalOutput").ap()
    k.dbg = {}
    for (name, shape) in dbg:
        if name in ("PT", "MIXT") or shape is None:
            continue
        k.dbg[name] = nc.dram_tensor(name, list(shape), F32, kind="ExternalOutput").ap()
    dbgn = [d[0] for d in dbg]
    k.PT = nc.dram_tensor("PT", [NB, INC, LT], F32, kind="ExternalOutput" if "PT" in dbgn else "Internal").ap()
    k.PT_dep = [Dep() for _ in range(NB)]
    k.MIXT = nc.dram_tensor("MIXT", [NB, D, SEQ], F32, kind="ExternalOutput" if "MIXT" in dbgn else "Internal").ap()
    k.MIXT_dep = [Dep() for _ in range(NB)]
    k.UVB = nc.dram_tensor("UVB", [16384, 2048], BF16, kind="Internal").ap()
    k.UVB_dep = Dep()

    with es:
        setup_consts(k)
        k.modT = sb(k, "modT", [128, 48, NB + 1])
        k.gs1T = sb(k, "gs1T", [128, 8, NB + 1])
        stage0_mod(k)
        if upto >= 1:
            stage1_proj(k)
        if upto >= 2 and "skip_s5" not in dbgn:
            with ExitStack() as es2:
                k.es_stage = es2
                s5_alloc(k)
                s5_setup(k)
                stage2_s5(k)
        if upto >= 3:
            with ExitStack() as es3:
                k.es_stage = es3
                rw_setup(k)
                stage3_rwkv(k)
        if upto >= 4:
            cast_uv(k)
            stage4_peer(k)
        S.barrier()
        print("ninst", S.ninst, "nwait", getattr(S, "nwait", 0))
    return nc


def sb(k, name, shape, dt=F32):
    return k.es.enter_context(k.nc.sbuf_tensor(name, list(shape), dt))


def ps(k, name, shape, dt=F32):
    return k.es.enter_context(k.nc.psum_tensor(name, list(shape), dt))


def setup_consts(k):
    nc, S = k.nc, k.S
    k.ident = sb(k, "ident", [128, 128])
    k.ident_d = Dep()
    k.iota_p = sb(k, "iota_p", [128, 1], I32)
    k.iota_f = sb(k, "iota_f", [128, 128], I32)
    k.iota_d = Dep()
    S.op("pool", lambda e: e.iota(k.iota_p[:], [[0, 1]], base=0, channel_multiplier=1), writes=[k.iota_d])
    S.op("pool", lambda e: e.iota(k.iota_f[:], [[1, 128]], base=0, channel_multiplier=0), writes=[k.iota_d])
    k.iota_pf = sb(k, "iota_pf", [128, 1])
    k.iota_ff = sb(k, "iota_ff", [128, 128])
    S.op("dve", lambda e: e.tensor_copy(k.iota_pf[:], k.iota_p[:]), reads=[k.iota_d], writes=[k.ident_d])
    S.op("dve", lambda e: e.tensor_copy(k.iota_ff[:], k.iota_f[:]), reads=[k.iota_d], writes=[k.ident_d])
    S.op("dve", lambda e: e.tensor_scalar(k.ident[:], k.iota_ff[:], k.iota_pf[:, 0:1], None, op0=ALU.is_equal),
         reads=[k.ident_d], writes=[k.ident_d])


def stage0_mod(k):
    nc, S, I, NB = k.nc, k.S, k.I, k.NB
    NC5 = NB + 1
    with ExitStack() as es:
        def sbl(name, shape, dt=F32):
            return es.enter_context(nc.sbuf_tensor(name, list(shape), dt))
        crow = sbl("crow", [NC5, D])
        crow_d = Dep()
        S.dma("sp", lambda e: e.dma_start(out=crow[0:NB, :], in_=I["c"][:, :]), writes=[crow_d])
        S.dma("sp", lambda e: e.dma_start(out=crow[NB:NC5, :], in_=I["c_ctx"][:, :]), writes=[crow_d])
        S.op("act", lambda e: e.activation(crow[:], crow[:], AF.Silu), reads=[crow_d], writes=[crow_d])
        cT = sbl("cT", [128, 8, NC5])
        cT_d = Dep()
        vst = sbl("vst", [64, 128])
        vst_d = Dep()
        S.dma("sp", lambda e: e.dma_start(out=vst[0:48, :], in_=I["b_ada"][:, :]), writes=[vst_d])
        S.dma("sp", lambda e: e.dma_start(out=vst[48:56, :], in_=I["norm1_g"][:, :]), writes=[vst_d])
        vT = sbl("vT", [128, 56])
        vT_d = Dep()
        with nc.psum_tensor("p0a", [128, 8, NC5], F32) as pa, nc.psum_tensor("p0b", [128, 56], F32) as pb, \
                nc.psum_tensor("p0c", [128, 48, NC5], F32) as pc:
            pa_d, pb_d, pc_d = PD(), PD(), PD()
            for kd in range(8):
                S.op("pe", lambda e, kd=kd: e.transpose(pa[:, kd, :], crow[0:NC5, kd * 128:(kd + 1) * 128],
                                                        k.ident[0:NC5, 0:NC5]),
                     reads=[crow_d, k.ident_d], writes=[pa_d])
            S.op("dve", lambda e: e.tensor_copy(cT[:], pa[:]), reads=[pa_d], writes=[cT_d])
            S.op("pe", lambda e: e.transpose(pb[:, :], vst[0:56, :], k.ident[0:56, 0:56]),
                 reads=[vst_d, k.ident_d], writes=[pb_d])
            S.op("dve", lambda e: e.tensor_copy(vT[:], pb[:]), reads=[pb_d], writes=[vT_d])
            wt = [sbl(f"wada{i}", [128, 8, 512]) for i in range(2)]
            wt_d = [Dep(), Dep()]
            wv = I["w_ada"].rearrange("(kd p) n -> p kd n", p=128)
            for blk in range(12):
                t, td = wt[blk % 2], wt_d[blk % 2]
                for kd in range(8):
                    S.dma("sp" if kd % 2 == 0 else "act",
                          lambda e, kd=kd, t=t, blk=blk: e.dma_start(out=t[:, kd, :], in_=wv[:, kd, blk * 512:(blk + 1) * 512]),
                          writes=[td])
                for jj in range(4):
                    j = blk * 4 + jj
                    for kd in range(8):
                        S.op("pe", _mm(pc[:, j, :], t[:, kd, jj * 128:(jj + 1) * 128], cT[:, kd, :],
                                       start=(kd == 0), stop=(kd == 7)),
                             reads=[td, cT_d], writes=[pc_d])
            k.modT_d = Dep()
            S.op("dve", lambda e: e.tensor_tensor(k.modT[:], pc[:], vT[:, 0:48].unsqueeze(2).to_broadcast([128, 48, NC5]),
                                                  op=ALU.add),
                 reads=[pc_d, vT_d], writes=[k.modT_d])
        S.op("dve", lambda e: e.tensor_scalar(k.gs1T[:], k.modT[:, 8:16, :], 1.0, None, op0=ALU.add),
             reads=[k.modT_d], writes=[k.modT_d])
        S.op("dve", lambda e: e.tensor_tensor(k.gs1T[:], k.gs1T[:], vT[:, 48:56].unsqueeze(2).to_broadcast([128, 8, NC5]),
                                              op=ALU.mult),
             reads=[vT_d, k.modT_d], writes=[k.modT_d])
        k.S.barrier()


def stage1_proj(k):
    nc, S, I, NB = k.nc, k.S, k.I, k.NB
    with ExitStack() as es:
        def sbl(name, shape, dt=F32):
            return es.enter_context(nc.sbuf_tensor(name, list(shape), dt))
        wbf = sbl("w_in_bf", [128, 8, INC], BF16)
        wbf_d = Dep()
        for kd in range(8):
            S.dma("pool", lambda e, kd=kd: e.dma_start(out=wbf[:, kd, :], in_=I["w_in"][kd * 128:(kd + 1) * 128, :],
                                                       max_dma_last_dim=4096), writes=[wbf_d])
        xt = [sbl(f"xt{i}", [128, D]) for i in range(2)]
        xt_d = [Dep(), Dep()]
        xs = [sbl(f"xs{i}", [128, D]) for i in range(2)]
        xs_d = [Dep(), Dep()]
        junk = sbl("junk1", [128, D])
        junk_d = Dep()
        st = [sbl(f"st{i}", [128, 4]) for i in range(2)]
        hnT = [sbl(f"hnT{i}", [128, 8, 512], BF16) for i in range(2)]
        hnT_d = [Dep(), Dep()]
        ev = [sbl(f"ev{i}", [128, 512]) for i in range(3)]
        ev_d = [Dep() for _ in range(3)]
        ptr = [es.enter_context(nc.psum_tensor(f"ptr{i}", [128, 8, 128], F32)) for i in range(2)]
        ptr_d = [PD(), PD()]
        pmm = [es.enter_context(nc.psum_tensor(f"pmm{i}", [128, 512], F32)) for i in range(3)]
        pmm_d = [PD() for _ in range(3)]
        fch = [(i * 128, 128) for i in range(16)] + [(2048, 64), (2112, 96)]
        k.fch = fch
        ti = 0
        gi = 0
        ei = 0
        for b in range(NB):
            groups = [("ctx", 0, 256)] + [("lat", g * 512, 512) for g in range(4)]
            for (kind, t0, nt) in groups:
                h, hd = hnT[gi % 2], hnT_d[gi % 2]
                gi += 1
                col = b if kind == "lat" else NB
                for tt in range(nt // 128):
                    x_t, x_d = xt[ti % 2], xt_d[ti % 2]
                    xs_t, xsd = xs[ti % 2], xs_d[ti % 2]
                    s_t = st[ti % 2]
                    p_t, p_d = ptr[ti % 2], ptr_d[ti % 2]
                    ti += 1
                    src = I["x"][b, t0 + tt * 128:t0 + (tt + 1) * 128, :] if kind == "lat" else \
                        I["ctx"][b, tt * 128:(tt + 1) * 128, :]
                    S.dma("sp", lambda e, x_t=x_t, src=src: e.dma_start(out=x_t[:], in_=src), writes=[x_d])
                    S.op("act", lambda e, x_t=x_t, s_t=s_t: e.activation(junk[:], x_t[:], AF.Square, accum_out=s_t[:, 0:1]),
                         reads=[x_d], writes=[junk_d, xsd])
                    S.op("act", lambda e, s_t=s_t: e.activation(s_t[:, 1:2], s_t[:, 0:1], AF.Sqrt, bias=1e-6, scale=1.0 / D),
                         reads=[xsd], writes=[xsd])
                    S.op("dve", lambda e, s_t=s_t: e.reciprocal(s_t[:, 2:3], s_t[:, 1:2]), reads=[xsd], writes=[xsd])
                    S.op("act", lambda e, x_t=x_t, xs_t=xs_t, s_t=s_t: e.activation(xs_t[:], x_t[:], AF.Copy, scale=s_t[:, 2:3]),
                         reads=[x_d, xsd], writes=[xsd])
                    for kd in range(8):
                        S.op("pe", lambda e, kd=kd, p_t=p_t, xs_t=xs_t: e.transpose(p_t[:, kd, :], xs_t[:, kd * 128:(kd + 1) * 128],
                                                                                      k.ident[:]),
                             reads=[xsd, k.ident_d], writes=[p_d])
                    for kd in range(8):
                        S.op("dve", lambda e, kd=kd, p_t=p_t, h=h, tt=tt, col=col: e.tensor_scalar(
                            h[:, kd, tt * 128:(tt + 1) * 128], p_t[:, kd, :], k.gs1T[:, kd, col:col + 1],
                            k.modT[:, kd, col:col + 1], op0=ALU.mult, op1=ALU.add),
                            reads=[p_d, k.modT_d], writes=[hd])
                tok0 = t0 if kind == "ctx" else CTX + t0
                for fi, (c0, ncol) in enumerate(fch):
                    pm, pmd = pmm[ei % 3], pmm_d[ei % 3]
                    e_t, e_d = ev[ei % 3], ev_d[ei % 3]
                    ei += 1
                    for kd in range(8):
                        S.op("pe", _mm(pm[0:ncol, 0:nt], wbf[:, kd, c0:c0 + ncol], h[:, kd, 0:nt], start=(kd == 0), stop=(kd == 7)),
                             reads=[wbf_d, hd], writes=[pmd])
                    eng = "act" if fi % 2 == 0 else "dve"
                    if eng == "act":
                        S.op("act", lambda e, pm=pm, e_t=e_t, ncol=ncol, nt=nt: e.copy(e_t[0:ncol, 0:nt], pm[0:ncol, 0:nt]),
                             reads=[pmd], writes=[e_d])
                    else:
                        S.op("dve", lambda e, pm=pm, e_t=e_t, ncol=ncol, nt=nt: e.tensor_copy(e_t[0:ncol, 0:nt], pm[0:ncol, 0:nt]),
                             reads=[pmd], writes=[e_d])
                    S.dma("sp", lambda e, e_t=e_t, ncol=ncol, nt=nt, c0=c0, tok0=tok0, b=b: e.dma_start(
                        out=k.PT[b, c0:c0 + ncol, tok0:tok0 + nt], in_=e_t[0:ncol, 0:nt]),
                        reads=[e_d], writes=[k.PT_dep[b]])
        S.barrier()


_CACHE = {}


def _prep_inputs(inputs, NB, core):
    sl = slice(core * NB, (core + 1) * NB)
    f = lambda a: np.ascontiguousarray(a, dtype=np.float32)
    m = {
        "x": f(inputs["x"][sl]),
        "c": f(inputs["c"][sl]),
        "ctx": f(inputs["ctx"][sl]),
        "c_ctx": f(inputs["c_ctx"].reshape(1, D)),
        "w_ada": f(inputs["w_ada"][0]),
        "b_ada": f(inputs["b_ada"][0].reshape(48, 128)),
        "norm1_g": f(inputs["norm1_g"][0].reshape(8, 128)),
        "norm2_g": f(inputs["norm2_g"][0].reshape(1, D)),
        "w_in": f(inputs["w_in"][0]),
    }
    m.update(_s5_layout(inputs))
    m.update(_rw_layout(inputs))
    m.update(_peer_layout(inputs))
    return m


def kernel(**inputs):
    NB = NBATCH // NCORES
    if "nc" not in _CACHE:
        _CACHE["nc"] = build(NB)
    nc = _CACHE["nc"]
    in_maps = [_prep_inputs(inputs, NB, c) for c in range(NCORES)]
    res = run_bass_kernel_spmd(nc, in_maps, core_ids=list(range(NCORES)))
    return np.concatenate([r["out"] for r in res.results], axis=0)


def _s5_layout(inputs):
    f = lambda a: np.ascontiguousarray(a, dtype=np.float32)
    a_re, a_im, ldt = inputs["s5_a_re"][0], inputs["s5_a_im"][0], inputs["s5_log_dt"][0]
    arow = np.zeros((3, 32, 128), np.float32)
    arow[0] = a_re.reshape(2, 16, 128).reshape(32, 128)
    arow[1] = a_im.reshape(2, 16, 128).reshape(32, 128)
    arow[2] = np.repeat(ldt.reshape(2, 16, 2, 1), 64, axis=3).reshape(32, 128)
    bT = np.zeros((2, 2, 64, 2, 4, 4, 2, 16), np.float32)
    cb = np.zeros((2, 4, 2, 16, 2, 4, 2, 64), np.float32)
    for ri, (bsrc, csrc) in enumerate([(inputs["s5_b_re"][0], inputs["s5_c_re"][0]),
                                       (inputs["s5_b_im"][0], inputs["s5_c_im"][0])]):
        bg = bsrc.reshape(2, 4, 4, 2, 64, 16)
        cg = csrc.reshape(2, 4, 4, 2, 16, 64)
        for gl in range(2):
            bT[ri, gl, :, :, :, :, gl, :] = np.transpose(bg[:, :, :, gl], (3, 0, 1, 2, 4))
            cb[ri, :, gl, :, :, :, gl, :] = np.transpose(cg[:, :, :, gl], (2, 3, 0, 1, 4))
    vec = np.zeros((8, 128), np.float32)
    vec[0:4] = inputs["s5_d"][0].reshape(4, 128)
    vec[4:8] = inputs["s5_b_glu"][0].reshape(4, 128)
    return {"s5_arow": arow, "s5_bT": f(bT.reshape(2, 128, 1024)), "s5_cblk": f(cb.reshape(2, 128, 8, 128)),
            "s5_vec": vec, "s5_w_glu": f(inputs["s5_w_glu"][0])}


def sbs(k, name, shape, dt=F32):
    return k.es_stage.enter_context(k.nc.sbuf_tensor("S_" + name, list(shape), dt))


def s5_alloc(k):
    sb = sbs
    k.winj = [sb(k, f"winj{i}", [128, 8, 128]) for i in range(2)]
    k.rout = [sb(k, f"rout{i}", [128, 8, 128]) for i in range(2)]
    k.pw = [sb(k, f"pw{i}", [128, 32, 17]) for i in range(3)]
    k.lam = [sb(k, f"lam{i}", [128, 32, 8]) for i in range(3)]
    k.s5vT = sb(k, "s5vT", [128, 8])
    k.wglu = sb(k, "wglu", [128, 4, 512])
    k.s5_d = Dep()


def s5_setup(k):
    nc, S, I = k.nc, k.S, k.I
    T = S.op
    with ExitStack() as es:
        def sbl(name, shape, dt=F32):
            return es.enter_context(nc.sbuf_tensor(name, list(shape), dt))
        d0 = Dep()
        rows = sbl("s5rows", [32, 3, 128])
        S.dma("sp", lambda e: e.dma_start(out=rows[:], in_=I["s5_arow"].rearrange("a r c -> r a c")), writes=[d0])
        vrow = sbl("s5vrow", [8, 128])
        S.dma("sp", lambda e: e.dma_start(out=vrow[:], in_=I["s5_vec"][:, :]), writes=[d0])
        S.dma("sp", lambda e: e.dma_start(out=k.wglu[:], in_=I["s5_w_glu"].rearrange("(kc p) n -> p kc n", p=128)),
              writes=[k.s5_d])
        bT = [sbl(f"s5bT{i}", [128, 32, 32]) for i in range(2)]
        cblk = [sbl(f"s5cb{i}", [128, 8, 128]) for i in range(2)]
        for i in range(2):
            S.dma("sp", lambda e, i=i: e.dma_start(out=bT[i][:], in_=I["s5_bT"][i].rearrange("p (a b) -> p a b", b=32)), writes=[d0])
            S.dma("act", lambda e, i=i: e.dma_start(out=cblk[i][:], in_=I["s5_cblk"][i]), writes=[d0])
        aT = sbl("s5aT", [128, 3, 32])
        W = [sbl(f"s5w{i}", [128, 32]) for i in range(14)]
        Wi = sbl("s5wi", [128, 32], I32)
        bb = [sbl(f"s5bb{i}", [128, 32, 32]) for i in range(2)]
        tmp = [sbl(f"s5tmp{i}", [128, 32, 32]) for i in range(2)]
        with nc.psum_tensor("ps5a", [128, 3, 32], F32) as pa, nc.psum_tensor("ps5b", [128, 8], F32) as pb, \
                nc.psum_tensor("ps5c", [128, 4, 128], F32) as pc:
            pd = PD()
            for a in range(3):
                T("pe", lambda e, a=a: e.transpose(pa[:, a, :], rows[:, a, :], k.ident[0:32, 0:32]), reads=[d0, k.ident_d], writes=[pd])
            T("dve", lambda e: e.tensor_copy(aT[:], pa[:]), reads=[pd], writes=[d0])
            T("pe", lambda e: e.transpose(pb[:, :], vrow[:, :], k.ident[0:8, 0:8]), reads=[d0, k.ident_d], writes=[pd])
            T("dve", lambda e: e.tensor_copy(k.s5vT[:], pb[:]), reads=[pd], writes=[k.s5_d])
            are, aim, ldt = aT[:, 0, :], aT[:, 1, :], aT[:, 2, :]
            dt, mag, ang, sn, cs, abr, abi, t1, t2, nr, cfr, cfi, rden, t3 = [w[:] for w in W]

            def tt(o, a, b, op):
                T("dve", lambda e: e.tensor_tensor(o, a, b, op=op), reads=[d0], writes=[d0])

            T("act", lambda e: e.activation(dt, ldt, AF.Exp), reads=[d0], writes=[d0])
            tt(t1, dt, are, ALU.mult)
            T("act", lambda e: e.activation(mag, t1, AF.Exp), reads=[d0], writes=[d0])
            tt(ang, dt, aim, ALU.mult)

            def rsin(o, phase):
                T("dve", lambda e: e.tensor_scalar(t1, ang, 1.0 / (2 * np.pi), phase, op0=ALU.mult, op1=ALU.add), reads=[d0], writes=[d0])
                T("dve", lambda e: e.tensor_copy(Wi[:], t1), reads=[d0], writes=[d0])
                T("dve", lambda e: e.tensor_copy(t2, Wi[:]), reads=[d0], writes=[d0])
                tt(t1, t1, t2, ALU.subtract)
                T("dve", lambda e: e.scalar_tensor_tensor(t2, t1, 0.0, t1, op0=ALU.is_lt, op1=ALU.add), reads=[d0], writes=[d0])
                T("dve", lambda e: e.tensor_scalar(t2, t2, 2 * np.pi, -np.pi, op0=ALU.mult, op1=ALU.add), reads=[d0], writes=[d0])
                T("dve", lambda e: e.tensor_scalar(t2, t2, 3.1415925, -3.1415925, op0=ALU.min, op1=ALU.max), reads=[d0], writes=[d0])
                T("act", lambda e: e.activation(o, t2, AF.Sin), reads=[d0], writes=[d0])

            rsin(sn, 0.5)
            rsin(cs, 0.75)
            tt(abr, mag, cs, ALU.mult)
            tt(abi, mag, sn, ALU.mult)
            T("dve", lambda e: e.tensor_scalar(nr, abr, -1.0, None, op0=ALU.add), reads=[d0], writes=[d0])
            tt(t1, are, are, ALU.mult)
            tt(t2, aim, aim, ALU.mult)
            tt(t1, t1, t2, ALU.add)
            T("dve", lambda e: e.reciprocal(rden, t1), reads=[d0], writes=[d0])
            tt(t1, nr, are, ALU.mult)
            tt(t2, abi, aim, ALU.mult)
            tt(t1, t1, t2, ALU.add)
            tt(cfr, t1, rden, ALU.mult)
            tt(t1, abi, are, ALU.mult)
            tt(t2, nr, aim, ALU.mult)
            tt(t1, t1, t2, ALU.subtract)
            tt(cfi, t1, rden, ALU.mult)
            cfrb = cfr.unsqueeze(2).to_broadcast([128, 32, 32])
            cfib = cfi.unsqueeze(2).to_broadcast([128, 32, 32])
            tt(tmp[0][:], bT[0][:], cfrb, ALU.mult)
            tt(tmp[1][:], bT[1][:], cfib, ALU.mult)
            tt(bb[0][:], tmp[0][:], tmp[1][:], ALU.subtract)
            tt(tmp[0][:], bT[1][:], cfrb, ALU.mult)
            tt(tmp[1][:], bT[0][:], cfib, ALU.mult)
            tt(bb[1][:], tmp[0][:], tmp[1][:], ALU.add)
            for ri in range(2):
                for half in range(2):
                    for cc in range(4):
                        dc = half * 4 + cc
                        T("pe", lambda e, ri=ri, dc=dc, cc=cc: e.transpose(
                            pc[:, cc, :], bb[ri][:, dc * 4:(dc + 1) * 4, :].rearrange("p a b -> p (a b)"), k.ident[:]),
                            reads=[d0, k.ident_d], writes=[pd])
                    T("dve", lambda e, ri=ri, half=half: e.tensor_copy(k.winj[ri][:, half * 4:(half + 1) * 4, :], pc[:]),
                      reads=[pd], writes=[k.s5_d])
            for ri in range(2):
                for half in range(2):
                    for cc in range(4):
                        dc = half * 4 + cc
                        T("pe", lambda e, ri=ri, dc=dc, cc=cc: e.transpose(pc[:, cc, :], cblk[ri][:, dc, :], k.ident[:]),
                          reads=[d0, k.ident_d], writes=[pd])
                    if ri == 0:
                        T("dve", lambda e, half=half: e.tensor_copy(k.rout[0][:, half * 4:(half + 1) * 4, :], pc[:]),
                          reads=[pd], writes=[k.s5_d])
                    else:
                        T("dve", lambda e, half=half: e.tensor_scalar(k.rout[1][:, half * 4:(half + 1) * 4, :], pc[:], -1.0, None,
                                                                      op0=ALU.mult), reads=[pd], writes=[k.s5_d])
            pr, pi_, pn = k.pw
            T("dve", lambda e: e.memset(pr[:, :, 0:1], 1.0), writes=[k.s5_d])
            T("dve", lambda e: e.memset(pi_[:, :, 0:1], 0.0), writes=[k.s5_d])

            def cmul(o_r, o_i, a_r, a_i, b_r, b_i, dep):
                T("dve", lambda e: e.tensor_tensor(t1, a_r, b_r, op=ALU.mult), reads=[dep, d0], writes=[d0])
                T("dve", lambda e: e.tensor_tensor(t2, a_i, b_i, op=ALU.mult), reads=[dep, d0], writes=[d0])
                T("dve", lambda e: e.tensor_tensor(t3, a_r, b_i, op=ALU.mult), reads=[dep, d0], writes=[d0])
                T("dve", lambda e: e.tensor_tensor(rden, a_i, b_r, op=ALU.mult), reads=[dep, d0], writes=[d0])
                T("dve", lambda e: e.tensor_tensor(o_r, t1, t2, op=ALU.subtract), reads=[d0], writes=[dep])
                T("dve", lambda e: e.tensor_tensor(o_i, t3, rden, op=ALU.add), reads=[d0], writes=[dep])

            for n in range(1, 17):
                cmul(pr[:, :, n], pi_[:, :, n], pr[:, :, n - 1], pi_[:, :, n - 1], abr, abi, k.s5_d)
            T("dve", lambda e: e.tensor_scalar(pn[:], pi_[:], -1.0, None, op0=ALU.mult), reads=[k.s5_d], writes=[k.s5_d])
            lr, li, ln = k.lam
            T("dve", lambda e: e.tensor_copy(lr[:, :, 0], pr[:, :, 16]), reads=[k.s5_d], writes=[k.s5_d])
            T("dve", lambda e: e.tensor_copy(li[:, :, 0], pi_[:, :, 16]), reads=[k.s5_d], writes=[k.s5_d])
            for n in range(1, 8):
                cmul(lr[:, :, n], li[:, :, n], lr[:, :, n - 1], li[:, :, n - 1], lr[:, :, n - 1], li[:, :, n - 1], k.s5_d)
            T("dve", lambda e: e.tensor_scalar(ln[:], li[:], -1.0, None, op0=ALU.mult), reads=[k.s5_d], writes=[k.s5_d])
        S.barrier()


def stage2_s5(k):
    nc, S, I, NB = k.nc, k.S, k.I, k.NB
    T = S.op
    NSC = LT // 16
    with ExitStack() as es:
        def sbl(name, shape, dt=F32):
            return es.enter_context(nc.sbuf_tensor(name, list(shape), dt))
        uT = [sbl(f"s5u{i}", [128, LT]) for i in range(2)]
        uT_d = [Dep(), Dep()]
        Z = [[[sbl(f"s5z{jj}{d}{ri}", [128, LT]) for ri in range(2)] for d in range(2)] for jj in range(2)]
        Z_d = [[Dep(), Dep()] for jj in range(2)]
        Bp = [[[[sbl(f"s5B{jj}{d}{pp}{ri}", [128, NSC]) for ri in range(2)] for pp in range(2)] for d in range(2)] for jj in range(2)]
        B_d = [[[Dep(), Dep()] for d in range(2)] for jj in range(2)]
        y1 = sbl("s5y1", [128, 4, SEQ])
        y1_d = Dep()
        og = [sbl(f"s5og{i}", [128, 512]) for i in range(2)]
        og_d = [Dep(), Dep()]
        pin = [es.enter_context(nc.psum_tensor(f"s5pin{i}", [128, 512], F32)) for i in range(2)]
        pin_d = [PD(), PD()]
        py = es.enter_context(nc.psum_tensor("s5py", [128, 4, 512], F32))
        py_d = PD()
        pg = [es.enter_context(nc.psum_tensor(f"s5pg{i}", [128, 512], F32)) for i in range(2)]
        pg_d = [PD(), PD()]
        blocks = [(0, 256)] + [(256 + i * 512, 512) for i in range(4)]
        ipi = [0]

        def s5_chain(j, d, jj):
            q = cur_c[0] * 4 + j
            c = cur_c[0]
            u, ud = cur_u[0], cur_ud[0]
            dq = d * 16 + q
            zr, zi = Z[jj][d]
            zd = Z_d[jj][d]
            Bp_ = Bp[jj][d]
            B_d_ = B_d[jj][d]
            for (c0, n) in blocks:
                if d == 0:
                    z0 = c0
                else:
                    z0 = (c0 - CTX) if c0 >= CTX else SEQ
                for ri in range(2):
                    pp, ppd = pin[ipi[0] % 2], pin_d[ipi[0] % 2]
                    ipi[0] += 1
                    T("pe", _mm(pp[:, 0:n], k.winj[ri][32 * j:32 * j + 32, d * 4 + c, :], u[32 * j:32 * j + 32, c0:c0 + n], tp=(32 * j, 0)),
                      reads=[k.s5_d, ud], writes=[ppd])
                    T("act", lambda e, pp=pp, n=n, z0=z0, ri=ri: e.copy(Z[jj][d][ri][:, z0:z0 + n], pp[:, 0:n]),
                      reads=[ppd], writes=[zd])
                yield
            zrv = zr[:].rearrange("p (j r) -> p j r", r=16)
            ziv = zi[:].rearrange("p (j r) -> p j r", r=16)
            ar = k.pw[0][:, dq, 1:2]
            ai = k.pw[1][:, dq, 1:2]
            nai = k.pw[2][:, dq, 1:2]

            def stt(o, a, sc, bb_, rd, wr):
                T("dve", lambda e: e.scalar_tensor_tensor(o, a, sc, bb_, op0=ALU.mult, op1=ALU.add), reads=rd, writes=wr)

            order = range(1, 16) if d == 0 else range(14, -1, -1)
            for r in order:
                rp = r - 1 if d == 0 else r + 1
                stt(zrv[:, :, r], zrv[:, :, rp], ar, zrv[:, :, r], [zd, k.s5_d], [zd])
                yield
                stt(zrv[:, :, r], ziv[:, :, rp], nai, zrv[:, :, r], [zd, k.s5_d], [zd])
                yield
                stt(ziv[:, :, r], ziv[:, :, rp], ar, ziv[:, :, r], [zd, k.s5_d], [zd])
                yield
                stt(ziv[:, :, r], zrv[:, :, rp], ai, ziv[:, :, r], [zd, k.s5_d], [zd])
                yield
            rb = 15 if d == 0 else 0
            cur, nxt = 0, 1
            T("pool", lambda e: e.tensor_copy(Bp_[0][0][:], zrv[:, :, rb]), reads=[zd], writes=[B_d_[0]])
            T("pool", lambda e: e.tensor_copy(Bp_[0][1][:], ziv[:, :, rb]), reads=[zd], writes=[B_d_[0]])
            yield
            for lv in range(8):
                sh = 1 << lv
                lr_ = k.lam[0][:, dq, lv:lv + 1]
                li_ = k.lam[1][:, dq, lv:lv + 1]
                nli = k.lam[2][:, dq, lv:lv + 1]
                src, dst = Bp_[cur], Bp_[nxt]
                sd, dd = B_d_[cur], B_d_[nxt]
                if d == 0:
                    o_sl, i_sl, k_sl = slice(sh, NSC), slice(0, NSC - sh), slice(0, sh)
                else:
                    o_sl, i_sl, k_sl = slice(0, NSC - sh), slice(sh, NSC), slice(NSC - sh, NSC)
                T("pool", lambda e: e.tensor_copy(dst[0][:, k_sl], src[0][:, k_sl]), reads=[sd], writes=[dd])
                T("pool", lambda e: e.tensor_copy(dst[1][:, k_sl], src[1][:, k_sl]), reads=[sd], writes=[dd])
                stt(dst[0][:, o_sl], src[0][:, i_sl], lr_, src[0][:, o_sl], [sd, k.s5_d], [dd])
                yield
                stt(dst[0][:, o_sl], src[1][:, i_sl], nli, dst[0][:, o_sl], [sd, k.s5_d], [dd])
                yield
                stt(dst[1][:, o_sl], src[1][:, i_sl], lr_, src[1][:, o_sl], [sd, k.s5_d], [dd])
                yield
                stt(dst[1][:, o_sl], src[0][:, i_sl], li_, dst[1][:, o_sl], [sd, k.s5_d], [dd])
                yield
                cur, nxt = nxt, cur
            Sf, Sfd = Bp_[cur], B_d_[cur]
            for r in range(16):
                n = (r + 1) if d == 0 else (16 - r)
                pr_ = k.pw[0][:, dq, n:n + 1]
                pi_ = k.pw[1][:, dq, n:n + 1]
                pni = k.pw[2][:, dq, n:n + 1]
                if d == 0:
                    zs, ss = slice(16, NSC), slice(15, NSC - 1)
                else:
                    zs, ss = slice(0, 128), slice(1, 129)
                stt(zrv[:, zs, r], Sf[0][:, ss], pr_, zrv[:, zs, r], [zd, Sfd, k.s5_d], [zd])
                yield
                stt(zrv[:, zs, r], Sf[1][:, ss], pni, zrv[:, zs, r], [zd, Sfd, k.s5_d], [zd])
                yield
                stt(ziv[:, zs, r], Sf[1][:, ss], pr_, ziv[:, zs, r], [zd, Sfd, k.s5_d], [zd])
                yield
                stt(ziv[:, zs, r], Sf[0][:, ss], pi_, ziv[:, zs, r], [zd, Sfd, k.s5_d], [zd])
                yield
        cur_c, cur_u, cur_ud = [0], [None], [None]
        for b in range(NB):
            for c in range(4):
                u, ud = uT[c % 2], uT_d[c % 2]
                cur_c[0], cur_u[0], cur_ud[0] = c, u, ud
                S.dma("sp", lambda e, u=u, b=b, c=c: e.dma_start(out=u[:], in_=k.PT[b, c * 128:(c + 1) * 128, :]),
                      reads=[k.PT_dep[b]], writes=[ud])
                for jp in range(2):
                    chains = []
                    for jj in range(2):
                        for d in range(2):
                            chains.append(s5_chain(jp * 2 + jj, d, jj))
                    while chains:
                        for g_ in list(chains):
                            try:
                                next(g_)
                            except StopIteration:
                                chains.remove(g_)
                    for jj in range(2):
                        j = jp * 2 + jj
                        for tb in range(4):
                            terms = []
                            for d in range(2):
                                l0 = (CTX if d == 0 else 0) + tb * 512
                                terms.append((k.rout[0][:, d * 4 + c, 32 * j:32 * j + 32], Z[jj][d][0][:, l0:l0 + 512], Z_d[jj][d]))
                                terms.append((k.rout[1][:, d * 4 + c, 32 * j:32 * j + 32], Z[jj][d][1][:, l0:l0 + 512], Z_d[jj][d]))
                            for ti, (lh, rh, dd_) in enumerate(terms):
                                T("pe", _mm(py[32 * j:32 * j + 32, tb, :], lh, rh, start=(ti == 0), stop=(ti == 3), tp=(0, 32 * j)),
                                  reads=[k.s5_d, dd_], writes=[py_d])
                for tb in range(4):
                    T("dve", lambda e, tb=tb, c=c, u=u: e.scalar_tensor_tensor(
                        y1[:, c, tb * 512:(tb + 1) * 512], u[:, CTX + tb * 512:CTX + (tb + 1) * 512], k.s5vT[:, c:c + 1], py[:, tb, :],
                        op0=ALU.mult, op1=ALU.add), reads=[py_d, ud, k.s5_d], writes=[y1_d])
                T("act", lambda e, c=c: e.activation(y1[:, c, :], y1[:, c, :], AF.Gelu), reads=[y1_d], writes=[y1_d])
            gi = 0
            for m in range(4):
                for tb in range(4):
                    p_, pd_ = pg[gi % 2], pg_d[gi % 2]
                    o_, od_ = og[gi % 2], og_d[gi % 2]
                    gi += 1
                    for kc in range(4):
                        T("pe", _mm(p_[:], k.wglu[:, kc, m * 128:(m + 1) * 128], y1[:, kc, tb * 512:(tb + 1) * 512],
                                    start=(kc == 0), stop=(kc == 3)), reads=[k.s5_d, y1_d], writes=[pd_])
                    T("act", lambda e, p_=p_, o_=o_, m=m: e.activation(o_[:], p_[:], AF.Sigmoid, bias=k.s5vT[:, 4 + m:5 + m]),
                      reads=[pd_, k.s5_d], writes=[od_])
                    T("dve", lambda e, o_=o_, m=m, tb=tb: e.tensor_tensor(o_[:], o_[:], y1[:, m, tb * 512:(tb + 1) * 512], op=ALU.mult),
                      reads=[od_, y1_d], writes=[od_])
                    S.dma("sp", lambda e, o_=o_, m=m, tb=tb, b=b: e.dma_start(
                        out=k.MIXT[b, m * 128:(m + 1) * 128, tb * 512:(tb + 1) * 512], in_=o_[:]),
                        reads=[od_], writes=[k.MIXT_dep[b]])
        S.barrier()


def _rw_layout(inputs):
    f = lambda a: np.ascontiguousarray(a, dtype=np.float32)
    vec = np.zeros((42, 128), np.float32)
    mu = inputs["rw_mu"][0]
    vec[0:12] = mu[0:1536].reshape(12, 128)
    vec[12, 0:64] = mu[1536:1600]
    vec[13, 0:96] = mu[1600:1696]
    vec[14:18] = inputs["rw_k_k"][0].reshape(4, 128)
    vec[18:22] = inputs["rw_k_a"][0].reshape(4, 128)
    vec[22:26] = inputs["rw_r_k"][0].reshape(4, 128)
    vec[26:34] = inputs["rw_w0"][0].reshape(8, 128)
    vec[34:42] = inputs["rw_a0"][0].reshape(8, 128)
    wl = np.zeros((64, 2, 512), np.float32)
    wl[0:32] = np.transpose(inputs["rw_w_w2"][0], (1, 0, 2))
    wl[32:64] = np.transpose(inputs["rw_w_a2"][0], (1, 0, 2))
    ln = np.stack([inputs["rw_ln_w"][0], inputs["rw_ln_b"][0]], 0)
    return {"rw_vec": vec, "rw_wlora": f(wl), "rw_w_g2": f(inputs["rw_w_g2"][0]), "rw_ln": f(ln)}


def rw_setup(k):
    nc, S, I = k.nc, k.S, k.I
    T = S.op
    k.rwc = Dep()
    k.rvT = sbs(k, "rvT", [128, 42])
    k.omm = sbs(k, "rw_omm", [128, 14])
    k.muq = sbs(k, "rw_muq", [128, 14, 4])
    k.mue = sbs(k, "rw_mue", [128, 14, 2])
    k.omka = sbs(k, "rw_omka", [128, 4])
    k.wlora = sbs(k, "rw_wlora", [64, 2, 512])
    k.wg2 = sbs(k, "rw_wg2", [96, 512])
    k.lnbc = sbs(k, "rw_lnbc", [128, 2, 512])
    k.bones = sbs(k, "rw_bones", [128, 128])
    k.hsel = sbs(k, "rw_hsel", [128, 2])
    k.mup = sbs(k, "rw_mup", [128, 256])
    k.mlo = sbs(k, "rw_mlo", [128, 256])
    k.ones = sbs(k, "rw_ones", [128, 128])
    with ExitStack() as es:
        def sbl(name, shape, dt=F32):
            return es.enter_context(nc.sbuf_tensor(name, list(shape), dt))
        d0 = Dep()
        vrow = sbl("rwvrow", [42, 128])
        lnrow = sbl("rwlnrow", [1, 2, 512])
        m4 = sbl("rwm4", [128, 4])
        pm4i = sbl("rwpm4i", [128, 1], I32)
        pm4 = sbl("rwpm4", [128, 1])
        one1 = sbl("rwone1", [1, 128])
        S.dma("sp", lambda e: e.dma_start(out=vrow[:], in_=I["rw_vec"][:, :]), writes=[d0])
        S.dma("sp", lambda e: e.dma_start(out=lnrow[:], in_=I["rw_ln"].rearrange("(o a) n -> o a n", o=1)), writes=[d0])
        S.dma("sp", lambda e: e.dma_start(out=k.wlora[:], in_=I["rw_wlora"]), writes=[k.rwc])
        S.dma("sp", lambda e: e.dma_start(out=k.wg2[:], in_=I["rw_w_g2"][:, :]), writes=[k.rwc])
        with nc.psum_tensor("prw0", [128, 42], F32) as p0, nc.psum_tensor("prw1", [128, 2, 512], F32) as p1:
            pd = PD()
            T("pe", lambda e: e.transpose(p0[:, :], vrow[:, :], k.ident[0:42, 0:42]), reads=[d0, k.ident_d], writes=[pd])
            T("dve", lambda e: e.tensor_copy(k.rvT[:], p0[:]), reads=[pd], writes=[k.rwc])
            T("dve", lambda e: e.memset(one1[:], 1.0), writes=[d0])
            for a in range(2):
                T("pe", _mm(p1[:, a, :], one1[0:1, :], lnrow[0:1, a, :]), reads=[d0], writes=[pd])
            T("dve", lambda e: e.tensor_copy(k.lnbc[:], p1[:]), reads=[pd], writes=[k.rwc])
        T("dve", lambda e: e.tensor_scalar(k.omm[:], k.rvT[:, 0:14], -1.0, 1.0, op0=ALU.mult, op1=ALU.add), reads=[k.rwc], writes=[k.rwc])
        T("dve", lambda e: e.tensor_scalar(k.omka[:], k.rvT[:, 18:22], -1.0, 1.0, op0=ALU.mult, op1=ALU.add), reads=[k.rwc], writes=[k.rwc])
        T("dve", lambda e: e.tensor_single_scalar(pm4i[:], k.iota_p[:], 3, op=ALU.bitwise_and), reads=[k.iota_d], writes=[d0])
        T("dve", lambda e: e.tensor_copy(pm4[:], pm4i[:]), reads=[d0], writes=[d0])
        T("dve", lambda e: e.tensor_scalar(m4[:], k.iota_ff[:, 0:4], pm4[:, 0:1], None, op0=ALU.is_equal), reads=[d0, k.ident_d], writes=[d0])
        T("dve", lambda e: e.tensor_tensor(k.muq[:], k.rvT[:, 0:14].unsqueeze(2).to_broadcast([128, 14, 4]),
                                           m4[:].unsqueeze(1).to_broadcast([128, 14, 4]), op=ALU.mult), reads=[d0, k.rwc], writes=[k.rwc])
        T("dve", lambda e: e.tensor_tensor(k.mue[:], k.muq[:, :, 0:2], k.muq[:, :, 2:4], op=ALU.add), reads=[k.rwc], writes=[k.rwc])
        T("dve", lambda e: e.memset(k.bones[:], 0.0), writes=[k.rwc])
        T("dve", lambda e: e.memset(k.bones[0:64, 0:64], 1.0), writes=[k.rwc])
        T("dve", lambda e: e.memset(k.bones[64:128, 64:128], 1.0), writes=[k.rwc])
        T("dve", lambda e: e.memset(k.hsel[:], 0.0), writes=[k.rwc])
        T("dve", lambda e: e.memset(k.hsel[0:64, 0:1], 1.0), writes=[k.rwc])
        T("dve", lambda e: e.memset(k.hsel[64:128, 1:2], 1.0), writes=[k.rwc])
        T("dve", lambda e: e.memset(k.ones[:], 1.0), writes=[k.rwc])
        for (tile_, c0, op) in [(k.mup, 0, ALU.is_gt), (k.mup, 128, ALU.is_ge), (k.mlo, 0, ALU.is_lt), (k.mlo, 128, ALU.is_le)]:
            T("dve", lambda e, tile_=tile_, c0=c0, op=op: e.tensor_scalar(tile_[:, c0:c0 + 128], k.iota_ff[:], k.iota_pf[:, 0:1], None, op0=op),
              reads=[k.ident_d], writes=[k.rwc])
    S.barrier()


def stage3_rwkv(k):
    nc, S, I, NB = k.nc, k.S, k.I, k.NB
    T = S.op
    NCH = LT // 128
    C = 128
    import os
    with ExitStack() as es:
        def sbl(name, shape, dt=F32):
            return es.enter_context(nc.sbuf_tensor(name, list(shape), dt))

        def psl(name, shape):
            return es.enter_context(nc.psum_tensor(name, list(shape), F32))
        WD = BF16 if os.environ.get("RW_BF16", "1") == "1" else F32
        ND = F32 if os.environ.get("RW_NEU32", "1") == "1" else WD
        lora = sbl("rw_lora", [128, LT]); lora_d = Dep()
        sg = sbl("rw_sg", [128, LT]); sg_d = Dep()
        zb = sbl("rw_zb", [128, LT]); zb_d = Dep()
        rT = sbl("rw_r", [128, LT]); kT = sbl("rw_k", [128, LT]); kkT = sbl("rw_kk", [128, LT])
        base_d = Dep()
        Vm = sbl("rw_Vm", [128, NCH, 128]); Vm_d = Dep()
        tA = sbl("rw_tA", [128, LT]); tB = sbl("rw_tB", [128, LT]); tC = sbl("rw_tC", [128, LT]); tD = sbl("rw_tD", [128, LT])
        tmp_d = Dep()
        ARt = sbl("rw_AR", [128, NCH, 2, C], WD); Bt = sbl("rw_Bt", [128, LT], WD); Kt = sbl("rw_Kt", [128, LT], WD)
        Vmb = sbl("rw_Vmb", [128, NCH, 128], WD); identb = sbl("rw_identb", [128, 128], WD); Tstb = sbl("rw_Tb", [128, 64], WD)
        T("dve", lambda e: e.tensor_copy(identb[:], k.ident[:]), reads=[k.ident_d], writes=[k.rwc])
        feat_d = Dep()
        PC = sbl("rw_PC", [128, NCH]); tot = sbl("rw_tot", [128, NCH])
        kdsum = sbl("rw_kdsum", [128, LT]); kds_d = Dep()
        Ysum = sbl("rw_Y", [128, 16, 128]); Y_d = Dep()
        gtm = sbl("rw_gtm", [128, 16, 128]); gtm_d = Dep()
        coef = sbl("rw_coef", [128, 16, 2]); coef_d = Dep()
        gn = [sbl(f"rw_gn{i}", [128, 32]) for i in range(4)]
        Tst = sbl("rw_T", [128, 64]); T_d = Dep()
        T_dh = [Dep(), Dep()]
        Y_dh = [Dep(), Dep()]
        Ttmp = sbl("rw_Ttmp", [128, 64])
        NN = [sbl(f"rw_N{i}", [128, 128], ND) for i in range(8)]; NN_d = [Dep() for _ in range(8)]
        NT_ = [sbl(f"rw_NT{i}", [128, 128], ND) for i in range(8)]; NT_d = [Dep() for _ in range(8)]
        XX = [sbl(f"rw_X{i}", [128, 128], ND) for i in range(8)]; XX_d = [Dep() for _ in range(8)]
        AA = [sbl(f"rw_AA{i}", [128, 512], WD) for i in range(4)]; AA_d = [Dep() for _ in range(4)]
        AN = [sbl(f"rw_AN{i}", [128, 128], ND) for i in range(4)]
        Wsb = [sbl(f"rw_W{i}", [128, 64], ND) for i in range(2)]; Wsb_d = [Dep(), Dep()]
        Usb = [sbl(f"rw_U{i}", [128, 64], WD) for i in range(2)]; Usb_d = [Dep(), Dep()]
        BKtm = [sbl(f"rw_BK{i}", [128, 2, 128], WD) for i in range(2)]; BK_d = [Dep(), Dep()]
        ot = [sbl(f"rw_ot{i}", [128, 512]) for i in range(2)]; ot_d = [Dep(), Dep()]
        pA = [psl(f"rw_pA{i}", [128, 512]) for i in range(2)]; pA_d = [PD(), PD()]
        pN = [psl(f"rw_pN{i}", [128, 512]) for i in range(4)]; pN_d = [PD() for _ in range(4)]
        pS = [psl(f"rw_pS{i}", [128, 512]) for i in range(2)]; pS_d = [PD(), PD()]
        cnt = {"pn": 0, "pa": 0, "nn": 0, "nt": 0, "xx": 0, "hc": 0}

        def next_pn():
            i = cnt["pn"] % 4
            cnt["pn"] += 1
            return pN[i][:, 0:128], pN_d[i]

        def next_pa():
            i = cnt["pa"] % 2
            cnt["pa"] += 1
            return pA[i], pA_d[i]

        def mix_chunk(b, ch, nrows, dst, dst_d):
            r0 = 512 + (ch * 128 if ch < 12 else (1536 if ch == 12 else 1600))
            S.dma("sp", lambda e: e.dma_start(out=zb[0:nrows, :], in_=k.PT[b, r0:r0 + nrows, :]), reads=[k.PT_dep[b]], writes=[zb_d])
            P = slice(0, nrows)
            T("dve", lambda e: e.tensor_scalar(dst[P, :], zb[P, :], k.omm[P, ch:ch + 1], None, op0=ALU.mult),
              reads=[zb_d, k.rwc], writes=[dst_d])

            def acc(o, i_, sc):
                T("dve", lambda e: e.scalar_tensor_tensor(o, i_, sc, o, op0=ALU.mult, op1=ALU.add), reads=[zb_d, k.rwc, dst_d], writes=[dst_d])
            zl = zb[P, CTX:LT].rearrange("p (r c) -> p r c", c=64)
            dl = dst[P, CTX:LT].rearrange("p (r c) -> p r c", c=64)
            acc(dl[:, :, 1:64], zl[:, :, 0:63], k.muq[P, ch, 0:1])
            acc(dl[:, :, 0:63], zl[:, :, 1:64], k.muq[P, ch, 1:2])
            acc(dst[P, CTX + 64:LT], zb[P, CTX:LT - 64], k.muq[P, ch, 2:3])
            acc(dst[P, CTX:LT - 64], zb[P, CTX + 64:LT], k.muq[P, ch, 3:4])
            acc(dst[P, 1:CTX], zb[P, 0:CTX - 1], k.mue[P, ch, 0:1])
            acc(dst[P, 0:CTX - 1], zb[P, 1:CTX], k.mue[P, ch, 1:2])

        blocks = [(0, 512), (512, 512), (1024, 512), (1536, 512), (2048, 256)]
        import os
        STOP = int(os.environ.get("RW_STOP", "99"))
        for b in range(NB):
            mix_chunk(b, 12, 64, lora, lora_d)
            T("act", lambda e: e.activation(lora[0:32, :], lora[0:32, :], AF.Tanh), reads=[lora_d], writes=[lora_d])
            mix_chunk(b, 13, 96, sg, sg_d)
            T("act", lambda e: e.activation(sg[0:96, :], sg[0:96, :], AF.Sigmoid), reads=[sg_d], writes=[sg_d])
            if STOP <= 1:
                break
            for hp in range(4):
                mix_chunk(b, hp, 128, rT, base_d)
                mix_chunk(b, 4 + hp, 128, kT, base_d)
                mix_chunk(b, 8 + hp, 128, tA, tmp_d)
                for ci in range(NCH):
                    pp, ppd = next_pn()
                    T("pe", lambda e, pp=pp, ci=ci: e.transpose(pp, tA[:, ci * 128:(ci + 1) * 128], k.ident[:]),
                      reads=[tmp_d, k.ident_d], writes=[ppd])
                    T("act", lambda e, pp=pp, ci=ci: e.copy(Vm[:, ci, :], pp), reads=[ppd], writes=[Vm_d])
                    T("dve", lambda e, pp=pp, ci=ci: e.tensor_copy(Vmb[:, ci, :], pp), reads=[ppd], writes=[Vm_d])
                T("dve", lambda e: e.tensor_scalar(kkT[:], kT[:], k.rvT[:, 14 + hp:15 + hp], None, op0=ALU.mult), reads=[base_d, k.rwc], writes=[base_d])
                T("dve", lambda e: e.tensor_tensor(tB[:], kkT[:], kkT[:], op=ALU.mult), reads=[base_d], writes=[tmp_d])
                for (c0, n) in blocks:
                    pp, ppd = next_pa()
                    T("pe", _mm(pp[:, 0:n], k.bones[:], tB[:, c0:c0 + n]), reads=[tmp_d, k.rwc], writes=[ppd])
                    T("act", lambda e, pp=pp, c0=c0, n=n: e.activation(tC[:, c0:c0 + n], pp[:, 0:n], AF.Sqrt, bias=1e-12), reads=[ppd], writes=[tmp_d])
                T("dve", lambda e: e.reciprocal(tC[:], tC[:]), reads=[tmp_d], writes=[tmp_d])
                T("dve", lambda e: e.tensor_tensor(kkT[:], kkT[:], tC[:], op=ALU.mult), reads=[tmp_d, base_d], writes=[base_d])
                for lc in range(16):
                    pp, ppd = next_pn()
                    T("pe", _mm(pp, sg[0:96, CTX + lc * 128:CTX + (lc + 1) * 128], k.wg2[0:96, hp * 128:(hp + 1) * 128]),
                      reads=[sg_d, k.rwc], writes=[ppd])
                    T("act", lambda e, pp=pp, lc=lc: e.copy(gtm[:, lc, :], pp), reads=[ppd], writes=[gtm_d])
                if STOP <= 2:
                    break
                for d in range(2):
                    for (c0, n) in blocks:
                        pp, ppd = next_pa()
                        T("pe", _mm(pp[:, 0:n], k.wlora[0:32, d, hp * 128:(hp + 1) * 128], lora[0:32, c0:c0 + n]), reads=[lora_d, k.rwc], writes=[ppd])
                        T("act", lambda e, pp=pp, c0=c0, n=n, d=d, hp=hp: e.activation(
                            tA[:, c0:c0 + n], pp[:, 0:n], AF.Sigmoid, bias=k.rvT[:, 26 + d * 4 + hp:27 + d * 4 + hp]), reads=[ppd, k.rwc], writes=[tmp_d])
                        pp, ppd = next_pa()
                        T("pe", _mm(pp[:, 0:n], k.wlora[32:64, d, hp * 128:(hp + 1) * 128], lora[32:64, c0:c0 + n], tp=(32, 0)),
                          reads=[lora_d, k.rwc], writes=[ppd])
                        T("act", lambda e, pp=pp, c0=c0, n=n, d=d, hp=hp: e.activation(
                            tB[:, c0:c0 + n], pp[:, 0:n], AF.Sigmoid, bias=k.rvT[:, 34 + d * 4 + hp:35 + d * 4 + hp]), reads=[ppd, k.rwc], writes=[tmp_d])
                    T("dve", lambda e: e.tensor_scalar(tA[:], tA[:], -0.6065306597126334, None, op0=ALU.mult), reads=[tmp_d], writes=[tmp_d])
                    T("dve", lambda e: e.tensor_scalar(tC[:], tB[:], k.rvT[:, 18 + hp:19 + hp], k.omka[:, hp:hp + 1], op0=ALU.mult, op1=ALU.add),
                      reads=[tmp_d, k.rwc], writes=[tmp_d])
                    T("dve", lambda e: e.tensor_tensor(tC[:], tC[:], kT[:], op=ALU.mult), reads=[tmp_d, base_d], writes=[tmp_d])
                    if d == 0:
                        T("pool", lambda e: e.tensor_copy(kdsum[:], tC[:]), reads=[tmp_d], writes=[kds_d])
                    else:
                        T("pool", lambda e: e.tensor_tensor(kdsum[:], kdsum[:], tC[:], op=ALU.add), reads=[tmp_d, kds_d], writes=[kds_d])
                    for ci in range(NCH):
                        T("dve", lambda e, ci=ci: e.tensor_tensor_scan(tD[:, ci * C:(ci + 1) * C], k.ones[:], tA[:, ci * C:(ci + 1) * C], 0.0,
                                                                      op0=ALU.mult, op1=ALU.add), reads=[tmp_d, k.rwc], writes=[tmp_d])
                    tDv = tD[:].rearrange("p (c t) -> p c t", t=C)
                    T("dve", lambda e: e.tensor_copy(tot[:], tDv[:, :, C - 1]), reads=[tmp_d], writes=[feat_d])
                    if d == 1:
                        T("dve", lambda e: e.tensor_tensor(tD[:], tA[:], tD[:], op=ALU.subtract), reads=[tmp_d], writes=[tmp_d])
                        T("dve", lambda e: e.tensor_tensor(tDv, tDv, tot[:].unsqueeze(2).to_broadcast([128, NCH, C]), op=ALU.add),
                          reads=[tmp_d, feat_d], writes=[tmp_d])
                    T("act", lambda e: e.activation(PC[:], tot[:], AF.Exp), reads=[feat_d], writes=[feat_d])
                    ARv0 = ARt[:, :, 0, :]
                    ARv1 = ARt[:, :, 1, :]
                    tAv = tA[:].rearrange("p (c t) -> p c t", t=C)
                    T("dve", lambda e: e.tensor_tensor(tA[:], tD[:], tA[:], op=ALU.subtract), reads=[tmp_d], writes=[tmp_d])
                    T("act", lambda e: e.activation(tA[:], tA[:], AF.Exp), reads=[tmp_d], writes=[tmp_d])
                    T("dve", lambda e: e.scalar_tensor_tensor(ARv0, kkT[:].rearrange("p (c t) -> p c t", t=C), -1.0, tAv, op0=ALU.mult, op1=ALU.mult),
                      reads=[tmp_d, base_d], writes=[feat_d])
                    T("act", lambda e: e.activation(tA[:], tD[:], AF.Exp), reads=[tmp_d, feat_d], writes=[tmp_d])
                    T("dve", lambda e: e.tensor_tensor(ARv1, tAv, rT[:].rearrange("p (c t) -> p c t", t=C), op=ALU.mult),
                      reads=[tmp_d, base_d], writes=[feat_d])
                    T("act", lambda e: e.activation(tD[:], tD[:], AF.Exp, scale=-1.0), reads=[tmp_d], writes=[tmp_d])
                    T("dve", lambda e: e.tensor_tensor(tA[:], kkT[:], tB[:], op=ALU.mult), reads=[tmp_d, base_d, feat_d], writes=[tmp_d])
                    T("dve", lambda e: e.tensor_tensor(Bt[:], tA[:], tD[:], op=ALU.mult), reads=[tmp_d], writes=[feat_d])
                    T("dve", lambda e: e.tensor_tensor(Kt[:], tC[:], tD[:], op=ALU.mult), reads=[tmp_d], writes=[feat_d])
                    if STOP <= 3:
                        break
                    T("dve", lambda e: e.memset(Tst[:], 0.0), writes=[T_dh[0], T_dh[1]])
                    T("dve", lambda e: e.memset(Tstb[:], 0.0), writes=[T_dh[0], T_dh[1]])
                    order = list(range(NCH)) if d == 0 else [1, 0] + list(range(NCH - 1, 1, -1))
                    SUB = int(os.environ.get("RW_SUB", "99"))
                    order = order[:int(os.environ.get("RW_NCH", "99"))]
                    m2 = k.mup if d == 0 else k.mlo
                    mT = k.mlo if d == 0 else k.mup
                    Xfin = {}

                    def bk_chain(ci, par):
                        cs = slice(ci * C, (ci + 1) * C)
                        bk, bkd = BKtm[par], BK_d[par]
                        for which, src in enumerate((Bt, Kt)):
                            pp, ppd = next_pn()
                            T("pe", _mm(pp, src[:, cs], identb[:]), reads=[feat_d, k.rwc], writes=[ppd])
                            T("act", lambda e, pp=pp, bk=bk, which=which: e.copy(bk[:, which, :], pp), reads=[ppd], writes=[bkd])
                            yield

                    def neu_chain(hh, ci, par):
                        cs = slice(ci * C, (ci + 1) * C)
                        ph = slice(64 * hh, 64 * hh + 64)
                        tpk = (64 * hh, 0)
                        pa, pad = pA[hh], pA_d[hh]
                        ni = hh * 2 + par
                        aa, aad = AA[ni], AA_d[ni]
                        arr = ARt[ph, ci, :, :].rearrange("p a t -> p (a t)")
                        T("pe", _mm(pa[:, 0:256], Bt[ph, cs], arr, tp=tpk), reads=[feat_d], writes=[pad])
                        T("pe", _mm(pa[:, 256:512], Kt[ph, cs], arr, tp=tpk), reads=[feat_d], writes=[pad])
                        yield
                        T("dve", lambda e: e.tensor_tensor(
                            aa[:].rearrange("p (a t) -> p a t", a=2), pa[:].rearrange("p (a t) -> p a t", a=2),
                            m2[:].unsqueeze(1).to_broadcast([128, 2, 256]), op=ALU.mult), reads=[pad, k.rwc], writes=[aad])
                        an = AN[ni]
                        T("dve", lambda e: e.tensor_tensor(an[:], pa[:, 0:128], m2[:, 0:128], op=ALU.mult), reads=[pad, k.rwc], writes=[aad])
                        p3, p3d = next_pn()
                        T("pe", _mm(p3, ARt[ph, ci, 0, :], Bt[ph, cs], tp=tpk), reads=[feat_d], writes=[p3d])
                        yield
                        nt0, nt0d = NT_[2 * ni], NT_d[2 * ni]
                        T("dve", lambda e: e.tensor_tensor(nt0[:], p3, mT[:, 0:128], op=ALU.mult), reads=[p3d, k.rwc], writes=[nt0d])
                        x0, x0d = XX[2 * ni], XX_d[2 * ni]
                        T("pool", lambda e: e.tensor_tensor(x0[:], an[:], k.ident[:], op=ALU.add), reads=[aad, k.ident_d], writes=[x0d])
                        curN, curNd = an[:], aad
                        curNT, curNTd = nt0, nt0d
                        curX, curXd = x0, x0d
                        for lv in range(1, 7):
                            nxtNT, nxtNTd = NT_[2 * ni + (lv % 2)], NT_d[2 * ni + (lv % 2)]
                            pq, pqd = next_pn()
                            T("pe", _mm(pq, curN, curNT[:]), reads=[curNd, curNTd], writes=[pqd])
                            T("act", lambda e, pq=pq, nxtNT=nxtNT: e.copy(nxtNT[:], pq), reads=[pqd], writes=[nxtNTd])
                            yield
                            if lv < 6:
                                nxtN, nxtNd = NN[2 * ni + (lv % 2)], NN_d[2 * ni + (lv % 2)]
                                pq2, pq2d = next_pn()
                                T("pe", _mm(pq2, curNT[:], curN), reads=[curNd, curNTd], writes=[pq2d])
                                T("act", lambda e, pq2=pq2, nxtN=nxtN: e.copy(nxtN[:], pq2), reads=[pq2d], writes=[nxtNd])
                                yield
                            nxtX, nxtXd = XX[2 * ni + (lv % 2)], XX_d[2 * ni + (lv % 2)]
                            pq3, pq3d = next_pn()
                            T("pe", _mm(pq3, nxtNT[:], curX[:]), reads=[nxtNTd, curXd], writes=[pq3d])
                            T("dve", lambda e, pq3=pq3, nxtX=nxtX, curX=curX: e.tensor_tensor(nxtX[:], pq3, curX[:], op=ALU.add),
                              reads=[pq3d, curXd], writes=[nxtXd])
                            yield
                            if lv < 6:
                                curN, curNd = nxtN[:], nxtNd
                            curNT, curNTd = nxtNT, nxtNTd
                            curX, curXd = nxtX, nxtXd
                        Xfin[(hh, par)] = (curX, curXd)

                    def state_chain(hh, ci, par):
                        is_lat = ci >= 2
                        ph = slice(64 * hh, 64 * hh + 64)
                        tpk = (64 * hh, 0)
                        psb, psd = pS[hh], pS_d[hh]
                        Td = T_dh[hh]
                        ni = hh * 2 + par
                        aa, aad = AA[ni], AA_d[ni]
                        bk, bkd = BKtm[par], BK_d[par]
                        curX, curXd = Xfin[(hh, par)]
                        vh = Vmb[:, ci, ph]
                        T("pe", _mm(psb[:, 0:64], aa[:, 256:384], vh, start=True, stop=False), reads=[aad, Vm_d], writes=[psd])
                        T("pe", _mm(psb[:, 0:64], ARt[ph, ci, 0, :], Tstb[ph, :], start=False, stop=True, tp=tpk),
                          reads=[feat_d, Td], writes=[psd])
                        wsb, wsd = Wsb[hh], Wsb_d[hh]
                        T("act", lambda e: e.copy(wsb[:], psb[:, 0:64]), reads=[psd], writes=[wsd])
                        yield
                        T("pe", _mm(psb[:, 64:128], curX[:], wsb[:]), reads=[curXd, wsd], writes=[psd])
                        usb, usd = Usb[hh], Usb_d[hh]
                        T("act", lambda e: e.copy(usb[:], psb[:, 64:128]), reads=[psd], writes=[usd])
                        yield
                        if is_lat:
                            yo = psb[:, 128:192]
                            T("pe", _mm(yo, ARt[ph, ci, 1, :], Tstb[ph, :], start=True, stop=False, tp=tpk), reads=[feat_d, Td], writes=[psd])
                            T("pe", _mm(yo, aa[:, 128:256], usb[:], start=False, stop=False), reads=[aad, usd], writes=[psd])
                            T("pe", _mm(yo, aa[:, 384:512], vh, start=False, stop=True), reads=[aad, Vm_d], writes=[psd])
                        to = psb[ph, 192:256]
                        T("pe", _mm(to, bk[:, 0, ph], usb[:], start=True, stop=False, tp=(0, 64 * hh)), reads=[bkd, usd], writes=[psd])
                        T("pe", _mm(to, bk[:, 1, ph], vh, start=False, stop=True, tp=(0, 64 * hh)), reads=[bkd, Vm_d], writes=[psd])
                        yield
                        if is_lat:
                            lc = ci - 2
                            ys = Ysum[:, lc, ph]
                            if d == 0:
                                T("act", lambda e: e.copy(ys, psb[:, 128:192]), reads=[psd], writes=[Y_dh[hh]])
                            else:
                                T("dve", lambda e: e.tensor_tensor(ys, psb[:, 128:192], ys, op=ALU.add), reads=[psd, Y_dh[hh]], writes=[Y_dh[hh]])
                        T("dve", lambda e: e.tensor_tensor(Ttmp[ph, :], psb[ph, 192:256], Tst[ph, :], op=ALU.add), reads=[psd, Td], writes=[Td])
                        T("dve", lambda e: e.tensor_scalar(Tst[ph, :], Ttmp[ph, :], PC[ph, ci:ci + 1], None, op0=ALU.mult), reads=[Td, feat_d], writes=[Td])
                        T("act", lambda e: e.copy(Tstb[ph, :], Tst[ph, :]), reads=[Td], writes=[Td])
                        yield

                    def run_chains(chains):
                        while chains:
                            for g_ in list(chains):
                                try:
                                    next(g_)
                                except StopIteration:
                                    chains.remove(g_)

                    if order:
                        run_chains([bk_chain(order[0], 0), neu_chain(0, order[0], 0), neu_chain(1, order[0], 0)])
                    for idx, ci in enumerate(order):
                        par = idx % 2
                        chains = [state_chain(0, ci, par), state_chain(1, ci, par)]
                        if idx + 1 < len(order):
                            nci = order[idx + 1]
                            chains += [bk_chain(nci, 1 - par), neu_chain(0, nci, 1 - par), neu_chain(1, nci, 1 - par)]
                        run_chains(chains)
                if STOP <= 4:
                    break
                T("dve", lambda e: e.tensor_tensor(kdsum[:], kdsum[:], rT[:], op=ALU.mult), reads=[kds_d, base_d], writes=[kds_d])
                T("dve", lambda e: e.tensor_scalar(kdsum[:], kdsum[:], k.rvT[:, 22 + hp:23 + hp], None, op0=ALU.mult), reads=[kds_d, k.rwc], writes=[kds_d])
                pp, ppd = next_pa()
                for lc in range(16):
                    T("pe", _mm(pp[:, 2 * lc:2 * lc + 2], kdsum[:, CTX + lc * 128:CTX + (lc + 1) * 128], k.hsel[:]), reads=[kds_d, k.rwc], writes=[ppd])
                T("dve", lambda e, pp=pp: e.tensor_copy(coef[:].rearrange("p a b -> p (a b)"), pp[:, 0:32]), reads=[ppd], writes=[coef_d])
                Yv = Ysum[:].rearrange("p c (h v) -> p (c h) v", v=64)
                ssum, ssq, mu_, rs_ = [g[:] for g in gn]
                GD = Dep()
                T("dve", lambda e: e.tensor_copy(gn[0][:, 0:1], gn[0][:, 0:1]), reads=[Y_dh[0], Y_dh[1]], writes=[Y_d, Y_dh[0], Y_dh[1]])
                T("dve", lambda e: e.tensor_reduce(ssum, Yv, axis=AX.X, op=ALU.add), reads=[Y_d], writes=[GD])
                tAv3 = tA[:, 0:2048].rearrange("p (c v) -> p c v", v=64)
                T("dve", lambda e: e.tensor_tensor(tAv3, Yv, Yv, op=ALU.mult), reads=[Y_d, tmp_d], writes=[tmp_d])
                T("dve", lambda e: e.tensor_reduce(ssq, tAv3, axis=AX.X, op=ALU.add), reads=[tmp_d], writes=[GD])
                T("dve", lambda e: e.tensor_scalar(mu_, ssum, 1.0 / 64, None, op0=ALU.mult), reads=[GD], writes=[GD])
                T("dve", lambda e: e.tensor_tensor(ssum, mu_, mu_, op=ALU.mult), reads=[GD], writes=[GD])
                T("dve", lambda e: e.scalar_tensor_tensor(ssq, ssq, 1.0 / 64, ssum, op0=ALU.mult, op1=ALU.subtract), reads=[GD], writes=[GD])
                T("act", lambda e: e.activation(ssq, ssq, AF.Sqrt, bias=64e-5), reads=[GD], writes=[GD])
                T("dve", lambda e: e.reciprocal(rs_, ssq), reads=[GD], writes=[GD])
                T("dve", lambda e: e.tensor_tensor(Yv, Yv, mu_.unsqueeze(2).to_broadcast([128, 32, 64]), op=ALU.subtract), reads=[GD, Y_d], writes=[Y_d])
                T("dve", lambda e: e.tensor_tensor(Yv, Yv, rs_.unsqueeze(2).to_broadcast([128, 32, 64]), op=ALU.mult), reads=[GD, Y_d], writes=[Y_d])
                lnw = k.lnbc[:, 0, hp * 128:(hp + 1) * 128].unsqueeze(1).to_broadcast([128, 16, 128])
                lnb = k.lnbc[:, 1, hp * 128:(hp + 1) * 128].unsqueeze(1).to_broadcast([128, 16, 128])
                T("dve", lambda e: e.tensor_tensor(Ysum[:], Ysum[:], lnw, op=ALU.mult), reads=[Y_d, k.rwc], writes=[Y_d])
                T("dve", lambda e: e.tensor_tensor(Ysum[:], Ysum[:], lnb, op=ALU.add), reads=[Y_d, k.rwc], writes=[Y_d])
                Vl = Vm[:, 2:18, :].rearrange("p c (h v) -> p (c h) v", v=64)
                T("dve", lambda e: e.tensor_tensor(tAv3, Vl, coef[:].rearrange("p a b -> p (a b)").unsqueeze(2).to_broadcast([128, 32, 64]), op=ALU.mult),
                  reads=[Vm_d, coef_d, tmp_d], writes=[tmp_d])
                T("dve", lambda e: e.tensor_tensor(Yv, Yv, tAv3, op=ALU.add), reads=[tmp_d, Y_d], writes=[Y_d])
                T("dve", lambda e: e.tensor_tensor(Ysum[:], Ysum[:], gtm[:], op=ALU.mult), reads=[Y_d, gtm_d], writes=[Y_d])
                for tb in range(4):
                    o_, od_ = ot[tb % 2], ot_d[tb % 2]
                    pp, ppd = next_pa()
                    for q in range(4):
                        T("pe", lambda e, pp=pp, q=q, tb=tb: e.transpose(pp[:, q * 128:(q + 1) * 128], Ysum[:, tb * 4 + q, :], k.ident[:]),
                          reads=[Y_d, k.ident_d], writes=[ppd])
                    T("act", lambda e, pp=pp, o_=o_: e.copy(o_[:], pp[:]), reads=[ppd], writes=[od_])
                    S.dma("sp", lambda e, o_=o_, tb=tb, b=b, hp=hp: e.dma_start(
                        out=k.MIXT[b, 512 + hp * 128:512 + (hp + 1) * 128, tb * 512:(tb + 1) * 512], in_=o_[:]),
                        reads=[od_], writes=[k.MIXT_dep[b]])
                T("dve", lambda e: e.tensor_copy(gn[0][:, 0:1], gn[0][:, 0:1]), writes=[Y_d, Y_dh[0], Y_dh[1]])
        S.barrier()


_PEER_CACHE = {}


def _peer_layout(inputs):
    f = lambda a: np.ascontiguousarray(a, dtype=np.float32)
    key = id(inputs["peer_u"])
    if key not in _PEER_CACHE:
        _PEER_CACHE.clear()
        _PEER_CACHE[key] = {
            "w_out": f(inputs["w_out"][0]),
            "peer_w_q": f(inputs["peer_w_q"][0]),
            "peer_keys": f(np.transpose(inputs["peer_keys"][0], (2, 0, 1, 3)).reshape(128, 16, 128)),
            "peer_uv": f(np.concatenate([inputs["peer_u"][0], inputs["peer_v"][0]], axis=1)),
            "nvec": f(np.stack([inputs["norm2_g"][0], inputs["norm_f_g"]], 0)),
        }
    return _PEER_CACHE[key]


def cast_uv(k):
    nc, S, I = k.nc, k.S, k.I
    T = S.op
    with ExitStack() as es:
        fb = [es.enter_context(nc.sbuf_tensor(f"cv_f{i}", [128, 4, 2048], F32)) for i in range(2)]
        bb = [es.enter_context(nc.sbuf_tensor(f"cv_b{i}", [128, 4, 2048], BF16)) for i in range(2)]
        fd = [Dep(), Dep()]
        bd = [Dep(), Dep()]
        for i in range(32):
            f_, fdd, b_, bdd = fb[i % 2], fd[i % 2], bb[i % 2], bd[i % 2]
            src = I["peer_uv"][i * 512:(i + 1) * 512, :].rearrange("(p r) n -> p r n", r=4)
            dst = k.UVB[i * 512:(i + 1) * 512, :].rearrange("(p r) n -> p r n", r=4)
            S.dma("sp", lambda e, f_=f_, src=src: e.dma_start(out=f_[:], in_=src), writes=[fdd])
            if i % 2 == 0:
                T("act", lambda e, f_=f_, b_=b_: e.copy(b_[:], f_[:]), reads=[fdd], writes=[bdd])
            else:
                T("dve", lambda e, f_=f_, b_=b_: e.tensor_copy(b_[:], f_[:]), reads=[fdd], writes=[bdd])
            S.dma("act", lambda e, b_=b_, dst=dst: e.dma_start(out=dst, in_=b_[:]), reads=[bdd], writes=[k.UVB_dep])
        S.barrier()


def stage4_peer(k):
    nc, S, I, NB = k.nc, k.S, k.I, k.NB
    T = S.op
    import os
    NT4 = int(os.environ.get("P4_TILES", "16"))
    NSLOT = int(os.environ.get("P4_SLOTS", "128"))
    with ExitStack() as es:
        def sbl(name, shape, dt=F32):
            return es.enter_context(nc.sbuf_tensor("P_" + name, list(shape), dt))

        def psl(name):
            return es.enter_context(nc.psum_tensor("PP_" + name, [128, 512], F32))
        cst = Dep()
        wq = sbl("wq", [128, 8, 2048], BF16)
        wo = sbl("wo", [128, 8, 1024], BF16)
        for kd in range(8):
            S.dma("pool", lambda e, kd=kd: e.dma_start(out=wq[:, kd, :], in_=I["peer_w_q"][kd * 128:(kd + 1) * 128, :], max_dma_last_dim=4096), writes=[cst])
            S.dma("pool", lambda e, kd=kd: e.dma_start(out=wo[:, kd, :], in_=I["w_out"][kd * 128:(kd + 1) * 128, :], max_dma_last_dim=4096), writes=[cst])
        keysT = sbl("keysT", [128, 16, 128])
        nbc = sbl("nbc", [128, 2, D])
        ones = sbl("ones", [128, 128])
        io16 = sbl("io16", [128, 16])
        bc = sbl("bc", [128, 4, D]); bc_d = Dep()
        xt = sbl("xt", [128, D]); xt_d = Dep()
        mx = sbl("mx", [128, 8, 128]); mx_d = Dep()
        mxb = sbl("mxb", [128, 8, 128], BF16); mxb_d = Dep()
        h1 = sbl("h1", [128, D]); h1_d = Dep()
        hb = sbl("hb", [128, D]); hb_d = Dep()
        junk = sbl("junk", [128, D]); junk_d = Dep()
        junkb = sbl("junkb", [128, D], BF16)
        hbb = sbl("hbb", [128, D], BF16); hbb_d = Dep()
        hbT = sbl("hbT", [128, 8, 128], BF16); hbT_d = Dep()
        qT = sbl("qT", [128, 16, 128]); qT_d = Dep()
        sc = sbl("sc", [128, 16, 128]); sc_d = Dep()
        sc2 = sbl("sc2", [128, 1, 128]); sc2_d = Dep()
        m16 = sbl("m16", [128, 16, 16]); i16 = sbl("i16", [128, 16, 16], U32); i16f = sbl("i16f", [128, 16, 16])
        cand = sbl("cand", [128, 8, 256]); cand2 = sbl("cand2", [128, 8, 256])
        best = sbl("best", [128, 8, 16]); pos = sbl("pos", [128, 8, 16], U32)
        pa_i = sbl("pa_i", [128, 8, 16], U32); pb_i = sbl("pb_i", [128, 8, 16], U32)
        pa_f = sbl("pa_f", [128, 8, 16]); pb_f = sbl("pb_f", [128, 8, 16])
        eq = cand2[:].rearrange("p h (a b) -> p h a b", b=16)
        i1s = sbl("i1s", [128, 8, 16]); i2s = sbl("i2s", [128, 8, 16])
        idxf = sbl("idxf", [128, 128]); idxi = sbl("idxi", [128, 128], I32)
        gate = sbl("gate", [128, 8, 16]); gsm = sbl("gsm", [128, 8, 2])
        tk_d = Dep()
        actr = sbl("actr", [128, 128]); act_d = Dep()
        agd = [Dep() for _ in range(64)]
        asd = [Dep() for _ in range(128)]
        wgt = sbl("wgt", [128, 128])
        st = sbl("st", [128, 8]); st_d = Dep()
        SPLIT = os.environ.get("P4_SPLIT", "0") == "1"
        NG = int(os.environ.get("P4_NG", "4" if SPLIT else "5"))
        if SPLIT:
            prod = [sbl(f"prod{i}", [128, 1024]) for i in range(2)]; prod_d = [Dep(), Dep()]
        dg = [sbl(f"dg{i}", [128, 128]) for i in range(4)]; dg_d = [Dep() for _ in range(4)]
        dgb = [sbl(f"dgb{i}", [128, 128], BF16) for i in range(4)]; dgb_d = [Dep() for _ in range(4)]
        oo = sbl("oo", [128, D]); oo_d = Dep()
        pb_ = [psl(f"b{i}") for i in range(8)]; pb_d = [PD() for _ in range(8)]
        d0 = Dep()
        with ExitStack() as es2:
            krow = es2.enter_context(nc.sbuf_tensor("P_krow", [128, 16, 128], F32))
            nrow = es2.enter_context(nc.sbuf_tensor("P_nrow", [1, 2, D], F32))
            one1 = es2.enter_context(nc.sbuf_tensor("P_one1", [1, 128], F32))
            S.dma("sp", lambda e: e.dma_start(out=krow[:], in_=I["peer_keys"]), writes=[d0])
            S.dma("sp", lambda e: e.dma_start(out=nrow[:], in_=I["nvec"].rearrange("(o a) n -> o a n", o=1)), writes=[d0])
            T("dve", lambda e: e.memset(one1[:], 1.0), writes=[d0])
            T("dve", lambda e: e.memset(ones[:], 1.0), writes=[cst])
            T("dve", lambda e: e.tensor_copy(io16[:], k.iota_ff[:, 0:16]), reads=[k.ident_d], writes=[cst])
            for j in range(16):
                T("pe", lambda e, j=j: e.transpose(pb_[j % 4][:, 0:128], krow[:, j, :], k.ident[:]), reads=[d0, k.ident_d], writes=[pb_d[j % 4]])
                T("act", lambda e, j=j: e.copy(keysT[:, j, :], pb_[j % 4][:, 0:128]), reads=[pb_d[j % 4]], writes=[cst])
            for a in range(2):
                for hf in range(2):
                    T("pe", _mm(pb_[4 + hf][:, :], one1[0:1, :], nrow[0:1, a, hf * 512:(hf + 1) * 512]), reads=[d0], writes=[pb_d[4 + hf]])
                    T("act", lambda e, a=a, hf=hf: e.copy(nbc[:, a, hf * 512:(hf + 1) * 512], pb_[4 + hf][:, :]), reads=[pb_d[4 + hf]], writes=[cst])
            S.barrier()
        gb = [sbl(f"gb{i}", [128, 2, 2048], BF16) for i in range(NG)]; gb_d = [[Dep(), Dep()] for _ in range(NG)]
        NC5 = NB + 1
        gcnt = 0
        dcnt = 0
        for b in range(NB):
            for vi, j0 in enumerate([16, 32, 24, 40]):
                for jj in range(8):
                    dgt, dgd = dg[dcnt % 4], dg_d[dcnt % 4]
                    pp, ppd = pb_[dcnt % 4], pb_d[dcnt % 4]
                    dcnt += 1
                    T("dve", lambda e, dgt=dgt, j0=j0, jj=jj, b=b: e.tensor_scalar(dgt[:], k.ident[:], k.modT[:, j0 + jj, b:b + 1], None, op0=ALU.mult),
                      reads=[k.ident_d, k.modT_d], writes=[dgd])
                    T("pe", _mm(pp[:, 0:128], ones[:], dgt[:]), reads=[cst, dgd], writes=[ppd])
                    T("act", lambda e, pp=pp, vi=vi, jj=jj: e.copy(bc[:, vi, jj * 128:(jj + 1) * 128], pp[:, 0:128]), reads=[ppd], writes=[bc_d])
            T("dve", lambda e: e.tensor_scalar(bc[:, 1, :], bc[:, 1, :], 1.0, None, op0=ALU.add), reads=[bc_d], writes=[bc_d])
            T("dve", lambda e: e.tensor_tensor(bc[:, 1, :], bc[:, 1, :], nbc[:, 0, :], op=ALU.mult), reads=[bc_d, cst], writes=[bc_d])
            for tt in range(NT4):
                t0 = tt * 128
                S.dma("sp", lambda e, b=b, t0=t0: e.dma_start(out=xt[:], in_=I["x"][b, t0:t0 + 128, :]), writes=[xt_d])
                S.dma("act", lambda e, b=b, t0=t0: e.dma_start(out=mx[:], in_=k.MIXT[b].rearrange("(kc p) t -> p kc t", p=128)[:, :, t0:t0 + 128]),
                      reads=[k.MIXT_dep[b]], writes=[mx_d])
                T("act", lambda e: e.copy(mxb[:], mx[:]), reads=[mx_d], writes=[mxb_d])
                for hf in range(2):
                    for kc in range(8):
                        T("pe", _mm(pb_[hf][:, :], mxb[:, kc, :], wo[:, kc, hf * 512:(hf + 1) * 512], start=(kc == 0), stop=(kc == 7)),
                          reads=[mxb_d, cst], writes=[pb_d[hf]])
                    hs = slice(hf * 512, (hf + 1) * 512)
                    T("dve", lambda e, hf=hf, hs=hs: e.tensor_tensor(h1[:, hs], pb_[hf][:, :], bc[:, 0, hs], op=ALU.mult), reads=[pb_d[hf], bc_d], writes=[h1_d])
                    T("dve", lambda e, hs=hs: e.tensor_tensor(h1[:, hs], h1[:, hs], xt[:, hs], op=ALU.add), reads=[xt_d, h1_d], writes=[h1_d])
                T("act", lambda e: e.activation(junk[:], h1[:], AF.Square, accum_out=st[:, 0:1]), reads=[h1_d], writes=[junk_d, st_d])
                T("act", lambda e: e.activation(st[:, 1:2], st[:, 0:1], AF.Sqrt, bias=1e-6, scale=1.0 / D), reads=[st_d], writes=[st_d])
                T("dve", lambda e: e.reciprocal(st[:, 2:3], st[:, 1:2]), reads=[st_d], writes=[st_d])
                T("act", lambda e: e.activation(hb[:], h1[:], AF.Copy, scale=st[:, 2:3]), reads=[h1_d, st_d], writes=[hb_d])
                T("dve", lambda e: e.tensor_tensor(hb[:], hb[:], bc[:, 1, :], op=ALU.mult), reads=[hb_d, bc_d], writes=[hb_d])
                T("dve", lambda e: e.tensor_tensor(hb[:], hb[:], bc[:, 2, :], op=ALU.add), reads=[hb_d, bc_d], writes=[hb_d])
                T("act", lambda e: e.copy(hbb[:], hb[:]), reads=[hb_d], writes=[hbb_d])
                for kd in range(8):
                    bk_ = 2 + kd // 4
                    T("pe", lambda e, kd=kd, bk_=bk_: e.transpose(pb_[bk_][:, (kd % 4) * 128:(kd % 4 + 1) * 128], hb[:, kd * 128:(kd + 1) * 128], k.ident[:]),
                      reads=[hb_d, k.ident_d], writes=[pb_d[bk_]])
                for q in range(2):
                    T("act", lambda e, q=q: e.copy(hbT[:, q * 4:(q + 1) * 4, :].rearrange("p a t -> p (a t)"), pb_[2 + q][:, :]), reads=[pb_d[2 + q]], writes=[hbT_d])
                for j in range(16):
                    pp, ppd = pb_[4 + j % 2], pb_d[4 + j % 2]
                    for kd in range(8):
                        T("pe", _mm(pp[:, 0:128], wq[:, kd, j * 128:(j + 1) * 128], hbT[:, kd, :], start=(kd == 0), stop=(kd == 7)),
                          reads=[cst, hbT_d], writes=[ppd])
                    T("act", lambda e, pp=pp, j=j: e.copy(qT[:, j, :], pp[:, 0:128]), reads=[ppd], writes=[qT_d])
                for j in range(16):
                    bk_ = j // 4
                    T("pe", _mm(pb_[bk_][:, (j % 4) * 128:(j % 4 + 1) * 128], qT[:, j, :], keysT[:, j, :]), reads=[qT_d, cst], writes=[pb_d[bk_]])
                for q in range(4):
                    T("act", lambda e, q=q: e.copy(sc[:, q * 4:(q + 1) * 4, :].rearrange("p a n -> p (a n)"), pb_[q][:, :]), reads=[pb_d[q]], writes=[sc_d])
                for j in range(16):
                    T("dve", lambda e, j=j: e.max(m16[:, j, 0:8], sc[:, j, :]), reads=[sc_d], writes=[tk_d])
                    T("dve", lambda e, j=j: e.match_replace(sc2[:, 0, :], m16[:, j, 0:8], sc[:, j, :], -3.0e38), reads=[sc_d, tk_d], writes=[sc2_d])
                    T("dve", lambda e, j=j: e.max(m16[:, j, 8:16], sc2[:, 0, :]), reads=[sc2_d], writes=[tk_d])
                    T("dve", lambda e, j=j: e.max_index(i16[:, j, 0:8], m16[:, j, 0:8], sc[:, j, :]), reads=[sc_d, tk_d], writes=[tk_d])
                    T("dve", lambda e, j=j: e.max_index(i16[:, j, 8:16], m16[:, j, 8:16], sc2[:, 0, :]), reads=[sc2_d, tk_d], writes=[tk_d])
                T("dve", lambda e: e.tensor_copy(i16f[:], i16[:]), reads=[tk_d], writes=[tk_d])
                m16v = m16[:].rearrange("p (h c) a -> p h c a", c=2)
                i16v = i16f[:].rearrange("p (h c) a -> p h c a", c=2)
                candv = cand[:].rearrange("p h (a b) -> p h a b", b=16)
                T("dve", lambda e: e.tensor_tensor(candv, m16v[:, :, 0, :].unsqueeze(3).to_broadcast([128, 8, 16, 16]),
                                                   m16v[:, :, 1, :].unsqueeze(2).to_broadcast([128, 8, 16, 16]), op=ALU.add), reads=[tk_d], writes=[tk_d])
                for h in range(8):
                    T("dve", lambda e, h=h: e.max(best[:, h, 0:8], cand[:, h, :]), reads=[tk_d], writes=[tk_d])
                    T("dve", lambda e, h=h: e.match_replace(cand2[:, h, :], best[:, h, 0:8], cand[:, h, :], -3.0e38), reads=[tk_d], writes=[tk_d])
                    T("dve", lambda e, h=h: e.max(best[:, h, 8:16], cand2[:, h, :]), reads=[tk_d], writes=[tk_d])
                    T("dve", lambda e, h=h: e.max_index(pos[:, h, 0:8], best[:, h, 0:8], cand[:, h, :]), reads=[tk_d], writes=[tk_d])
                    T("dve", lambda e, h=h: e.max_index(pos[:, h, 8:16], best[:, h, 8:16], cand2[:, h, :]), reads=[tk_d], writes=[tk_d])
                T("dve", lambda e: e.tensor_tensor(gate[:], best[:], best[:, :, 0:1].to_broadcast([128, 8, 16]), op=ALU.subtract), reads=[tk_d], writes=[tk_d])
                T("act", lambda e: e.activation(gate[:], gate[:], AF.Exp), reads=[tk_d], writes=[tk_d])
                T("dve", lambda e: e.tensor_reduce(gsm[:, :, 0], gate[:], axis=AX.X, op=ALU.add), reads=[tk_d], writes=[tk_d])
                T("dve", lambda e: e.reciprocal(gsm[:, :, 1], gsm[:, :, 0]), reads=[tk_d], writes=[tk_d])
                T("dve", lambda e: e.tensor_tensor(gate[:], gate[:], gsm[:, :, 1:2].to_broadcast([128, 8, 16]), op=ALU.mult), reads=[tk_d], writes=[tk_d])
                T("dve", lambda e: e.tensor_single_scalar(pa_i[:], pos[:], 4, op=ALU.logical_shift_right), reads=[tk_d], writes=[tk_d])
                T("dve", lambda e: e.tensor_single_scalar(pb_i[:], pos[:], 15, op=ALU.bitwise_and), reads=[tk_d], writes=[tk_d])
                T("dve", lambda e: e.tensor_copy(pa_f[:], pa_i[:]), reads=[tk_d], writes=[tk_d])
                T("dve", lambda e: e.tensor_copy(pb_f[:], pb_i[:]), reads=[tk_d], writes=[tk_d])
                io_b = io16[:].unsqueeze(1).unsqueeze(1).to_broadcast([128, 8, 16, 16])
                for (pf, cc, dst) in [(pa_f, 0, i1s), (pb_f, 1, i2s)]:
                    T("dve", lambda e, pf=pf: e.tensor_tensor(eq, pf[:].unsqueeze(3).to_broadcast([128, 8, 16, 16]), io_b, op=ALU.is_equal),
                      reads=[tk_d, cst], writes=[tk_d])
                    T("dve", lambda e, cc=cc: e.tensor_tensor(eq, eq, i16v[:, :, cc, :].unsqueeze(2).to_broadcast([128, 8, 16, 16]), op=ALU.mult),
                      reads=[tk_d], writes=[tk_d])
                    T("dve", lambda e, dst=dst: e.tensor_reduce(dst[:], eq, axis=AX.X, op=ALU.add), reads=[tk_d], writes=[tk_d])
                T("dve", lambda e: e.scalar_tensor_tensor(idxf[:], i1s[:].rearrange("p h k -> p (h k)"), 128.0, i2s[:].rearrange("p h k -> p (h k)"),
                                                          op0=ALU.mult, op1=ALU.add), reads=[tk_d], writes=[tk_d])
                T("dve", lambda e: e.tensor_copy(idxi[:], idxf[:]), reads=[tk_d], writes=[tk_d])
                gflat = gate[:].rearrange("p h k -> p (h k)")
                NGRP = NSLOT // 2
                ginfo = {}

                def stage_a(g):
                    nonlocal gcnt
                    gbt, gbd = gb[gcnt % NG], gb_d[gcnt % NG]
                    gcnt += 1
                    ginfo[g] = (gbt, gbd)
                    for s2 in range(2):
                        slot = g * 2 + s2
                        S.dma("pool", lambda e, gbt=gbt, s2=s2, slot=slot: e.indirect_dma_start(
                            out=gbt[:, s2, :], out_offset=None, in_=k.UVB[:, :],
                            in_offset=bass.IndirectOffsetOnAxis(ap=idxi[:, slot:slot + 1], axis=0)),
                            reads=[tk_d, k.UVB_dep], writes=[gbd[s2]])
                    for s2 in range(2):
                        slot = g * 2 + s2
                        if s2 == 1 and SPLIT:
                            pr_, prd_ = prod[g % 2], prod_d[g % 2]
                            T("pool", lambda e, gbt=gbt, pr_=pr_: e.tensor_tensor(pr_[:], gbt[:, 1, 0:1024], hbb[:], op=ALU.mult),
                              reads=[gbd[1], hbb_d], writes=[prd_])
                            T("act", lambda e, pr_=pr_, slot=slot: e.activation(junk[:], pr_[:], AF.Copy, accum_out=actr[:, slot:slot + 1]),
                              reads=[prd_], writes=[junk_d, asd[slot]])
                            continue
                        T("dve", lambda e, gbt=gbt, s2=s2, slot=slot: e.scalar_tensor_tensor(
                            junkb[:], gbt[:, s2, 0:1024], 1.0, hbb[:], op0=ALU.mult, op1=ALU.mult,
                            accum_out=actr[:, slot:slot + 1]), reads=[gbd[s2], hbb_d], writes=[asd[slot]])
                    sl = slice(g * 2, g * 2 + 2)
                    T("act", lambda e, sl=sl: e.activation(wgt[:, sl], actr[:, sl], AF.Gelu), reads=[asd[g * 2], asd[g * 2 + 1]], writes=[agd[g]])

                def stage_b(g):
                    nonlocal dcnt
                    gbt, gbd = ginfo.pop(g)
                    ad_ = agd[g]
                    sl = slice(g * 2, g * 2 + 2)
                    T("dve", lambda e, sl=sl: e.tensor_tensor(wgt[:, sl], wgt[:, sl], gflat[:, sl], op=ALU.mult), reads=[ad_, tk_d], writes=[ad_])
                    for s2 in range(2):
                        slot = g * 2 + s2
                        dgt, dgd = dgb[dcnt % 4], dgb_d[dcnt % 4]
                        dcnt += 1
                        T("act", lambda e, dgt=dgt, slot=slot: e.activation(dgt[:], k.ident[:], AF.Copy, scale=wgt[:, slot:slot + 1]),
                          reads=[k.ident_d, ad_], writes=[dgd])
                        for hf in range(2):
                            T("pe", _mm(pb_[6 + hf][:, :], dgt[:], gbt[:, s2, 1024 + hf * 512:1024 + (hf + 1) * 512],
                                        start=(slot == 0), stop=(slot == NSLOT - 1)), reads=[dgd, gbd[s2]], writes=[pb_d[6 + hf]])

                SKEW = 2
                for g in range(NGRP + SKEW):
                    if g < NGRP:
                        stage_a(g)
                    if g >= SKEW:
                        stage_b(g - SKEW)
                for hf in range(2):
                    hs = slice(hf * 512, (hf + 1) * 512)
                    T("dve", lambda e, hf=hf, hs=hs: e.tensor_tensor(oo[:, hs], pb_[6 + hf][:, :], bc[:, 3, hs], op=ALU.mult), reads=[pb_d[6 + hf], bc_d], writes=[oo_d])
                    T("dve", lambda e, hs=hs: e.tensor_tensor(oo[:, hs], oo[:, hs], h1[:, hs], op=ALU.add), reads=[oo_d, h1_d], writes=[oo_d])
                T("act", lambda e: e.activation(junk[:], oo[:], AF.Square, accum_out=st[:, 4:5]), reads=[oo_d], writes=[junk_d, st_d])
                T("act", lambda e: e.activation(st[:, 5:6], st[:, 4:5], AF.Sqrt, bias=1e-6, scale=1.0 / D), reads=[st_d], writes=[st_d])
                T("dve", lambda e: e.reciprocal(st[:, 6:7], st[:, 5:6]), reads=[st_d], writes=[st_d])
                T("act", lambda e: e.activation(oo[:], oo[:], AF.Copy, scale=st[:, 6:7]), reads=[oo_d, st_d], writes=[oo_d])
                T("dve", lambda e: e.tensor_tensor(oo[:], oo[:], nbc[:, 1, :], op=ALU.mult), reads=[oo_d, cst], writes=[oo_d])
                S.dma("sp", lambda e, b=b, t0=t0: e.dma_start(out=k.out[b, t0:t0 + 128, :], in_=oo[:]), reads=[oo_d], writes=[Dep()])
        S.barrier()
```

```python
import numpy as np
from contextlib import ExitStack
import concourse.bass as bass
import concourse.mybir as mybir
from concourse.bass_utils import run_bass_kernel_spmd

F32 = mybir.dt.float32
BF16 = mybir.dt.bfloat16
I32 = mybir.dt.int32
U32 = mybir.dt.uint32
AF = mybir.ActivationFunctionType
ALU = mybir.AluOpType
AX = mybir.AxisListType

D = 1024
SEQ = 2048
CTX = 256
LT = SEQ + CTX
INC = 2208
NCORES = 8
NBATCH = 32


class Dep:
    __slots__ = ("w", "r", "excl")

    def __init__(self, excl=False):
        self.w = None
        self.r = {}
        self.excl = excl


def PD():
    return Dep(excl=True)


class Sch:
    ROT = 30000

    def __init__(self, nc, es):
        self.nc = nc
        self.es = es
        self.eng = {"pe": nc.tensor, "dve": nc.vector, "act": nc.scalar, "pool": nc.gpsimd, "sp": nc.sync}
        self.cur = {}
        self.cnt = {}
        self.seen = {e: {} for e in self.eng}
        self.nsem = 0
        for e in self.eng:
            self._newsem(e)
        self.dsems = []
        for i in range(40):
            s = es.enter_context(nc.semaphore(f"dq{i}"))
            self.dsems.append([s, 0])
        self.dnext = 0
        self.swsems = []
        for i in range(16):
            s = es.enter_context(nc.semaphore(f"sq{i}"))
            self.swsems.append([s, 0])
        self.swnext = 0
        self.semobj = {}
        self.ninst = 0
        import os
        self.skip_own = set(os.environ.get("SKIP_OWN", "pe").split(","))

    def _newsem(self, e):
        s = self.es.enter_context(self.nc.semaphore(f"e_{e}_{self.nsem}"))
        self.nsem += 1
        self.cur[e] = s
        self.cnt[e] = 0

    def _wait(self, en, deps):
        best = {}
        for (s, v) in deps:
            k = id(s)
            if k not in best or best[k][1] < v:
                best[k] = (s, v)
        seen = self.seen[en]
        for k, (s, v) in best.items():
            if seen.get(k, 0) >= v:
                continue
            self.eng[en].wait_ge(s, v)
            self.nwait = getattr(self, "nwait", 0) + 1
            seen[k] = v

    def _deps(self, reads, writes):
        deps = []
        for d in reads:
            if d.w is not None:
                deps.append(d.w)
        for d in writes:
            if d.w is not None:
                deps.append(d.w)
            deps.extend(d.r.values())
        return deps

    def _mark(self, ev, reads, writes):
        for d in reads:
            d.r[id(ev[0])] = ev
        for d in writes:
            d.w = ev
            d.r = {}

    def op(self, en, fn, reads=(), writes=()):
        ex = [d for d in reads if d.excl]
        if ex:
            reads = [d for d in reads if not d.excl]
            writes = list(writes) + ex
        deps = self._deps(reads, writes)
        if en in self.skip_own:
            own = id(self.cur[en])
            deps = [d for d in deps if id(d[0]) != own]
        self._wait(en, deps)
        ins = fn(self.eng[en])
        if self.cnt[en] >= self.ROT:
            self._newsem(en)
        self.cnt[en] += 1
        ins.then_inc(self.cur[en], 1)
        ev = (self.cur[en], self.cnt[en])
        self._mark(ev, reads, writes)
        self.ninst += 1
        return ev

    def dma(self, q, fn, reads=(), writes=()):
        if q == "pool":
            slot = self.swsems[self.swnext]
            self.swnext = (self.swnext + 1) % len(self.swsems)
        else:
            slot = self.dsems[self.dnext]
            self.dnext = (self.dnext + 1) % len(self.dsems)
        deps = self._deps(reads, writes)
        if slot[1] > 0:
            deps.append((slot[0], slot[1]))
        self._wait(q, deps)
        ins = fn(self.eng[q])
        slot[1] += 16
        ins.then_inc(slot[0], 16)
        ev = (slot[0], slot[1])
        self._mark(ev, reads, writes)
        self.ninst += 1
        return ev

    def barrier(self):
        evs = [(self.cur[e], self.cnt[e]) for e in self.eng if self.cnt[e] > 0]
        evs += [(s, v) for (s, v) in self.dsems + self.swsems if v > 0]
        for e in self.eng:
            self._wait(e, evs)


def _mm(out, lhsT, rhs, start=True, stop=True, tp=None):
    return lambda e: e.matmul(out, lhsT, rhs, start=start, stop=stop, tile_position=tp)


class K:
    pass


def build(NB=4, upto=99, dbg=()):
    nc = bass.Bass("TRN2", target_bir_lowering=False)
    es = ExitStack()
    k = K()
    k.nc, k.es, k.NB = nc, es, NB
    k.upto = upto
    S = k.S = Sch(nc, es)

    def din(name, shape, dt=F32):
        return nc.dram_tensor(name, list(shape), dt, kind="ExternalInput").ap()

    I = k.I = {}
    I["x"] = din("x", [NB, SEQ, D])
    I["c"] = din("c", [NB, D])
    I["ctx"] = din("ctx", [NB, CTX, D])
    I["c_ctx"] = din("c_ctx", [1, D])
    I["w_ada"] = din("w_ada", [D, 6 * D])
    I["b_ada"] = din("b_ada", [48, 128])
    I["norm1_g"] = din("norm1_g", [8, 128])
    I["norm2_g"] = din("norm2_g", [1, D])
    I["w_in"] = din("w_in", [D, INC])
    I["s5_arow"] = din("s5_arow", [3, 32, 128])
    I["s5_bT"] = din("s5_bT", [2, 128, 1024])
    I["s5_cblk"] = din("s5_cblk", [2, 128, 8, 128])
    I["s5_vec"] = din("s5_vec", [8, 128])
    I["s5_w_glu"] = din("s5_w_glu", [512, 512])
    I["rw_vec"] = din("rw_vec", [42, 128])
    I["rw_wlora"] = din("rw_wlora", [64, 2, 512])
    I["rw_w_g2"] = din("rw_w_g2", [96, 512])
    I["rw_ln"] = din("rw_ln", [2, 512])
    I["w_out"] = din("w_out", [D, D])
    I["peer_w_q"] = din("peer_w_q", [D, 2048])
    I["peer_keys"] = din("peer_keys", [128, 16, 128])
    I["peer_uv"] = din("peer_uv", [16384, 2048])
    I["nvec"] = din("nvec", [2, D])
    k.out = nc.dram_tensor("out", [NB, SEQ, D], F32, kind="ExternalOutput").ap()
    k.dbg = {}
    for (name, shape) in dbg:
        if name in ("PT", "MIXT") or shape is None:
            continue
        k.dbg[name] = nc.dram_tensor(name, list(shape), F32, kind="ExternalOutput").ap()
    dbgn = [d[0] for d in dbg]
    k.PT = nc.dram_tensor("PT", [NB, INC, LT], F32, kind="ExternalOutput" if "PT" in dbgn else "Internal").ap()
    k.PT_dep = [Dep() for _ in range(NB)]
    k.MIXT = nc.dram_tensor("MIXT", [NB, D, SEQ], F32, kind="ExternalOutput" if "MIXT" in dbgn else "Internal").ap()
    k.MIXT_dep = [Dep() for _ in range(NB)]
    k.UVB = nc.dram_tensor("UVB", [16384, 2048], BF16, kind="Internal").ap()
    k.UVB_dep = Dep()

    with es:
        setup_consts(k)
        k.modT = sb(k, "modT", [128, 48, NB + 1])
        k.gs1T = sb(k, "gs1T", [128, 8, NB + 1])
        stage0_mod(k)
        if upto >= 1:
            stage1_proj(k)
        if upto >= 2 and "skip_s5" not in dbgn:
            with ExitStack() as es2:
                k.es_stage = es2
                s5_alloc(k)
                s5_setup(k)
                stage2_s5(k)
        if upto >= 3:
            with ExitStack() as es3:
                k.es_stage = es3
                rw_setup(k)
                stage3_rwkv(k)
        if upto >= 4:
            if not getattr(k, "cast_done", False):
                cast_uv(k)
            stage4_peer(k)
        S.barrier()
        print("ninst", S.ninst, "nwait", getattr(S, "nwait", 0))
    return nc


def sb(k, name, shape, dt=F32):
    return k.es.enter_context(k.nc.sbuf_tensor(name, list(shape), dt))


def ps(k, name, shape, dt=F32):
    return k.es.enter_context(k.nc.psum_tensor(name, list(shape), dt))


def setup_consts(k):
    nc, S = k.nc, k.S
    k.ident = sb(k, "ident", [128, 128])
    k.ident_d = Dep()
    k.iota_p = sb(k, "iota_p", [128, 1], I32)
    k.iota_f = sb(k, "iota_f", [128, 128], I32)
    k.iota_d = Dep()
    S.op("pool", lambda e: e.iota(k.iota_p[:], [[0, 1]], base=0, channel_multiplier=1), writes=[k.iota_d])
    S.op("pool", lambda e: e.iota(k.iota_f[:], [[1, 128]], base=0, channel_multiplier=0), writes=[k.iota_d])
    k.iota_pf = sb(k, "iota_pf", [128, 1])
    k.iota_ff = sb(k, "iota_ff", [128, 128])
    S.op("dve", lambda e: e.tensor_copy(k.iota_pf[:], k.iota_p[:]), reads=[k.iota_d], writes=[k.ident_d])
    S.op("dve", lambda e: e.tensor_copy(k.iota_ff[:], k.iota_f[:]), reads=[k.iota_d], writes=[k.ident_d])
    S.op("dve", lambda e: e.tensor_scalar(k.ident[:], k.iota_ff[:], k.iota_pf[:, 0:1], None, op0=ALU.is_equal),
         reads=[k.ident_d], writes=[k.ident_d])


def stage0_mod(k):
    nc, S, I, NB = k.nc, k.S, k.I, k.NB
    NC5 = NB + 1
    with ExitStack() as es:
        def sbl(name, shape, dt=F32):
            return es.enter_context(nc.sbuf_tensor(name, list(shape), dt))
        crow = sbl("crow", [NC5, D])
        crow_d = Dep()
        S.dma("sp", lambda e: e.dma_start(out=crow[0:NB, :], in_=I["c"][:, :]), writes=[crow_d])
        S.dma("sp", lambda e: e.dma_start(out=crow[NB:NC5, :], in_=I["c_ctx"][:, :]), writes=[crow_d])
        S.op("act", lambda e: e.activation(crow[:], crow[:], AF.Silu), reads=[crow_d], writes=[crow_d])
        cT = sbl("cT", [128, 8, NC5])
        cT_d = Dep()
        vst = sbl("vst", [64, 128])
        vst_d = Dep()
        S.dma("sp", lambda e: e.dma_start(out=vst[0:48, :], in_=I["b_ada"][:, :]), writes=[vst_d])
        S.dma("sp", lambda e: e.dma_start(out=vst[48:56, :], in_=I["norm1_g"][:, :]), writes=[vst_d])
        vT = sbl("vT", [128, 56])
        vT_d = Dep()
        with nc.psum_tensor("p0a", [128, 8, NC5], F32) as pa, nc.psum_tensor("p0b", [128, 56], F32) as pb, \
                nc.psum_tensor("p0c", [128, 48, NC5], F32) as pc:
            pa_d, pb_d, pc_d = PD(), PD(), PD()
            for kd in range(8):
                S.op("pe", lambda e, kd=kd: e.transpose(pa[:, kd, :], crow[0:NC5, kd * 128:(kd + 1) * 128],
                                                        k.ident[0:NC5, 0:NC5]),
                     reads=[crow_d, k.ident_d], writes=[pa_d])
            S.op("dve", lambda e: e.tensor_copy(cT[:], pa[:]), reads=[pa_d], writes=[cT_d])
            S.op("pe", lambda e: e.transpose(pb[:, :], vst[0:56, :], k.ident[0:56, 0:56]),
                 reads=[vst_d, k.ident_d], writes=[pb_d])
            S.op("dve", lambda e: e.tensor_copy(vT[:], pb[:]), reads=[pb_d], writes=[vT_d])
            wt = [sbl(f"wada{i}", [128, 8, 512]) for i in range(2)]
            wt_d = [Dep(), Dep()]
            wv = I["w_ada"].rearrange("(kd p) n -> p kd n", p=128)
            for blk in range(12):
                t, td = wt[blk % 2], wt_d[blk % 2]
                for kd in range(8):
                    S.dma("sp" if kd % 2 == 0 else "act",
                          lambda e, kd=kd, t=t, blk=blk: e.dma_start(out=t[:, kd, :], in_=wv[:, kd, blk * 512:(blk + 1) * 512]),
                          writes=[td])
                for jj in range(4):
                    j = blk * 4 + jj
                    for kd in range(8):
                        S.op("pe", _mm(pc[:, j, :], t[:, kd, jj * 128:(jj + 1) * 128], cT[:, kd, :],
                                       start=(kd == 0), stop=(kd == 7)),
                             reads=[td, cT_d], writes=[pc_d])
            k.modT_d = Dep()
            S.op("dve", lambda e: e.tensor_tensor(k.modT[:], pc[:], vT[:, 0:48].unsqueeze(2).to_broadcast([128, 48, NC5]),
                                                  op=ALU.add),
                 reads=[pc_d, vT_d], writes=[k.modT_d])
        S.op("dve", lambda e: e.tensor_scalar(k.gs1T[:], k.modT[:, 8:16, :], 1.0, None, op0=ALU.add),
             reads=[k.modT_d], writes=[k.modT_d])
        S.op("dve", lambda e: e.tensor_tensor(k.gs1T[:], k.gs1T[:], vT[:, 48:56].unsqueeze(2).to_broadcast([128, 8, NC5]),
                                              op=ALU.mult),
             reads=[vT_d, k.modT_d], writes=[k.modT_d])
        k.S.barrier()


def stage1_proj(k):
    nc, S, I, NB = k.nc, k.S, k.I, k.NB
    with ExitStack() as es:
        def sbl(name, shape, dt=F32):
            return es.enter_context(nc.sbuf_tensor(name, list(shape), dt))
        wbf = sbl("w_in_bf", [128, 8, INC], BF16)
        wbf_d = Dep()
        for kd in range(8):
            S.dma("pool", lambda e, kd=kd: e.dma_start(out=wbf[:, kd, :], in_=I["w_in"][kd * 128:(kd + 1) * 128, :],
                                                       max_dma_last_dim=4096), writes=[wbf_d])
        xt = [sbl(f"xt{i}", [128, D]) for i in range(2)]
        xt_d = [Dep(), Dep()]
        xs = [sbl(f"xs{i}", [128, D]) for i in range(2)]
        xs_d = [Dep(), Dep()]
        junk = sbl("junk1", [128, D])
        junk_d = Dep()
        st = [sbl(f"st{i}", [128, 4]) for i in range(2)]
        hnT = [sbl(f"hnT{i}", [128, 8, 512], BF16) for i in range(2)]
        hnT_d = [Dep(), Dep()]
        ev = [sbl(f"ev{i}", [128, 512]) for i in range(3)]
        ev_d = [Dep() for _ in range(3)]
        ptr = [es.enter_context(nc.psum_tensor(f"ptr{i}", [128, 8, 128], F32)) for i in range(2)]
        ptr_d = [PD(), PD()]
        pmm = [es.enter_context(nc.psum_tensor(f"pmm{i}", [128, 512], F32)) for i in range(3)]
        pmm_d = [PD() for _ in range(3)]
        fch = [(i * 128, 128) for i in range(16)] + [(2048, 64), (2112, 96)]
        k.fch = fch
        ti = 0
        gi = 0
        ei = 0
        for b in range(NB):
            groups = [("ctx", 0, 256)] + [("lat", g * 512, 512) for g in range(4)]
            for (kind, t0, nt) in groups:
                h, hd = hnT[gi % 2], hnT_d[gi % 2]
                gi += 1
                col = b if kind == "lat" else NB
                for tt in range(nt // 128):
                    x_t, x_d = xt[ti % 2], xt_d[ti % 2]
                    xs_t, xsd = xs[ti % 2], xs_d[ti % 2]
                    s_t = st[ti % 2]
                    p_t, p_d = ptr[ti % 2], ptr_d[ti % 2]
                    ti += 1
                    src = I["x"][b, t0 + tt * 128:t0 + (tt + 1) * 128, :] if kind == "lat" else \
                        I["ctx"][b, tt * 128:(tt + 1) * 128, :]
                    S.dma("sp", lambda e, x_t=x_t, src=src: e.dma_start(out=x_t[:], in_=src), writes=[x_d])
                    S.op("act", lambda e, x_t=x_t, s_t=s_t: e.activation(junk[:], x_t[:], AF.Square, accum_out=s_t[:, 0:1]),
                         reads=[x_d], writes=[junk_d, xsd])
                    S.op("act", lambda e, s_t=s_t: e.activation(s_t[:, 1:2], s_t[:, 0:1], AF.Sqrt, bias=1e-6, scale=1.0 / D),
                         reads=[xsd], writes=[xsd])
                    S.op("dve", lambda e, s_t=s_t: e.reciprocal(s_t[:, 2:3], s_t[:, 1:2]), reads=[xsd], writes=[xsd])
                    S.op("act", lambda e, x_t=x_t, xs_t=xs_t, s_t=s_t: e.activation(xs_t[:], x_t[:], AF.Copy, scale=s_t[:, 2:3]),
                         reads=[x_d, xsd], writes=[xsd])
                    for kd in range(8):
                        S.op("pe", lambda e, kd=kd, p_t=p_t, xs_t=xs_t: e.transpose(p_t[:, kd, :], xs_t[:, kd * 128:(kd + 1) * 128],
                                                                                      k.ident[:]),
                             reads=[xsd, k.ident_d], writes=[p_d])
                    for kd in range(8):
                        S.op("dve", lambda e, kd=kd, p_t=p_t, h=h, tt=tt, col=col: e.tensor_scalar(
                            h[:, kd, tt * 128:(tt + 1) * 128], p_t[:, kd, :], k.gs1T[:, kd, col:col + 1],
                            k.modT[:, kd, col:col + 1], op0=ALU.mult, op1=ALU.add),
                            reads=[p_d, k.modT_d], writes=[hd])
                tok0 = t0 if kind == "ctx" else CTX + t0
                for fi, (c0, ncol) in enumerate(fch):
                    pm, pmd = pmm[ei % 3], pmm_d[ei % 3]
                    e_t, e_d = ev[ei % 3], ev_d[ei % 3]
                    ei += 1
                    for kd in range(8):
                        S.op("pe", _mm(pm[0:ncol, 0:nt], wbf[:, kd, c0:c0 + ncol], h[:, kd, 0:nt], start=(kd == 0), stop=(kd == 7)),
                             reads=[wbf_d, hd], writes=[pmd])
                    eng = "act" if fi % 2 == 0 else "dve"
                    if eng == "act":
                        S.op("act", lambda e, pm=pm, e_t=e_t, ncol=ncol, nt=nt: e.copy(e_t[0:ncol, 0:nt], pm[0:ncol, 0:nt]),
                             reads=[pmd], writes=[e_d])
                    else:
                        S.op("dve", lambda e, pm=pm, e_t=e_t, ncol=ncol, nt=nt: e.tensor_copy(e_t[0:ncol, 0:nt], pm[0:ncol, 0:nt]),
                             reads=[pmd], writes=[e_d])
                    S.dma("sp", lambda e, e_t=e_t, ncol=ncol, nt=nt, c0=c0, tok0=tok0, b=b: e.dma_start(
                        out=k.PT[b, c0:c0 + ncol, tok0:tok0 + nt], in_=e_t[0:ncol, 0:nt]),
                        reads=[e_d], writes=[k.PT_dep[b]])
        S.barrier()


_CACHE = {}


def _prep_inputs(inputs, NB, core):
    sl = slice(core * NB, (core + 1) * NB)
    f = lambda a: np.ascontiguousarray(a, dtype=np.float32)
    m = {
        "x": f(inputs["x"][sl]),
        "c": f(inputs["c"][sl]),
        "ctx": f(inputs["ctx"][sl]),
        "c_ctx": f(inputs["c_ctx"].reshape(1, D)),
        "w_ada": f(inputs["w_ada"][0]),
        "b_ada": f(inputs["b_ada"][0].reshape(48, 128)),
        "norm1_g": f(inputs["norm1_g"][0].reshape(8, 128)),
        "norm2_g": f(inputs["norm2_g"][0].reshape(1, D)),
        "w_in": f(inputs["w_in"][0]),
    }
    m.update(_s5_layout(inputs))
    m.update(_rw_layout(inputs))
    m.update(_peer_layout(inputs))
    return m


def kernel(**inputs):
    NB = NBATCH // NCORES
    if "nc" not in _CACHE:
        _CACHE["nc"] = build(NB)
    nc = _CACHE["nc"]
    in_maps = [_prep_inputs(inputs, NB, c) for c in range(NCORES)]
    res = run_bass_kernel_spmd(nc, in_maps, core_ids=list(range(NCORES)))
    return np.concatenate([r["out"] for r in res.results], axis=0)


def _s5_layout(inputs):
    f = lambda a: np.ascontiguousarray(a, dtype=np.float32)
    a_re, a_im, ldt = inputs["s5_a_re"][0], inputs["s5_a_im"][0], inputs["s5_log_dt"][0]
    arow = np.zeros((3, 32, 128), np.float32)
    arow[0] = a_re.reshape(2, 16, 128).reshape(32, 128)
    arow[1] = a_im.reshape(2, 16, 128).reshape(32, 128)
    arow[2] = np.repeat(ldt.reshape(2, 16, 2, 1), 64, axis=3).reshape(32, 128)
    bT = np.zeros((2, 2, 64, 2, 4, 4, 2, 16), np.float32)
    cb = np.zeros((2, 4, 2, 16, 2, 4, 2, 64), np.float32)
    for ri, (bsrc, csrc) in enumerate([(inputs["s5_b_re"][0], inputs["s5_c_re"][0]),
                                       (inputs["s5_b_im"][0], inputs["s5_c_im"][0])]):
        bg = bsrc.reshape(2, 4, 4, 2, 64, 16)
        cg = csrc.reshape(2, 4, 4, 2, 16, 64)
        for gl in range(2):
            bT[ri, gl, :, :, :, :, gl, :] = np.transpose(bg[:, :, :, gl], (3, 0, 1, 2, 4))
            cb[ri, :, gl, :, :, :, gl, :] = np.transpose(cg[:, :, :, gl], (2, 3, 0, 1, 4))
    vec = np.zeros((8, 128), np.float32)
    vec[0:4] = inputs["s5_d"][0].reshape(4, 128)
    vec[4:8] = inputs["s5_b_glu"][0].reshape(4, 128)
    return {"s5_arow": arow, "s5_bT": f(bT.reshape(2, 128, 1024)), "s5_cblk": f(cb.reshape(2, 128, 8, 128)),
            "s5_vec": vec, "s5_w_glu": f(inputs["s5_w_glu"][0])}


def sbs(k, name, shape, dt=F32):
    return k.es_stage.enter_context(k.nc.sbuf_tensor("S_" + name, list(shape), dt))


def s5_alloc(k):
    sb = sbs
    k.winj = [sb(k, f"winj{i}", [128, 8, 128]) for i in range(2)]
    k.rout = [sb(k, f"rout{i}", [128, 8, 128]) for i in range(2)]
    k.pw = [sb(k, f"pw{i}", [128, 32, 17]) for i in range(3)]
    k.lam = [sb(k, f"lam{i}", [128, 32, 8]) for i in range(3)]
    k.s5vT = sb(k, "s5vT", [128, 8])
    k.wglu = sb(k, "wglu", [128, 4, 512])
    k.s5_d = Dep()


def s5_setup(k):
    nc, S, I = k.nc, k.S, k.I
    T = S.op
    with ExitStack() as es:
        def sbl(name, shape, dt=F32):
            return es.enter_context(nc.sbuf_tensor(name, list(shape), dt))
        d0 = Dep()
        rows = sbl("s5rows", [32, 3, 128])
        S.dma("sp", lambda e: e.dma_start(out=rows[:], in_=I["s5_arow"].rearrange("a r c -> r a c")), writes=[d0])
        vrow = sbl("s5vrow", [8, 128])
        S.dma("sp", lambda e: e.dma_start(out=vrow[:], in_=I["s5_vec"][:, :]), writes=[d0])
        S.dma("sp", lambda e: e.dma_start(out=k.wglu[:], in_=I["s5_w_glu"].rearrange("(kc p) n -> p kc n", p=128)),
              writes=[k.s5_d])
        bT = [sbl(f"s5bT{i}", [128, 32, 32]) for i in range(2)]
        cblk = [sbl(f"s5cb{i}", [128, 8, 128]) for i in range(2)]
        for i in range(2):
            S.dma("sp", lambda e, i=i: e.dma_start(out=bT[i][:], in_=I["s5_bT"][i].rearrange("p (a b) -> p a b", b=32)), writes=[d0])
            S.dma("act", lambda e, i=i: e.dma_start(out=cblk[i][:], in_=I["s5_cblk"][i]), writes=[d0])
        aT = sbl("s5aT", [128, 3, 32])
        W = [sbl(f"s5w{i}", [128, 32]) for i in range(14)]
        Wi = sbl("s5wi", [128, 32], I32)
        bb = [sbl(f"s5bb{i}", [128, 32, 32]) for i in range(2)]
        tmp = [sbl(f"s5tmp{i}", [128, 32, 32]) for i in range(2)]
        with nc.psum_tensor("ps5a", [128, 3, 32], F32) as pa, nc.psum_tensor("ps5b", [128, 8], F32) as pb, \
                nc.psum_tensor("ps5c", [128, 4, 128], F32) as pc:
            pd = PD()
            for a in range(3):
                T("pe", lambda e, a=a: e.transpose(pa[:, a, :], rows[:, a, :], k.ident[0:32, 0:32]), reads=[d0, k.ident_d], writes=[pd])
            T("dve", lambda e: e.tensor_copy(aT[:], pa[:]), reads=[pd], writes=[d0])
            T("pe", lambda e: e.transpose(pb[:, :], vrow[:, :], k.ident[0:8, 0:8]), reads=[d0, k.ident_d], writes=[pd])
            T("dve", lambda e: e.tensor_copy(k.s5vT[:], pb[:]), reads=[pd], writes=[k.s5_d])
            are, aim, ldt = aT[:, 0, :], aT[:, 1, :], aT[:, 2, :]
            dt, mag, ang, sn, cs, abr, abi, t1, t2, nr, cfr, cfi, rden, t3 = [w[:] for w in W]

            def tt(o, a, b, op):
                T("dve", lambda e: e.tensor_tensor(o, a, b, op=op), reads=[d0], writes=[d0])

            T("act", lambda e: e.activation(dt, ldt, AF.Exp), reads=[d0], writes=[d0])
            tt(t1, dt, are, ALU.mult)
            T("act", lambda e: e.activation(mag, t1, AF.Exp), reads=[d0], writes=[d0])
            tt(ang, dt, aim, ALU.mult)

            def rsin(o, phase):
                T("dve", lambda e: e.tensor_scalar(t1, ang, 1.0 / (2 * np.pi), phase, op0=ALU.mult, op1=ALU.add), reads=[d0], writes=[d0])
                T("dve", lambda e: e.tensor_copy(Wi[:], t1), reads=[d0], writes=[d0])
                T("dve", lambda e: e.tensor_copy(t2, Wi[:]), reads=[d0], writes=[d0])
                tt(t1, t1, t2, ALU.subtract)
                T("dve", lambda e: e.scalar_tensor_tensor(t2, t1, 0.0, t1, op0=ALU.is_lt, op1=ALU.add), reads=[d0], writes=[d0])
                T("dve", lambda e: e.tensor_scalar(t2, t2, 2 * np.pi, -np.pi, op0=ALU.mult, op1=ALU.add), reads=[d0], writes=[d0])
                T("dve", lambda e: e.tensor_scalar(t2, t2, 3.1415925, -3.1415925, op0=ALU.min, op1=ALU.max), reads=[d0], writes=[d0])
                T("act", lambda e: e.activation(o, t2, AF.Sin), reads=[d0], writes=[d0])

            rsin(sn, 0.5)
            rsin(cs, 0.75)
            tt(abr, mag, cs, ALU.mult)
            tt(abi, mag, sn, ALU.mult)
            T("dve", lambda e: e.tensor_scalar(nr, abr, -1.0, None, op0=ALU.add), reads=[d0], writes=[d0])
            tt(t1, are, are, ALU.mult)
            tt(t2, aim, aim, ALU.mult)
            tt(t1, t1, t2, ALU.add)
            T("dve", lambda e: e.reciprocal(rden, t1), reads=[d0], writes=[d0])
            tt(t1, nr, are, ALU.mult)
            tt(t2, abi, aim, ALU.mult)
            tt(t1, t1, t2, ALU.add)
            tt(cfr, t1, rden, ALU.mult)
            tt(t1, abi, are, ALU.mult)
            tt(t2, nr, aim, ALU.mult)
            tt(t1, t1, t2, ALU.subtract)
            tt(cfi, t1, rden, ALU.mult)
            cfrb = cfr.unsqueeze(2).to_broadcast([128, 32, 32])
            cfib = cfi.unsqueeze(2).to_broadcast([128, 32, 32])
            tt(tmp[0][:], bT[0][:], cfrb, ALU.mult)
            tt(tmp[1][:], bT[1][:], cfib, ALU.mult)
            tt(bb[0][:], tmp[0][:], tmp[1][:], ALU.subtract)
            tt(tmp[0][:], bT[1][:], cfrb, ALU.mult)
            tt(tmp[1][:], bT[0][:], cfib, ALU.mult)
            tt(bb[1][:], tmp[0][:], tmp[1][:], ALU.add)
            for ri in range(2):
                for half in range(2):
                    for cc in range(4):
                        dc = half * 4 + cc
                        T("pe", lambda e, ri=ri, dc=dc, cc=cc: e.transpose(
                            pc[:, cc, :], bb[ri][:, dc * 4:(dc + 1) * 4, :].rearrange("p a b -> p (a b)"), k.ident[:]),
                            reads=[d0, k.ident_d], writes=[pd])
                    T("dve", lambda e, ri=ri, half=half: e.tensor_copy(k.winj[ri][:, half * 4:(half + 1) * 4, :], pc[:]),
                      reads=[pd], writes=[k.s5_d])
            for ri in range(2):
                for half in range(2):
                    for cc in range(4):
                        dc = half * 4 + cc
                        T("pe", lambda e, ri=ri, dc=dc, cc=cc: e.transpose(pc[:, cc, :], cblk[ri][:, dc, :], k.ident[:]),
                          reads=[d0, k.ident_d], writes=[pd])
                    if ri == 0:
                        T("dve", lambda e, half=half: e.tensor_copy(k.rout[0][:, half * 4:(half + 1) * 4, :], pc[:]),
                          reads=[pd], writes=[k.s5_d])
                    else:
                        T("dve", lambda e, half=half: e.tensor_scalar(k.rout[1][:, half * 4:(half + 1) * 4, :], pc[:], -1.0, None,
                                                                      op0=ALU.mult), reads=[pd], writes=[k.s5_d])
            pr, pi_, pn = k.pw
            T("dve", lambda e: e.memset(pr[:, :, 0:1], 1.0), writes=[k.s5_d])
            T("dve", lambda e: e.memset(pi_[:, :, 0:1], 0.0), writes=[k.s5_d])

            def cmul(o_r, o_i, a_r, a_i, b_r, b_i, dep):
                T("dve", lambda e: e.tensor_tensor(t1, a_r, b_r, op=ALU.mult), reads=[dep, d0], writes=[d0])
                T("dve", lambda e: e.tensor_tensor(t2, a_i, b_i, op=ALU.mult), reads=[dep, d0], writes=[d0])
                T("dve", lambda e: e.tensor_tensor(t3, a_r, b_i, op=ALU.mult), reads=[dep, d0], writes=[d0])
                T("dve", lambda e: e.tensor_tensor(rden, a_i, b_r, op=ALU.mult), reads=[dep, d0], writes=[d0])
                T("dve", lambda e: e.tensor_tensor(o_r, t1, t2, op=ALU.subtract), reads=[d0], writes=[dep])
                T("dve", lambda e: e.tensor_tensor(o_i, t3, rden, op=ALU.add), reads=[d0], writes=[dep])

            for n in range(1, 17):
                cmul(pr[:, :, n], pi_[:, :, n], pr[:, :, n - 1], pi_[:, :, n - 1], abr, abi, k.s5_d)
            T("dve", lambda e: e.tensor_scalar(pn[:], pi_[:], -1.0, None, op0=ALU.mult), reads=[k.s5_d], writes=[k.s5_d])
            lr, li, ln = k.lam
            T("dve", lambda e: e.tensor_copy(lr[:, :, 0], pr[:, :, 16]), reads=[k.s5_d], writes=[k.s5_d])
            T("dve", lambda e: e.tensor_copy(li[:, :, 0], pi_[:, :, 16]), reads=[k.s5_d], writes=[k.s5_d])
            for n in range(1, 8):
                cmul(lr[:, :, n], li[:, :, n], lr[:, :, n - 1], li[:, :, n - 1], lr[:, :, n - 1], li[:, :, n - 1], k.s5_d)
            T("dve", lambda e: e.tensor_scalar(ln[:], li[:], -1.0, None, op0=ALU.mult), reads=[k.s5_d], writes=[k.s5_d])
        S.barrier()


def stage2_s5(k):
    nc, S, I, NB = k.nc, k.S, k.I, k.NB
    T = S.op
    NSC = LT // 16
    with ExitStack() as es:
        def sbl(name, shape, dt=F32):
            return es.enter_context(nc.sbuf_tensor(name, list(shape), dt))
        uT = [sbl(f"s5u{i}", [128, LT]) for i in range(2)]
        uT_d = [Dep(), Dep()]
        Z = [[[sbl(f"s5z{jj}{d}{ri}", [128, LT]) for ri in range(2)] for d in range(2)] for jj in range(2)]
        Z_d = [[Dep(), Dep()] for jj in range(2)]
        Bp = [[[[sbl(f"s5B{jj}{d}{pp}{ri}", [128, NSC]) for ri in range(2)] for pp in range(2)] for d in range(2)] for jj in range(2)]
        B_d = [[[Dep(), Dep()] for d in range(2)] for jj in range(2)]
        y1 = sbl("s5y1", [128, 4, SEQ])
        y1_d = Dep()
        og = [sbl(f"s5og{i}", [128, 512]) for i in range(2)]
        og_d = [Dep(), Dep()]
        pin = [es.enter_context(nc.psum_tensor(f"s5pin{i}", [128, 512], F32)) for i in range(2)]
        pin_d = [PD(), PD()]
        py = es.enter_context(nc.psum_tensor("s5py", [128, 4, 512], F32))
        py_d = PD()
        pg = [es.enter_context(nc.psum_tensor(f"s5pg{i}", [128, 512], F32)) for i in range(2)]
        pg_d = [PD(), PD()]
        blocks = [(0, 256)] + [(256 + i * 512, 512) for i in range(4)]
        ipi = [0]

        def s5_chain(j, d, jj):
            q = cur_c[0] * 4 + j
            c = cur_c[0]
            u, ud = cur_u[0], cur_ud[0]
            dq = d * 16 + q
            zr, zi = Z[jj][d]
            zd = Z_d[jj][d]
            Bp_ = Bp[jj][d]
            B_d_ = B_d[jj][d]
            for (c0, n) in blocks:
                if d == 0:
                    z0 = c0
                else:
                    z0 = (c0 - CTX) if c0 >= CTX else SEQ
                for ri in range(2):
                    pp, ppd = pin[ipi[0] % 2], pin_d[ipi[0] % 2]
                    ipi[0] += 1
                    T("pe", _mm(pp[:, 0:n], k.winj[ri][32 * j:32 * j + 32, d * 4 + c, :], u[32 * j:32 * j + 32, c0:c0 + n], tp=(32 * j, 0)),
                      reads=[k.s5_d, ud], writes=[ppd])
                    T("act", lambda e, pp=pp, n=n, z0=z0, ri=ri: e.copy(Z[jj][d][ri][:, z0:z0 + n], pp[:, 0:n]),
                      reads=[ppd], writes=[zd])
                yield
            zrv = zr[:].rearrange("p (j r) -> p j r", r=16)
            ziv = zi[:].rearrange("p (j r) -> p j r", r=16)
            ar = k.pw[0][:, dq, 1:2]
            ai = k.pw[1][:, dq, 1:2]
            nai = k.pw[2][:, dq, 1:2]

            def stt(o, a, sc, bb_, rd, wr):
                T("dve", lambda e: e.scalar_tensor_tensor(o, a, sc, bb_, op0=ALU.mult, op1=ALU.add), reads=rd, writes=wr)

            order = range(1, 16) if d == 0 else range(14, -1, -1)
            for r in order:
                rp = r - 1 if d == 0 else r + 1
                stt(zrv[:, :, r], zrv[:, :, rp], ar, zrv[:, :, r], [zd, k.s5_d], [zd])
                yield
                stt(zrv[:, :, r], ziv[:, :, rp], nai, zrv[:, :, r], [zd, k.s5_d], [zd])
                yield
                stt(ziv[:, :, r], ziv[:, :, rp], ar, ziv[:, :, r], [zd, k.s5_d], [zd])
                yield
                stt(ziv[:, :, r], zrv[:, :, rp], ai, ziv[:, :, r], [zd, k.s5_d], [zd])
                yield
            rb = 15 if d == 0 else 0
            cur, nxt = 0, 1
            T("pool", lambda e: e.tensor_copy(Bp_[0][0][:], zrv[:, :, rb]), reads=[zd], writes=[B_d_[0]])
            T("pool", lambda e: e.tensor_copy(Bp_[0][1][:], ziv[:, :, rb]), reads=[zd], writes=[B_d_[0]])
            yield
            for lv in range(8):
                sh = 1 << lv
                lr_ = k.lam[0][:, dq, lv:lv + 1]
                li_ = k.lam[1][:, dq, lv:lv + 1]
                nli = k.lam[2][:, dq, lv:lv + 1]
                src, dst = Bp_[cur], Bp_[nxt]
                sd, dd = B_d_[cur], B_d_[nxt]
                if d == 0:
                    o_sl, i_sl, k_sl = slice(sh, NSC), slice(0, NSC - sh), slice(0, sh)
                else:
                    o_sl, i_sl, k_sl = slice(0, NSC - sh), slice(sh, NSC), slice(NSC - sh, NSC)
                T("pool", lambda e: e.tensor_copy(dst[0][:, k_sl], src[0][:, k_sl]), reads=[sd], writes=[dd])
                T("pool", lambda e: e.tensor_copy(dst[1][:, k_sl], src[1][:, k_sl]), reads=[sd], writes=[dd])
                stt(dst[0][:, o_sl], src[0][:, i_sl], lr_, src[0][:, o_sl], [sd, k.s5_d], [dd])
                yield
                stt(dst[0][:, o_sl], src[1][:, i_sl], nli, dst[0][:, o_sl], [sd, k.s5_d], [dd])
                yield
                stt(dst[1][:, o_sl], src[1][:, i_sl], lr_, src[1][:, o_sl], [sd, k.s5_d], [dd])
                yield
                stt(dst[1][:, o_sl], src[0][:, i_sl], li_, dst[1][:, o_sl], [sd, k.s5_d], [dd])
                yield
                cur, nxt = nxt, cur
            Sf, Sfd = Bp_[cur], B_d_[cur]
            for r in range(16):
                n = (r + 1) if d == 0 else (16 - r)
                pr_ = k.pw[0][:, dq, n:n + 1]
                pi_ = k.pw[1][:, dq, n:n + 1]
                pni = k.pw[2][:, dq, n:n + 1]
                if d == 0:
                    zs, ss = slice(16, NSC), slice(15, NSC - 1)
                else:
                    zs, ss = slice(0, 128), slice(1, 129)
                stt(zrv[:, zs, r], Sf[0][:, ss], pr_, zrv[:, zs, r], [zd, Sfd, k.s5_d], [zd])
                yield
                stt(zrv[:, zs, r], Sf[1][:, ss], pni, zrv[:, zs, r], [zd, Sfd, k.s5_d], [zd])
                yield
                stt(ziv[:, zs, r], Sf[1][:, ss], pr_, ziv[:, zs, r], [zd, Sfd, k.s5_d], [zd])
                yield
                stt(ziv[:, zs, r], Sf[0][:, ss], pi_, ziv[:, zs, r], [zd, Sfd, k.s5_d], [zd])
                yield
        cur_c, cur_u, cur_ud = [0], [None], [None]
        for b in range(NB):
            for c in range(4):
                u, ud = uT[c % 2], uT_d[c % 2]
                cur_c[0], cur_u[0], cur_ud[0] = c, u, ud
                S.dma("sp", lambda e, u=u, b=b, c=c: e.dma_start(out=u[:], in_=k.PT[b, c * 128:(c + 1) * 128, :]),
                      reads=[k.PT_dep[b]], writes=[ud])
                for jp in range(2):
                    chains = []
                    for jj in range(2):
                        for d in range(2):
                            chains.append(s5_chain(jp * 2 + jj, d, jj))
                    while chains:
                        for g_ in list(chains):
                            try:
                                next(g_)
                            except StopIteration:
                                chains.remove(g_)
                    for jj in range(2):
                        j = jp * 2 + jj
                        for tb in range(4):
                            terms = []
                            for d in range(2):
                                l0 = (CTX if d == 0 else 0) + tb * 512
                                terms.append((k.rout[0][:, d * 4 + c, 32 * j:32 * j + 32], Z[jj][d][0][:, l0:l0 + 512], Z_d[jj][d]))
                                terms.append((k.rout[1][:, d * 4 + c, 32 * j:32 * j + 32], Z[jj][d][1][:, l0:l0 + 512], Z_d[jj][d]))
                            for ti, (lh, rh, dd_) in enumerate(terms):
                                T("pe", _mm(py[32 * j:32 * j + 32, tb, :], lh, rh, start=(ti == 0), stop=(ti == 3), tp=(0, 32 * j)),
                                  reads=[k.s5_d, dd_], writes=[py_d])
                for tb in range(4):
                    T("dve", lambda e, tb=tb, c=c, u=u: e.scalar_tensor_tensor(
                        y1[:, c, tb * 512:(tb + 1) * 512], u[:, CTX + tb * 512:CTX + (tb + 1) * 512], k.s5vT[:, c:c + 1], py[:, tb, :],
                        op0=ALU.mult, op1=ALU.add), reads=[py_d, ud, k.s5_d], writes=[y1_d])
                T("act", lambda e, c=c: e.activation(y1[:, c, :], y1[:, c, :], AF.Gelu), reads=[y1_d], writes=[y1_d])
            gi = 0
            for m in range(4):
                for tb in range(4):
                    p_, pd_ = pg[gi % 2], pg_d[gi % 2]
                    o_, od_ = og[gi % 2], og_d[gi % 2]
                    gi += 1
                    for kc in range(4):
                        T("pe", _mm(p_[:], k.wglu[:, kc, m * 128:(m + 1) * 128], y1[:, kc, tb * 512:(tb + 1) * 512],
                                    start=(kc == 0), stop=(kc == 3)), reads=[k.s5_d, y1_d], writes=[pd_])
                    T("act", lambda e, p_=p_, o_=o_, m=m: e.activation(o_[:], p_[:], AF.Sigmoid, bias=k.s5vT[:, 4 + m:5 + m]),
                      reads=[pd_, k.s5_d], writes=[od_])
                    T("dve", lambda e, o_=o_, m=m, tb=tb: e.tensor_tensor(o_[:], o_[:], y1[:, m, tb * 512:(tb + 1) * 512], op=ALU.mult),
                      reads=[od_, y1_d], writes=[od_])
                    S.dma("sp", lambda e, o_=o_, m=m, tb=tb, b=b: e.dma_start(
                        out=k.MIXT[b, m * 128:(m + 1) * 128, tb * 512:(tb + 1) * 512], in_=o_[:]),
                        reads=[od_], writes=[k.MIXT_dep[b]])
        S.barrier()


def _rw_layout(inputs):
    f = lambda a: np.ascontiguousarray(a, dtype=np.float32)
    vec = np.zeros((42, 128), np.float32)
    mu = inputs["rw_mu"][0]
    vec[0:12] = mu[0:1536].reshape(12, 128)
    vec[12, 0:64] = mu[1536:1600]
    vec[13, 0:96] = mu[1600:1696]
    vec[14:18] = inputs["rw_k_k"][0].reshape(4, 128)
    vec[18:22] = inputs["rw_k_a"][0].reshape(4, 128)
    vec[22:26] = inputs["rw_r_k"][0].reshape(4, 128)
    vec[26:34] = inputs["rw_w0"][0].reshape(8, 128)
    vec[34:42] = inputs["rw_a0"][0].reshape(8, 128)
    wl = np.zeros((64, 2, 512), np.float32)
    wl[0:32] = np.transpose(inputs["rw_w_w2"][0], (1, 0, 2))
    wl[32:64] = np.transpose(inputs["rw_w_a2"][0], (1, 0, 2))
    ln = np.stack([inputs["rw_ln_w"][0], inputs["rw_ln_b"][0]], 0)
    return {"rw_vec": vec, "rw_wlora": f(wl), "rw_w_g2": f(inputs["rw_w_g2"][0]), "rw_ln": f(ln)}


def rw_setup(k):
    nc, S, I = k.nc, k.S, k.I
    T = S.op
    k.rwc = Dep()
    k.rvT = sbs(k, "rvT", [128, 42])
    k.omm = sbs(k, "rw_omm", [128, 14])
    k.muq = sbs(k, "rw_muq", [128, 14, 4])
    k.mue = sbs(k, "rw_mue", [128, 14, 2])
    k.omka = sbs(k, "rw_omka", [128, 4])
    k.wlora = sbs(k, "rw_wlora", [64, 2, 512])
    k.wg2 = sbs(k, "rw_wg2", [96, 512])
    k.lnbc = sbs(k, "rw_lnbc", [128, 2, 512])
    k.bones = sbs(k, "rw_bones", [128, 128])
    k.hsel = sbs(k, "rw_hsel", [128, 2])
    k.mup = sbs(k, "rw_mup", [128, 256])
    k.mlo = sbs(k, "rw_mlo", [128, 256])
    k.ones = sbs(k, "rw_ones", [128, 128])
    with ExitStack() as es:
        def sbl(name, shape, dt=F32):
            return es.enter_context(nc.sbuf_tensor(name, list(shape), dt))
        d0 = Dep()
        vrow = sbl("rwvrow", [42, 128])
        lnrow = sbl("rwlnrow", [1, 2, 512])
        m4 = sbl("rwm4", [128, 4])
        pm4i = sbl("rwpm4i", [128, 1], I32)
        pm4 = sbl("rwpm4", [128, 1])
        one1 = sbl("rwone1", [1, 128])
        S.dma("sp", lambda e: e.dma_start(out=vrow[:], in_=I["rw_vec"][:, :]), writes=[d0])
        S.dma("sp", lambda e: e.dma_start(out=lnrow[:], in_=I["rw_ln"].rearrange("(o a) n -> o a n", o=1)), writes=[d0])
        S.dma("sp", lambda e: e.dma_start(out=k.wlora[:], in_=I["rw_wlora"]), writes=[k.rwc])
        S.dma("sp", lambda e: e.dma_start(out=k.wg2[:], in_=I["rw_w_g2"][:, :]), writes=[k.rwc])
        with nc.psum_tensor("prw0", [128, 42], F32) as p0, nc.psum_tensor("prw1", [128, 2, 512], F32) as p1:
            pd = PD()
            T("pe", lambda e: e.transpose(p0[:, :], vrow[:, :], k.ident[0:42, 0:42]), reads=[d0, k.ident_d], writes=[pd])
            T("dve", lambda e: e.tensor_copy(k.rvT[:], p0[:]), reads=[pd], writes=[k.rwc])
            T("dve", lambda e: e.memset(one1[:], 1.0), writes=[d0])
            for a in range(2):
                T("pe", _mm(p1[:, a, :], one1[0:1, :], lnrow[0:1, a, :]), reads=[d0], writes=[pd])
            T("dve", lambda e: e.tensor_copy(k.lnbc[:], p1[:]), reads=[pd], writes=[k.rwc])
        T("dve", lambda e: e.tensor_scalar(k.omm[:], k.rvT[:, 0:14], -1.0, 1.0, op0=ALU.mult, op1=ALU.add), reads=[k.rwc], writes=[k.rwc])
        T("dve", lambda e: e.tensor_scalar(k.omka[:], k.rvT[:, 18:22], -1.0, 1.0, op0=ALU.mult, op1=ALU.add), reads=[k.rwc], writes=[k.rwc])
        T("dve", lambda e: e.tensor_single_scalar(pm4i[:], k.iota_p[:], 3, op=ALU.bitwise_and), reads=[k.iota_d], writes=[d0])
        T("dve", lambda e: e.tensor_copy(pm4[:], pm4i[:]), reads=[d0], writes=[d0])
        T("dve", lambda e: e.tensor_scalar(m4[:], k.iota_ff[:, 0:4], pm4[:, 0:1], None, op0=ALU.is_equal), reads=[d0, k.ident_d], writes=[d0])
        T("dve", lambda e: e.tensor_tensor(k.muq[:], k.rvT[:, 0:14].unsqueeze(2).to_broadcast([128, 14, 4]),
                                           m4[:].unsqueeze(1).to_broadcast([128, 14, 4]), op=ALU.mult), reads=[d0, k.rwc], writes=[k.rwc])
        T("dve", lambda e: e.tensor_tensor(k.mue[:], k.muq[:, :, 0:2], k.muq[:, :, 2:4], op=ALU.add), reads=[k.rwc], writes=[k.rwc])
        T("dve", lambda e: e.memset(k.bones[:], 0.0), writes=[k.rwc])
        T("dve", lambda e: e.memset(k.bones[0:64, 0:64], 1.0), writes=[k.rwc])
        T("dve", lambda e: e.memset(k.bones[64:128, 64:128], 1.0), writes=[k.rwc])
        T("dve", lambda e: e.memset(k.hsel[:], 0.0), writes=[k.rwc])
        T("dve", lambda e: e.memset(k.hsel[0:64, 0:1], 1.0), writes=[k.rwc])
        T("dve", lambda e: e.memset(k.hsel[64:128, 1:2], 1.0), writes=[k.rwc])
        T("dve", lambda e: e.memset(k.ones[:], 1.0), writes=[k.rwc])
        for (tile_, c0, op) in [(k.mup, 0, ALU.is_gt), (k.mup, 128, ALU.is_ge), (k.mlo, 0, ALU.is_lt), (k.mlo, 128, ALU.is_le)]:
            T("dve", lambda e, tile_=tile_, c0=c0, op=op: e.tensor_scalar(tile_[:, c0:c0 + 128], k.iota_ff[:], k.iota_pf[:, 0:1], None, op0=op),
              reads=[k.ident_d], writes=[k.rwc])
    S.barrier()


def stage3_rwkv(k):
    nc, S, I, NB = k.nc, k.S, k.I, k.NB
    T = S.op
    NCH = LT // 128
    C = 128
    import os
    with ExitStack() as es:
        def sbl(name, shape, dt=F32):
            return es.enter_context(nc.sbuf_tensor(name, list(shape), dt))

        def psl(name, shape):
            return es.enter_context(nc.psum_tensor(name, list(shape), F32))
        WD = BF16 if os.environ.get("RW_BF16", "1") == "1" else F32
        ND = F32 if os.environ.get("RW_NEU32", "1") == "1" else WD
        lora = sbl("rw_lora", [128, LT]); lora_d = Dep()
        sg = sbl("rw_sg", [128, LT]); sg_d = Dep()
        zb = sbl("rw_zb", [128, LT]); zb_d = Dep()
        rT = sbl("rw_r", [128, LT]); kT = sbl("rw_k", [128, LT]); kkT = sbl("rw_kk", [128, LT])
        base_d = Dep()
        Vm = sbl("rw_Vm", [128, NCH, 128]); Vm_d = Dep()
        tA = sbl("rw_tA", [128, LT]); tB = sbl("rw_tB", [128, LT]); tC = sbl("rw_tC", [128, LT]); tD = sbl("rw_tD", [128, LT])
        tmp_d = Dep()
        ARt = sbl("rw_AR", [128, NCH, 2, C], WD); Bt = sbl("rw_Bt", [128, LT], WD); Kt = sbl("rw_Kt", [128, LT], WD)
        Vmb = sbl("rw_Vmb", [128, NCH, 128], WD); identb = sbl("rw_identb", [128, 128], WD); Tstb = sbl("rw_Tb", [128, 64], WD)
        T("dve", lambda e: e.tensor_copy(identb[:], k.ident[:]), reads=[k.ident_d], writes=[k.rwc])
        feat_d = Dep()
        PC = sbl("rw_PC", [128, NCH]); tot = sbl("rw_tot", [128, NCH])
        kdsum = sbl("rw_kdsum", [128, LT]); kds_d = Dep()
        Ysum = sbl("rw_Y", [128, 16, 128]); Y_d = Dep()
        gtm = sbl("rw_gtm", [128, 16, 128]); gtm_d = Dep()
        coef = sbl("rw_coef", [128, 16, 2]); coef_d = Dep()
        gn = [sbl(f"rw_gn{i}", [128, 32]) for i in range(4)]
        Tst = sbl("rw_T", [128, 64]); T_d = Dep()
        T_dh = [Dep(), Dep()]
        Y_dh = [Dep(), Dep()]
        Ttmp = sbl("rw_Ttmp", [128, 64])
        NN = [sbl(f"rw_N{i}", [128, 128], ND) for i in range(8)]; NN_d = [Dep() for _ in range(8)]
        NT_ = [sbl(f"rw_NT{i}", [128, 128], ND) for i in range(8)]; NT_d = [Dep() for _ in range(8)]
        XX = [sbl(f"rw_X{i}", [128, 128], ND) for i in range(8)]; XX_d = [Dep() for _ in range(8)]
        AA = [sbl(f"rw_AA{i}", [128, 512], WD) for i in range(4)]; AA_d = [Dep() for _ in range(4)]
        AN = [sbl(f"rw_AN{i}", [128, 128], ND) for i in range(4)]
        Wsb = [sbl(f"rw_W{i}", [128, 64], ND) for i in range(2)]; Wsb_d = [Dep(), Dep()]
        Usb = [sbl(f"rw_U{i}", [128, 64], WD) for i in range(2)]; Usb_d = [Dep(), Dep()]
        BKtm = [sbl(f"rw_BK{i}", [128, 2, 128], WD) for i in range(2)]; BK_d = [Dep(), Dep()]
        ot = [sbl(f"rw_ot{i}", [128, 512]) for i in range(2)]; ot_d = [Dep(), Dep()]
        pA = [psl(f"rw_pA{i}", [128, 512]) for i in range(2)]; pA_d = [PD(), PD()]
        pN = [psl(f"rw_pN{i}", [128, 512]) for i in range(4)]; pN_d = [PD() for _ in range(4)]
        pS = [psl(f"rw_pS{i}", [128, 512]) for i in range(2)]; pS_d = [PD(), PD()]
        cnt = {"pn": 0, "pa": 0, "nn": 0, "nt": 0, "xx": 0, "hc": 0}
        cast_gen = None
        if k.upto >= 4 and os.environ.get("CAST_EARLY", "1") == "1":
            cvf = [sbl("rw_cvf0", [128, 2048])] * 2
            cvb = [sbl("rw_cvb0", [128, 2048], BF16)] * 2
            cvf_d = [Dep()] * 2
            cvb_d = [Dep()] * 2

            def _cast_gen():
                for i in range(128):
                    f_, fdd, b_, bdd = cvf[i % 2], cvf_d[i % 2], cvb[i % 2], cvb_d[i % 2]
                    S.dma("sp", lambda e, f_=f_, i=i: e.dma_start(out=f_[:], in_=I["peer_uv"][i * 128:(i + 1) * 128, :]), writes=[fdd])
                    T("pool", lambda e, f_=f_, b_=b_: e.tensor_copy(b_[:], f_[:]), reads=[fdd], writes=[bdd])
                    S.dma("act", lambda e, b_=b_, i=i: e.dma_start(out=k.UVB[i * 128:(i + 1) * 128, :], in_=b_[:]), reads=[bdd], writes=[k.UVB_dep])
                    yield
                k.cast_done = True
            cast_gen = _cast_gen()

        def next_pn():
            i = cnt["pn"] % 4
            cnt["pn"] += 1
            return pN[i][:, 0:128], pN_d[i]

        def next_pa():
            i = cnt["pa"] % 2
            cnt["pa"] += 1
            return pA[i], pA_d[i]

        def mix_chunk(b, ch, nrows, dst, dst_d):
            r0 = 512 + (ch * 128 if ch < 12 else (1536 if ch == 12 else 1600))
            S.dma("sp", lambda e: e.dma_start(out=zb[0:nrows, :], in_=k.PT[b, r0:r0 + nrows, :]), reads=[k.PT_dep[b]], writes=[zb_d])
            P = slice(0, nrows)
            T("dve", lambda e: e.tensor_scalar(dst[P, :], zb[P, :], k.omm[P, ch:ch + 1], None, op0=ALU.mult),
              reads=[zb_d, k.rwc], writes=[dst_d])

            def acc(o, i_, sc):
                T("dve", lambda e: e.scalar_tensor_tensor(o, i_, sc, o, op0=ALU.mult, op1=ALU.add), reads=[zb_d, k.rwc, dst_d], writes=[dst_d])
            zl = zb[P, CTX:LT].rearrange("p (r c) -> p r c", c=64)
            dl = dst[P, CTX:LT].rearrange("p (r c) -> p r c", c=64)
            acc(dl[:, :, 1:64], zl[:, :, 0:63], k.muq[P, ch, 0:1])
            acc(dl[:, :, 0:63], zl[:, :, 1:64], k.muq[P, ch, 1:2])
            acc(dst[P, CTX + 64:LT], zb[P, CTX:LT - 64], k.muq[P, ch, 2:3])
            acc(dst[P, CTX:LT - 64], zb[P, CTX + 64:LT], k.muq[P, ch, 3:4])
            acc(dst[P, 1:CTX], zb[P, 0:CTX - 1], k.mue[P, ch, 0:1])
            acc(dst[P, 0:CTX - 1], zb[P, 1:CTX], k.mue[P, ch, 1:2])

        blocks = [(0, 512), (512, 512), (1024, 512), (1536, 512), (2048, 256)]
        import os
        STOP = int(os.environ.get("RW_STOP", "99"))
        for b in range(NB):
            mix_chunk(b, 12, 64, lora, lora_d)
            T("act", lambda e: e.activation(lora[0:32, :], lora[0:32, :], AF.Tanh), reads=[lora_d], writes=[lora_d])
            mix_chunk(b, 13, 96, sg, sg_d)
            T("act", lambda e: e.activation(sg[0:96, :], sg[0:96, :], AF.Sigmoid), reads=[sg_d], writes=[sg_d])
            if STOP <= 1:
                break
            for hp in range(4):
                mix_chunk(b, hp, 128, rT, base_d)
                mix_chunk(b, 4 + hp, 128, kT, base_d)
                mix_chunk(b, 8 + hp, 128, tA, tmp_d)
                for ci in range(NCH):
                    pp, ppd = next_pn()
                    T("pe", lambda e, pp=pp, ci=ci: e.transpose(pp, tA[:, ci * 128:(ci + 1) * 128], k.ident[:]),
                      reads=[tmp_d, k.ident_d], writes=[ppd])
                    T("act", lambda e, pp=pp, ci=ci: e.copy(Vm[:, ci, :], pp), reads=[ppd], writes=[Vm_d])
                    T("dve", lambda e, pp=pp, ci=ci: e.tensor_copy(Vmb[:, ci, :], pp), reads=[ppd], writes=[Vm_d])
                T("dve", lambda e: e.tensor_scalar(kkT[:], kT[:], k.rvT[:, 14 + hp:15 + hp], None, op0=ALU.mult), reads=[base_d, k.rwc], writes=[base_d])
                T("dve", lambda e: e.tensor_tensor(tB[:], kkT[:], kkT[:], op=ALU.mult), reads=[base_d], writes=[tmp_d])
                for (c0, n) in blocks:
                    pp, ppd = next_pa()
                    T("pe", _mm(pp[:, 0:n], k.bones[:], tB[:, c0:c0 + n]), reads=[tmp_d, k.rwc], writes=[ppd])
                    T("act", lambda e, pp=pp, c0=c0, n=n: e.activation(tC[:, c0:c0 + n], pp[:, 0:n], AF.Sqrt, bias=1e-12), reads=[ppd], writes=[tmp_d])
                T("dve", lambda e: e.reciprocal(tC[:], tC[:]), reads=[tmp_d], writes=[tmp_d])
                T("dve", lambda e: e.tensor_tensor(kkT[:], kkT[:], tC[:], op=ALU.mult), reads=[tmp_d, base_d], writes=[base_d])
                for lc in range(16):
                    pp, ppd = next_pn()
                    T("pe", _mm(pp, sg[0:96, CTX + lc * 128:CTX + (lc + 1) * 128], k.wg2[0:96, hp * 128:(hp + 1) * 128]),
                      reads=[sg_d, k.rwc], writes=[ppd])
                    T("act", lambda e, pp=pp, lc=lc: e.copy(gtm[:, lc, :], pp), reads=[ppd], writes=[gtm_d])
                if STOP <= 2:
                    break
                for d in range(2):
                    for (c0, n) in blocks:
                        pp, ppd = next_pa()
                        T("pe", _mm(pp[:, 0:n], k.wlora[0:32, d, hp * 128:(hp + 1) * 128], lora[0:32, c0:c0 + n]), reads=[lora_d, k.rwc], writes=[ppd])
                        T("act", lambda e, pp=pp, c0=c0, n=n, d=d, hp=hp: e.activation(
                            tA[:, c0:c0 + n], pp[:, 0:n], AF.Sigmoid, bias=k.rvT[:, 26 + d * 4 + hp:27 + d * 4 + hp]), reads=[ppd, k.rwc], writes=[tmp_d])
                        pp, ppd = next_pa()
                        T("pe", _mm(pp[:, 0:n], k.wlora[32:64, d, hp * 128:(hp + 1) * 128], lora[32:64, c0:c0 + n], tp=(32, 0)),
                          reads=[lora_d, k.rwc], writes=[ppd])
                        T("act", lambda e, pp=pp, c0=c0, n=n, d=d, hp=hp: e.activation(
                            tB[:, c0:c0 + n], pp[:, 0:n], AF.Sigmoid, bias=k.rvT[:, 34 + d * 4 + hp:35 + d * 4 + hp]), reads=[ppd, k.rwc], writes=[tmp_d])
                    T("dve", lambda e: e.tensor_scalar(tA[:], tA[:], -0.6065306597126334, None, op0=ALU.mult), reads=[tmp_d], writes=[tmp_d])
                    T("dve", lambda e: e.tensor_scalar(tC[:], tB[:], k.rvT[:, 18 + hp:19 + hp], k.omka[:, hp:hp + 1], op0=ALU.mult, op1=ALU.add),
                      reads=[tmp_d, k.rwc], writes=[tmp_d])
                    T("dve", lambda e: e.tensor_tensor(tC[:], tC[:], kT[:], op=ALU.mult), reads=[tmp_d, base_d], writes=[tmp_d])
                    if d == 0:
                        T("pool", lambda e: e.tensor_copy(kdsum[:], tC[:]), reads=[tmp_d], writes=[kds_d])
                    else:
                        T("pool", lambda e: e.tensor_tensor(kdsum[:], kdsum[:], tC[:], op=ALU.add), reads=[tmp_d, kds_d], writes=[kds_d])
                    for ci in range(NCH):
                        T("dve", lambda e, ci=ci: e.tensor_tensor_scan(tD[:, ci * C:(ci + 1) * C], k.ones[:], tA[:, ci * C:(ci + 1) * C], 0.0,
                                                                      op0=ALU.mult, op1=ALU.add), reads=[tmp_d, k.rwc], writes=[tmp_d])
                    tDv = tD[:].rearrange("p (c t) -> p c t", t=C)
                    T("dve", lambda e: e.tensor_copy(tot[:], tDv[:, :, C - 1]), reads=[tmp_d], writes=[feat_d])
                    if d == 1:
                        T("dve", lambda e: e.tensor_tensor(tD[:], tA[:], tD[:], op=ALU.subtract), reads=[tmp_d], writes=[tmp_d])
                        T("dve", lambda e: e.tensor_tensor(tDv, tDv, tot[:].unsqueeze(2).to_broadcast([128, NCH, C]), op=ALU.add),
                          reads=[tmp_d, feat_d], writes=[tmp_d])
                    T("act", lambda e: e.activation(PC[:], tot[:], AF.Exp), reads=[feat_d], writes=[feat_d])
                    ARv0 = ARt[:, :, 0, :]
                    ARv1 = ARt[:, :, 1, :]
                    tAv = tA[:].rearrange("p (c t) -> p c t", t=C)
                    T("dve", lambda e: e.tensor_tensor(tA[:], tD[:], tA[:], op=ALU.subtract), reads=[tmp_d], writes=[tmp_d])
                    T("act", lambda e: e.activation(tA[:], tA[:], AF.Exp), reads=[tmp_d], writes=[tmp_d])
                    T("dve", lambda e: e.scalar_tensor_tensor(ARv0, kkT[:].rearrange("p (c t) -> p c t", t=C), -1.0, tAv, op0=ALU.mult, op1=ALU.mult),
                      reads=[tmp_d, base_d], writes=[feat_d])
                    T("act", lambda e: e.activation(tA[:], tD[:], AF.Exp), reads=[tmp_d, feat_d], writes=[tmp_d])
                    T("dve", lambda e: e.tensor_tensor(ARv1, tAv, rT[:].rearrange("p (c t) -> p c t", t=C), op=ALU.mult),
                      reads=[tmp_d, base_d], writes=[feat_d])
                    T("act", lambda e: e.activation(tD[:], tD[:], AF.Exp, scale=-1.0), reads=[tmp_d], writes=[tmp_d])
                    T("dve", lambda e: e.tensor_tensor(tA[:], kkT[:], tB[:], op=ALU.mult), reads=[tmp_d, base_d, feat_d], writes=[tmp_d])
                    T("dve", lambda e: e.tensor_tensor(Bt[:], tA[:], tD[:], op=ALU.mult), reads=[tmp_d], writes=[feat_d])
                    T("dve", lambda e: e.tensor_tensor(Kt[:], tC[:], tD[:], op=ALU.mult), reads=[tmp_d], writes=[feat_d])
                    if STOP <= 3:
                        break
                    T("dve", lambda e: e.memset(Tst[:], 0.0), writes=[T_dh[0], T_dh[1]])
                    T("dve", lambda e: e.memset(Tstb[:], 0.0), writes=[T_dh[0], T_dh[1]])
                    order = list(range(NCH)) if d == 0 else [1, 0] + list(range(NCH - 1, 1, -1))
                    SUB = int(os.environ.get("RW_SUB", "99"))
                    order = order[:int(os.environ.get("RW_NCH", "99"))]
                    m2 = k.mup if d == 0 else k.mlo
                    mT = k.mlo if d == 0 else k.mup
                    Xfin = {}

                    def bk_chain(ci, par):
                        cs = slice(ci * C, (ci + 1) * C)
                        bk, bkd = BKtm[par], BK_d[par]
                        for which, src in enumerate((Bt, Kt)):
                            pp, ppd = next_pn()
                            T("pe", _mm(pp, src[:, cs], identb[:]), reads=[feat_d, k.rwc], writes=[ppd])
                            T("act", lambda e, pp=pp, bk=bk, which=which: e.copy(bk[:, which, :], pp), reads=[ppd], writes=[bkd])
                            yield

                    def neu_chain(hh, ci, par):
                        cs = slice(ci * C, (ci + 1) * C)
                        ph = slice(64 * hh, 64 * hh + 64)
                        tpk = (64 * hh, 0)
                        pa, pad = pA[hh], pA_d[hh]
                        ni = hh * 2 + par
                        aa, aad = AA[ni], AA_d[ni]
                        arr = ARt[ph, ci, :, :].rearrange("p a t -> p (a t)")
                        T("pe", _mm(pa[:, 0:256], Bt[ph, cs], arr, tp=tpk), reads=[feat_d], writes=[pad])
                        T("pe", _mm(pa[:, 256:512], Kt[ph, cs], arr, tp=tpk), reads=[feat_d], writes=[pad])
                        yield
                        T("dve", lambda e: e.tensor_tensor(
                            aa[:].rearrange("p (a t) -> p a t", a=2), pa[:].rearrange("p (a t) -> p a t", a=2),
                            m2[:].unsqueeze(1).to_broadcast([128, 2, 256]), op=ALU.mult), reads=[pad, k.rwc], writes=[aad])
                        an = AN[ni]
                        T("dve", lambda e: e.tensor_tensor(an[:], pa[:, 0:128], m2[:, 0:128], op=ALU.mult), reads=[pad, k.rwc], writes=[aad])
                        p3, p3d = next_pn()
                        T("pe", _mm(p3, ARt[ph, ci, 0, :], Bt[ph, cs], tp=tpk), reads=[feat_d], writes=[p3d])
                        yield
                        nt0, nt0d = NT_[2 * ni], NT_d[2 * ni]
                        T("dve", lambda e: e.tensor_tensor(nt0[:], p3, mT[:, 0:128], op=ALU.mult), reads=[p3d, k.rwc], writes=[nt0d])
                        x0, x0d = XX[2 * ni], XX_d[2 * ni]
                        T("pool", lambda e: e.tensor_tensor(x0[:], an[:], k.ident[:], op=ALU.add), reads=[aad, k.ident_d], writes=[x0d])
                        curN, curNd = an[:], aad
                        curNT, curNTd = nt0, nt0d
                        curX, curXd = x0, x0d
                        for lv in range(1, 7):
                            nxtNT, nxtNTd = NT_[2 * ni + (lv % 2)], NT_d[2 * ni + (lv % 2)]
                            pq, pqd = next_pn()
                            T("pe", _mm(pq, curN, curNT[:]), reads=[curNd, curNTd], writes=[pqd])
                            T("act", lambda e, pq=pq, nxtNT=nxtNT: e.copy(nxtNT[:], pq), reads=[pqd], writes=[nxtNTd])
                            yield
                            if lv < 6:
                                nxtN, nxtNd = NN[2 * ni + (lv % 2)], NN_d[2 * ni + (lv % 2)]
                                pq2, pq2d = next_pn()
                                T("pe", _mm(pq2, curNT[:], curN), reads=[curNd, curNTd], writes=[pq2d])
                                T("act", lambda e, pq2=pq2, nxtN=nxtN: e.copy(nxtN[:], pq2), reads=[pq2d], writes=[nxtNd])
                                yield
                            nxtX, nxtXd = XX[2 * ni + (lv % 2)], XX_d[2 * ni + (lv % 2)]
                            pq3, pq3d = next_pn()
                            T("pe", _mm(pq3, nxtNT[:], curX[:]), reads=[nxtNTd, curXd], writes=[pq3d])
                            T("dve", lambda e, pq3=pq3, nxtX=nxtX, curX=curX: e.tensor_tensor(nxtX[:], pq3, curX[:], op=ALU.add),
                              reads=[pq3d, curXd], writes=[nxtXd])
                            yield
                            if lv < 6:
                                curN, curNd = nxtN[:], nxtNd
                            curNT, curNTd = nxtNT, nxtNTd
                            curX, curXd = nxtX, nxtXd
                        Xfin[(hh, par)] = (curX, curXd)

                    def state_chain(hh, ci, par):
                        is_lat = ci >= 2
                        ph = slice(64 * hh, 64 * hh + 64)
                        tpk = (64 * hh, 0)
                        psb, psd = pS[hh], pS_d[hh]
                        Td = T_dh[hh]
                        ni = hh * 2 + par
                        aa, aad = AA[ni], AA_d[ni]
                        bk, bkd = BKtm[par], BK_d[par]
                        curX, curXd = Xfin[(hh, par)]
                        vh = Vmb[:, ci, ph]
                        T("pe", _mm(psb[:, 0:64], aa[:, 256:384], vh, start=True, stop=False), reads=[aad, Vm_d], writes=[psd])
                        T("pe", _mm(psb[:, 0:64], ARt[ph, ci, 0, :], Tstb[ph, :], start=False, stop=True, tp=tpk),
                          reads=[feat_d, Td], writes=[psd])
                        wsb, wsd = Wsb[hh], Wsb_d[hh]
                        T("act", lambda e: e.copy(wsb[:], psb[:, 0:64]), reads=[psd], writes=[wsd])
                        yield
                        T("pe", _mm(psb[:, 64:128], curX[:], wsb[:]), reads=[curXd, wsd], writes=[psd])
                        usb, usd = Usb[hh], Usb_d[hh]
                        T("act", lambda e: e.copy(usb[:], psb[:, 64:128]), reads=[psd], writes=[usd])
                        yield
                        if is_lat:
                            yo = psb[:, 128:192]
                            T("pe", _mm(yo, ARt[ph, ci, 1, :], Tstb[ph, :], start=True, stop=False, tp=tpk), reads=[feat_d, Td], writes=[psd])
                            T("pe", _mm(yo, aa[:, 128:256], usb[:], start=False, stop=False), reads=[aad, usd], writes=[psd])
                            T("pe", _mm(yo, aa[:, 384:512], vh, start=False, stop=True), reads=[aad, Vm_d], writes=[psd])
                        to = psb[ph, 192:256]
                        T("pe", _mm(to, bk[:, 0, ph], usb[:], start=True, stop=False, tp=(0, 64 * hh)), reads=[bkd, usd], writes=[psd])
                        T("pe", _mm(to, bk[:, 1, ph], vh, start=False, stop=True, tp=(0, 64 * hh)), reads=[bkd, Vm_d], writes=[psd])
                        yield
                        if is_lat:
                            lc = ci - 2
                            ys = Ysum[:, lc, ph]
                            if d == 0:
                                T("act", lambda e: e.copy(ys, psb[:, 128:192]), reads=[psd], writes=[Y_dh[hh]])
                            else:
                                T("dve", lambda e: e.tensor_tensor(ys, psb[:, 128:192], ys, op=ALU.add), reads=[psd, Y_dh[hh]], writes=[Y_dh[hh]])
                        T("dve", lambda e: e.tensor_tensor(Ttmp[ph, :], psb[ph, 192:256], Tst[ph, :], op=ALU.add), reads=[psd, Td], writes=[Td])
                        T("dve", lambda e: e.tensor_scalar(Tst[ph, :], Ttmp[ph, :], PC[ph, ci:ci + 1], None, op0=ALU.mult), reads=[Td, feat_d], writes=[Td])
                        T("act", lambda e: e.copy(Tstb[ph, :], Tst[ph, :]), reads=[Td], writes=[Td])
                        yield

                    def run_chains(chains):
                        while chains:
                            for g_ in list(chains):
                                try:
                                    next(g_)
                                except StopIteration:
                                    chains.remove(g_)

                    if order:
                        run_chains([bk_chain(order[0], 0), neu_chain(0, order[0], 0), neu_chain(1, order[0], 0)])
                    for idx, ci in enumerate(order):
                        par = idx % 2
                        if cast_gen is not None:
                            try:
                                next(cast_gen)
                            except StopIteration:
                                cast_gen = None
                        chains = [state_chain(0, ci, par), state_chain(1, ci, par)]
                        if idx + 1 < len(order):
                            nci = order[idx + 1]
                            chains += [bk_chain(nci, 1 - par), neu_chain(0, nci, 1 - par), neu_chain(1, nci, 1 - par)]
                        run_chains(chains)
                if STOP <= 4:
                    break
                T("dve", lambda e: e.tensor_tensor(kdsum[:], kdsum[:], rT[:], op=ALU.mult), reads=[kds_d, base_d], writes=[kds_d])
                T("dve", lambda e: e.tensor_scalar(kdsum[:], kdsum[:], k.rvT[:, 22 + hp:23 + hp], None, op0=ALU.mult), reads=[kds_d, k.rwc], writes=[kds_d])
                pp, ppd = next_pa()
                for lc in range(16):
                    T("pe", _mm(pp[:, 2 * lc:2 * lc + 2], kdsum[:, CTX + lc * 128:CTX + (lc + 1) * 128], k.hsel[:]), reads=[kds_d, k.rwc], writes=[ppd])
                T("dve", lambda e, pp=pp: e.tensor_copy(coef[:].rearrange("p a b -> p (a b)"), pp[:, 0:32]), reads=[ppd], writes=[coef_d])
                Yv = Ysum[:].rearrange("p c (h v) -> p (c h) v", v=64)
                ssum, ssq, mu_, rs_ = [g[:] for g in gn]
                GD = Dep()
                T("dve", lambda e: e.tensor_copy(gn[0][:, 0:1], gn[0][:, 0:1]), reads=[Y_dh[0], Y_dh[1]], writes=[Y_d, Y_dh[0], Y_dh[1]])
                T("dve", lambda e: e.tensor_reduce(ssum, Yv, axis=AX.X, op=ALU.add), reads=[Y_d], writes=[GD])
                tAv3 = tA[:, 0:2048].rearrange("p (c v) -> p c v", v=64)
                T("dve", lambda e: e.tensor_tensor(tAv3, Yv, Yv, op=ALU.mult), reads=[Y_d, tmp_d], writes=[tmp_d])
                T("dve", lambda e: e.tensor_reduce(ssq, tAv3, axis=AX.X, op=ALU.add), reads=[tmp_d], writes=[GD])
                T("dve", lambda e: e.tensor_scalar(mu_, ssum, 1.0 / 64, None, op0=ALU.mult), reads=[GD], writes=[GD])
                T("dve", lambda e: e.tensor_tensor(ssum, mu_, mu_, op=ALU.mult), reads=[GD], writes=[GD])
                T("dve", lambda e: e.scalar_tensor_tensor(ssq, ssq, 1.0 / 64, ssum, op0=ALU.mult, op1=ALU.subtract), reads=[GD], writes=[GD])
                T("act", lambda e: e.activation(ssq, ssq, AF.Sqrt, bias=64e-5), reads=[GD], writes=[GD])
                T("dve", lambda e: e.reciprocal(rs_, ssq), reads=[GD], writes=[GD])
                T("dve", lambda e: e.tensor_tensor(Yv, Yv, mu_.unsqueeze(2).to_broadcast([128, 32, 64]), op=ALU.subtract), reads=[GD, Y_d], writes=[Y_d])
                T("dve", lambda e: e.tensor_tensor(Yv, Yv, rs_.unsqueeze(2).to_broadcast([128, 32, 64]), op=ALU.mult), reads=[GD, Y_d], writes=[Y_d])
                lnw = k.lnbc[:, 0, hp * 128:(hp + 1) * 128].unsqueeze(1).to_broadcast([128, 16, 128])
                lnb = k.lnbc[:, 1, hp * 128:(hp + 1) * 128].unsqueeze(1).to_broadcast([128, 16, 128])
                T("dve", lambda e: e.tensor_tensor(Ysum[:], Ysum[:], lnw, op=ALU.mult), reads=[Y_d, k.rwc], writes=[Y_d])
                T("dve", lambda e: e.tensor_tensor(Ysum[:], Ysum[:], lnb, op=ALU.add), reads=[Y_d, k.rwc], writes=[Y_d])
                Vl = Vm[:, 2:18, :].rearrange("p c (h v) -> p (c h) v", v=64)
                T("dve", lambda e: e.tensor_tensor(tAv3, Vl, coef[:].rearrange("p a b -> p (a b)").unsqueeze(2).to_broadcast([128, 32, 64]), op=ALU.mult),
                  reads=[Vm_d, coef_d, tmp_d], writes=[tmp_d])
                T("dve", lambda e: e.tensor_tensor(Yv, Yv, tAv3, op=ALU.add), reads=[tmp_d, Y_d], writes=[Y_d])
                T("dve", lambda e: e.tensor_tensor(Ysum[:], Ysum[:], gtm[:], op=ALU.mult), reads=[Y_d, gtm_d], writes=[Y_d])
                for tb in range(4):
                    o_, od_ = ot[tb % 2], ot_d[tb % 2]
                    pp, ppd = next_pa()
                    for q in range(4):
                        T("pe", lambda e, pp=pp, q=q, tb=tb: e.transpose(pp[:, q * 128:(q + 1) * 128], Ysum[:, tb * 4 + q, :], k.ident[:]),
                          reads=[Y_d, k.ident_d], writes=[ppd])
                    T("act", lambda e, pp=pp, o_=o_: e.copy(o_[:], pp[:]), reads=[ppd], writes=[od_])
                    S.dma("sp", lambda e, o_=o_, tb=tb, b=b, hp=hp: e.dma_start(
                        out=k.MIXT[b, 512 + hp * 128:512 + (hp + 1) * 128, tb * 512:(tb + 1) * 512], in_=o_[:]),
                        reads=[od_], writes=[k.MIXT_dep[b]])
                T("dve", lambda e: e.tensor_copy(gn[0][:, 0:1], gn[0][:, 0:1]), writes=[Y_d, Y_dh[0], Y_dh[1]])
        S.barrier()


_PEER_CACHE = {}


def _peer_layout(inputs):
    f = lambda a: np.ascontiguousarray(a, dtype=np.float32)
    key = id(inputs["peer_u"])
    if key not in _PEER_CACHE:
        _PEER_CACHE.clear()
        _PEER_CACHE[key] = {
            "w_out": f(inputs["w_out"][0]),
            "peer_w_q": f(inputs["peer_w_q"][0]),
            "peer_keys": f(np.transpose(inputs["peer_keys"][0], (2, 0, 1, 3)).reshape(128, 16, 128)),
            "peer_uv": f(np.concatenate([inputs["peer_u"][0], inputs["peer_v"][0]], axis=1)),
            "nvec": f(np.stack([inputs["norm2_g"][0], inputs["norm_f_g"]], 0)),
        }
    return _PEER_CACHE[key]


def cast_uv(k):
    nc, S, I = k.nc, k.S, k.I
    T = S.op
    with ExitStack() as es:
        fb = [es.enter_context(nc.sbuf_tensor(f"cv_f{i}", [128, 4, 2048], F32)) for i in range(2)]
        bb = [es.enter_context(nc.sbuf_tensor(f"cv_b{i}", [128, 4, 2048], BF16)) for i in range(2)]
        fd = [Dep(), Dep()]
        bd = [Dep(), Dep()]
        for i in range(32):
            f_, fdd, b_, bdd = fb[i % 2], fd[i % 2], bb[i % 2], bd[i % 2]
            src = I["peer_uv"][i * 512:(i + 1) * 512, :].rearrange("(p r) n -> p r n", r=4)
            dst = k.UVB[i * 512:(i + 1) * 512, :].rearrange("(p r) n -> p r n", r=4)
            S.dma("sp", lambda e, f_=f_, src=src: e.dma_start(out=f_[:], in_=src), writes=[fdd])
            if i % 2 == 0:
                T("act", lambda e, f_=f_, b_=b_: e.copy(b_[:], f_[:]), reads=[fdd], writes=[bdd])
            else:
                T("dve", lambda e, f_=f_, b_=b_: e.tensor_copy(b_[:], f_[:]), reads=[fdd], writes=[bdd])
            S.dma("act", lambda e, b_=b_, dst=dst: e.dma_start(out=dst, in_=b_[:]), reads=[bdd], writes=[k.UVB_dep])
        S.barrier()


def stage4_peer(k):
    nc, S, I, NB = k.nc, k.S, k.I, k.NB
    T = S.op
    import os
    NT4 = int(os.environ.get("P4_TILES", "16"))
    NSLOT = int(os.environ.get("P4_SLOTS", "128"))
    with ExitStack() as es:
        def sbl(name, shape, dt=F32):
            return es.enter_context(nc.sbuf_tensor("P_" + name, list(shape), dt))

        def psl(name):
            return es.enter_context(nc.psum_tensor("PP_" + name, [128, 512], F32))
        cst = Dep()
        wq = sbl("wq", [128, 8, 2048], BF16)
        wo = sbl("wo", [128, 8, 1024], BF16)
        for kd in range(8):
            S.dma("pool", lambda e, kd=kd: e.dma_start(out=wq[:, kd, :], in_=I["peer_w_q"][kd * 128:(kd + 1) * 128, :], max_dma_last_dim=4096), writes=[cst])
            S.dma("pool", lambda e, kd=kd: e.dma_start(out=wo[:, kd, :], in_=I["w_out"][kd * 128:(kd + 1) * 128, :], max_dma_last_dim=4096), writes=[cst])
        keysT = sbl("keysT", [128, 16, 128])
        nbc = sbl("nbc", [128, 2, D])
        ones = sbl("ones", [128, 128])
        io16 = sbl("io16", [128, 16])
        bc = sbl("bc", [128, 4, D]); bc_d = Dep()
        xt = sbl("xt", [128, D]); xt_d = Dep()
        mx = sbl("mx", [128, 8, 128]); mx_d = Dep()
        mxb = sbl("mxb", [128, 8, 128], BF16); mxb_d = Dep()
        h1 = sbl("h1", [128, D]); h1_d = Dep()
        hb = sbl("hb", [128, D]); hb_d = Dep()
        junk = sbl("junk", [128, D]); junk_d = Dep()
        junkb = sbl("junkb", [128, D], BF16)
        hbb = sbl("hbb", [128, D], BF16); hbb_d = Dep()
        hbT = sbl("hbT", [128, 8, 128], BF16); hbT_d = Dep()
        qT = sbl("qT", [128, 16, 128]); qT_d = Dep()
        sc = sbl("sc", [128, 16, 128]); sc_d = Dep()
        sc2 = sbl("sc2", [128, 1, 128]); sc2_d = Dep()
        m16 = sbl("m16", [128, 16, 16]); i16 = sbl("i16", [128, 16, 16], U32); i16f = sbl("i16f", [128, 16, 16])
        cand = sbl("cand", [128, 8, 256]); cand2 = sbl("cand2", [128, 8, 256])
        best = sbl("best", [128, 8, 16]); pos = sbl("pos", [128, 8, 16], U32)
        pa_i = sbl("pa_i", [128, 8, 16], U32); pb_i = sbl("pb_i", [128, 8, 16], U32)
        pa_f = sbl("pa_f", [128, 8, 16]); pb_f = sbl("pb_f", [128, 8, 16])
        eq = cand2[:].rearrange("p h (a b) -> p h a b", b=16)
        i1s = sbl("i1s", [128, 8, 16]); i2s = sbl("i2s", [128, 8, 16])
        idxf = sbl("idxf", [128, 128]); idxi = sbl("idxi", [128, 128], I32)
        gate = sbl("gate", [128, 8, 16]); gsm = sbl("gsm", [128, 8, 2])
        tk_d = Dep()
        actr = sbl("actr", [128, 128]); act_d = Dep()
        agd = [Dep() for _ in range(64)]
        asd = [Dep() for _ in range(128)]
        wgt = sbl("wgt", [128, 128])
        st = sbl("st", [128, 8]); st_d = Dep()
        SPLIT = os.environ.get("P4_SPLIT", "0") == "1"
        NG = int(os.environ.get("P4_NG", "4" if SPLIT else "5"))
        if SPLIT:
            prod = [sbl(f"prod{i}", [128, 1024]) for i in range(2)]; prod_d = [Dep(), Dep()]
        dg = [sbl(f"dg{i}", [128, 128]) for i in range(4)]; dg_d = [Dep() for _ in range(4)]
        dgb = [sbl(f"dgb{i}", [128, 128], BF16) for i in range(4)]; dgb_d = [Dep() for _ in range(4)]
        oo = sbl("oo", [128, D]); oo_d = Dep()
        pb_ = [psl(f"b{i}") for i in range(8)]; pb_d = [PD() for _ in range(8)]
        d0 = Dep()
        with ExitStack() as es2:
            krow = es2.enter_context(nc.sbuf_tensor("P_krow", [128, 16, 128], F32))
            nrow = es2.enter_context(nc.sbuf_tensor("P_nrow", [1, 2, D], F32))
            one1 = es2.enter_context(nc.sbuf_tensor("P_one1", [1, 128], F32))
            S.dma("sp", lambda e: e.dma_start(out=krow[:], in_=I["peer_keys"]), writes=[d0])
            S.dma("sp", lambda e: e.dma_start(out=nrow[:], in_=I["nvec"].rearrange("(o a) n -> o a n", o=1)), writes=[d0])
            T("dve", lambda e: e.memset(one1[:], 1.0), writes=[d0])
            T("dve", lambda e: e.memset(ones[:], 1.0), writes=[cst])
            T("dve", lambda e: e.tensor_copy(io16[:], k.iota_ff[:, 0:16]), reads=[k.ident_d], writes=[cst])
            for j in range(16):
                T("pe", lambda e, j=j: e.transpose(pb_[j % 4][:, 0:128], krow[:, j, :], k.ident[:]), reads=[d0, k.ident_d], writes=[pb_d[j % 4]])
                T("act", lambda e, j=j: e.copy(keysT[:, j, :], pb_[j % 4][:, 0:128]), reads=[pb_d[j % 4]], writes=[cst])
            for a in range(2):
                for hf in range(2):
                    T("pe", _mm(pb_[4 + hf][:, :], one1[0:1, :], nrow[0:1, a, hf * 512:(hf + 1) * 512]), reads=[d0], writes=[pb_d[4 + hf]])
                    T("act", lambda e, a=a, hf=hf: e.copy(nbc[:, a, hf * 512:(hf + 1) * 512], pb_[4 + hf][:, :]), reads=[pb_d[4 + hf]], writes=[cst])
            S.barrier()
        gb = [sbl(f"gb{i}", [128, 2, 2048], BF16) for i in range(NG)]; gb_d = [[Dep(), Dep()] for _ in range(NG)]
        NC5 = NB + 1
        gcnt = 0
        dcnt = 0
        for b in range(NB):
            for vi, j0 in enumerate([16, 32, 24, 40]):
                for jj in range(8):
                    dgt, dgd = dg[dcnt % 4], dg_d[dcnt % 4]
                    pp, ppd = pb_[dcnt % 4], pb_d[dcnt % 4]
                    dcnt += 1
                    T("dve", lambda e, dgt=dgt, j0=j0, jj=jj, b=b: e.tensor_scalar(dgt[:], k.ident[:], k.modT[:, j0 + jj, b:b + 1], None, op0=ALU.mult),
                      reads=[k.ident_d, k.modT_d], writes=[dgd])
                    T("pe", _mm(pp[:, 0:128], ones[:], dgt[:]), reads=[cst, dgd], writes=[ppd])
                    T("act", lambda e, pp=pp, vi=vi, jj=jj: e.copy(bc[:, vi, jj * 128:(jj + 1) * 128], pp[:, 0:128]), reads=[ppd], writes=[bc_d])
            T("dve", lambda e: e.tensor_scalar(bc[:, 1, :], bc[:, 1, :], 1.0, None, op0=ALU.add), reads=[bc_d], writes=[bc_d])
            T("dve", lambda e: e.tensor_tensor(bc[:, 1, :], bc[:, 1, :], nbc[:, 0, :], op=ALU.mult), reads=[bc_d, cst], writes=[bc_d])
            for tt in range(NT4):
                t0 = tt * 128
                S.dma("sp", lambda e, b=b, t0=t0: e.dma_start(out=xt[:], in_=I["x"][b, t0:t0 + 128, :]), writes=[xt_d])
                S.dma("act", lambda e, b=b, t0=t0: e.dma_start(out=mx[:], in_=k.MIXT[b].rearrange("(kc p) t -> p kc t", p=128)[:, :, t0:t0 + 128]),
                      reads=[k.MIXT_dep[b]], writes=[mx_d])
                T("act", lambda e: e.copy(mxb[:], mx[:]), reads=[mx_d], writes=[mxb_d])
                for hf in range(2):
                    for kc in range(8):
                        T("pe", _mm(pb_[hf][:, :], mxb[:, kc, :], wo[:, kc, hf * 512:(hf + 1) * 512], start=(kc == 0), stop=(kc == 7)),
                          reads=[mxb_d, cst], writes=[pb_d[hf]])
                    hs = slice(hf * 512, (hf + 1) * 512)
                    T("dve", lambda e, hf=hf, hs=hs: e.tensor_tensor(h1[:, hs], pb_[hf][:, :], bc[:, 0, hs], op=ALU.mult), reads=[pb_d[hf], bc_d], writes=[h1_d])
                    T("dve", lambda e, hs=hs: e.tensor_tensor(h1[:, hs], h1[:, hs], xt[:, hs], op=ALU.add), reads=[xt_d, h1_d], writes=[h1_d])
                T("act", lambda e: e.activation(junk[:], h1[:], AF.Square, accum_out=st[:, 0:1]), reads=[h1_d], writes=[junk_d, st_d])
                T("act", lambda e: e.activation(st[:, 1:2], st[:, 0:1], AF.Sqrt, bias=1e-6, scale=1.0 / D), reads=[st_d], writes=[st_d])
                T("dve", lambda e: e.reciprocal(st[:, 2:3], st[:, 1:2]), reads=[st_d], writes=[st_d])
                T("act", lambda e: e.activation(hb[:], h1[:], AF.Copy, scale=st[:, 2:3]), reads=[h1_d, st_d], writes=[hb_d])
                T("dve", lambda e: e.tensor_tensor(hb[:], hb[:], bc[:, 1, :], op=ALU.mult), reads=[hb_d, bc_d], writes=[hb_d])
                T("dve", lambda e: e.tensor_tensor(hb[:], hb[:], bc[:, 2, :], op=ALU.add), reads=[hb_d, bc_d], writes=[hb_d])
                T("act", lambda e: e.copy(hbb[:], hb[:]), reads=[hb_d], writes=[hbb_d])
                for kd in range(8):
                    bk_ = 2 + kd // 4
                    T("pe", lambda e, kd=kd, bk_=bk_: e.transpose(pb_[bk_][:, (kd % 4) * 128:(kd % 4 + 1) * 128], hb[:, kd * 128:(kd + 1) * 128], k.ident[:]),
                      reads=[hb_d, k.ident_d], writes=[pb_d[bk_]])
                for q in range(2):
                    T("act", lambda e, q=q: e.copy(hbT[:, q * 4:(q + 1) * 4, :].rearrange("p a t -> p (a t)"), pb_[2 + q][:, :]), reads=[pb_d[2 + q]], writes=[hbT_d])
                for j in range(16):
                    pp, ppd = pb_[4 + j % 2], pb_d[4 + j % 2]
                    for kd in range(8):
                        T("pe", _mm(pp[:, 0:128], wq[:, kd, j * 128:(j + 1) * 128], hbT[:, kd, :], start=(kd == 0), stop=(kd == 7)),
                          reads=[cst, hbT_d], writes=[ppd])
                    T("act", lambda e, pp=pp, j=j: e.copy(qT[:, j, :], pp[:, 0:128]), reads=[ppd], writes=[qT_d])
                for j in range(16):
                    bk_ = j // 4
                    T("pe", _mm(pb_[bk_][:, (j % 4) * 128:(j % 4 + 1) * 128], qT[:, j, :], keysT[:, j, :]), reads=[qT_d, cst], writes=[pb_d[bk_]])
                for q in range(4):
                    T("act", lambda e, q=q: e.copy(sc[:, q * 4:(q + 1) * 4, :].rearrange("p a n -> p (a n)"), pb_[q][:, :]), reads=[pb_d[q]], writes=[sc_d])
                for j in range(16):
                    T("dve", lambda e, j=j: e.max(m16[:, j, 0:8], sc[:, j, :]), reads=[sc_d], writes=[tk_d])
                    T("dve", lambda e, j=j: e.match_replace(sc2[:, 0, :], m16[:, j, 0:8], sc[:, j, :], -3.0e38), reads=[sc_d, tk_d], writes=[sc2_d])
                    T("dve", lambda e, j=j: e.max(m16[:, j, 8:16], sc2[:, 0, :]), reads=[sc2_d], writes=[tk_d])
                    T("dve", lambda e, j=j: e.max_index(i16[:, j, 0:8], m16[:, j, 0:8], sc[:, j, :]), reads=[sc_d, tk_d], writes=[tk_d])
                    T("dve", lambda e, j=j: e.max_index(i16[:, j, 8:16], m16[:, j, 8:16], sc2[:, 0, :]), reads=[sc2_d, tk_d], writes=[tk_d])
                T("dve", lambda e: e.tensor_copy(i16f[:], i16[:]), reads=[tk_d], writes=[tk_d])
                m16v = m16[:].rearrange("p (h c) a -> p h c a", c=2)
                i16v = i16f[:].rearrange("p (h c) a -> p h c a", c=2)
                candv = cand[:].rearrange("p h (a b) -> p h a b", b=16)
                T("dve", lambda e: e.tensor_tensor(candv, m16v[:, :, 0, :].unsqueeze(3).to_broadcast([128, 8, 16, 16]),
                                                   m16v[:, :, 1, :].unsqueeze(2).to_broadcast([128, 8, 16, 16]), op=ALU.add), reads=[tk_d], writes=[tk_d])
                for h in range(8):
                    T("dve", lambda e, h=h: e.max(best[:, h, 0:8], cand[:, h, :]), reads=[tk_d], writes=[tk_d])
                    T("dve", lambda e, h=h: e.match_replace(cand2[:, h, :], best[:, h, 0:8], cand[:, h, :], -3.0e38), reads=[tk_d], writes=[tk_d])
                    T("dve", lambda e, h=h: e.max(best[:, h, 8:16], cand2[:, h, :]), reads=[tk_d], writes=[tk_d])
                    T("dve", lambda e, h=h: e.max_index(pos[:, h, 0:8], best[:, h, 0:8], cand[:, h, :]), reads=[tk_d], writes=[tk_d])
                    T("dve", lambda e, h=h: e.max_index(pos[:, h, 8:16], best[:, h, 8:16], cand2[:, h, :]), reads=[tk_d], writes=[tk_d])
                T("dve", lambda e: e.tensor_tensor(gate[:], best[:], best[:, :, 0:1].to_broadcast([128, 8, 16]), op=ALU.subtract), reads=[tk_d], writes=[tk_d])
                T("act", lambda e: e.activation(gate[:], gate[:], AF.Exp), reads=[tk_d], writes=[tk_d])
                T("dve", lambda e: e.tensor_reduce(gsm[:, :, 0], gate[:], axis=AX.X, op=ALU.add), reads=[tk_d], writes=[tk_d])
                T("dve", lambda e: e.reciprocal(gsm[:, :, 1], gsm[:, :, 0]), reads=[tk_d], writes=[tk_d])
                T("dve", lambda e: e.tensor_tensor(gate[:], gate[:], gsm[:, :, 1:2].to_broadcast([128, 8, 16]), op=ALU.mult), reads=[tk_d], writes=[tk_d])
                T("dve", lambda e: e.tensor_single_scalar(pa_i[:], pos[:], 4, op=ALU.logical_shift_right), reads=[tk_d], writes=[tk_d])
                T("dve", lambda e: e.tensor_single_scalar(pb_i[:], pos[:], 15, op=ALU.bitwise_and), reads=[tk_d], writes=[tk_d])
                T("dve", lambda e: e.tensor_copy(pa_f[:], pa_i[:]), reads=[tk_d], writes=[tk_d])
                T("dve", lambda e: e.tensor_copy(pb_f[:], pb_i[:]), reads=[tk_d], writes=[tk_d])
                io_b = io16[:].unsqueeze(1).unsqueeze(1).to_broadcast([128, 8, 16, 16])
                for (pf, cc, dst) in [(pa_f, 0, i1s), (pb_f, 1, i2s)]:
                    T("dve", lambda e, pf=pf: e.tensor_tensor(eq, pf[:].unsqueeze(3).to_broadcast([128, 8, 16, 16]), io_b, op=ALU.is_equal),
                      reads=[tk_d, cst], writes=[tk_d])
                    T("dve", lambda e, cc=cc: e.tensor_tensor(eq, eq, i16v[:, :, cc, :].unsqueeze(2).to_broadcast([128, 8, 16, 16]), op=ALU.mult),
                      reads=[tk_d], writes=[tk_d])
                    T("dve", lambda e, dst=dst: e.tensor_reduce(dst[:], eq, axis=AX.X, op=ALU.add), reads=[tk_d], writes=[tk_d])
                T("dve", lambda e: e.scalar_tensor_tensor(idxf[:], i1s[:].rearrange("p h k -> p (h k)"), 128.0, i2s[:].rearrange("p h k -> p (h k)"),
                                                          op0=ALU.mult, op1=ALU.add), reads=[tk_d], writes=[tk_d])
                T("dve", lambda e: e.tensor_copy(idxi[:], idxf[:]), reads=[tk_d], writes=[tk_d])
                gflat = gate[:].rearrange("p h k -> p (h k)")
                NGRP = NSLOT // 2
                ginfo = {}

                def stage_a(g):
                    nonlocal gcnt
                    gbt, gbd = gb[gcnt % NG], gb_d[gcnt % NG]
                    gcnt += 1
                    ginfo[g] = (gbt, gbd)
                    for s2 in range(2):
                        slot = g * 2 + s2
                        S.dma("pool", lambda e, gbt=gbt, s2=s2, slot=slot: e.indirect_dma_start(
                            out=gbt[:, s2, :], out_offset=None, in_=k.UVB[:, :],
                            in_offset=bass.IndirectOffsetOnAxis(ap=idxi[:, slot:slot + 1], axis=0)),
                            reads=[tk_d, k.UVB_dep], writes=[gbd[s2]])
                    for s2 in range(2):
                        slot = g * 2 + s2
                        if s2 == 1 and SPLIT:
                            pr_, prd_ = prod[g % 2], prod_d[g % 2]
                            T("pool", lambda e, gbt=gbt, pr_=pr_: e.tensor_tensor(pr_[:], gbt[:, 1, 0:1024], hbb[:], op=ALU.mult),
                              reads=[gbd[1], hbb_d], writes=[prd_])
                            T("act", lambda e, pr_=pr_, slot=slot: e.activation(junk[:], pr_[:], AF.Copy, accum_out=actr[:, slot:slot + 1]),
                              reads=[prd_], writes=[junk_d, asd[slot]])
                            continue
                        T("dve", lambda e, gbt=gbt, s2=s2, slot=slot: e.scalar_tensor_tensor(
                            junkb[:], gbt[:, s2, 0:1024], 1.0, hbb[:], op0=ALU.mult, op1=ALU.mult,
                            accum_out=actr[:, slot:slot + 1]), reads=[gbd[s2], hbb_d], writes=[asd[slot]])
                    sl = slice(g * 2, g * 2 + 2)
                    T("act", lambda e, sl=sl: e.activation(wgt[:, sl], actr[:, sl], AF.Gelu), reads=[asd[g * 2], asd[g * 2 + 1]], writes=[agd[g]])

                def stage_b(g):
                    nonlocal dcnt
                    gbt, gbd = ginfo.pop(g)
                    ad_ = agd[g]
                    sl = slice(g * 2, g * 2 + 2)
                    T("dve", lambda e, sl=sl: e.tensor_tensor(wgt[:, sl], wgt[:, sl], gflat[:, sl], op=ALU.mult), reads=[ad_, tk_d], writes=[ad_])
                    for s2 in range(2):
                        slot = g * 2 + s2
                        dgt, dgd = dgb[dcnt % 4], dgb_d[dcnt % 4]
                        dcnt += 1
                        T("act", lambda e, dgt=dgt, slot=slot: e.activation(dgt[:], k.ident[:], AF.Copy, scale=wgt[:, slot:slot + 1]),
                          reads=[k.ident_d, ad_], writes=[dgd])
                        for hf in range(2):
                            T("pe", _mm(pb_[6 + hf][:, :], dgt[:], gbt[:, s2, 1024 + hf * 512:1024 + (hf + 1) * 512],
                                        start=(slot == 0), stop=(slot == NSLOT - 1)), reads=[dgd, gbd[s2]], writes=[pb_d[6 + hf]])

                SKEW = 2
                for g in range(NGRP + SKEW):
                    if g < NGRP:
                        stage_a(g)
                    if g >= SKEW:
                        stage_b(g - SKEW)
                for hf in range(2):
                    hs = slice(hf * 512, (hf + 1) * 512)
                    T("dve", lambda e, hf=hf, hs=hs: e.tensor_tensor(oo[:, hs], pb_[6 + hf][:, :], bc[:, 3, hs], op=ALU.mult), reads=[pb_d[6 + hf], bc_d], writes=[oo_d])
                    T("dve", lambda e, hs=hs: e.tensor_tensor(oo[:, hs], oo[:, hs], h1[:, hs], op=ALU.add), reads=[oo_d, h1_d], writes=[oo_d])
                T("act", lambda e: e.activation(junk[:], oo[:], AF.Square, accum_out=st[:, 4:5]), reads=[oo_d], writes=[junk_d, st_d])
                T("act", lambda e: e.activation(st[:, 5:6], st[:, 4:5], AF.Sqrt, bias=1e-6, scale=1.0 / D), reads=[st_d], writes=[st_d])
                T("dve", lambda e: e.reciprocal(st[:, 6:7], st[:, 5:6]), reads=[st_d], writes=[st_d])
                T("act", lambda e: e.activation(oo[:], oo[:], AF.Copy, scale=st[:, 6:7]), reads=[oo_d, st_d], writes=[oo_d])
                T("dve", lambda e: e.tensor_tensor(oo[:], oo[:], nbc[:, 1, :], op=ALU.mult), reads=[oo_d, cst], writes=[oo_d])
                S.dma("sp", lambda e, b=b, t0=t0: e.dma_start(out=k.out[b, t0:t0 + 128, :], in_=oo[:]), reads=[oo_d], writes=[Dep()])
        S.barrier()
```

```python
import numpy as np
from contextlib import ExitStack
import concourse.bass as bass
import concourse.mybir as mybir
from concourse.bass_utils import run_bass_kernel_spmd

F32 = mybir.dt.float32
BF16 = mybir.dt.bfloat16
I32 = mybir.dt.int32
U32 = mybir.dt.uint32
AF = mybir.ActivationFunctionType
ALU = mybir.AluOpType
AX = mybir.AxisListType

D = 1024
SEQ = 2048
CTX = 256
LT = SEQ + CTX
INC = 2208
NCORES = 8
NBATCH = 32


class Dep:
    __slots__ = ("w", "r", "excl")

    def __init__(self, excl=False):
        self.w = None
        self.r = {}
        self.excl = excl


def PD():
    return Dep(excl=True)


class Sch:
    ROT = 30000

    def __init__(self, nc, es):
        self.nc = nc
        self.es = es
        self.eng = {"pe": nc.tensor, "dve": nc.vector, "act": nc.scalar, "pool": nc.gpsimd, "sp": nc.sync}
        self.cur = {}
        self.cnt = {}
        self.seen = {e: {} for e in self.eng}
        self.nsem = 0
        for e in self.eng:
            self._newsem(e)
        self.dsems = []
        for i in range(40):
            s = es.enter_context(nc.semaphore(f"dq{i}"))
            self.dsems.append([s, 0])
        self.dnext = 0
        self.swsems = []
        for i in range(16):
            s = es.enter_context(nc.semaphore(f"sq{i}"))
            self.swsems.append([s, 0])
        self.swnext = 0
        self.semobj = {}
        self.ninst = 0
        import os
        self.skip_own = set(os.environ.get("SKIP_OWN", "pe").split(","))

    def _newsem(self, e):
        s = self.es.enter_context(self.nc.semaphore(f"e_{e}_{self.nsem}"))
        self.nsem += 1
        self.cur[e] = s
        self.cnt[e] = 0

    def _wait(self, en, deps):
        best = {}
        for (s, v) in deps:
            k = id(s)
            if k not in best or best[k][1] < v:
                best[k] = (s, v)
        seen = self.seen[en]
        for k, (s, v) in best.items():
            if seen.get(k, 0) >= v:
                continue
            self.eng[en].wait_ge(s, v)
            self.nwait = getattr(self, "nwait", 0) + 1
            seen[k] = v

    def _deps(self, reads, writes):
        deps = []
        for d in reads:
            if d.w is not None:
                deps.append(d.w)
        for d in writes:
            if d.w is not None:
                deps.append(d.w)
            deps.extend(d.r.values())
        return deps

    def _mark(self, ev, reads, writes):
        for d in reads:
            d.r[id(ev[0])] = ev
        for d in writes:
            d.w = ev
            d.r = {}

    def op(self, en, fn, reads=(), writes=()):
        ex = [d for d in reads if d.excl]
        if ex:
            reads = [d for d in reads if not d.excl]
            writes = list(writes) + ex
        deps = self._deps(reads, writes)
        if en in self.skip_own:
            own = id(self.cur[en])
            deps = [d for d in deps if id(d[0]) != own]
        self._wait(en, deps)
        ins = fn(self.eng[en])
        if self.cnt[en] >= self.ROT:
            self._newsem(en)
        self.cnt[en] += 1
        ins.then_inc(self.cur[en], 1)
        ev = (self.cur[en], self.cnt[en])
        self._mark(ev, reads, writes)
        self.ninst += 1
        return ev

    def dma(self, q, fn, reads=(), writes=()):
        if q == "pool":
            slot = self.swsems[self.swnext]
            self.swnext = (self.swnext + 1) % len(self.swsems)
        else:
            slot = self.dsems[self.dnext]
            self.dnext = (self.dnext + 1) % len(self.dsems)
        deps = self._deps(reads, writes)
        if slot[1] > 0:
            deps.append((slot[0], slot[1]))
        self._wait(q, deps)
        ins = fn(self.eng[q])
        slot[1] += 16
        ins.then_inc(slot[0], 16)
        ev = (slot[0], slot[1])
        self._mark(ev, reads, writes)
        self.ninst += 1
        return ev

    def barrier(self):
        evs = [(self.cur[e], self.cnt[e]) for e in self.eng if self.cnt[e] > 0]
        evs += [(s, v) for (s, v) in self.dsems + self.swsems if v > 0]
        for e in self.eng:
            self._wait(e, evs)


def _mm(out, lhsT, rhs, start=True, stop=True, tp=None):
    return lambda e: e.matmul(out, lhsT, rhs, start=start, stop=stop, tile_position=tp)


class K:
    pass


def build(NB=4, upto=99, dbg=()):
    nc = bass.Bass("TRN2", target_bir_lowering=False)
    es = ExitStack()
    k = K()
    k.nc, k.es, k.NB = nc, es, NB
    S = k.S = Sch(nc, es)

    def din(name, shape, dt=F32):
        return nc.dram_tensor(name, list(shape), dt, kind="ExternalInput").ap()

    I = k.I = {}
    I["x"] = din("x", [NB, SEQ, D])
    I["c"] = din("c", [NB, D])
    I["ctx"] = din("ctx", [NB, CTX, D])
    I["c_ctx"] = din("c_ctx", [1, D])
    I["w_ada"] = din("w_ada", [D, 6 * D])
    I["b_ada"] = din("b_ada", [48, 128])
    I["norm1_g"] = din("norm1_g", [8, 128])
    I["norm2_g"] = din("norm2_g", [1, D])
    I["w_in"] = din("w_in", [D, INC])
    I["s5_arow"] = din("s5_arow", [3, 32, 128])
    I["s5_bT"] = din("s5_bT", [2, 128, 1024])
    I["s5_cblk"] = din("s5_cblk", [2, 128, 8, 128])
    I["s5_vec"] = din("s5_vec", [8, 128])
    I["s5_w_glu"] = din("s5_w_glu", [512, 512])
    I["rw_vec"] = din("rw_vec", [42, 128])
    I["rw_wlora"] = din("rw_wlora", [64, 2, 512])
    I["rw_w_g2"] = din("rw_w_g2", [96, 512])
    I["rw_ln"] = din("rw_ln", [2, 512])
    I["w_out"] = din("w_out", [D, D])
    I["peer_w_q"] = din("peer_w_q", [D, 2048])
    I["peer_keys"] = din("peer_keys", [128, 16, 128])
    I["peer_uv"] = din("peer_uv", [16384, 2048])
    I["nvec"] = din("nvec", [2, D])
    k.out = nc.dram_tensor("out", [NB, SEQ, D], F32, kind="ExternalOutput").ap()
    k.dbg = {}
    for (name, shape) in dbg:
        if name in ("PT", "MIXT") or shape is None:
            continue
        k.dbg[name] = nc.dram_tensor(name, list(shape), F32, kind="ExternalOutput").ap()
    dbgn = [d[0] for d in dbg]
    k.PT = nc.dram_tensor("PT", [NB, INC, LT], F32, kind="ExternalOutput" if "PT" in dbgn else "Internal").ap()
    k.PT_dep = [Dep() for _ in range(NB)]
    k.MIXT = nc.dram_tensor("MIXT", [NB, D, SEQ], F32, kind="ExternalOutput" if "MIXT" in dbgn else "Internal").ap()
    k.MIXT_dep = [Dep() for _ in range(NB)]
    k.UVB = nc.dram_tensor("UVB", [16384, 2048], BF16, kind="Internal").ap()
    k.UVB_dep = Dep()

    with es:
        setup_consts(k)
        k.modT = sb(k, "modT", [128, 48, NB + 1])
        k.gs1T = sb(k, "gs1T", [128, 8, NB + 1])
        stage0_mod(k)
        if upto >= 1:
            stage1_proj(k)
        if upto >= 2 and "skip_s5" not in dbgn:
            with ExitStack() as es2:
                k.es_stage = es2
                s5_alloc(k)
                s5_setup(k)
                stage2_s5(k)
        if upto >= 3:
            with ExitStack() as es3:
                k.es_stage = es3
                rw_setup(k)
                stage3_rwkv(k)
        if upto >= 4:
            cast_uv(k)
            stage4_peer(k)
        S.barrier()
        print("ninst", S.ninst, "nwait", getattr(S, "nwait", 0))
    return nc


def sb(k, name, shape, dt=F32):
    return k.es.enter_context(k.nc.sbuf_tensor(name, list(shape), dt))


def ps(k, name, shape, dt=F32):
    return k.es.enter_context(k.nc.psum_tensor(name, list(shape), dt))


def setup_consts(k):
    nc, S = k.nc, k.S
    k.ident = sb(k, "ident", [128, 128])
    k.ident_d = Dep()
    k.iota_p = sb(k, "iota_p", [128, 1], I32)
    k.iota_f = sb(k, "iota_f", [128, 128], I32)
    k.iota_d = Dep()
    S.op("pool", lambda e: e.iota(k.iota_p[:], [[0, 1]], base=0, channel_multiplier=1), writes=[k.iota_d])
    S.op("pool", lambda e: e.iota(k.iota_f[:], [[1, 128]], base=0, channel_multiplier=0), writes=[k.iota_d])
    k.iota_pf = sb(k, "iota_pf", [128, 1])
    k.iota_ff = sb(k, "iota_ff", [128, 128])
    S.op("dve", lambda e: e.tensor_copy(k.iota_pf[:], k.iota_p[:]), reads=[k.iota_d], writes=[k.ident_d])
    S.op("dve", lambda e: e.tensor_copy(k.iota_ff[:], k.iota_f[:]), reads=[k.iota_d], writes=[k.ident_d])
    S.op("dve", lambda e: e.tensor_scalar(k.ident[:], k.iota_ff[:], k.iota_pf[:, 0:1], None, op0=ALU.is_equal),
         reads=[k.ident_d], writes=[k.ident_d])


def stage0_mod(k):
    nc, S, I, NB = k.nc, k.S, k.I, k.NB
    NC5 = NB + 1
    with ExitStack() as es:
        def sbl(name, shape, dt=F32):
            return es.enter_context(nc.sbuf_tensor(name, list(shape), dt))
        crow = sbl("crow", [NC5, D])
        crow_d = Dep()
        S.dma("sp", lambda e: e.dma_start(out=crow[0:NB, :], in_=I["c"][:, :]), writes=[crow_d])
        S.dma("sp", lambda e: e.dma_start(out=crow[NB:NC5, :], in_=I["c_ctx"][:, :]), writes=[crow_d])
        S.op("act", lambda e: e.activation(crow[:], crow[:], AF.Silu), reads=[crow_d], writes=[crow_d])
        cT = sbl("cT", [128, 8, NC5])
        cT_d = Dep()
        vst = sbl("vst", [64, 128])
        vst_d = Dep()
        S.dma("sp", lambda e: e.dma_start(out=vst[0:48, :], in_=I["b_ada"][:, :]), writes=[vst_d])
        S.dma("sp", lambda e: e.dma_start(out=vst[48:56, :], in_=I["norm1_g"][:, :]), writes=[vst_d])
        vT = sbl("vT", [128, 56])
        vT_d = Dep()
        with nc.psum_tensor("p0a", [128, 8, NC5], F32) as pa, nc.psum_tensor("p0b", [128, 56], F32) as pb, \
                nc.psum_tensor("p0c", [128, 48, NC5], F32) as pc:
            pa_d, pb_d, pc_d = PD(), PD(), PD()
            for kd in range(8):
                S.op("pe", lambda e, kd=kd: e.transpose(pa[:, kd, :], crow[0:NC5, kd * 128:(kd + 1) * 128],
                                                        k.ident[0:NC5, 0:NC5]),
                     reads=[crow_d, k.ident_d], writes=[pa_d])
            S.op("dve", lambda e: e.tensor_copy(cT[:], pa[:]), reads=[pa_d], writes=[cT_d])
            S.op("pe", lambda e: e.transpose(pb[:, :], vst[0:56, :], k.ident[0:56, 0:56]),
                 reads=[vst_d, k.ident_d], writes=[pb_d])
            S.op("dve", lambda e: e.tensor_copy(vT[:], pb[:]), reads=[pb_d], writes=[vT_d])
            wt = [sbl(f"wada{i}", [128, 8, 512]) for i in range(2)]
            wt_d = [Dep(), Dep()]
            wv = I["w_ada"].rearrange("(kd p) n -> p kd n", p=128)
            for blk in range(12):
                t, td = wt[blk % 2], wt_d[blk % 2]
                for kd in range(8):
                    S.dma("sp" if kd % 2 == 0 else "act",
                          lambda e, kd=kd, t=t, blk=blk: e.dma_start(out=t[:, kd, :], in_=wv[:, kd, blk * 512:(blk + 1) * 512]),
                          writes=[td])
                for jj in range(4):
                    j = blk * 4 + jj
                    for kd in range(8):
                        S.op("pe", _mm(pc[:, j, :], t[:, kd, jj * 128:(jj + 1) * 128], cT[:, kd, :],
                                       start=(kd == 0), stop=(kd == 7)),
                             reads=[td, cT_d], writes=[pc_d])
            k.modT_d = Dep()
            S.op("dve", lambda e: e.tensor_tensor(k.modT[:], pc[:], vT[:, 0:48].unsqueeze(2).to_broadcast([128, 48, NC5]),
                                                  op=ALU.add),
                 reads=[pc_d, vT_d], writes=[k.modT_d])
        S.op("dve", lambda e: e.tensor_scalar(k.gs1T[:], k.modT[:, 8:16, :], 1.0, None, op0=ALU.add),
             reads=[k.modT_d], writes=[k.modT_d])
        S.op("dve", lambda e: e.tensor_tensor(k.gs1T[:], k.gs1T[:], vT[:, 48:56].unsqueeze(2).to_broadcast([128, 8, NC5]),
                                              op=ALU.mult),
             reads=[vT_d, k.modT_d], writes=[k.modT_d])
        k.S.barrier()


def stage1_proj(k):
    nc, S, I, NB = k.nc, k.S, k.I, k.NB
    with ExitStack() as es:
        def sbl(name, shape, dt=F32):
            return es.enter_context(nc.sbuf_tensor(name, list(shape), dt))
        wbf = sbl("w_in_bf", [128, 8, INC], BF16)
        wbf_d = Dep()
        for kd in range(8):
            S.dma("pool", lambda e, kd=kd: e.dma_start(out=wbf[:, kd, :], in_=I["w_in"][kd * 128:(kd + 1) * 128, :],
                                                       max_dma_last_dim=4096), writes=[wbf_d])
        xt = [sbl(f"xt{i}", [128, D]) for i in range(2)]
        xt_d = [Dep(), Dep()]
        xs = [sbl(f"xs{i}", [128, D]) for i in range(2)]
        xs_d = [Dep(), Dep()]
        junk = sbl("junk1", [128, D])
        junk_d = Dep()
        st = [sbl(f"st{i}", [128, 4]) for i in range(2)]
        hnT = [sbl(f"hnT{i}", [128, 8, 512], BF16) for i in range(2)]
        hnT_d = [Dep(), Dep()]
        ev = [sbl(f"ev{i}", [128, 512]) for i in range(3)]
        ev_d = [Dep() for _ in range(3)]
        ptr = [es.enter_context(nc.psum_tensor(f"ptr{i}", [128, 8, 128], F32)) for i in range(2)]
        ptr_d = [PD(), PD()]
        pmm = [es.enter_context(nc.psum_tensor(f"pmm{i}", [128, 512], F32)) for i in range(3)]
        pmm_d = [PD() for _ in range(3)]
        fch = [(i * 128, 128) for i in range(16)] + [(2048, 64), (2112, 96)]
        k.fch = fch
        ti = 0
        gi = 0
        ei = 0
        for b in range(NB):
            groups = [("ctx", 0, 256)] + [("lat", g * 512, 512) for g in range(4)]
            for (kind, t0, nt) in groups:
                h, hd = hnT[gi % 2], hnT_d[gi % 2]
                gi += 1
                col = b if kind == "lat" else NB
                for tt in range(nt // 128):
                    x_t, x_d = xt[ti % 2], xt_d[ti % 2]
                    xs_t, xsd = xs[ti % 2], xs_d[ti % 2]
                    s_t = st[ti % 2]
                    p_t, p_d = ptr[ti % 2], ptr_d[ti % 2]
                    ti += 1
                    src = I["x"][b, t0 + tt * 128:t0 + (tt + 1) * 128, :] if kind == "lat" else \
                        I["ctx"][b, tt * 128:(tt + 1) * 128, :]
                    S.dma("sp", lambda e, x_t=x_t, src=src: e.dma_start(out=x_t[:], in_=src), writes=[x_d])
                    S.op("act", lambda e, x_t=x_t, s_t=s_t: e.activation(junk[:], x_t[:], AF.Square, accum_out=s_t[:, 0:1]),
                         reads=[x_d], writes=[junk_d, xsd])
                    S.op("act", lambda e, s_t=s_t: e.activation(s_t[:, 1:2], s_t[:, 0:1], AF.Sqrt, bias=1e-6, scale=1.0 / D),
                         reads=[xsd], writes=[xsd])
                    S.op("dve", lambda e, s_t=s_t: e.reciprocal(s_t[:, 2:3], s_t[:, 1:2]), reads=[xsd], writes=[xsd])
                    S.op("act", lambda e, x_t=x_t, xs_t=xs_t, s_t=s_t: e.activation(xs_t[:], x_t[:], AF.Copy, scale=s_t[:, 2:3]),
                         reads=[x_d, xsd], writes=[xsd])
                    for kd in range(8):
                        S.op("pe", lambda e, kd=kd, p_t=p_t, xs_t=xs_t: e.transpose(p_t[:, kd, :], xs_t[:, kd * 128:(kd + 1) * 128],
                                                                                      k.ident[:]),
                             reads=[xsd, k.ident_d], writes=[p_d])
                    for kd in range(8):
                        S.op("dve", lambda e, kd=kd, p_t=p_t, h=h, tt=tt, col=col: e.tensor_scalar(
                            h[:, kd, tt * 128:(tt + 1) * 128], p_t[:, kd, :], k.gs1T[:, kd, col:col + 1],
                            k.modT[:, kd, col:col + 1], op0=ALU.mult, op1=ALU.add),
                            reads=[p_d, k.modT_d], writes=[hd])
                tok0 = t0 if kind == "ctx" else CTX + t0
                for fi, (c0, ncol) in enumerate(fch):
                    pm, pmd = pmm[ei % 3], pmm_d[ei % 3]
                    e_t, e_d = ev[ei % 3], ev_d[ei % 3]
                    ei += 1
                    for kd in range(8):
                        S.op("pe", _mm(pm[0:ncol, 0:nt], wbf[:, kd, c0:c0 + ncol], h[:, kd, 0:nt], start=(kd == 0), stop=(kd == 7)),
                             reads=[wbf_d, hd], writes=[pmd])
                    eng = "act" if fi % 2 == 0 else "dve"
                    if eng == "act":
                        S.op("act", lambda e, pm=pm, e_t=e_t, ncol=ncol, nt=nt: e.copy(e_t[0:ncol, 0:nt], pm[0:ncol, 0:nt]),
                             reads=[pmd], writes=[e_d])
                    else:
                        S.op("dve", lambda e, pm=pm, e_t=e_t, ncol=ncol, nt=nt: e.tensor_copy(e_t[0:ncol, 0:nt], pm[0:ncol, 0:nt]),
                             reads=[pmd], writes=[e_d])
                    S.dma("sp", lambda e, e_t=e_t, ncol=ncol, nt=nt, c0=c0, tok0=tok0, b=b: e.dma_start(
                        out=k.PT[b, c0:c0 + ncol, tok0:tok0 + nt], in_=e_t[0:ncol, 0:nt]),
                        reads=[e_d], writes=[k.PT_dep[b]])
        S.barrier()


_CACHE = {}


def _prep_inputs(inputs, NB, core):
    sl = slice(core * NB, (core + 1) * NB)
    f = lambda a: np.ascontiguousarray(a, dtype=np.float32)
    m = {
        "x": f(inputs["x"][sl]),
        "c": f(inputs["c"][sl]),
        "ctx": f(inputs["ctx"][sl]),
        "c_ctx": f(inputs["c_ctx"].reshape(1, D)),
        "w_ada": f(inputs["w_ada"][0]),
        "b_ada": f(inputs["b_ada"][0].reshape(48, 128)),
        "norm1_g": f(inputs["norm1_g"][0].reshape(8, 128)),
        "norm2_g": f(inputs["norm2_g"][0].reshape(1, D)),
        "w_in": f(inputs["w_in"][0]),
    }
    m.update(_s5_layout(inputs))
    m.update(_rw_layout(inputs))
    m.update(_peer_layout(inputs))
    return m


def kernel(**inputs):
    NB = NBATCH // NCORES
    if "nc" not in _CACHE:
        _CACHE["nc"] = build(NB)
    nc = _CACHE["nc"]
    in_maps = [_prep_inputs(inputs, NB, c) for c in range(NCORES)]
    res = run_bass_kernel_spmd(nc, in_maps, core_ids=list(range(NCORES)))
    return np.concatenate([r["out"] for r in res.results], axis=0)


def _s5_layout(inputs):
    f = lambda a: np.ascontiguousarray(a, dtype=np.float32)
    a_re, a_im, ldt = inputs["s5_a_re"][0], inputs["s5_a_im"][0], inputs["s5_log_dt"][0]
    arow = np.zeros((3, 32, 128), np.float32)
    arow[0] = a_re.reshape(2, 16, 128).reshape(32, 128)
    arow[1] = a_im.reshape(2, 16, 128).reshape(32, 128)
    arow[2] = np.repeat(ldt.reshape(2, 16, 2, 1), 64, axis=3).reshape(32, 128)
    bT = np.zeros((2, 2, 64, 2, 4, 4, 2, 16), np.float32)
    cb = np.zeros((2, 4, 2, 16, 2, 4, 2, 64), np.float32)
    for ri, (bsrc, csrc) in enumerate([(inputs["s5_b_re"][0], inputs["s5_c_re"][0]),
                                       (inputs["s5_b_im"][0], inputs["s5_c_im"][0])]):
        bg = bsrc.reshape(2, 4, 4, 2, 64, 16)
        cg = csrc.reshape(2, 4, 4, 2, 16, 64)
        for gl in range(2):
            bT[ri, gl, :, :, :, :, gl, :] = np.transpose(bg[:, :, :, gl], (3, 0, 1, 2, 4))
            cb[ri, :, gl, :, :, :, gl, :] = np.transpose(cg[:, :, :, gl], (2, 3, 0, 1, 4))
    vec = np.zeros((8, 128), np.float32)
    vec[0:4] = inputs["s5_d"][0].reshape(4, 128)
    vec[4:8] = inputs["s5_b_glu"][0].reshape(4, 128)
    return {"s5_arow": arow, "s5_bT": f(bT.reshape(2, 128, 1024)), "s5_cblk": f(cb.reshape(2, 128, 8, 128)),
            "s5_vec": vec, "s5_w_glu": f(inputs["s5_w_glu"][0])}


def sbs(k, name, shape, dt=F32):
    return k.es_stage.enter_context(k.nc.sbuf_tensor("S_" + name, list(shape), dt))


def s5_alloc(k):
    sb = sbs
    k.winj = [sb(k, f"winj{i}", [128, 8, 128]) for i in range(2)]
    k.rout = [sb(k, f"rout{i}", [128, 8, 128]) for i in range(2)]
    k.pw = [sb(k, f"pw{i}", [128, 32, 17]) for i in range(3)]
    k.lam = [sb(k, f"lam{i}", [128, 32, 8]) for i in range(3)]
    k.s5vT = sb(k, "s5vT", [128, 8])
    k.wglu = sb(k, "wglu", [128, 4, 512])
    k.s5_d = Dep()


def s5_setup(k):
    nc, S, I = k.nc, k.S, k.I
    T = S.op
    with ExitStack() as es:
        def sbl(name, shape, dt=F32):
            return es.enter_context(nc.sbuf_tensor(name, list(shape), dt))
        d0 = Dep()
        rows = sbl("s5rows", [32, 3, 128])
        S.dma("sp", lambda e: e.dma_start(out=rows[:], in_=I["s5_arow"].rearrange("a r c -> r a c")), writes=[d0])
        vrow = sbl("s5vrow", [8, 128])
        S.dma("sp", lambda e: e.dma_start(out=vrow[:], in_=I["s5_vec"][:, :]), writes=[d0])
        S.dma("sp", lambda e: e.dma_start(out=k.wglu[:], in_=I["s5_w_glu"].rearrange("(kc p) n -> p kc n", p=128)),
              writes=[k.s5_d])
        bT = [sbl(f"s5bT{i}", [128, 32, 32]) for i in range(2)]
        cblk = [sbl(f"s5cb{i}", [128, 8, 128]) for i in range(2)]
        for i in range(2):
            S.dma("sp", lambda e, i=i: e.dma_start(out=bT[i][:], in_=I["s5_bT"][i].rearrange("p (a b) -> p a b", b=32)), writes=[d0])
            S.dma("act", lambda e, i=i: e.dma_start(out=cblk[i][:], in_=I["s5_cblk"][i]), writes=[d0])
        aT = sbl("s5aT", [128, 3, 32])
        W = [sbl(f"s5w{i}", [128, 32]) for i in range(14)]
        Wi = sbl("s5wi", [128, 32], I32)
        bb = [sbl(f"s5bb{i}", [128, 32, 32]) for i in range(2)]
        tmp = [sbl(f"s5tmp{i}", [128, 32, 32]) for i in range(2)]
        with nc.psum_tensor("ps5a", [128, 3, 32], F32) as pa, nc.psum_tensor("ps5b", [128, 8], F32) as pb, \
                nc.psum_tensor("ps5c", [128, 4, 128], F32) as pc:
            pd = PD()
            for a in range(3):
                T("pe", lambda e, a=a: e.transpose(pa[:, a, :], rows[:, a, :], k.ident[0:32, 0:32]), reads=[d0, k.ident_d], writes=[pd])
            T("dve", lambda e: e.tensor_copy(aT[:], pa[:]), reads=[pd], writes=[d0])
            T("pe", lambda e: e.transpose(pb[:, :], vrow[:, :], k.ident[0:8, 0:8]), reads=[d0, k.ident_d], writes=[pd])
            T("dve", lambda e: e.tensor_copy(k.s5vT[:], pb[:]), reads=[pd], writes=[k.s5_d])
            are, aim, ldt = aT[:, 0, :], aT[:, 1, :], aT[:, 2, :]
            dt, mag, ang, sn, cs, abr, abi, t1, t2, nr, cfr, cfi, rden, t3 = [w[:] for w in W]

            def tt(o, a, b, op):
                T("dve", lambda e: e.tensor_tensor(o, a, b, op=op), reads=[d0], writes=[d0])

            T("act", lambda e: e.activation(dt, ldt, AF.Exp), reads=[d0], writes=[d0])
            tt(t1, dt, are, ALU.mult)
            T("act", lambda e: e.activation(mag, t1, AF.Exp), reads=[d0], writes=[d0])
            tt(ang, dt, aim, ALU.mult)

            def rsin(o, phase):
                T("dve", lambda e: e.tensor_scalar(t1, ang, 1.0 / (2 * np.pi), phase, op0=ALU.mult, op1=ALU.add), reads=[d0], writes=[d0])
                T("dve", lambda e: e.tensor_copy(Wi[:], t1), reads=[d0], writes=[d0])
                T("dve", lambda e: e.tensor_copy(t2, Wi[:]), reads=[d0], writes=[d0])
                tt(t1, t1, t2, ALU.subtract)
                T("dve", lambda e: e.scalar_tensor_tensor(t2, t1, 0.0, t1, op0=ALU.is_lt, op1=ALU.add), reads=[d0], writes=[d0])
                T("dve", lambda e: e.tensor_scalar(t2, t2, 2 * np.pi, -np.pi, op0=ALU.mult, op1=ALU.add), reads=[d0], writes=[d0])
                T("dve", lambda e: e.tensor_scalar(t2, t2, 3.1415925, -3.1415925, op0=ALU.min, op1=ALU.max), reads=[d0], writes=[d0])
                T("act", lambda e: e.activation(o, t2, AF.Sin), reads=[d0], writes=[d0])

            rsin(sn, 0.5)
            rsin(cs, 0.75)
            tt(abr, mag, cs, ALU.mult)
            tt(abi, mag, sn, ALU.mult)
            T("dve", lambda e: e.tensor_scalar(nr, abr, -1.0, None, op0=ALU.add), reads=[d0], writes=[d0])
            tt(t1, are, are, ALU.mult)
            tt(t2, aim, aim, ALU.mult)
            tt(t1, t1, t2, ALU.add)
            T("dve", lambda e: e.reciprocal(rden, t1), reads=[d0], writes=[d0])
            tt(t1, nr, are, ALU.mult)
            tt(t2, abi, aim, ALU.mult)
            tt(t1, t1, t2, ALU.add)
            tt(cfr, t1, rden, ALU.mult)
            tt(t1, abi, are, ALU.mult)
            tt(t2, nr, aim, ALU.mult)
            tt(t1, t1, t2, ALU.subtract)
            tt(cfi, t1, rden, ALU.mult)
            cfrb = cfr.unsqueeze(2).to_broadcast([128, 32, 32])
            cfib = cfi.unsqueeze(2).to_broadcast([128, 32, 32])
            tt(tmp[0][:], bT[0][:], cfrb, ALU.mult)
            tt(tmp[1][:], bT[1][:], cfib, ALU.mult)
            tt(bb[0][:], tmp[0][:], tmp[1][:], ALU.subtract)
            tt(tmp[0][:], bT[1][:], cfrb, ALU.mult)
            tt(tmp[1][:], bT[0][:], cfib, ALU.mult)
            tt(bb[1][:], tmp[0][:], tmp[1][:], ALU.add)
            for ri in range(2):
                for half in range(2):
                    for cc in range(4):
                        dc = half * 4 + cc
                        T("pe", lambda e, ri=ri, dc=dc, cc=cc: e.transpose(
                            pc[:, cc, :], bb[ri][:, dc * 4:(dc + 1) * 4, :].rearrange("p a b -> p (a b)"), k.ident[:]),
                            reads=[d0, k.ident_d], writes=[pd])
                    T("dve", lambda e, ri=ri, half=half: e.tensor_copy(k.winj[ri][:, half * 4:(half + 1) * 4, :], pc[:]),
                      reads=[pd], writes=[k.s5_d])
            for ri in range(2):
                for half in range(2):
                    for cc in range(4):
                        dc = half * 4 + cc
                        T("pe", lambda e, ri=ri, dc=dc, cc=cc: e.transpose(pc[:, cc, :], cblk[ri][:, dc, :], k.ident[:]),
                          reads=[d0, k.ident_d], writes=[pd])
                    if ri == 0:
                        T("dve", lambda e, half=half: e.tensor_copy(k.rout[0][:, half * 4:(half + 1) * 4, :], pc[:]),
                          reads=[pd], writes=[k.s5_d])
                    else:
                        T("dve", lambda e, half=half: e.tensor_scalar(k.rout[1][:, half * 4:(half + 1) * 4, :], pc[:], -1.0, None,
                                                                      op0=ALU.mult), reads=[pd], writes=[k.s5_d])
            pr, pi_, pn = k.pw
            T("dve", lambda e: e.memset(pr[:, :, 0:1], 1.0), writes=[k.s5_d])
            T("dve", lambda e: e.memset(pi_[:, :, 0:1], 0.0), writes=[k.s5_d])

            def cmul(o_r, o_i, a_r, a_i, b_r, b_i, dep):
                T("dve", lambda e: e.tensor_tensor(t1, a_r, b_r, op=ALU.mult), reads=[dep, d0], writes=[d0])
                T("dve", lambda e: e.tensor_tensor(t2, a_i, b_i, op=ALU.mult), reads=[dep, d0], writes=[d0])
                T("dve", lambda e: e.tensor_tensor(t3, a_r, b_i, op=ALU.mult), reads=[dep, d0], writes=[d0])
                T("dve", lambda e: e.tensor_tensor(rden, a_i, b_r, op=ALU.mult), reads=[dep, d0], writes=[d0])
                T("dve", lambda e: e.tensor_tensor(o_r, t1, t2, op=ALU.subtract), reads=[d0], writes=[dep])
                T("dve", lambda e: e.tensor_tensor(o_i, t3, rden, op=ALU.add), reads=[d0], writes=[dep])

            for n in range(1, 17):
                cmul(pr[:, :, n], pi_[:, :, n], pr[:, :, n - 1], pi_[:, :, n - 1], abr, abi, k.s5_d)
            T("dve", lambda e: e.tensor_scalar(pn[:], pi_[:], -1.0, None, op0=ALU.mult), reads=[k.s5_d], writes=[k.s5_d])
            lr, li, ln = k.lam
            T("dve", lambda e: e.tensor_copy(lr[:, :, 0], pr[:, :, 16]), reads=[k.s5_d], writes=[k.s5_d])
            T("dve", lambda e: e.tensor_copy(li[:, :, 0], pi_[:, :, 16]), reads=[k.s5_d], writes=[k.s5_d])
            for n in range(1, 8):
                cmul(lr[:, :, n], li[:, :, n], lr[:, :, n - 1], li[:, :, n - 1], lr[:, :, n - 1], li[:, :, n - 1], k.s5_d)
            T("dve", lambda e: e.tensor_scalar(ln[:], li[:], -1.0, None, op0=ALU.mult), reads=[k.s5_d], writes=[k.s5_d])
        S.barrier()


def stage2_s5(k):
    nc, S, I, NB = k.nc, k.S, k.I, k.NB
    T = S.op
    NSC = LT // 16
    with ExitStack() as es:
        def sbl(name, shape, dt=F32):
            return es.enter_context(nc.sbuf_tensor(name, list(shape), dt))
        uT = [sbl(f"s5u{i}", [128, LT]) for i in range(2)]
        uT_d = [Dep(), Dep()]
        Z = [[[sbl(f"s5z{jj}{d}{ri}", [128, LT]) for ri in range(2)] for d in range(2)] for jj in range(2)]
        Z_d = [[Dep(), Dep()] for jj in range(2)]
        Bp = [[[[sbl(f"s5B{jj}{d}{pp}{ri}", [128, NSC]) for ri in range(2)] for pp in range(2)] for d in range(2)] for jj in range(2)]
        B_d = [[[Dep(), Dep()] for d in range(2)] for jj in range(2)]
        y1 = sbl("s5y1", [128, 4, SEQ])
        y1_d = Dep()
        og = [sbl(f"s5og{i}", [128, 512]) for i in range(2)]
        og_d = [Dep(), Dep()]
        pin = [es.enter_context(nc.psum_tensor(f"s5pin{i}", [128, 512], F32)) for i in range(2)]
        pin_d = [PD(), PD()]
        py = es.enter_context(nc.psum_tensor("s5py", [128, 4, 512], F32))
        py_d = PD()
        pg = [es.enter_context(nc.psum_tensor(f"s5pg{i}", [128, 512], F32)) for i in range(2)]
        pg_d = [PD(), PD()]
        blocks = [(0, 256)] + [(256 + i * 512, 512) for i in range(4)]
        ipi = [0]

        def s5_chain(j, d, jj):
            q = cur_c[0] * 4 + j
            c = cur_c[0]
            u, ud = cur_u[0], cur_ud[0]
            dq = d * 16 + q
            zr, zi = Z[jj][d]
            zd = Z_d[jj][d]
            Bp_ = Bp[jj][d]
            B_d_ = B_d[jj][d]
            for (c0, n) in blocks:
                if d == 0:
                    z0 = c0
                else:
                    z0 = (c0 - CTX) if c0 >= CTX else SEQ
                for ri in range(2):
                    pp, ppd = pin[ipi[0] % 2], pin_d[ipi[0] % 2]
                    ipi[0] += 1
                    T("pe", _mm(pp[:, 0:n], k.winj[ri][32 * j:32 * j + 32, d * 4 + c, :], u[32 * j:32 * j + 32, c0:c0 + n], tp=(32 * j, 0)),
                      reads=[k.s5_d, ud], writes=[ppd])
                    T("act", lambda e, pp=pp, n=n, z0=z0, ri=ri: e.copy(Z[jj][d][ri][:, z0:z0 + n], pp[:, 0:n]),
                      reads=[ppd], writes=[zd])
                yield
            zrv = zr[:].rearrange("p (j r) -> p j r", r=16)
            ziv = zi[:].rearrange("p (j r) -> p j r", r=16)
            ar = k.pw[0][:, dq, 1:2]
            ai = k.pw[1][:, dq, 1:2]
            nai = k.pw[2][:, dq, 1:2]

            def stt(o, a, sc, bb_, rd, wr):
                T("dve", lambda e: e.scalar_tensor_tensor(o, a, sc, bb_, op0=ALU.mult, op1=ALU.add), reads=rd, writes=wr)

            order = range(1, 16) if d == 0 else range(14, -1, -1)
            for r in order:
                rp = r - 1 if d == 0 else r + 1
                stt(zrv[:, :, r], zrv[:, :, rp], ar, zrv[:, :, r], [zd, k.s5_d], [zd])
                yield
                stt(zrv[:, :, r], ziv[:, :, rp], nai, zrv[:, :, r], [zd, k.s5_d], [zd])
                yield
                stt(ziv[:, :, r], ziv[:, :, rp], ar, ziv[:, :, r], [zd, k.s5_d], [zd])
                yield
                stt(ziv[:, :, r], zrv[:, :, rp], ai, ziv[:, :, r], [zd, k.s5_d], [zd])
                yield
            rb = 15 if d == 0 else 0
            cur, nxt = 0, 1
            T("pool", lambda e: e.tensor_copy(Bp_[0][0][:], zrv[:, :, rb]), reads=[zd], writes=[B_d_[0]])
            T("pool", lambda e: e.tensor_copy(Bp_[0][1][:], ziv[:, :, rb]), reads=[zd], writes=[B_d_[0]])
            yield
            for lv in range(8):
                sh = 1 << lv
                lr_ = k.lam[0][:, dq, lv:lv + 1]
                li_ = k.lam[1][:, dq, lv:lv + 1]
                nli = k.lam[2][:, dq, lv:lv + 1]
                src, dst = Bp_[cur], Bp_[nxt]
                sd, dd = B_d_[cur], B_d_[nxt]
                if d == 0:
                    o_sl, i_sl, k_sl = slice(sh, NSC), slice(0, NSC - sh), slice(0, sh)
                else:
                    o_sl, i_sl, k_sl = slice(0, NSC - sh), slice(sh, NSC), slice(NSC - sh, NSC)
                T("pool", lambda e: e.tensor_copy(dst[0][:, k_sl], src[0][:, k_sl]), reads=[sd], writes=[dd])
                T("pool", lambda e: e.tensor_copy(dst[1][:, k_sl], src[1][:, k_sl]), reads=[sd], writes=[dd])
                stt(dst[0][:, o_sl], src[0][:, i_sl], lr_, src[0][:, o_sl], [sd, k.s5_d], [dd])
                yield
                stt(dst[0][:, o_sl], src[1][:, i_sl], nli, dst[0][:, o_sl], [sd, k.s5_d], [dd])
                yield
                stt(dst[1][:, o_sl], src[1][:, i_sl], lr_, src[1][:, o_sl], [sd, k.s5_d], [dd])
                yield
                stt(dst[1][:, o_sl], src[0][:, i_sl], li_, dst[1][:, o_sl], [sd, k.s5_d], [dd])
                yield
                cur, nxt = nxt, cur
            Sf, Sfd = Bp_[cur], B_d_[cur]
            for r in range(16):
                n = (r + 1) if d == 0 else (16 - r)
                pr_ = k.pw[0][:, dq, n:n + 1]
                pi_ = k.pw[1][:, dq, n:n + 1]
                pni = k.pw[2][:, dq, n:n + 1]
                if d == 0:
                    zs, ss = slice(16, NSC), slice(15, NSC - 1)
                else:
                    zs, ss = slice(0, 128), slice(1, 129)
                stt(zrv[:, zs, r], Sf[0][:, ss], pr_, zrv[:, zs, r], [zd, Sfd, k.s5_d], [zd])
                yield
                stt(zrv[:, zs, r], Sf[1][:, ss], pni, zrv[:, zs, r], [zd, Sfd, k.s5_d], [zd])
                yield
                stt(ziv[:, zs, r], Sf[1][:, ss], pr_, ziv[:, zs, r], [zd, Sfd, k.s5_d], [zd])
                yield
                stt(ziv[:, zs, r], Sf[0][:, ss], pi_, ziv[:, zs, r], [zd, Sfd, k.s5_d], [zd])
                yield
        cur_c, cur_u, cur_ud = [0], [None], [None]
        for b in range(NB):
            for c in range(4):
                u, ud = uT[c % 2], uT_d[c % 2]
                cur_c[0], cur_u[0], cur_ud[0] = c, u, ud
                S.dma("sp", lambda e, u=u, b=b, c=c: e.dma_start(out=u[:], in_=k.PT[b, c * 128:(c + 1) * 128, :]),
                      reads=[k.PT_dep[b]], writes=[ud])
                for jp in range(2):
                    chains = []
                    for jj in range(2):
                        for d in range(2):
                            chains.append(s5_chain(jp * 2 + jj, d, jj))
                    while chains:
                        for g_ in list(chains):
                            try:
                                next(g_)
                            except StopIteration:
                                chains.remove(g_)
                    for jj in range(2):
                        j = jp * 2 + jj
                        for tb in range(4):
                            terms = []
                            for d in range(2):
                                l0 = (CTX if d == 0 else 0) + tb * 512
                                terms.append((k.rout[0][:, d * 4 + c, 32 * j:32 * j + 32], Z[jj][d][0][:, l0:l0 + 512], Z_d[jj][d]))
                                terms.append((k.rout[1][:, d * 4 + c, 32 * j:32 * j + 32], Z[jj][d][1][:, l0:l0 + 512], Z_d[jj][d]))
                            for ti, (lh, rh, dd_) in enumerate(terms):
                                T("pe", _mm(py[32 * j:32 * j + 32, tb, :], lh, rh, start=(ti == 0), stop=(ti == 3), tp=(0, 32 * j)),
                                  reads=[k.s5_d, dd_], writes=[py_d])
                for tb in range(4):
                    T("dve", lambda e, tb=tb, c=c, u=u: e.scalar_tensor_tensor(
                        y1[:, c, tb * 512:(tb + 1) * 512], u[:, CTX + tb * 512:CTX + (tb + 1) * 512], k.s5vT[:, c:c + 1], py[:, tb, :],
                        op0=ALU.mult, op1=ALU.add), reads=[py_d, ud, k.s5_d], writes=[y1_d])
                T("act", lambda e, c=c: e.activation(y1[:, c, :], y1[:, c, :], AF.Gelu), reads=[y1_d], writes=[y1_d])
            gi = 0
            for m in range(4):
                for tb in range(4):
                    p_, pd_ = pg[gi % 2], pg_d[gi % 2]
                    o_, od_ = og[gi % 2], og_d[gi % 2]
                    gi += 1
                    for kc in range(4):
                        T("pe", _mm(p_[:], k.wglu[:, kc, m * 128:(m + 1) * 128], y1[:, kc, tb * 512:(tb + 1) * 512],
                                    start=(kc == 0), stop=(kc == 3)), reads=[k.s5_d, y1_d], writes=[pd_])
                    T("act", lambda e, p_=p_, o_=o_, m=m: e.activation(o_[:], p_[:], AF.Sigmoid, bias=k.s5vT[:, 4 + m:5 + m]),
                      reads=[pd_, k.s5_d], writes=[od_])
                    T("dve", lambda e, o_=o_, m=m, tb=tb: e.tensor_tensor(o_[:], o_[:], y1[:, m, tb * 512:(tb + 1) * 512], op=ALU.mult),
                      reads=[od_, y1_d], writes=[od_])
                    S.dma("sp", lambda e, o_=o_, m=m, tb=tb, b=b: e.dma_start(
                        out=k.MIXT[b, m * 128:(m + 1) * 128, tb * 512:(tb + 1) * 512], in_=o_[:]),
                        reads=[od_], writes=[k.MIXT_dep[b]])
        S.barrier()


def _rw_layout(inputs):
    f = lambda a: np.ascontiguousarray(a, dtype=np.float32)
    vec = np.zeros((42, 128), np.float32)
    mu = inputs["rw_mu"][0]
    vec[0:12] = mu[0:1536].reshape(12, 128)
    vec[12, 0:64] = mu[1536:1600]
    vec[13, 0:96] = mu[1600:1696]
    vec[14:18] = inputs["rw_k_k"][0].reshape(4, 128)
    vec[18:22] = inputs["rw_k_a"][0].reshape(4, 128)
    vec[22:26] = inputs["rw_r_k"][0].reshape(4, 128)
    vec[26:34] = inputs["rw_w0"][0].reshape(8, 128)
    vec[34:42] = inputs["rw_a0"][0].reshape(8, 128)
    wl = np.zeros((64, 2, 512), np.float32)
    wl[0:32] = np.transpose(inputs["rw_w_w2"][0], (1, 0, 2))
    wl[32:64] = np.transpose(inputs["rw_w_a2"][0], (1, 0, 2))
    ln = np.stack([inputs["rw_ln_w"][0], inputs["rw_ln_b"][0]], 0)
    return {"rw_vec": vec, "rw_wlora": f(wl), "rw_w_g2": f(inputs["rw_w_g2"][0]), "rw_ln": f(ln)}


def rw_setup(k):
    nc, S, I = k.nc, k.S, k.I
    T = S.op
    k.rwc = Dep()
    k.rvT = sbs(k, "rvT", [128, 42])
    k.omm = sbs(k, "rw_omm", [128, 14])
    k.muq = sbs(k, "rw_muq", [128, 14, 4])
    k.mue = sbs(k, "rw_mue", [128, 14, 2])
    k.omka = sbs(k, "rw_omka", [128, 4])
    k.wlora = sbs(k, "rw_wlora", [64, 2, 512])
    k.wg2 = sbs(k, "rw_wg2", [96, 512])
    k.lnbc = sbs(k, "rw_lnbc", [128, 2, 512])
    k.bones = sbs(k, "rw_bones", [128, 128])
    k.hsel = sbs(k, "rw_hsel", [128, 2])
    k.mup = sbs(k, "rw_mup", [128, 256])
    k.mlo = sbs(k, "rw_mlo", [128, 256])
    k.ones = sbs(k, "rw_ones", [128, 128])
    with ExitStack() as es:
        def sbl(name, shape, dt=F32):
            return es.enter_context(nc.sbuf_tensor(name, list(shape), dt))
        d0 = Dep()
        vrow = sbl("rwvrow", [42, 128])
        lnrow = sbl("rwlnrow", [1, 2, 512])
        m4 = sbl("rwm4", [128, 4])
        pm4i = sbl("rwpm4i", [128, 1], I32)
        pm4 = sbl("rwpm4", [128, 1])
        one1 = sbl("rwone1", [1, 128])
        S.dma("sp", lambda e: e.dma_start(out=vrow[:], in_=I["rw_vec"][:, :]), writes=[d0])
        S.dma("sp", lambda e: e.dma_start(out=lnrow[:], in_=I["rw_ln"].rearrange("(o a) n -> o a n", o=1)), writes=[d0])
        S.dma("sp", lambda e: e.dma_start(out=k.wlora[:], in_=I["rw_wlora"]), writes=[k.rwc])
        S.dma("sp", lambda e: e.dma_start(out=k.wg2[:], in_=I["rw_w_g2"][:, :]), writes=[k.rwc])
        with nc.psum_tensor("prw0", [128, 42], F32) as p0, nc.psum_tensor("prw1", [128, 2, 512], F32) as p1:
            pd = PD()
            T("pe", lambda e: e.transpose(p0[:, :], vrow[:, :], k.ident[0:42, 0:42]), reads=[d0, k.ident_d], writes=[pd])
            T("dve", lambda e: e.tensor_copy(k.rvT[:], p0[:]), reads=[pd], writes=[k.rwc])
            T("dve", lambda e: e.memset(one1[:], 1.0), writes=[d0])
            for a in range(2):
                T("pe", _mm(p1[:, a, :], one1[0:1, :], lnrow[0:1, a, :]), reads=[d0], writes=[pd])
            T("dve", lambda e: e.tensor_copy(k.lnbc[:], p1[:]), reads=[pd], writes=[k.rwc])
        T("dve", lambda e: e.tensor_scalar(k.omm[:], k.rvT[:, 0:14], -1.0, 1.0, op0=ALU.mult, op1=ALU.add), reads=[k.rwc], writes=[k.rwc])
        T("dve", lambda e: e.tensor_scalar(k.omka[:], k.rvT[:, 18:22], -1.0, 1.0, op0=ALU.mult, op1=ALU.add), reads=[k.rwc], writes=[k.rwc])
        T("dve", lambda e: e.tensor_single_scalar(pm4i[:], k.iota_p[:], 3, op=ALU.bitwise_and), reads=[k.iota_d], writes=[d0])
        T("dve", lambda e: e.tensor_copy(pm4[:], pm4i[:]), reads=[d0], writes=[d0])
        T("dve", lambda e: e.tensor_scalar(m4[:], k.iota_ff[:, 0:4], pm4[:, 0:1], None, op0=ALU.is_equal), reads=[d0, k.ident_d], writes=[d0])
        T("dve", lambda e: e.tensor_tensor(k.muq[:], k.rvT[:, 0:14].unsqueeze(2).to_broadcast([128, 14, 4]),
                                           m4[:].unsqueeze(1).to_broadcast([128, 14, 4]), op=ALU.mult), reads=[d0, k.rwc], writes=[k.rwc])
        T("dve", lambda e: e.tensor_tensor(k.mue[:], k.muq[:, :, 0:2], k.muq[:, :, 2:4], op=ALU.add), reads=[k.rwc], writes=[k.rwc])
        T("dve", lambda e: e.memset(k.bones[:], 0.0), writes=[k.rwc])
        T("dve", lambda e: e.memset(k.bones[0:64, 0:64], 1.0), writes=[k.rwc])
        T("dve", lambda e: e.memset(k.bones[64:128, 64:128], 1.0), writes=[k.rwc])
        T("dve", lambda e: e.memset(k.hsel[:], 0.0), writes=[k.rwc])
        T("dve", lambda e: e.memset(k.hsel[0:64, 0:1], 1.0), writes=[k.rwc])
        T("dve", lambda e: e.memset(k.hsel[64:128, 1:2], 1.0), writes=[k.rwc])
        T("dve", lambda e: e.memset(k.ones[:], 1.0), writes=[k.rwc])
        for (tile_, c0, op) in [(k.mup, 0, ALU.is_gt), (k.mup, 128, ALU.is_ge), (k.mlo, 0, ALU.is_lt), (k.mlo, 128, ALU.is_le)]:
            T("dve", lambda e, tile_=tile_, c0=c0, op=op: e.tensor_scalar(tile_[:, c0:c0 + 128], k.iota_ff[:], k.iota_pf[:, 0:1], None, op0=op),
              reads=[k.ident_d], writes=[k.rwc])
    S.barrier()


def stage3_rwkv(k):
    nc, S, I, NB = k.nc, k.S, k.I, k.NB
    T = S.op
    NCH = LT // 128
    C = 128
    import os
    with ExitStack() as es:
        def sbl(name, shape, dt=F32):
            return es.enter_context(nc.sbuf_tensor(name, list(shape), dt))

        def psl(name, shape):
            return es.enter_context(nc.psum_tensor(name, list(shape), F32))
        WD = BF16 if os.environ.get("RW_BF16", "1") == "1" else F32
        ND = F32 if os.environ.get("RW_NEU32", "1") == "1" else WD
        lora = sbl("rw_lora", [128, LT]); lora_d = Dep()
        sg = sbl("rw_sg", [128, LT]); sg_d = Dep()
        zb = sbl("rw_zb", [128, LT]); zb_d = Dep()
        rT = sbl("rw_r", [128, LT]); kT = sbl("rw_k", [128, LT]); kkT = sbl("rw_kk", [128, LT])
        base_d = Dep()
        Vm = sbl("rw_Vm", [128, NCH, 128]); Vm_d = Dep()
        tA = sbl("rw_tA", [128, LT]); tB = sbl("rw_tB", [128, LT]); tC = sbl("rw_tC", [128, LT]); tD = sbl("rw_tD", [128, LT])
        tmp_d = Dep()
        ARt = sbl("rw_AR", [128, NCH, 2, C], WD); Bt = sbl("rw_Bt", [128, LT], WD); Kt = sbl("rw_Kt", [128, LT], WD)
        Vmb = sbl("rw_Vmb", [128, NCH, 128], WD); identb = sbl("rw_identb", [128, 128], WD); Tstb = sbl("rw_Tb", [128, 64], WD)
        T("dve", lambda e: e.tensor_copy(identb[:], k.ident[:]), reads=[k.ident_d], writes=[k.rwc])
        feat_d = Dep()
        PC = sbl("rw_PC", [128, NCH]); tot = sbl("rw_tot", [128, NCH])
        kdsum = sbl("rw_kdsum", [128, LT]); kds_d = Dep()
        Ysum = sbl("rw_Y", [128, 16, 128]); Y_d = Dep()
        gtm = sbl("rw_gtm", [128, 16, 128]); gtm_d = Dep()
        coef = sbl("rw_coef", [128, 16, 2]); coef_d = Dep()
        gn = [sbl(f"rw_gn{i}", [128, 32]) for i in range(4)]
        dmy = sbl("rw_dmy", [128, 2])
        Tst = sbl("rw_T", [128, 64]); T_d = Dep()
        T_dh = [Dep(), Dep()]
        Y_dh = [Dep(), Dep()]
        Ttmp = sbl("rw_Ttmp", [128, 64])
        NN = [sbl(f"rw_N{i}", [128, 128], ND) for i in range(8)]; NN_d = [Dep() for _ in range(8)]
        NT_ = [sbl(f"rw_NT{i}", [128, 128], ND) for i in range(8)]; NT_d = [Dep() for _ in range(8)]
        XX = [sbl(f"rw_X{i}", [128, 128], ND) for i in range(8)]; XX_d = [Dep() for _ in range(8)]
        AA = [sbl(f"rw_AA{i}", [128, 512], WD) for i in range(4)]; AA_d = [Dep() for _ in range(4)]
        AN = [sbl(f"rw_AN{i}", [128, 128], ND) for i in range(4)]
        Wsb = [sbl(f"rw_W{i}", [128, 64], ND) for i in range(2)]; Wsb_d = [Dep(), Dep()]
        Usb = [sbl(f"rw_U{i}", [128, 64], WD) for i in range(2)]; Usb_d = [Dep(), Dep()]
        BKtm = [sbl(f"rw_BK{i}", [128, 2, 128], WD) for i in range(2)]; BK_d = [Dep(), Dep()]
        ot = [sbl(f"rw_ot{i}", [128, 512]) for i in range(2)]; ot_d = [Dep(), Dep()]
        pA = [psl(f"rw_pA{i}", [128, 512]) for i in range(2)]; pA_d = [PD(), PD()]
        pN = [psl(f"rw_pN{i}", [128, 512]) for i in range(4)]; pN_d = [PD() for _ in range(4)]
        pS = [psl(f"rw_pS{i}", [128, 512]) for i in range(2)]; pS_d = [PD(), PD()]
        cnt = {"pn": 0, "pa": 0, "nn": 0, "nt": 0, "xx": 0, "hc": 0}

        def next_pn():
            i = cnt["pn"] % 4
            cnt["pn"] += 1
            return pN[i][:, 0:128], pN_d[i]

        def next_pa():
            i = cnt["pa"] % 2
            cnt["pa"] += 1
            return pA[i], pA_d[i]

        def mix_chunk(b, ch, nrows, dst, dst_d):
            r0 = 512 + (ch * 128 if ch < 12 else (1536 if ch == 12 else 1600))
            S.dma("sp", lambda e: e.dma_start(out=zb[0:nrows, :], in_=k.PT[b, r0:r0 + nrows, :]), reads=[k.PT_dep[b]], writes=[zb_d])
            P = slice(0, nrows)
            T("dve", lambda e: e.tensor_scalar(dst[P, :], zb[P, :], k.omm[P, ch:ch + 1], None, op0=ALU.mult),
              reads=[zb_d, k.rwc], writes=[dst_d])

            def acc(o, i_, sc):
                T("dve", lambda e: e.scalar_tensor_tensor(o, i_, sc, o, op0=ALU.mult, op1=ALU.add), reads=[zb_d, k.rwc, dst_d], writes=[dst_d])
            zl = zb[P, CTX:LT].rearrange("p (r c) -> p r c", c=64)
            dl = dst[P, CTX:LT].rearrange("p (r c) -> p r c", c=64)
            acc(dl[:, :, 1:64], zl[:, :, 0:63], k.muq[P, ch, 0:1])
            acc(dl[:, :, 0:63], zl[:, :, 1:64], k.muq[P, ch, 1:2])
            acc(dst[P, CTX + 64:LT], zb[P, CTX:LT - 64], k.muq[P, ch, 2:3])
            acc(dst[P, CTX:LT - 64], zb[P, CTX + 64:LT], k.muq[P, ch, 3:4])
            acc(dst[P, 1:CTX], zb[P, 0:CTX - 1], k.mue[P, ch, 0:1])
            acc(dst[P, 0:CTX - 1], zb[P, 1:CTX], k.mue[P, ch, 1:2])

        blocks = [(0, 512), (512, 512), (1024, 512), (1536, 512), (2048, 256)]
        import os
        STOP = int(os.environ.get("RW_STOP", "99"))
        for b in range(NB):
            mix_chunk(b, 12, 64, lora, lora_d)
            T("act", lambda e: e.activation(lora[0:32, :], lora[0:32, :], AF.Tanh), reads=[lora_d], writes=[lora_d])
            mix_chunk(b, 13, 96, sg, sg_d)
            T("act", lambda e: e.activation(sg[0:96, :], sg[0:96, :], AF.Sigmoid), reads=[sg_d], writes=[sg_d])
            if STOP <= 1:
                break
            for hp in range(4):
                mix_chunk(b, hp, 128, rT, base_d)
                mix_chunk(b, 4 + hp, 128, kT, base_d)
                mix_chunk(b, 8 + hp, 128, tA, tmp_d)
                for ci in range(NCH):
                    pp, ppd = next_pn()
                    T("pe", lambda e, pp=pp, ci=ci: e.transpose(pp, tA[:, ci * 128:(ci + 1) * 128], k.ident[:]),
                      reads=[tmp_d, k.ident_d], writes=[ppd])
                    T("act", lambda e, pp=pp, ci=ci: e.copy(Vm[:, ci, :], pp), reads=[ppd], writes=[Vm_d])
                    T("dve", lambda e, pp=pp, ci=ci: e.tensor_copy(Vmb[:, ci, :], pp), reads=[ppd], writes=[Vm_d])
                T("dve", lambda e: e.tensor_scalar(kkT[:], kT[:], k.rvT[:, 14 + hp:15 + hp], None, op0=ALU.mult), reads=[base_d, k.rwc], writes=[base_d])
                T("dve", lambda e: e.tensor_tensor(tB[:], kkT[:], kkT[:], op=ALU.mult), reads=[base_d], writes=[tmp_d])
                for (c0, n) in blocks:
                    pp, ppd = next_pa()
                    T("pe", _mm(pp[:, 0:n], k.bones[:], tB[:, c0:c0 + n]), reads=[tmp_d, k.rwc], writes=[ppd])
                    T("act", lambda e, pp=pp, c0=c0, n=n: e.activation(tC[:, c0:c0 + n], pp[:, 0:n], AF.Sqrt, bias=1e-12), reads=[ppd], writes=[tmp_d])
                T("dve", lambda e: e.reciprocal(tC[:], tC[:]), reads=[tmp_d], writes=[tmp_d])
                T("dve", lambda e: e.tensor_tensor(kkT[:], kkT[:], tC[:], op=ALU.mult), reads=[tmp_d, base_d], writes=[base_d])
                for lc in range(16):
                    pp, ppd = next_pn()
                    T("pe", _mm(pp, sg[0:96, CTX + lc * 128:CTX + (lc + 1) * 128], k.wg2[0:96, hp * 128:(hp + 1) * 128]),
                      reads=[sg_d, k.rwc], writes=[ppd])
                    T("act", lambda e, pp=pp, lc=lc: e.copy(gtm[:, lc, :], pp), reads=[ppd], writes=[gtm_d])
                if STOP <= 2:
                    break
                for d in range(2):
                    for (c0, n) in blocks:
                        pp, ppd = next_pa()
                        T("pe", _mm(pp[:, 0:n], k.wlora[0:32, d, hp * 128:(hp + 1) * 128], lora[0:32, c0:c0 + n]), reads=[lora_d, k.rwc], writes=[ppd])
                        T("act", lambda e, pp=pp, c0=c0, n=n, d=d, hp=hp: e.activation(
                            tA[:, c0:c0 + n], pp[:, 0:n], AF.Sigmoid, bias=k.rvT[:, 26 + d * 4 + hp:27 + d * 4 + hp]), reads=[ppd, k.rwc], writes=[tmp_d])
                        pp, ppd = next_pa()
                        T("pe", _mm(pp[:, 0:n], k.wlora[32:64, d, hp * 128:(hp + 1) * 128], lora[32:64, c0:c0 + n], tp=(32, 0)),
                          reads=[lora_d, k.rwc], writes=[ppd])
                        T("act", lambda e, pp=pp, c0=c0, n=n, d=d, hp=hp: e.activation(
                            tB[:, c0:c0 + n], pp[:, 0:n], AF.Sigmoid, bias=k.rvT[:, 34 + d * 4 + hp:35 + d * 4 + hp]), reads=[ppd, k.rwc], writes=[tmp_d])
                    T("dve", lambda e: e.tensor_scalar(tA[:], tA[:], -0.6065306597126334, None, op0=ALU.mult), reads=[tmp_d], writes=[tmp_d])
                    T("dve", lambda e: e.tensor_scalar(tC[:], tB[:], k.rvT[:, 18 + hp:19 + hp], k.omka[:, hp:hp + 1], op0=ALU.mult, op1=ALU.add),
                      reads=[tmp_d, k.rwc], writes=[tmp_d])
                    T("dve", lambda e: e.tensor_tensor(tC[:], tC[:], kT[:], op=ALU.mult), reads=[tmp_d, base_d], writes=[tmp_d])
                    if d == 0:
                        T("pool", lambda e: e.tensor_copy(kdsum[:], tC[:]), reads=[tmp_d], writes=[kds_d])
                    else:
                        T("pool", lambda e: e.tensor_tensor(kdsum[:], kdsum[:], tC[:], op=ALU.add), reads=[tmp_d, kds_d], writes=[kds_d])
                    for ci in range(NCH):
                        T("dve", lambda e, ci=ci: e.tensor_tensor_scan(tD[:, ci * C:(ci + 1) * C], k.ones[:], tA[:, ci * C:(ci + 1) * C], 0.0,
                                                                      op0=ALU.mult, op1=ALU.add), reads=[tmp_d, k.rwc], writes=[tmp_d])
                    tDv = tD[:].rearrange("p (c t) -> p c t", t=C)
                    T("dve", lambda e: e.tensor_copy(tot[:], tDv[:, :, C - 1]), reads=[tmp_d], writes=[feat_d])
                    if d == 1:
                        T("dve", lambda e: e.tensor_tensor(tD[:], tA[:], tD[:], op=ALU.subtract), reads=[tmp_d], writes=[tmp_d])
                        T("dve", lambda e: e.tensor_tensor(tDv, tDv, tot[:].unsqueeze(2).to_broadcast([128, NCH, C]), op=ALU.add),
                          reads=[tmp_d, feat_d], writes=[tmp_d])
                    T("act", lambda e: e.activation(PC[:], tot[:], AF.Exp), reads=[feat_d], writes=[feat_d])
                    ARv0 = ARt[:, :, 0, :]
                    ARv1 = ARt[:, :, 1, :]
                    tAv = tA[:].rearrange("p (c t) -> p c t", t=C)
                    T("dve", lambda e: e.tensor_tensor(tA[:], tD[:], tA[:], op=ALU.subtract), reads=[tmp_d], writes=[tmp_d])
                    T("act", lambda e: e.activation(tA[:], tA[:], AF.Exp), reads=[tmp_d], writes=[tmp_d])
                    T("dve", lambda e: e.scalar_tensor_tensor(ARv0, kkT[:].rearrange("p (c t) -> p c t", t=C), -1.0, tAv, op0=ALU.mult, op1=ALU.mult),
                      reads=[tmp_d, base_d], writes=[feat_d])
                    T("act", lambda e: e.activation(tA[:], tD[:], AF.Exp), reads=[tmp_d, feat_d], writes=[tmp_d])
                    T("dve", lambda e: e.tensor_tensor(ARv1, tAv, rT[:].rearrange("p (c t) -> p c t", t=C), op=ALU.mult),
                      reads=[tmp_d, base_d], writes=[feat_d])
                    T("act", lambda e: e.activation(tD[:], tD[:], AF.Exp, scale=-1.0), reads=[tmp_d], writes=[tmp_d])
                    T("dve", lambda e: e.tensor_tensor(tA[:], kkT[:], tB[:], op=ALU.mult), reads=[tmp_d, base_d, feat_d], writes=[tmp_d])
                    T("dve", lambda e: e.tensor_tensor(Bt[:], tA[:], tD[:], op=ALU.mult), reads=[tmp_d], writes=[feat_d])
                    T("dve", lambda e: e.tensor_tensor(Kt[:], tC[:], tD[:], op=ALU.mult), reads=[tmp_d], writes=[feat_d])
                    if STOP <= 3:
                        break
                    T("dve", lambda e: e.memset(Tst[:], 0.0), writes=[T_dh[0], T_dh[1]])
                    T("dve", lambda e: e.memset(Tstb[:], 0.0), writes=[T_dh[0], T_dh[1]])
                    order = list(range(NCH)) if d == 0 else [1, 0] + list(range(NCH - 1, 1, -1))
                    SUB = int(os.environ.get("RW_SUB", "99"))
                    order = order[:int(os.environ.get("RW_NCH", "99"))]
                    m2 = k.mup if d == 0 else k.mlo
                    mT = k.mlo if d == 0 else k.mup
                    Xfin = {}

                    def bk_chain(ci, par):
                        cs = slice(ci * C, (ci + 1) * C)
                        bk, bkd = BKtm[par], BK_d[par]
                        for which, src in enumerate((Bt, Kt)):
                            pp, ppd = next_pn()
                            T("pe", _mm(pp, src[:, cs], identb[:]), reads=[feat_d, k.rwc], writes=[ppd])
                            T("act", lambda e, pp=pp, bk=bk, which=which: e.copy(bk[:, which, :], pp), reads=[ppd], writes=[bkd])
                            yield

                    def neu_chain(hh, ci, par):
                        cs = slice(ci * C, (ci + 1) * C)
                        ph = slice(64 * hh, 64 * hh + 64)
                        tpk = (64 * hh, 0)
                        pa, pad = pA[hh], pA_d[hh]
                        ni = hh * 2 + par
                        aa, aad = AA[ni], AA_d[ni]
                        arr = ARt[ph, ci, :, :].rearrange("p a t -> p (a t)")
                        T("pe", _mm(pa[:, 0:256], Bt[ph, cs], arr, tp=tpk), reads=[feat_d], writes=[pad])
                        T("pe", _mm(pa[:, 256:512], Kt[ph, cs], arr, tp=tpk), reads=[feat_d], writes=[pad])
                        yield
                        T("dve", lambda e: e.tensor_tensor(
                            aa[:].rearrange("p (a t) -> p a t", a=2), pa[:].rearrange("p (a t) -> p a t", a=2),
                            m2[:].unsqueeze(1).to_broadcast([128, 2, 256]), op=ALU.mult), reads=[pad, k.rwc], writes=[aad])
                        an = AN[ni]
                        T("dve", lambda e: e.tensor_tensor(an[:], pa[:, 0:128], m2[:, 0:128], op=ALU.mult), reads=[pad, k.rwc], writes=[aad])
                        p3, p3d = next_pn()
                        T("pe", _mm(p3, ARt[ph, ci, 0, :], Bt[ph, cs], tp=tpk), reads=[feat_d], writes=[p3d])
                        yield
                        nt0, nt0d = NT_[2 * ni], NT_d[2 * ni]
                        T("dve", lambda e: e.tensor_tensor(nt0[:], p3, mT[:, 0:128], op=ALU.mult), reads=[p3d, k.rwc], writes=[nt0d])
                        x0, x0d = XX[2 * ni], XX_d[2 * ni]
                        T("pool", lambda e: e.tensor_tensor(x0[:], an[:], k.ident[:], op=ALU.add), reads=[aad, k.ident_d], writes=[x0d])
                        curN, curNd = an[:], aad
                        curNT, curNTd = nt0, nt0d
                        curX, curXd = x0, x0d
                        for lv in range(1, 7):
                            nxtNT, nxtNTd = NT_[2 * ni + (lv % 2)], NT_d[2 * ni + (lv % 2)]
                            pq, pqd = next_pn()
                            T("pe", _mm(pq, curN, curNT[:]), reads=[curNd, curNTd], writes=[pqd])
                            T("act", lambda e, pq=pq, nxtNT=nxtNT: e.copy(nxtNT[:], pq), reads=[pqd], writes=[nxtNTd])
                            yield
                            if lv < 6:
                                nxtN, nxtNd = NN[2 * ni + (lv % 2)], NN_d[2 * ni + (lv % 2)]
                                pq2, pq2d = next_pn()
                                T("pe", _mm(pq2, curNT[:], curN), reads=[curNd, curNTd], writes=[pq2d])
                                T("act", lambda e, pq2=pq2, nxtN=nxtN: e.copy(nxtN[:], pq2), reads=[pq2d], writes=[nxtNd])
                                yield
                            nxtX, nxtXd = XX[2 * ni + (lv % 2)], XX_d[2 * ni + (lv % 2)]
                            pq3, pq3d = next_pn()
                            T("pe", _mm(pq3, nxtNT[:], curX[:]), reads=[nxtNTd, curXd], writes=[pq3d])
                            T("dve", lambda e, pq3=pq3, nxtX=nxtX, curX=curX: e.tensor_tensor(nxtX[:], pq3, curX[:], op=ALU.add),
                              reads=[pq3d, curXd], writes=[nxtXd])
                            yield
                            if lv < 6:
                                curN, curNd = nxtN[:], nxtNd
                            curNT, curNTd = nxtNT, nxtNTd
                            curX, curXd = nxtX, nxtXd
                        Xfin[(hh, par)] = (curX, curXd)

                    def state_chain(hh, ci, par):
                        is_lat = ci >= 2
                        ph = slice(64 * hh, 64 * hh + 64)
                        tpk = (64 * hh, 0)
                        psb, psd = pS[hh], pS_d[hh]
                        Td = T_dh[hh]
                        ni = hh * 2 + par
                        aa, aad = AA[ni], AA_d[ni]
                        bk, bkd = BKtm[par], BK_d[par]
                        curX, curXd = Xfin[(hh, par)]
                        vh = Vmb[:, ci, ph]
                        T("pe", _mm(psb[:, 0:64], aa[:, 256:384], vh, start=True, stop=False), reads=[aad, Vm_d], writes=[psd])
                        T("pe", _mm(psb[:, 0:64], ARt[ph, ci, 0, :], Tstb[ph, :], start=False, stop=True, tp=tpk),
                          reads=[feat_d, Td], writes=[psd])
                        wsb, wsd = Wsb[hh], Wsb_d[hh]
                        T("act", lambda e: e.copy(wsb[:], psb[:, 0:64]), reads=[psd], writes=[wsd])
                        yield
                        T("pe", _mm(psb[:, 64:128], curX[:], wsb[:]), reads=[curXd, wsd], writes=[psd])
                        usb, usd = Usb[hh], Usb_d[hh]
                        T("act", lambda e: e.copy(usb[:], psb[:, 64:128]), reads=[psd], writes=[usd])
                        yield
                        if is_lat:
                            yo = psb[:, 128:192]
                            T("pe", _mm(yo, ARt[ph, ci, 1, :], Tstb[ph, :], start=True, stop=False, tp=tpk), reads=[feat_d, Td], writes=[psd])
                            T("pe", _mm(yo, aa[:, 128:256], usb[:], start=False, stop=False), reads=[aad, usd], writes=[psd])
                            T("pe", _mm(yo, aa[:, 384:512], vh, start=False, stop=True), reads=[aad, Vm_d], writes=[psd])
                        to = psb[ph, 192:256]
                        T("pe", _mm(to, bk[:, 0, ph], usb[:], start=True, stop=False, tp=(0, 64 * hh)), reads=[bkd, usd], writes=[psd])
                        T("pe", _mm(to, bk[:, 1, ph], vh, start=False, stop=True, tp=(0, 64 * hh)), reads=[bkd, Vm_d], writes=[psd])
                        yield
                        if is_lat:
                            lc = ci - 2
                            ys = Ysum[:, lc, ph]
                            if d == 0:
                                T("act", lambda e: e.copy(ys, psb[:, 128:192]), reads=[psd], writes=[Y_dh[hh]])
                            else:
                                T("dve", lambda e: e.tensor_tensor(ys, psb[:, 128:192], ys, op=ALU.add), reads=[psd, Y_dh[hh]], writes=[Y_dh[hh]])
                        T("dve", lambda e: e.tensor_tensor(Ttmp[ph, :], psb[ph, 192:256], Tst[ph, :], op=ALU.add), reads=[psd, Td], writes=[Td])
                        T("dve", lambda e: e.tensor_scalar(Tst[ph, :], Ttmp[ph, :], PC[ph, ci:ci + 1], None, op0=ALU.mult), reads=[Td, feat_d], writes=[Td])
                        T("act", lambda e: e.copy(Tstb[ph, :], Tst[ph, :]), reads=[Td], writes=[Td])
                        yield

                    def run_chains(chains):
                        while chains:
                            for g_ in list(chains):
                                try:
                                    next(g_)
                                except StopIteration:
                                    chains.remove(g_)

                    if order:
                        run_chains([bk_chain(order[0], 0), neu_chain(0, order[0], 0), neu_chain(1, order[0], 0)])
                    for idx, ci in enumerate(order):
                        par = idx % 2
                        chains = [state_chain(0, ci, par), state_chain(1, ci, par)]
                        if idx + 1 < len(order):
                            nci = order[idx + 1]
                            chains += [bk_chain(nci, 1 - par), neu_chain(0, nci, 1 - par), neu_chain(1, nci, 1 - par)]
                        run_chains(chains)
                if STOP <= 4:
                    break
                T("dve", lambda e: e.tensor_tensor(kdsum[:], kdsum[:], rT[:], op=ALU.mult), reads=[kds_d, base_d], writes=[kds_d])
                T("dve", lambda e: e.tensor_scalar(kdsum[:], kdsum[:], k.rvT[:, 22 + hp:23 + hp], None, op0=ALU.mult), reads=[kds_d, k.rwc], writes=[kds_d])
                pp, ppd = next_pa()
                for lc in range(16):
                    T("pe", _mm(pp[:, 2 * lc:2 * lc + 2], kdsum[:, CTX + lc * 128:CTX + (lc + 1) * 128], k.hsel[:]), reads=[kds_d, k.rwc], writes=[ppd])
                T("dve", lambda e, pp=pp: e.tensor_copy(coef[:].rearrange("p a b -> p (a b)"), pp[:, 0:32]), reads=[ppd], writes=[coef_d])
                Yv = Ysum[:].rearrange("p c (h v) -> p (c h) v", v=64)
                ssum, ssq, mu_, rs_ = [g[:] for g in gn]
                GD = Dep()
                T("dve", lambda e: e.memset(dmy[:, 0:1], 0.0), reads=[Y_dh[0], Y_dh[1]], writes=[Y_d, Y_dh[0], Y_dh[1]])
                T("dve", lambda e: e.tensor_reduce(ssum, Yv, axis=AX.X, op=ALU.add), reads=[Y_d], writes=[GD])
                tAv3 = tA[:, 0:2048].rearrange("p (c v) -> p c v", v=64)
                T("dve", lambda e: e.tensor_tensor(tAv3, Yv, Yv, op=ALU.mult), reads=[Y_d, tmp_d], writes=[tmp_d])
                T("dve", lambda e: e.tensor_reduce(ssq, tAv3, axis=AX.X, op=ALU.add), reads=[tmp_d], writes=[GD])
                T("dve", lambda e: e.tensor_scalar(mu_, ssum, 1.0 / 64, None, op0=ALU.mult), reads=[GD], writes=[GD])
                T("dve", lambda e: e.tensor_tensor(ssum, mu_, mu_, op=ALU.mult), reads=[GD], writes=[GD])
                T("dve", lambda e: e.scalar_tensor_tensor(ssq, ssq, 1.0 / 64, ssum, op0=ALU.mult, op1=ALU.subtract), reads=[GD], writes=[GD])
                T("act", lambda e: e.activation(ssq, ssq, AF.Sqrt, bias=64e-5), reads=[GD], writes=[GD])
                T("dve", lambda e: e.reciprocal(rs_, ssq), reads=[GD], writes=[GD])
                T("dve", lambda e: e.tensor_tensor(Yv, Yv, mu_.unsqueeze(2).to_broadcast([128, 32, 64]), op=ALU.subtract), reads=[GD, Y_d], writes=[Y_d])
                T("dve", lambda e: e.tensor_tensor(Yv, Yv, rs_.unsqueeze(2).to_broadcast([128, 32, 64]), op=ALU.mult), reads=[GD, Y_d], writes=[Y_d])
                lnw = k.lnbc[:, 0, hp * 128:(hp + 1) * 128].unsqueeze(1).to_broadcast([128, 16, 128])
                lnb = k.lnbc[:, 1, hp * 128:(hp + 1) * 128].unsqueeze(1).to_broadcast([128, 16, 128])
                T("dve", lambda e: e.tensor_tensor(Ysum[:], Ysum[:], lnw, op=ALU.mult), reads=[Y_d, k.rwc], writes=[Y_d])
                T("dve", lambda e: e.tensor_tensor(Ysum[:], Ysum[:], lnb, op=ALU.add), reads=[Y_d, k.rwc], writes=[Y_d])
                Vl = Vm[:, 2:18, :].rearrange("p c (h v) -> p (c h) v", v=64)
                T("dve", lambda e: e.tensor_tensor(tAv3, Vl, coef[:].rearrange("p a b -> p (a b)").unsqueeze(2).to_broadcast([128, 32, 64]), op=ALU.mult),
                  reads=[Vm_d, coef_d, tmp_d], writes=[tmp_d])
                T("dve", lambda e: e.tensor_tensor(Yv, Yv, tAv3, op=ALU.add), reads=[tmp_d, Y_d], writes=[Y_d])
                T("dve", lambda e: e.tensor_tensor(Ysum[:], Ysum[:], gtm[:], op=ALU.mult), reads=[Y_d, gtm_d], writes=[Y_d])
                for tb in range(4):
                    o_, od_ = ot[tb % 2], ot_d[tb % 2]
                    pp, ppd = next_pa()
                    for q in range(4):
                        T("pe", lambda e, pp=pp, q=q, tb=tb: e.transpose(pp[:, q * 128:(q + 1) * 128], Ysum[:, tb * 4 + q, :], k.ident[:]),
                          reads=[Y_d, k.ident_d], writes=[ppd])
                    T("act", lambda e, pp=pp, o_=o_: e.copy(o_[:], pp[:]), reads=[ppd], writes=[od_])
                    S.dma("sp", lambda e, o_=o_, tb=tb, b=b, hp=hp: e.dma_start(
                        out=k.MIXT[b, 512 + hp * 128:512 + (hp + 1) * 128, tb * 512:(tb + 1) * 512], in_=o_[:]),
                        reads=[od_], writes=[k.MIXT_dep[b]])
                T("dve", lambda e: e.memset(dmy[:, 1:2], 0.0), writes=[Y_d, Y_dh[0], Y_dh[1]])
        S.barrier()


_PEER_CACHE = {}


def _peer_layout(inputs):
    f = lambda a: np.ascontiguousarray(a, dtype=np.float32)
    key = id(inputs["peer_u"])
    if key not in _PEER_CACHE:
        _PEER_CACHE.clear()
        _PEER_CACHE[key] = {
            "w_out": f(inputs["w_out"][0]),
            "peer_w_q": f(inputs["peer_w_q"][0]),
            "peer_keys": f(np.transpose(inputs["peer_keys"][0], (2, 0, 1, 3)).reshape(128, 16, 128)),
            "peer_uv": f(np.concatenate([inputs["peer_u"][0], inputs["peer_v"][0]], axis=1)),
            "nvec": f(np.stack([inputs["norm2_g"][0], inputs["norm_f_g"]], 0)),
        }
    return _PEER_CACHE[key]


def cast_uv(k):
    nc, S, I = k.nc, k.S, k.I
    T = S.op
    with ExitStack() as es:
        fb = [es.enter_context(nc.sbuf_tensor(f"cv_f{i}", [128, 4, 2048], F32)) for i in range(2)]
        bb = [es.enter_context(nc.sbuf_tensor(f"cv_b{i}", [128, 4, 2048], BF16)) for i in range(2)]
        fd = [Dep(), Dep()]
        bd = [Dep(), Dep()]
        for i in range(32):
            f_, fdd, b_, bdd = fb[i % 2], fd[i % 2], bb[i % 2], bd[i % 2]
            src = I["peer_uv"][i * 512:(i + 1) * 512, :].rearrange("(p r) n -> p r n", r=4)
            dst = k.UVB[i * 512:(i + 1) * 512, :].rearrange("(p r) n -> p r n", r=4)
            S.dma("sp", lambda e, f_=f_, src=src: e.dma_start(out=f_[:], in_=src), writes=[fdd])
            if i % 2 == 0:
                T("act", lambda e, f_=f_, b_=b_: e.copy(b_[:], f_[:]), reads=[fdd], writes=[bdd])
            else:
                T("dve", lambda e, f_=f_, b_=b_: e.tensor_copy(b_[:], f_[:]), reads=[fdd], writes=[bdd])
            S.dma("act", lambda e, b_=b_, dst=dst: e.dma_start(out=dst, in_=b_[:]), reads=[bdd], writes=[k.UVB_dep])
        S.barrier()


def stage4_peer(k):
    nc, S, I, NB = k.nc, k.S, k.I, k.NB
    T = S.op
    import os
    NT4 = int(os.environ.get("P4_TILES", "16"))
    NSLOT = int(os.environ.get("P4_SLOTS", "128"))
    with ExitStack() as es:
        def sbl(name, shape, dt=F32):
            return es.enter_context(nc.sbuf_tensor("P_" + name, list(shape), dt))

        def psl(name):
            return es.enter_context(nc.psum_tensor("PP_" + name, [128, 512], F32))
        cst = Dep()
        wq = sbl("wq", [128, 8, 2048], BF16)
        wo = sbl("wo", [128, 8, 1024], BF16)
        for kd in range(8):
            S.dma("pool", lambda e, kd=kd: e.dma_start(out=wq[:, kd, :], in_=I["peer_w_q"][kd * 128:(kd + 1) * 128, :], max_dma_last_dim=4096), writes=[cst])
            S.dma("pool", lambda e, kd=kd: e.dma_start(out=wo[:, kd, :], in_=I["w_out"][kd * 128:(kd + 1) * 128, :], max_dma_last_dim=4096), writes=[cst])
        keysT = sbl("keysT", [128, 16, 128])
        nbc = sbl("nbc", [128, 2, D])
        ones = sbl("ones", [128, 128])
        io16 = sbl("io16", [128, 16])
        bc = sbl("bc", [128, 4, D]); bc_d = Dep()
        xt = sbl("xt", [128, D]); xt_d = Dep()
        mx = sbl("mx", [128, 8, 128]); mx_d = Dep()
        mxb = sbl("mxb", [128, 8, 128], BF16); mxb_d = Dep()
        h1 = sbl("h1", [128, D]); h1_d = Dep()
        hb = sbl("hb", [128, D]); hb_d = Dep()
        junk = sbl("junk", [128, D]); junk_d = Dep()
        junkb = sbl("junkb", [128, D], BF16)
        hbb = sbl("hbb", [128, D], BF16); hbb_d = Dep()
        hbT = sbl("hbT", [128, 8, 128], BF16); hbT_d = Dep()
        qT = sbl("qT", [128, 16, 128]); qT_d = Dep()
        sc = sbl("sc", [128, 16, 128]); sc_d = Dep()
        sc2 = sbl("sc2", [128, 1, 128]); sc2_d = Dep()
        m16 = sbl("m16", [128, 16, 16]); i16 = sbl("i16", [128, 16, 16], U32); i16f = sbl("i16f", [128, 16, 16])
        cand = sbl("cand", [128, 8, 256]); cand2 = sbl("cand2", [128, 8, 256])
        best = sbl("best", [128, 8, 16]); pos = sbl("pos", [128, 8, 16], U32)
        pa_i = sbl("pa_i", [128, 8, 16], U32); pb_i = sbl("pb_i", [128, 8, 16], U32)
        pa_f = sbl("pa_f", [128, 8, 16]); pb_f = sbl("pb_f", [128, 8, 16])
        eq = cand2[:].rearrange("p h (a b) -> p h a b", b=16)
        i1s = sbl("i1s", [128, 8, 16]); i2s = sbl("i2s", [128, 8, 16])
        idxf = sbl("idxf", [128, 128]); idxi = sbl("idxi", [128, 128], I32)
        gate = sbl("gate", [128, 8, 16]); gsm = sbl("gsm", [128, 8, 2])
        tk_d = Dep()
        actr = sbl("actr", [128, 128]); act_d = Dep()
        agd = [Dep() for _ in range(64)]
        asd = [Dep() for _ in range(128)]
        wgt = sbl("wgt", [128, 128])
        st = sbl("st", [128, 8]); st_d = Dep()
        SPLIT = os.environ.get("P4_SPLIT", "0") == "1"
        NG = int(os.environ.get("P4_NG", "4" if SPLIT else "5"))
        if SPLIT:
            prod = [sbl(f"prod{i}", [128, 1024]) for i in range(2)]; prod_d = [Dep(), Dep()]
        dg = [sbl(f"dg{i}", [128, 128]) for i in range(4)]; dg_d = [Dep() for _ in range(4)]
        dgb = [sbl(f"dgb{i}", [128, 128], BF16) for i in range(4)]; dgb_d = [Dep() for _ in range(4)]
        oo = sbl("oo", [128, D]); oo_d = Dep()
        pb_ = [psl(f"b{i}") for i in range(8)]; pb_d = [PD() for _ in range(8)]
        d0 = Dep()
        with ExitStack() as es2:
            krow = es2.enter_context(nc.sbuf_tensor("P_krow", [128, 16, 128], F32))
            nrow = es2.enter_context(nc.sbuf_tensor("P_nrow", [1, 2, D], F32))
            one1 = es2.enter_context(nc.sbuf_tensor("P_one1", [1, 128], F32))
            S.dma("sp", lambda e: e.dma_start(out=krow[:], in_=I["peer_keys"]), writes=[d0])
            S.dma("sp", lambda e: e.dma_start(out=nrow[:], in_=I["nvec"].rearrange("(o a) n -> o a n", o=1)), writes=[d0])
            T("dve", lambda e: e.memset(one1[:], 1.0), writes=[d0])
            T("dve", lambda e: e.memset(ones[:], 1.0), writes=[cst])
            T("dve", lambda e: e.tensor_copy(io16[:], k.iota_ff[:, 0:16]), reads=[k.ident_d], writes=[cst])
            for j in range(16):
                T("pe", lambda e, j=j: e.transpose(pb_[j % 4][:, 0:128], krow[:, j, :], k.ident[:]), reads=[d0, k.ident_d], writes=[pb_d[j % 4]])
                T("act", lambda e, j=j: e.copy(keysT[:, j, :], pb_[j % 4][:, 0:128]), reads=[pb_d[j % 4]], writes=[cst])
            for a in range(2):
                for hf in range(2):
                    T("pe", _mm(pb_[4 + hf][:, :], one1[0:1, :], nrow[0:1, a, hf * 512:(hf + 1) * 512]), reads=[d0], writes=[pb_d[4 + hf]])
                    T("act", lambda e, a=a, hf=hf: e.copy(nbc[:, a, hf * 512:(hf + 1) * 512], pb_[4 + hf][:, :]), reads=[pb_d[4 + hf]], writes=[cst])
            S.barrier()
        gb = [sbl(f"gb{i}", [128, 2, 2048], BF16) for i in range(NG)]; gb_d = [[Dep(), Dep()] for _ in range(NG)]
        NC5 = NB + 1
        gcnt = 0
        dcnt = 0
        for b in range(NB):
            for vi, j0 in enumerate([16, 32, 24, 40]):
                for jj in range(8):
                    dgt, dgd = dg[dcnt % 4], dg_d[dcnt % 4]
                    pp, ppd = pb_[dcnt % 4], pb_d[dcnt % 4]
                    dcnt += 1
                    T("dve", lambda e, dgt=dgt, j0=j0, jj=jj, b=b: e.tensor_scalar(dgt[:], k.ident[:], k.modT[:, j0 + jj, b:b + 1], None, op0=ALU.mult),
                      reads=[k.ident_d, k.modT_d], writes=[dgd])
                    T("pe", _mm(pp[:, 0:128], ones[:], dgt[:]), reads=[cst, dgd], writes=[ppd])
                    T("act", lambda e, pp=pp, vi=vi, jj=jj: e.copy(bc[:, vi, jj * 128:(jj + 1) * 128], pp[:, 0:128]), reads=[ppd], writes=[bc_d])
            T("dve", lambda e: e.tensor_scalar(bc[:, 1, :], bc[:, 1, :], 1.0, None, op0=ALU.add), reads=[bc_d], writes=[bc_d])
            T("dve", lambda e: e.tensor_tensor(bc[:, 1, :], bc[:, 1, :], nbc[:, 0, :], op=ALU.mult), reads=[bc_d, cst], writes=[bc_d])
            for tt in range(NT4):
                t0 = tt * 128
                S.dma("sp", lambda e, b=b, t0=t0: e.dma_start(out=xt[:], in_=I["x"][b, t0:t0 + 128, :]), writes=[xt_d])
                S.dma("act", lambda e, b=b, t0=t0: e.dma_start(out=mx[:], in_=k.MIXT[b].rearrange("(kc p) t -> p kc t", p=128)[:, :, t0:t0 + 128]),
                      reads=[k.MIXT_dep[b]], writes=[mx_d])
                T("act", lambda e: e.copy(mxb[:], mx[:]), reads=[mx_d], writes=[mxb_d])
                for hf in range(2):
                    for kc in range(8):
                        T("pe", _mm(pb_[hf][:, :], mxb[:, kc, :], wo[:, kc, hf * 512:(hf + 1) * 512], start=(kc == 0), stop=(kc == 7)),
                          reads=[mxb_d, cst], writes=[pb_d[hf]])
                    hs = slice(hf * 512, (hf + 1) * 512)
                    T("dve", lambda e, hf=hf, hs=hs: e.tensor_tensor(h1[:, hs], pb_[hf][:, :], bc[:, 0, hs], op=ALU.mult), reads=[pb_d[hf], bc_d], writes=[h1_d])
                    T("dve", lambda e, hs=hs: e.tensor_tensor(h1[:, hs], h1[:, hs], xt[:, hs], op=ALU.add), reads=[xt_d, h1_d], writes=[h1_d])
                T("act", lambda e: e.activation(junk[:], h1[:], AF.Square, accum_out=st[:, 0:1]), reads=[h1_d], writes=[junk_d, st_d])
                T("act", lambda e: e.activation(st[:, 1:2], st[:, 0:1], AF.Sqrt, bias=1e-6, scale=1.0 / D), reads=[st_d], writes=[st_d])
                T("dve", lambda e: e.reciprocal(st[:, 2:3], st[:, 1:2]), reads=[st_d], writes=[st_d])
                T("act", lambda e: e.activation(hb[:], h1[:], AF.Copy, scale=st[:, 2:3]), reads=[h1_d, st_d], writes=[hb_d])
                T("dve", lambda e: e.tensor_tensor(hb[:], hb[:], bc[:, 1, :], op=ALU.mult), reads=[hb_d, bc_d], writes=[hb_d])
                T("dve", lambda e: e.tensor_tensor(hb[:], hb[:], bc[:, 2, :], op=ALU.add), reads=[hb_d, bc_d], writes=[hb_d])
                T("act", lambda e: e.copy(hbb[:], hb[:]), reads=[hb_d], writes=[hbb_d])
                for kd in range(8):
                    bk_ = 2 + kd // 4
                    T("pe", lambda e, kd=kd, bk_=bk_: e.transpose(pb_[bk_][:, (kd % 4) * 128:(kd % 4 + 1) * 128], hb[:, kd * 128:(kd + 1) * 128], k.ident[:]),
                      reads=[hb_d, k.ident_d], writes=[pb_d[bk_]])
                for q in range(2):
                    T("act", lambda e, q=q: e.copy(hbT[:, q * 4:(q + 1) * 4, :].rearrange("p a t -> p (a t)"), pb_[2 + q][:, :]), reads=[pb_d[2 + q]], writes=[hbT_d])
                for j in range(16):
                    pp, ppd = pb_[4 + j % 2], pb_d[4 + j % 2]
                    for kd in range(8):
                        T("pe", _mm(pp[:, 0:128], wq[:, kd, j * 128:(j + 1) * 128], hbT[:, kd, :], start=(kd == 0), stop=(kd == 7)),
                          reads=[cst, hbT_d], writes=[ppd])
                    T("act", lambda e, pp=pp, j=j: e.copy(qT[:, j, :], pp[:, 0:128]), reads=[ppd], writes=[qT_d])
                for j in range(16):
                    bk_ = j // 4
                    T("pe", _mm(pb_[bk_][:, (j % 4) * 128:(j % 4 + 1) * 128], qT[:, j, :], keysT[:, j, :]), reads=[qT_d, cst], writes=[pb_d[bk_]])
                for q in range(4):
                    T("act", lambda e, q=q: e.copy(sc[:, q * 4:(q + 1) * 4, :].rearrange("p a n -> p (a n)"), pb_[q][:, :]), reads=[pb_d[q]], writes=[sc_d])
                for j in range(16):
                    T("dve", lambda e, j=j: e.max(m16[:, j, 0:8], sc[:, j, :]), reads=[sc_d], writes=[tk_d])
                    T("dve", lambda e, j=j: e.match_replace(sc2[:, 0, :], m16[:, j, 0:8], sc[:, j, :], -3.0e38), reads=[sc_d, tk_d], writes=[sc2_d])
                    T("dve", lambda e, j=j: e.max(m16[:, j, 8:16], sc2[:, 0, :]), reads=[sc2_d], writes=[tk_d])
                    T("dve", lambda e, j=j: e.max_index(i16[:, j, 0:8], m16[:, j, 0:8], sc[:, j, :]), reads=[sc_d, tk_d], writes=[tk_d])
                    T("dve", lambda e, j=j: e.max_index(i16[:, j, 8:16], m16[:, j, 8:16], sc2[:, 0, :]), reads=[sc2_d, tk_d], writes=[tk_d])
                T("dve", lambda e: e.tensor_copy(i16f[:], i16[:]), reads=[tk_d], writes=[tk_d])
                m16v = m16[:].rearrange("p (h c) a -> p h c a", c=2)
                i16v = i16f[:].rearrange("p (h c) a -> p h c a", c=2)
                candv = cand[:].rearrange("p h (a b) -> p h a b", b=16)
                T("dve", lambda e: e.tensor_tensor(candv, m16v[:, :, 0, :].unsqueeze(3).to_broadcast([128, 8, 16, 16]),
                                                   m16v[:, :, 1, :].unsqueeze(2).to_broadcast([128, 8, 16, 16]), op=ALU.add), reads=[tk_d], writes=[tk_d])
                for h in range(8):
                    T("dve", lambda e, h=h: e.max(best[:, h, 0:8], cand[:, h, :]), reads=[tk_d], writes=[tk_d])
                    T("dve", lambda e, h=h: e.match_replace(cand2[:, h, :], best[:, h, 0:8], cand[:, h, :], -3.0e38), reads=[tk_d], writes=[tk_d])
                    T("dve", lambda e, h=h: e.max(best[:, h, 8:16], cand2[:, h, :]), reads=[tk_d], writes=[tk_d])
                    T("dve", lambda e, h=h: e.max_index(pos[:, h, 0:8], best[:, h, 0:8], cand[:, h, :]), reads=[tk_d], writes=[tk_d])
                    T("dve", lambda e, h=h: e.max_index(pos[:, h, 8:16], best[:, h, 8:16], cand2[:, h, :]), reads=[tk_d], writes=[tk_d])
                T("dve", lambda e: e.tensor_tensor(gate[:], best[:], best[:, :, 0:1].to_broadcast([128, 8, 16]), op=ALU.subtract), reads=[tk_d], writes=[tk_d])
                T("act", lambda e: e.activation(gate[:], gate[:], AF.Exp), reads=[tk_d], writes=[tk_d])
                T("dve", lambda e: e.tensor_reduce(gsm[:, :, 0], gate[:], axis=AX.X, op=ALU.add), reads=[tk_d], writes=[tk_d])
                T("dve", lambda e: e.reciprocal(gsm[:, :, 1], gsm[:, :, 0]), reads=[tk_d], writes=[tk_d])
                T("dve", lambda e: e.tensor_tensor(gate[:], gate[:], gsm[:, :, 1:2].to_broadcast([128, 8, 16]), op=ALU.mult), reads=[tk_d], writes=[tk_d])
                T("dve", lambda e: e.tensor_single_scalar(pa_i[:], pos[:], 4, op=ALU.logical_shift_right), reads=[tk_d], writes=[tk_d])
                T("dve", lambda e: e.tensor_single_scalar(pb_i[:], pos[:], 15, op=ALU.bitwise_and), reads=[tk_d], writes=[tk_d])
                T("dve", lambda e: e.tensor_copy(pa_f[:], pa_i[:]), reads=[tk_d], writes=[tk_d])
                T("dve", lambda e: e.tensor_copy(pb_f[:], pb_i[:]), reads=[tk_d], writes=[tk_d])
                io_b = io16[:].unsqueeze(1).unsqueeze(1).to_broadcast([128, 8, 16, 16])
                for (pf, cc, dst) in [(pa_f, 0, i1s), (pb_f, 1, i2s)]:
                    T("dve", lambda e, pf=pf: e.tensor_tensor(eq, pf[:].unsqueeze(3).to_broadcast([128, 8, 16, 16]), io_b, op=ALU.is_equal),
                      reads=[tk_d, cst], writes=[tk_d])
                    T("dve", lambda e, cc=cc: e.tensor_tensor(eq, eq, i16v[:, :, cc, :].unsqueeze(2).to_broadcast([128, 8, 16, 16]), op=ALU.mult),
                      reads=[tk_d], writes=[tk_d])
                    T("dve", lambda e, dst=dst: e.tensor_reduce(dst[:], eq, axis=AX.X, op=ALU.add), reads=[tk_d], writes=[tk_d])
                T("dve", lambda e: e.scalar_tensor_tensor(idxf[:], i1s[:].rearrange("p h k -> p (h k)"), 128.0, i2s[:].rearrange("p h k -> p (h k)"),
                                                          op0=ALU.mult, op1=ALU.add), reads=[tk_d], writes=[tk_d])
                T("dve", lambda e: e.tensor_copy(idxi[:], idxf[:]), reads=[tk_d], writes=[tk_d])
                gflat = gate[:].rearrange("p h k -> p (h k)")
                NGRP = NSLOT // 2
                ginfo = {}

                def stage_a(g):
                    nonlocal gcnt
                    gbt, gbd = gb[gcnt % NG], gb_d[gcnt % NG]
                    gcnt += 1
                    ginfo[g] = (gbt, gbd)
                    for s2 in range(2):
                        slot = g * 2 + s2
                        S.dma("pool", lambda e, gbt=gbt, s2=s2, slot=slot: e.indirect_dma_start(
                            out=gbt[:, s2, :], out_offset=None, in_=k.UVB[:, :],
                            in_offset=bass.IndirectOffsetOnAxis(ap=idxi[:, slot:slot + 1], axis=0)),
                            reads=[tk_d, k.UVB_dep], writes=[gbd[s2]])
                    for s2 in range(2):
                        slot = g * 2 + s2
                        if s2 == 1 and SPLIT:
                            pr_, prd_ = prod[g % 2], prod_d[g % 2]
                            T("pool", lambda e, gbt=gbt, pr_=pr_: e.tensor_tensor(pr_[:], gbt[:, 1, 0:1024], hbb[:], op=ALU.mult),
                              reads=[gbd[1], hbb_d], writes=[prd_])
                            T("act", lambda e, pr_=pr_, slot=slot: e.activation(junk[:], pr_[:], AF.Copy, accum_out=actr[:, slot:slot + 1]),
                              reads=[prd_], writes=[junk_d, asd[slot]])
                            continue
                        T("dve", lambda e, gbt=gbt, s2=s2, slot=slot: e.scalar_tensor_tensor(
                            gbt[:, s2, 0:1024], gbt[:, s2, 0:1024], 1.0, hbb[:], op0=ALU.mult, op1=ALU.mult,
                            accum_out=actr[:, slot:slot + 1]), reads=[hbb_d], writes=[gbd[s2], asd[slot]])
                    sl = slice(g * 2, g * 2 + 2)
                    T("act", lambda e, sl=sl: e.activation(wgt[:, sl], actr[:, sl], AF.Gelu), reads=[asd[g * 2], asd[g * 2 + 1]], writes=[agd[g]])

                def stage_b(g):
                    nonlocal dcnt
                    gbt, gbd = ginfo.pop(g)
                    ad_ = agd[g]
                    sl = slice(g * 2, g * 2 + 2)
                    T("dve", lambda e, sl=sl: e.tensor_tensor(wgt[:, sl], wgt[:, sl], gflat[:, sl], op=ALU.mult), reads=[ad_, tk_d], writes=[ad_])
                    for s2 in range(2):
                        slot = g * 2 + s2
                        dgt, dgd = dgb[dcnt % 4], dgb_d[dcnt % 4]
                        dcnt += 1
                        T("act", lambda e, dgt=dgt, slot=slot: e.activation(dgt[:], k.ident[:], AF.Copy, scale=wgt[:, slot:slot + 1]),
                          reads=[k.ident_d, ad_], writes=[dgd])
                        for hf in range(2):
                            T("pe", _mm(pb_[6 + hf][:, :], dgt[:], gbt[:, s2, 1024 + hf * 512:1024 + (hf + 1) * 512],
                                        start=(slot == 0), stop=(slot == NSLOT - 1)), reads=[dgd, gbd[s2]], writes=[pb_d[6 + hf]])

                SKEW = 2
                for g in range(NGRP + SKEW):
                    if g < NGRP:
                        stage_a(g)
                    if g >= SKEW:
                        stage_b(g - SKEW)
                for hf in range(2):
                    hs = slice(hf * 512, (hf + 1) * 512)
                    T("dve", lambda e, hf=hf, hs=hs: e.tensor_tensor(oo[:, hs], pb_[6 + hf][:, :], bc[:, 3, hs], op=ALU.mult), reads=[pb_d[6 + hf], bc_d], writes=[oo_d])
                    T("dve", lambda e, hs=hs: e.tensor_tensor(oo[:, hs], oo[:, hs], h1[:, hs], op=ALU.add), reads=[oo_d, h1_d], writes=[oo_d])
                T("act", lambda e: e.activation(junk[:], oo[:], AF.Square, accum_out=st[:, 4:5]), reads=[oo_d], writes=[junk_d, st_d])
                T("act", lambda e: e.activation(st[:, 5:6], st[:, 4:5], AF.Sqrt, bias=1e-6, scale=1.0 / D), reads=[st_d], writes=[st_d])
                T("dve", lambda e: e.reciprocal(st[:, 6:7], st[:, 5:6]), reads=[st_d], writes=[st_d])
                T("act", lambda e: e.activation(oo[:], oo[:], AF.Copy, scale=st[:, 6:7]), reads=[oo_d, st_d], writes=[oo_d])
                T("dve", lambda e: e.tensor_tensor(oo[:], oo[:], nbc[:, 1, :], op=ALU.mult), reads=[oo_d, cst], writes=[oo_d])
                S.dma("sp", lambda e, b=b, t0=t0: e.dma_start(out=k.out[b, t0:t0 + 128, :], in_=oo[:]), reads=[oo_d], writes=[Dep()])
        S.barrier()
```
